# Optimizing a Trainium2 kernel written in Bass

```python
import jax
import jax.numpy as jnp
from jax import lax
import numpy as np

D_MODEL = 2048
BATCH = 4
SEQ = 4096
DEPTH = 4

N_META = 16
BLOCK = 128
FRONT_PAD = (-N_META) % BLOCK
N_A_LAYERS = DEPTH // 2
N_B_LAYERS = DEPTH - N_A_LAYERS
RWKV_HEAD = 64
RWKV_HEADS = D_MODEL // RWKV_HEAD
LORA_DECAY = 96
LORA_ICLR = 96
LORA_VRES = 64
LORA_GATE = 256
SB_HEAD = 128
SB_HEADS = D_MODEL // SB_HEAD
D_FF = -(-8 * D_MODEL // (3 * 256)) * 256
RMS_EPS = 1e-6
GN_EPS = 64e-5

kernel_name = 'hybrid_rwkv7_stickbreaking_yoco'


def _rms_norm(x, g):
    xf = x.astype(jnp.float32)
    y = xf * lax.rsqrt(jnp.mean(xf * xf, axis=-1, keepdims=True) + RMS_EPS)
    return (y * g.astype(jnp.float32)).astype(x.dtype)


def _swiglu(x, w_gate, w_up, w_down):
    return (jax.nn.silu(x @ w_gate) * (x @ w_up)) @ w_down


def _token_shift(x):
    return jnp.pad(x, ((0, 0), (1, 0), (0, 0)))[:, :-1]


def _rwkv7_time_mix(h, v_first, mu, w_r, w_k, w_v, w_o, dec_w0, dec_w1, dec_w2,
                    a_w0, a_w1, a_w2, g_w1, g_w2, k_k, k_a, r_k, gn_w, gn_b, vres):
    b, l, d = h.shape
    xx = _token_shift(h) - h
    xr, xw, xk, xv, xa, xg = (h + xx * mu[i] for i in range(6))
    r = xr @ w_r
    k = xk @ w_k
    v = xv @ w_v
    w = -jax.nn.softplus(-(dec_w0 + jnp.tanh(xw @ dec_w1) @ dec_w2)) - 0.5
    decay = jnp.exp(-jnp.exp(w.astype(jnp.float32)))
    a = jax.nn.sigmoid(a_w0 + (xa @ a_w1) @ a_w2)
    g = jax.nn.sigmoid(xg @ g_w1) @ g_w2
    if vres is None:
        v_first = v
    else:
        v_w0, v_w1, v_w2 = vres
        v = v + (v_first - v) * jax.nn.sigmoid(v_w0 + (xv @ v_w1) @ v_w2)
    heads = lambda t: t.reshape(b, l, RWKV_HEADS, RWKV_HEAD).astype(jnp.float32)
    kk = heads(k * k_k)
    kk = kk * lax.rsqrt(jnp.maximum(jnp.sum(kk * kk, axis=-1, keepdims=True), 1e-24))
    k = k * (1.0 + (a - 1.0) * k_a)
    rh, kh, vh, ah, dh = heads(r), heads(k), heads(v), heads(a), heads(decay)

    def step(state, inp):
        r_t, d_t, k_t, v_t, kk_t, a_t = inp
        sa = jnp.einsum('bhvk,bhk->bhv', state, -kk_t)
        state = (state * d_t[:, :, None, :]
                 + sa[..., None] * (kk_t * a_t)[:, :, None, :]
                 + v_t[..., None] * k_t[:, :, None, :])
        return state, jnp.einsum('bhvk,bhk->bhv', state, r_t)

    seq_first = lambda t: jnp.swapaxes(t, 0, 1)
    s0 = jnp.zeros((b, RWKV_HEADS, RWKV_HEAD, RWKV_HEAD), jnp.float32)
    _, y = lax.scan(step, s0, tuple(seq_first(t) for t in (rh, dh, kh, vh, kk, ah)))
    y = seq_first(y)
    mean = jnp.mean(y, axis=-1, keepdims=True)
    var = jnp.mean(jnp.square(y - mean), axis=-1, keepdims=True)
    y = ((y - mean) * lax.rsqrt(var + GN_EPS)).reshape(b, l, d) * gn_w + gn_b
    bonus = (jnp.sum(rh * kh * r_k, axis=-1, keepdims=True) * vh).reshape(b, l, d)
    out = ((y + bonus).astype(h.dtype) * g) @ w_o
    return out, v_first


def _pad_front_heads(t):
    return jnp.pad(t, ((0, 0), (FRONT_PAD, 0), (0, 0), (0, 0))).transpose(0, 2, 1, 3)


def _shared_kv(h, w_k, w_v, k_gain):
    b, l, _ = h.shape
    k = _rms_norm((h @ w_k).reshape(b, l, SB_HEADS, SB_HEAD), k_gain)
    v = (h @ w_v).reshape(b, l, SB_HEADS, SB_HEAD)
    return _pad_front_heads(k), _pad_front_heads(v)


def _stick_breaking(h, k, v, w_q, q_gain, w_o):
    b, l, d = h.shape
    q = _pad_front_heads(_rms_norm((h @ w_q).reshape(b, l, SB_HEADS, SB_HEAD), q_gain))
    p = l + FRONT_PAD
    nb = p // BLOCK
    q_blocks = q.reshape(b, SB_HEADS, nb, BLOCK, SB_HEAD).transpose(2, 0, 1, 3, 4)
    key_pos = jnp.arange(p)
    scale = SB_HEAD ** -0.5

    def one_block(args):
        q_blk, blk = args
        q_pos = blk * BLOCK + jnp.arange(BLOCK)
        valid = (key_pos[None, :] < q_pos[:, None]) & (key_pos[None, :] >= FRONT_PAD)
        z = jnp.einsum('bhqd,bhkd->bhqk', q_blk, k).astype(jnp.float32) * scale
        log_keep = jnp.where(valid, -jax.nn.softplus(z), 0.0)
        later = lax.cumsum(log_keep, axis=3, reverse=True) - log_keep
        weights = jnp.where(valid, jnp.exp(jax.nn.log_sigmoid(z) + later), 0.0)
        return jnp.einsum('bhqk,bhkd->bhqd', weights.astype(v.dtype), v)

    o = lax.map(one_block, (q_blocks, jnp.arange(nb)))
    o = o.transpose(1, 0, 3, 2, 4).reshape(b, p, d)[:, FRONT_PAD:]
    return o @ w_o


def setup_inputs(seed: int = 0) -> dict:
    key = jax.random.key(seed)
    ks = iter(jax.random.split(key, 48))
    f32 = jnp.float32

    def nrm(shape, scale):
        return jax.random.normal(next(ks), shape, f32) * scale

    def unif(shape, lo, hi):
        return jax.random.uniform(next(ks), shape, f32, lo, hi)

    D, F, NA, NB = D_MODEL, D_FF, N_A_LAYERS, N_B_LAYERS
    d_in = D ** -0.5
    d_out = D ** -0.5 * (2 * DEPTH) ** -0.5
    return {
        'x': nrm((BATCH, SEQ, D), 1.0),
        'meta_tokens': nrm((N_META, D), 1.0),
        'mix_norm_g': 1.0 + nrm((DEPTH, D), 0.02),
        'ffn_norm_g': 1.0 + nrm((DEPTH, D), 0.02),
        'ffn_w_gate': nrm((DEPTH, D, F), d_in),
        'ffn_w_up': nrm((DEPTH, D, F), d_in),
        'ffn_w_down': nrm((DEPTH, F, D), F ** -0.5 * (2 * DEPTH) ** -0.5),
        'rwkv_mu': unif((NA, 6, D), 0.0, 1.0),
        'rwkv_w_r': nrm((NA, D, D), d_in),
        'rwkv_w_k': nrm((NA, D, D), d_in),
        'rwkv_w_v': nrm((NA, D, D), d_in),
        'rwkv_w_o': nrm((NA, D, D), d_out),
        'rwkv_dec_w0': unif((NA, D), -6.0, -1.0),
        'rwkv_dec_w1': nrm((NA, D, LORA_DECAY), d_in),
        'rwkv_dec_w2': nrm((NA, LORA_DECAY, D), 0.1 * LORA_DECAY ** -0.5),
        'rwkv_a_w0': nrm((NA, D), 0.1),
        'rwkv_a_w1': nrm((NA, D, LORA_ICLR), d_in),
        'rwkv_a_w2': nrm((NA, LORA_ICLR, D), 0.1 * LORA_ICLR ** -0.5),
        'rwkv_g_w1': nrm((NA, D, LORA_GATE), d_in),
        'rwkv_g_w2': nrm((NA, LORA_GATE, D), LORA_GATE ** -0.5),
        'rwkv_k_k': 0.85 + nrm((NA, D), 0.02),
        'rwkv_k_a': 1.0 + nrm((NA, D), 0.02),
        'rwkv_r_k': nrm((NA, RWKV_HEADS, RWKV_HEAD), 0.1),
        'rwkv_gn_w': 1.0 + nrm((NA, D), 0.02),
        'rwkv_gn_b': nrm((NA, D), 0.02),
        'rwkv_v_w0': 1.0 + nrm((NA - 1, D), 0.1),
        'rwkv_v_w1': nrm((NA - 1, D, LORA_VRES), d_in),
        'rwkv_v_w2': nrm((NA - 1, LORA_VRES, D), 0.1 * LORA_VRES ** -0.5),
        'kv_norm_g': 1.0 + nrm((D,), 0.02),
        'sb_w_k': nrm((D, D), d_in),
        'sb_w_v': nrm((D, D), d_in),
        'sb_k_gain': 1.0 + nrm((SB_HEAD,), 0.02),
        'sb_w_q': nrm((NB, D, D), d_in),
        'sb_q_gain': 1.0 + nrm((NB, SB_HEAD), 0.02),
        'sb_w_o': nrm((NB, D, D), d_out),
    }


def reference(x, meta_tokens, mix_norm_g, ffn_norm_g, ffn_w_gate, ffn_w_up, ffn_w_down,
              rwkv_mu, rwkv_w_r, rwkv_w_k, rwkv_w_v, rwkv_w_o, rwkv_dec_w0, rwkv_dec_w1,
              rwkv_dec_w2, rwkv_a_w0, rwkv_a_w1, rwkv_a_w2, rwkv_g_w1, rwkv_g_w2, rwkv_k_k,
              rwkv_k_a, rwkv_r_k, rwkv_gn_w, rwkv_gn_b, rwkv_v_w0, rwkv_v_w1, rwkv_v_w2,
              kv_norm_g, sb_w_k, sb_w_v, sb_k_gain, sb_w_q, sb_q_gain, sb_w_o):
    b = x.shape[0]
    meta = jnp.broadcast_to(meta_tokens[None].astype(x.dtype), (b, N_META, D_MODEL))
    h = jnp.concatenate([meta, x], axis=1)
    v_first = None
    k_shared = v_shared = None
    for layer in range(DEPTH):
        hn = _rms_norm(h, mix_norm_g[layer])
        if layer < N_A_LAYERS:
            i = layer
            vres = None if i == 0 else (rwkv_v_w0[i - 1], rwkv_v_w1[i - 1], rwkv_v_w2[i - 1])
            mix, v_first = _rwkv7_time_mix(
                hn, v_first, rwkv_mu[i], rwkv_w_r[i], rwkv_w_k[i], rwkv_w_v[i], rwkv_w_o[i],
                rwkv_dec_w0[i], rwkv_dec_w1[i], rwkv_dec_w2[i], rwkv_a_w0[i], rwkv_a_w1[i],
                rwkv_a_w2[i], rwkv_g_w1[i], rwkv_g_w2[i], rwkv_k_k[i], rwkv_k_a[i], rwkv_r_k[i],
                rwkv_gn_w[i], rwkv_gn_b[i], vres)
        else:
            j = layer - N_A_LAYERS
            if j == 0:
                k_shared, v_shared = _shared_kv(_rms_norm(h, kv_norm_g), sb_w_k, sb_w_v, sb_k_gain)
            mix = _stick_breaking(hn, k_shared, v_shared, sb_w_q[j], sb_q_gain[j], sb_w_o[j])
        h = h + mix
        h = h + _swiglu(_rms_norm(h, ffn_norm_g[layer]), ffn_w_gate[layer], ffn_w_up[layer],
                        ffn_w_down[layer])
    return h[:, N_META:]
```

```python
import numpy as np
import ml_dtypes
from contextlib import ExitStack
import concourse.bass as bass
import concourse.mybir as mybir
from concourse.bass_utils import run_bass_kernel_spmd

F32 = mybir.dt.float32
BF16 = mybir.dt.bfloat16
AF = mybir.ActivationFunctionType
ALU = mybir.AluOpType
AX = mybir.AxisListType

D = 2048
KC = 16
FF = 5632
FC = 44
NMETA = 16
RH = 32
SH = 16
RMS_EPS = 1e-6
GN_EPS = 64e-5
ENG = ('pe', 'act', 'dve', 'pool', 'sp')
NDS = 48


_UID = [0]


def un(name):
    _UID[0] += 1
    return '%s_%d' % (name, _UID[0])


class DSem:
    def __init__(self, h):
        self.h = h
        self.count = 0


class Buf:
    __slots__ = ('w', 'r', 'ds', 'name')

    def __init__(self, name=''):
        self.w = None
        self.r = {}
        self.ds = None
        self.name = name


class Sched:
    def __init__(self, nc, es):
        self.nc = nc
        self.eng = {'pe': nc.tensor, 'act': nc.scalar, 'dve': nc.vector, 'pool': nc.gpsimd, 'sp': nc.sync}
        self.sem = {e: es.enter_context(nc.semaphore('s_' + e)) for e in ENG}
        self.cnt = {e: 0 for e in ENG}
        self.seen = {e: {} for e in ENG}
        self.free_ds = [DSem(es.enter_context(nc.semaphore('d%d' % i))) for i in range(NDS)]
        self.used_ds = []
        self.bufs = []
        self.ninst = 0

    def buf(self, name=''):
        b = Buf(name)
        self.bufs.append(b)
        return b

    def _waits(self, e, evs):
        need = {}
        for key, val in evs:
            if key == 'pe' and e == 'pe':
                continue
            if self.seen[e].get(key, 0) >= val:
                continue
            if need.get(key, 0) < val:
                need[key] = val
        for key, val in need.items():
            self.seen[e][key] = val
            h = self.sem[key] if isinstance(key, str) else key.h
            self.eng[e].wait_ge(h, val)
            self.ninst += 1

    @staticmethod
    def _deps(reads, writes):
        evs = []
        for b in reads:
            if b.w is not None:
                evs.append(b.w)
        for b in writes:
            if b.w is not None:
                evs.append(b.w)
            evs.extend(b.r.items())
        return evs

    @staticmethod
    def _mark(ev, reads, writes):
        k, v = ev
        for b in reads:
            if b.r.get(k, 0) < v:
                b.r[k] = v
        for b in writes:
            b.w = ev
            b.r = {}

    def op(self, e, fn, reads=(), writes=(), signal=True):
        self._waits(e, self._deps(reads, writes))
        ins = fn(self.eng[e])
        self.ninst += 1
        if signal:
            self.cnt[e] += 1
            ins.then_inc(self.sem[e], 1)
            ev = (e, self.cnt[e])
        else:
            ev = (e, self.cnt[e] + 1)
        self._mark(ev, reads, writes)
        return ins

    def dma(self, q, fn, reads=(), writes=(), owner=None):
        self._waits(q, self._deps(reads, writes))
        if owner.ds is None:
            owner.ds = self.free_ds.pop()
            self.used_ds.append(owner.ds)
        ds = owner.ds
        ins = fn(self.eng[q])
        self.ninst += 1
        ds.count += 16
        ins.then_inc(ds.h, 16)
        self._mark((ds, ds.count), reads, writes)
        return ins

    def barrier(self):
        evs = [(e, self.cnt[e]) for e in ENG if self.cnt[e] > 0]
        evs += [(ds, ds.count) for ds in self.used_ds]
        for e in ENG:
            self._waits(e, evs)
        for b in self.bufs:
            b.w = None
            b.r = {}
            b.ds = None
        self.free_ds.extend(self.used_ds)
        self.used_ds = []
        self.bufs = []


class Ring:
    def __init__(self, S, es, kind, name, shape, dtype, n):
        nc = S.nc
        self.t = []
        self.b = []
        for i in range(n):
            if kind == 'sb':
                t = es.enter_context(nc.sbuf_tensor(un('%s%d' % (name, i)), shape, dtype))
            else:
                t = es.enter_context(nc.psum_tensor(un('%s%d' % (name, i)), shape, dtype))
            self.t.append(t)
            self.b.append(S.buf(name))
        self.i = 0
        self.n = n

    def next(self):
        i = self.i % self.n
        self.i += 1
        return self.t[i], self.b[i]


def tile1(S, es, name, shape, dtype, kind='sb'):
    r = Ring(S, es, kind, name, shape, dtype, 1)
    return r.t[0], r.b[0]


def groups(T, g):
    out = []
    t0 = 0
    while t0 < T:
        out.append((t0, min(g, T - t0)))
        t0 += g
    return out


def fm(d):
    return d.rearrange("kc p t -> p kc t")


def colvec(w1d, n):
    return w1d.rearrange("(kc p) -> p kc", p=128)


def load_consts(S, es, C):
    nc = S.nc
    t = {}
    t['ident'] = es.enter_context(nc.sbuf_tensor(un('c_ident'), [128, 128], F32))
    t['identb'] = es.enter_context(nc.sbuf_tensor(un('c_identb'), [128, 128], BF16))
    t['ones'] = es.enter_context(nc.sbuf_tensor(un('c_ones'), [128, 128], F32))
    t['eps'] = es.enter_context(nc.sbuf_tensor(un('c_eps'), [128, 2], F32))
    b = S.buf('const')
    S.dma('sp', lambda e: e.dma_start(out=t['ident'][:], in_=C['ident']), writes=[b], owner=b)
    S.dma('pool', lambda e: e.dma_start(out=t['identb'][:], in_=C['ident']), writes=[b], owner=b)
    S.op('dve', lambda e: e.memset(t['ones'][:], 1.0), writes=[b])
    S.op('dve', lambda e: e.memset(t['eps'][:, 0:1], RMS_EPS), writes=[b])
    S.op('dve', lambda e: e.memset(t['eps'][:, 1:2], GN_EPS), writes=[b])
    S.barrier()
    return t


def phase_in_transpose(S, CT, xin, hT, T):
    nc = S.nc
    with ExitStack() as es:
        cb = S.buf('c')
        xr = Ring(S, es, 'sb', 'it_x', [128, D], F32, 3)
        hr = Ring(S, es, 'sb', 'it_h', [128, KC, 512], F32, 2)
        pr = Ring(S, es, 'ps', 'it_p', [128, 512], F32, 4)
        hv = fm(hT)
        for (t0, tn) in groups(T, 512):
            hb, hbb = hr.next()
            for j in range(tn // 128):
                xt, xb = xr.next()
                r0 = t0 + j * 128
                S.dma('sp', lambda e, xt=xt, r0=r0: e.dma_start(out=xt[:], in_=xin[r0:r0 + 128, :]),
                      writes=[xb], owner=xb)
                for q in range(4):
                    ps, pb = pr.next()
                    for i in range(4):
                        kc = 4 * q + i
                        S.op('pe', lambda e, ps=ps, xt=xt, kc=kc, i=i: e.transpose(
                            ps[:, i * 128:(i + 1) * 128], xt[:, kc * 128:(kc + 1) * 128], CT['ident'][:]),
                            reads=[xb, cb], writes=[pb], signal=(i == 3))
                    eng = 'dve' if q % 2 == 0 else 'act'
                    if eng == 'dve':
                        S.op('dve', lambda e, ps=ps, hb=hb, q=q, j=j: e.tensor_copy(
                            out=hb[:, 4 * q:4 * q + 4, j * 128:(j + 1) * 128],
                            in_=ps[:].rearrange("p (a b) -> p a b", a=4)), reads=[pb], writes=[hbb])
                    else:
                        S.op('act', lambda e, ps=ps, hb=hb, q=q, j=j: e.copy(
                            out=hb[:, 4 * q:4 * q + 4, j * 128:(j + 1) * 128],
                            in_=ps[:].rearrange("p (a b) -> p a b", a=4)), reads=[pb], writes=[hbb])
            S.dma('sp', lambda e, hb=hb, t0=t0, tn=tn: e.dma_start(out=hv[:, :, t0:t0 + tn], in_=hb[:, :, 0:tn]),
                  reads=[hbb], owner=hbb)
        S.barrier()


COLS = {}
_ci = 0
for _l in range(4):
    COLS[('ffn_norm_g', _l)] = _ci; _ci += 1
for _l in range(4):
    COLS[('mix_norm_g', _l)] = _ci; _ci += 1
COLS[('kv_norm_g', 0)] = _ci; _ci += 1
for _l in range(2):
    for _i in range(6):
        COLS[('rwkv_mu', _l, _i)] = _ci; _ci += 1
NCOLS = _ci


def pack_cols(inputs):
    out = np.zeros((128, NCOLS, KC), np.float32)
    for key, i in COLS.items():
        a = inputs[key[0]]
        v = a[key[1]] if key[0] != 'kv_norm_g' else a
        if key[0] == 'rwkv_mu':
            v = v[key[2]]
        out[:, i, :] = np.asarray(v).reshape(KC, 128).T
    return out


def load_cols(S, es, name, cols, idxs):
    nc = S.nc
    n = len(idxs)
    t = es.enter_context(nc.sbuf_tensor(un(name), [128, n, KC], F32))
    b = S.buf(name)
    for i, ix in enumerate(idxs):
        S.dma('sp', lambda e, i=i, ix=ix: e.dma_start(out=t[:, i, :], in_=cols[:, ix, :]), writes=[b], owner=b)
    return t, b


def phase_norm(S, CT, hsrc, cols, gidx, xout, T):
    nc = S.nc
    with ExitStack() as es:
        cb = S.buf('c')
        g, gb = load_cols(S, es, 'n_g', cols, [gidx])
        hr = Ring(S, es, 'sb', 'n_h', [128, KC, 512], F32, 2)
        sr = Ring(S, es, 'sb', 'n_sq', [128, 512], F32, 3)
        rr = Ring(S, es, 'sb', 'n_rs', [128, 512], F32, 2)
        orr = Ring(S, es, 'sb', 'n_o', [128, KC, 512], BF16, 2)
        pr = Ring(S, es, 'ps', 'n_p', [128, 512], F32, 2)
        hv = fm(hsrc)
        ov = fm(xout)
        for (t0, tn) in groups(T, 512):
            h, hb = hr.next()
            S.dma('sp', lambda e, h=h, t0=t0, tn=tn: e.dma_start(out=h[:, :, 0:tn], in_=hv[:, :, t0:t0 + tn]),
                  writes=[hb], owner=hb)
            ps, pb = pr.next()
            for kc in range(KC):
                sq, sb_ = sr.next()
                S.op('act', lambda e, sq=sq, h=h, kc=kc, tn=tn: e.activation(
                    out=sq[:, 0:tn], in_=h[:, kc, 0:tn], func=AF.Square), reads=[hb], writes=[sb_])
                S.op('pe', lambda e, ps=ps, sq=sq, kc=kc, tn=tn: e.matmul(
                    ps[:, 0:tn], CT['ones'][:], sq[:, 0:tn], start=(kc == 0), stop=(kc == KC - 1)),
                    reads=[sb_, cb], writes=[pb], signal=True)
            rs, rb = rr.next()
            S.op('act', lambda e, rs=rs, ps=ps, tn=tn: e.activation(
                out=rs[:, 0:tn], in_=ps[:, 0:tn], func=AF.Ln, bias=CT['eps'][:, 0:1], scale=1.0 / D),
                reads=[pb, cb], writes=[rb])
            S.op('act', lambda e, rs=rs, tn=tn: e.activation(
                out=rs[:, 0:tn], in_=rs[:, 0:tn], func=AF.Exp, scale=-0.5), reads=[rb], writes=[rb])
            o, ob = orr.next()
            for kc in range(KC):
                S.op('dve', lambda e, o=o, h=h, rs=rs, kc=kc, tn=tn: e.scalar_tensor_tensor(
                    out=o[:, kc, 0:tn], in0=h[:, kc, 0:tn], scalar=g[:, 0, kc:kc + 1], in1=rs[:, 0:tn],
                    op0=ALU.mult, op1=ALU.mult), reads=[hb, rb, gb], writes=[ob])
            S.dma('sp', lambda e, o=o, t0=t0, tn=tn: e.dma_start(out=ov[:, :, t0:t0 + tn], in_=o[:, :, 0:tn]),
                  reads=[ob], owner=ob)
        S.barrier()


def load_resident(S, es, name, src, kcn, T):
    nc = S.nc
    X = es.enter_context(nc.sbuf_tensor(un(name), [128, kcn, T], BF16))
    Xb = S.buf(name)
    sv = fm(src)
    for k0 in range(0, kcn, 4):
        k1 = min(kcn, k0 + 4)
        S.dma('sp', lambda e, k0=k0, k1=k1: e.dma_start(out=X[:, k0:k1, :], in_=sv[:, k0:k1, 0:T]),
              writes=[Xb], owner=Xb)
    return X, Xb


def wview(W):
    return W.rearrange("(kc p) f -> p kc f", p=128)


def phase_ffn_gateup(S, CT, xn, Wg, Wu, actT, T):
    nc = S.nc
    with ExitStack() as es:
        X, Xb = load_resident(S, es, 'fg_x', xn, KC, T)
        wr = Ring(S, es, 'sb', 'fg_w', [128, 2, KC, 256], BF16, 2)
        ar = Ring(S, es, 'sb', 'fg_a', [128, T], BF16, 2)
        tr = Ring(S, es, 'sb', 'fg_t', [128, 512], F32, 3)
        pg = Ring(S, es, 'ps', 'fg_pg', [128, 512], F32, 3)
        pu = Ring(S, es, 'ps', 'fg_pu', [128, 512], F32, 3)
        wgv = wview(Wg)
        wuv = wview(Wu)
        for og in range(FC // 2):
            w, wb = wr.next()
            c0 = og * 256
            for k0 in range(0, KC, 8):
                S.dma('pool', lambda e, w=w, c0=c0, k0=k0: e.dma_start(out=w[:, 0, k0:k0 + 8, :], in_=wgv[:, k0:k0 + 8, c0:c0 + 256]),
                      writes=[wb], owner=wb)
                S.dma('pool', lambda e, w=w, c0=c0, k0=k0: e.dma_start(out=w[:, 1, k0:k0 + 8, :], in_=wuv[:, k0:k0 + 8, c0:c0 + 256]),
                      writes=[wb], owner=wb)
            for ol in range(2):
                oc = og * 2 + ol
                a, ab = ar.next()
                for (t0, tn) in groups(T, 512):
                    p1, p1b = pg.next()
                    p2, p2b = pu.next()
                    for kc in range(KC):
                        S.op('pe', lambda e, p1=p1, w=w, ol=ol, kc=kc, t0=t0, tn=tn: e.matmul(
                            p1[:, 0:tn], w[:, 0, kc, ol * 128:(ol + 1) * 128], X[:, kc, t0:t0 + tn],
                            start=(kc == 0), stop=(kc == KC - 1)), reads=[wb, Xb], writes=[p1b], signal=(kc == KC - 1))
                    for kc in range(KC):
                        S.op('pe', lambda e, p2=p2, w=w, ol=ol, kc=kc, t0=t0, tn=tn: e.matmul(
                            p2[:, 0:tn], w[:, 1, kc, ol * 128:(ol + 1) * 128], X[:, kc, t0:t0 + tn],
                            start=(kc == 0), stop=(kc == KC - 1)), reads=[wb, Xb], writes=[p2b], signal=(kc == KC - 1))
                    tt, tb = tr.next()
                    S.op('act', lambda e, tt=tt, p1=p1, tn=tn: e.activation(out=tt[:, 0:tn], in_=p1[:, 0:tn], func=AF.Silu),
                         reads=[p1b], writes=[tb])
                    S.op('dve', lambda e, a=a, tt=tt, p2=p2, t0=t0, tn=tn: e.tensor_tensor(
                        out=a[:, t0:t0 + tn], in0=tt[:, 0:tn], in1=p2[:, 0:tn], op=ALU.mult),
                        reads=[tb, p2b], writes=[ab])
                S.dma('sp', lambda e, a=a, oc=oc: e.dma_start(out=actT[oc, :, 0:T], in_=a[:]), reads=[ab], owner=ab)
        S.barrier()


def phase_ffn_down(S, CT, actT, Wd, hT, T):
    nc = S.nc
    with ExitStack() as es:
        wt = es.enter_context(nc.sbuf_tensor(un('fd_w'), [128, FC, 512], BF16))
        wb = S.buf('fd_w')
        ar = Ring(S, es, 'sb', 'fd_a', [128, FC, 512], BF16, 2)
        hr = Ring(S, es, 'sb', 'fd_h', [128, 512], F32, 4)
        pr = Ring(S, es, 'ps', 'fd_p', [128, 512], F32, 4)
        wv = wview(Wd)
        av = fm(actT)
        for q in range(4):
            for k0 in range(0, FC, 11):
                S.dma('pool', lambda e, k0=k0, q=q: e.dma_start(out=wt[:, k0:k0 + 11, :], in_=wv[:, k0:k0 + 11, q * 512:(q + 1) * 512]),
                      writes=[wb], owner=wb)
            for (t0, tn) in groups(T, 512):
                a, ab = ar.next()
                for k0 in range(0, FC, 11):
                    S.dma('sp', lambda e, a=a, k0=k0, t0=t0, tn=tn: e.dma_start(out=a[:, k0:k0 + 11, 0:tn], in_=av[:, k0:k0 + 11, t0:t0 + tn]),
                          writes=[ab], owner=ab)
                for ol in range(4):
                    dc = q * 4 + ol
                    h, hb = hr.next()
                    S.dma('sp', lambda e, h=h, dc=dc, t0=t0, tn=tn: e.dma_start(out=h[:, 0:tn], in_=hT[dc, :, t0:t0 + tn]),
                          writes=[hb], owner=hb)
                    ps, pb = pr.next()
                    for fc in range(FC):
                        S.op('pe', lambda e, ps=ps, a=a, ol=ol, fc=fc, tn=tn: e.matmul(
                            ps[:, 0:tn], wt[:, fc, ol * 128:(ol + 1) * 128], a[:, fc, 0:tn],
                            start=(fc == 0), stop=(fc == FC - 1)), reads=[wb, ab], writes=[pb], signal=(fc == FC - 1))
                    S.op('dve', lambda e, h=h, ps=ps, tn=tn: e.tensor_tensor(
                        out=h[:, 0:tn], in0=h[:, 0:tn], in1=ps[:, 0:tn], op=ALU.add), reads=[pb, hb], writes=[hb])
                    S.dma('sp', lambda e, h=h, dc=dc, t0=t0, tn=tn: e.dma_start(out=hT[dc, :, t0:t0 + tn], in_=h[:, 0:tn]),
                          reads=[hb], owner=hb)
        S.barrier()


def phase_out_transpose(S, CT, hT, out, T, col0):
    nc = S.nc
    with ExitStack() as es:
        cb = S.buf('c')
        hr = Ring(S, es, 'sb', 'ot_h', [128, KC, 128], F32, 3)
        orr = Ring(S, es, 'sb', 'ot_o', [128, D], F32, 2)
        pr = Ring(S, es, 'ps', 'ot_p', [128, 512], F32, 4)
        hv = fm(hT)
        for (t0, tn) in groups(T, 128):
            h, hb = hr.next()
            S.dma('sp', lambda e, h=h, t0=t0: e.dma_start(out=h[:], in_=hv[:, :, col0 + t0:col0 + t0 + 128]),
                  writes=[hb], owner=hb)
            o, ob = orr.next()
            for q in range(4):
                ps, pb = pr.next()
                for i in range(4):
                    kc = 4 * q + i
                    S.op('pe', lambda e, ps=ps, h=h, kc=kc, i=i: e.transpose(
                        ps[:, i * 128:(i + 1) * 128], h[:, kc, :], CT['ident'][:]),
                        reads=[hb, cb], writes=[pb], signal=(i == 3))
                if q % 2 == 0:
                    S.op('dve', lambda e, ps=ps, o=o, q=q: e.tensor_copy(out=o[:, q * 512:(q + 1) * 512], in_=ps[:]),
                         reads=[pb], writes=[ob])
                else:
                    S.op('act', lambda e, ps=ps, o=o, q=q: e.copy(out=o[:, q * 512:(q + 1) * 512], in_=ps[:]),
                         reads=[pb], writes=[ob])
            S.dma('sp', lambda e, o=o, t0=t0: e.dma_start(out=out[t0:t0 + 128, :], in_=o[:]), reads=[ob], owner=ob)
        S.barrier()


def phase_rwkv_mix(S, CT, hT, cols, layer, xmix, T):
    nc = S.nc
    G = 256
    with ExitStack() as es:
        cb = S.buf('c')
        idxs = [COLS[('mix_norm_g', layer)]] + [COLS[('rwkv_mu', layer, i)] for i in range(6)]
        g, gb = load_cols(S, es, 'm_g', cols, idxs)
        hr = Ring(S, es, 'sb', 'm_h', [128, KC, G], F32, 2)
        sr = Ring(S, es, 'sb', 'm_sq', [128, G], F32, 3)
        rr = Ring(S, es, 'sb', 'm_rs', [128, G], F32, 2)
        hnr = Ring(S, es, 'sb', 'm_hn', [128, KC, G + 1], F32, 2)
        xxr = Ring(S, es, 'sb', 'm_xx', [128, KC, G], F32, 1)
        orr = Ring(S, es, 'sb', 'm_o', [128, KC, G], BF16, 6)
        pr = Ring(S, es, 'ps', 'm_p', [128, G], F32, 2)
        hv = fm(hT)
        prev = None
        for (t0, tn) in groups(T, G):
            h, hb = hr.next()
            S.dma('sp', lambda e: e.dma_start(out=h[:, :, 0:tn], in_=hv[:, :, t0:t0 + tn]), writes=[hb], owner=hb)
            ps, pb = pr.next()
            for kc in range(KC):
                sq, sb_ = sr.next()
                S.op('act', lambda e: e.activation(out=sq[:, 0:tn], in_=h[:, kc, 0:tn], func=AF.Square), reads=[hb], writes=[sb_])
                S.op('pe', lambda e: e.matmul(ps[:, 0:tn], CT['ones'][:], sq[:, 0:tn], start=(kc == 0), stop=(kc == KC - 1)),
                     reads=[sb_, cb], writes=[pb], signal=True)
            rs, rb = rr.next()
            S.op('act', lambda e: e.activation(out=rs[:, 0:tn], in_=ps[:, 0:tn], func=AF.Ln, bias=CT['eps'][:, 0:1], scale=1.0 / D),
                 reads=[pb, cb], writes=[rb])
            S.op('act', lambda e: e.activation(out=rs[:, 0:tn], in_=rs[:, 0:tn], func=AF.Exp, scale=-0.5), reads=[rb], writes=[rb])
            hn, hnb = hnr.next()
            if prev is None:
                S.op('pool', lambda e: e.memset(hn[:, :, 0:1], 0.0), writes=[hnb])
            else:
                phn, phnb, ptn = prev
                S.op('pool', lambda e: e.tensor_copy(out=hn[:, :, 0:1], in_=phn[:, :, ptn:ptn + 1]), reads=[phnb], writes=[hnb])
            for kc in range(KC):
                S.op('dve', lambda e: e.scalar_tensor_tensor(
                    out=hn[:, kc, 1:1 + tn], in0=h[:, kc, 0:tn], scalar=g[:, 0, kc:kc + 1], in1=rs[:, 0:tn],
                    op0=ALU.mult, op1=ALU.mult), reads=[hb, rb, gb], writes=[hnb])
            xx, xxb = xxr.next()
            S.op('pool', lambda e: e.tensor_tensor(out=xx[:, :, 0:tn], in0=hn[:, :, 0:tn], in1=hn[:, :, 1:1 + tn], op=ALU.subtract),
                 reads=[hnb], writes=[xxb])
            for i in range(6):
                o, ob = orr.next()
                for kc in range(KC):
                    S.op('dve', lambda e: e.scalar_tensor_tensor(
                        out=o[:, kc, 0:tn], in0=xx[:, kc, 0:tn], scalar=g[:, 1 + i, kc:kc + 1], in1=hn[:, kc, 1:1 + tn],
                        op0=ALU.mult, op1=ALU.add), reads=[xxb, hnb, gb], writes=[ob])
                S.dma('sp', lambda e: e.dma_start(out=fm(xmix[i])[:, :, t0:t0 + tn], in_=o[:, :, 0:tn]), reads=[ob], owner=ob)
            prev = (hn, hnb, tn)
        S.barrier()


def gemm_tm(S, CT, es, X, Xb, T, W, fout, epilogue, wname='gw', pname='gp'):
    wr = Ring(S, es, 'sb', wname, [128, KC, 512], BF16, 2)
    pr = Ring(S, es, 'ps', pname, [128, 512], F32, 4)
    wv = wview(W)
    for og in range(fout // 512):
        w, wb = wr.next()
        for k0 in range(0, KC, 8):
            S.dma('pool', lambda e: e.dma_start(out=w[:, k0:k0 + 8, :], in_=wv[:, k0:k0 + 8, og * 512:(og + 1) * 512]),
                  writes=[wb], owner=wb)
        for (t0, tn) in groups(T, 128):
            ps, pb = pr.next()
            for kc in range(KC):
                S.op('pe', lambda e: e.matmul(ps[:], X[:, kc, t0:t0 + 128], w[:, kc, :], start=(kc == 0), stop=(kc == KC - 1)),
                     reads=[wb, Xb], writes=[pb], signal=(kc == KC - 1))
            epilogue(og, t0, ps, pb)


def phase_proj_tm(S, CT, xsrc, W, out, T, odt):
    nc = S.nc
    with ExitStack() as es:
        X, Xb = load_resident(S, es, 'pj_x', xsrc, KC, T)
        orr = Ring(S, es, 'sb', 'pj_o', [128, 512], odt, 4)
        cnt = [0]

        def epi(og, t0, ps, pb):
            o, ob = orr.next()
            cnt[0] += 1
            if cnt[0] % 2 == 0:
                S.op('dve', lambda e: e.tensor_copy(out=o[:], in_=ps[:]), reads=[pb], writes=[ob])
            else:
                S.op('act', lambda e: e.copy(out=o[:], in_=ps[:]), reads=[pb], writes=[ob])
            S.dma('sp', lambda e: e.dma_start(out=out[t0:t0 + 128, og * 512:(og + 1) * 512], in_=o[:]), reads=[ob], owner=ob)

        gemm_tm(S, CT, es, X, Xb, T, W, D, epi)
        S.barrier()


def phase_lora(S, CT, xsrc, w1, w2, w0row, rdim, hid_func, out_func, out, T, odt):
    nc = S.nc
    nrc = (rdim + 127) // 128
    rp = min(rdim, 128)
    with ExitStack() as es:
        X, Xb = load_resident(S, es, 'lo_x', xsrc, KC, T)
        w1t = es.enter_context(nc.sbuf_tensor(un('lo_w1'), [128, KC, rdim], BF16))
        w1b = S.buf('w1')
        S.dma('pool', lambda e: e.dma_start(out=w1t[:], in_=wview(w1)), writes=[w1b], owner=w1b)
        w2t = es.enter_context(nc.sbuf_tensor(un('lo_w2'), [rp, nrc, D], BF16))
        w2b = S.buf('w2')
        S.dma('pool', lambda e: e.dma_start(out=w2t[:], in_=w2.rearrange("(c p) f -> p c f", p=rp)), writes=[w2b], owner=w2b)
        hid = es.enter_context(nc.sbuf_tensor(un('lo_hid'), [rp, nrc, T], BF16))
        hidb = S.buf('hid')
        if w0row is not None:
            w0t = es.enter_context(nc.sbuf_tensor(un('lo_w0'), [128, D], F32))
            w0b = S.buf('w0')
            S.dma('sp', lambda e: e.dma_start(out=w0t[:], in_=w0row.partition_broadcast(128)), writes=[w0b], owner=w0b)
        pr = Ring(S, es, 'ps', 'lo_p', [128, 512], F32, 3)
        for rc in range(nrc):
            for (t0, tn) in groups(T, 512):
                ps, pb = pr.next()
                for kc in range(KC):
                    S.op('pe', lambda e: e.matmul(ps[0:rp, 0:tn], w1t[:, kc, rc * 128:rc * 128 + rp], X[:, kc, t0:t0 + tn],
                                                  start=(kc == 0), stop=(kc == KC - 1)), reads=[w1b, Xb], writes=[pb], signal=(kc == KC - 1))
                S.op('act', lambda e: e.activation(out=hid[:, rc, t0:t0 + tn], in_=ps[0:rp, 0:tn], func=hid_func), reads=[pb], writes=[hidb])
        orr = Ring(S, es, 'sb', 'lo_o', [128, 512], odt, 4)
        tr = Ring(S, es, 'sb', 'lo_t', [128, 512], F32, 3)
        p2 = Ring(S, es, 'ps', 'lo_p2', [128, 512], F32, 3)
        for og in range(4):
            for (t0, tn) in groups(T, 128):
                ps, pb = p2.next()
                for rc in range(nrc):
                    S.op('pe', lambda e: e.matmul(ps[:], hid[:, rc, t0:t0 + 128], w2t[:, rc, og * 512:(og + 1) * 512],
                                                  start=(rc == 0), stop=(rc == nrc - 1)), reads=[hidb, w2b], writes=[pb], signal=(rc == nrc - 1))
                o, ob = orr.next()
                if w0row is not None:
                    tt, tb = tr.next()
                    S.op('dve', lambda e: e.tensor_tensor(out=tt[:], in0=ps[:], in1=w0t[:, og * 512:(og + 1) * 512], op=ALU.add),
                         reads=[pb, w0b], writes=[tb])
                    S.op('act', lambda e: e.activation(out=o[:], in_=tt[:], func=out_func), reads=[tb], writes=[ob])
                else:
                    S.op('act', lambda e: e.activation(out=o[:], in_=ps[:], func=out_func), reads=[pb], writes=[ob])
                S.dma('sp', lambda e: e.dma_start(out=out[t0:t0 + 128, og * 512:(og + 1) * 512], in_=o[:]), reads=[ob], owner=ob)
        S.barrier()

DEC_C = 0.6065306597126334
import os as _os
PREP_STAGE = int(_os.environ.get('PREP_STAGE', '9'))
SCAN_STAGE = int(_os.environ.get('SCAN_STAGE', '9'))
RUN_SCAN = _os.environ.get('RUN_SCAN', '1') == '1'
SCORE_MODE = int(_os.environ.get('SCORE_MODE', '2'))


def bc3(ap2, n):
    return ap2.unsqueeze(2).broadcast_to([128, ap2.shape[1], n])


def hv3(ap2):
    return ap2.rearrange("p (h c) -> p h c", c=64)


def v4(ap2, a):
    return ap2.rearrange("p (a b) -> p a b", a=a)


def load_bc(S, es, name, row, dt=F32):
    nc = S.nc
    t = es.enter_context(nc.sbuf_tensor(un(name), [128, D], dt))
    b = S.buf(name)
    S.dma('sp', lambda e: e.dma_start(out=t[:], in_=row.partition_broadcast(128)), writes=[b], owner=b)
    return t, b


def phase_rwkv_prep(S, CT, I, layer, Z, T):
    nc = S.nc
    with ExitStack() as es:
        cb = S.buf('c')
        kkbc, kkb = load_bc(S, es, 'pp_kk', I['rwkv_k_k'][layer])
        kabc, kab = load_bc(S, es, 'pp_ka', I['rwkv_k_a'][layer])
        rkbc, rkb = load_bc(S, es, 'pp_rk', I['rwkv_r_k'][layer].rearrange("h c -> (h c)"))
        cm = es.enter_context(nc.sbuf_tensor(un('pp_cm'), [128, 128], F32))
        cmb = S.buf('cm')
        S.dma('sp', lambda e: e.dma_start(out=cm[:], in_=I['cmask'][0]), writes=[cmb], owner=cmb)
        negc = es.enter_context(nc.sbuf_tensor(un('pp_negc'), [128, 128], F32))
        S.op('dve', lambda e: e.memset(negc[:], -DEC_C), writes=[cmb])
        r_in = Ring(S, es, 'sb', 'pp_r', [128, D], BF16, 2)
        k_in = Ring(S, es, 'sb', 'pp_k', [128, D], BF16, 2)
        a_in = Ring(S, es, 'sb', 'pp_a', [128, D], BF16, 2)
        v_in = Ring(S, es, 'sb', 'pp_v', [128, D], BF16, 2)
        s_in = Ring(S, es, 'sb', 'pp_s', [128, D], F32, 2)
        if layer == 1:
            vg_in = Ring(S, es, 'sb', 'pp_vg', [128, D], BF16, 1)
            vf_in = Ring(S, es, 'sb', 'pp_vf', [128, D], BF16, 1)
            o_vv = Ring(S, es, 'sb', 'pp_ovv', [128, D], BF16, 2)
        T1, T1b = tile1(S, es, 'pp_t1', [128, D], F32)
        T2, T2b = tile1(S, es, 'pp_t2', [128, D], F32)
        T3, T3b = tile1(S, es, 'pp_t3', [128, D], F32)
        T4, T4b = tile1(S, es, 'pp_t4', [128, D], F32)
        ss, ssb = tile1(S, es, 'pp_ss', [128, 64], F32)
        Er = Ring(S, es, 'sb', 'pp_e', [128, 512], F32, 4)
        o_r = Ring(S, es, 'sb', 'pp_or', [128, D], BF16, 2)
        o_kp = Ring(S, es, 'sb', 'pp_okp', [128, D], BF16, 2)
        o_kt = Ring(S, es, 'sb', 'pp_okt', [128, D], BF16, 2)
        o_bn = Ring(S, es, 'sb', 'pp_obn', [128, D], BF16, 2)
        o_bo = Ring(S, es, 'sb', 'pp_obo', [128, D], BF16, 2)
        xt_r = Ring(S, es, 'sb', 'pp_xt', [128, KC, 128], BF16, 4)
        gm_r = Ring(S, es, 'sb', 'pp_gm', [128, KC], F32, 2)
        pcw = Ring(S, es, 'ps', 'pp_pcw', [128, 512], F32, 4)
        ptr = Ring(S, es, 'ps', 'pp_ptr', [128, 1024], BF16, 2)
        for ci, (t0, tn) in enumerate(groups(T, 128)):
            def ld(ring, src):
                t, b = ring.next()
                S.dma('sp', lambda e: e.dma_start(out=t[:], in_=src[t0:t0 + 128, :]), writes=[b], owner=b)
                return t, b
            r, rb = ld(r_in, Z['r'])
            k, kb = ld(k_in, Z['k'])
            a, ab = ld(a_in, Z['a'])
            v, vb = ld(v_in, Z['v'])
            sg, sgb = ld(s_in, Z['sigd'])
            if layer == 1:
                vg, vgb = ld(vg_in, Z['vg'])
                vf, vfb = ld(vf_in, Z['vfirst'])
                vv, vvb = o_vv.next()
                S.op('dve', lambda e: e.tensor_tensor(out=T4[:], in0=vf[:], in1=v[:], op=ALU.subtract), reads=[vfb, vb], writes=[T4b])
                S.op('dve', lambda e: e.tensor_tensor(out=T4[:], in0=T4[:], in1=vg[:], op=ALU.mult), reads=[vgb, T4b], writes=[T4b])
                S.op('dve', lambda e: e.tensor_tensor(out=vv[:], in0=T4[:], in1=v[:], op=ALU.add), reads=[vb, T4b], writes=[vvb])
                S.dma('sp', lambda e: e.dma_start(out=Z['vv'][t0:t0 + 128, :], in_=vv[:]), reads=[vvb], owner=vvb)
            else:
                vv, vvb = v, vb
            S.op('dve', lambda e: e.tensor_tensor(out=T1[:], in0=k[:], in1=kkbc[:], op=ALU.mult), reads=[kb, kkb], writes=[T1b])
            S.op('act', lambda e: e.activation(out=T2[:], in_=T1[:], func=AF.Square), reads=[T1b], writes=[T2b])
            S.op('dve', lambda e: e.tensor_reduce(out=ss[:, 0:32], in_=hv3(T2[:]), axis=AX.X, op=ALU.add), reads=[T2b], writes=[ssb])
            S.op('dve', lambda e: e.tensor_scalar_max(out=ss[:, 0:32], in0=ss[:, 0:32], scalar1=1e-24), reads=[ssb], writes=[ssb])
            S.op('act', lambda e: e.activation(out=ss[:, 0:32], in_=ss[:, 0:32], func=AF.Ln), reads=[ssb], writes=[ssb])
            S.op('act', lambda e: e.activation(out=ss[:, 0:32], in_=ss[:, 0:32], func=AF.Exp, scale=-0.5), reads=[ssb], writes=[ssb])
            S.op('dve', lambda e: e.tensor_tensor(out=hv3(T1[:]), in0=hv3(T1[:]), in1=bc3(ss[:, 0:32], 64), op=ALU.mult),
                 reads=[ssb, T1b], writes=[T1b])
            if PREP_STAGE < 2:
                continue
            S.op('dve', lambda e: e.scalar_tensor_tensor(out=T2[:], in0=a[:], scalar=-1.0, in1=kabc[:], op0=ALU.add, op1=ALU.mult),
                 reads=[ab, kab], writes=[T2b])
            S.op('dve', lambda e: e.scalar_tensor_tensor(out=T2[:], in0=T2[:], scalar=1.0, in1=k[:], op0=ALU.add, op1=ALU.mult),
                 reads=[kb, T2b], writes=[T2b])
            S.op('dve', lambda e: e.scalar_tensor_tensor(out=T3[:], in0=T1[:], scalar=-1.0, in1=a[:], op0=ALU.mult, op1=ALU.mult),
                 reads=[T1b, ab], writes=[T3b])
            S.op('dve', lambda e: e.tensor_tensor(out=T4[:], in0=r[:], in1=T2[:], op=ALU.mult), reads=[rb, T2b], writes=[T4b])
            S.op('pool', lambda e: e.tensor_tensor(out=T4[:], in0=T4[:], in1=rkbc[:], op=ALU.mult), reads=[rkb, T4b], writes=[T4b])
            S.op('dve', lambda e: e.tensor_reduce(out=ss[:, 32:64], in_=hv3(T4[:]), axis=AX.X, op=ALU.add), reads=[T4b], writes=[ssb])
            bo, bob = o_bo.next()
            S.op('pool', lambda e: e.tensor_tensor(out=hv3(bo[:]), in0=hv3(vv[:]), in1=bc3(ss[:, 32:64], 64), op=ALU.mult),
                 reads=[ssb, vvb], writes=[bob])
            S.dma('sp', lambda e: e.dma_start(out=Z['bonus'][t0:t0 + 128, :], in_=bo[:]), reads=[bob], owner=bob)
            if PREP_STAGE < 3:
                continue
            orr_, orb = o_r.next()
            okp, okpb = o_kp.next()
            okt, oktb = o_kt.next()
            obn, obnb = o_bn.next()
            for og in range(4):
                sl = slice(og * 512, (og + 1) * 512)
                ps, pb = pcw.next()
                S.op('pe', lambda e: e.matmul(ps[:], cm[:], sg[:, sl], start=True, stop=True), reads=[cmb, sgb], writes=[pb])
                E1, E1b = Er.next()
                S.op('act', lambda e: e.activation(out=E1[:], in_=ps[:], func=AF.Exp), reads=[pb], writes=[E1b])
                S.op('pool', lambda e: e.tensor_tensor(out=orr_[:, sl], in0=r[:, sl], in1=E1[:], op=ALU.mult), reads=[rb, E1b], writes=[orb])
                E2, E2b = Er.next()
                S.op('act', lambda e: e.activation(out=E2[:], in_=ps[:], func=AF.Exp, scale=-1.0), reads=[pb], writes=[E2b])
                S.op('pool', lambda e: e.tensor_tensor(out=okt[:, sl], in0=T2[:, sl], in1=E2[:], op=ALU.mult), reads=[T2b, E2b], writes=[oktb])
                S.op('dve', lambda e: e.tensor_tensor(out=obn[:, sl], in0=T3[:, sl], in1=E2[:], op=ALU.mult), reads=[T3b, E2b], writes=[obnb])
                E3, E3b = Er.next()
                S.op('dve', lambda e: e.scalar_tensor_tensor(out=E3[:], in0=sg[:, sl], scalar=DEC_C, in1=ps[:], op0=ALU.mult, op1=ALU.add),
                     reads=[sgb, pb], writes=[E3b])
                S.op('act', lambda e: e.activation(out=E3[:], in_=E3[:], func=AF.Exp), reads=[E3b], writes=[E3b])
                S.op('pool', lambda e: e.tensor_tensor(out=okp[:, sl], in0=T1[:, sl], in1=E3[:], op=ALU.mult), reads=[T1b, E3b], writes=[okpb])
            S.dma('sp', lambda e: e.dma_start(out=Z['kt'][t0:t0 + 128, :], in_=okt[:]), reads=[oktb], owner=oktb)
            S.dma('sp', lambda e: e.dma_start(out=Z['bn'][t0:t0 + 128, :], in_=obn[:]), reads=[obnb], owner=obnb)
            if PREP_STAGE < 4:
                continue
            gm, gmb = gm_r.next()
            for q4 in range(4):
                pg, pgb = pcw.next()
                for i4 in range(4):
                    kc = 4 * q4 + i4
                    S.op('pe', lambda e: e.matmul(pg[:, i4 * 128:(i4 + 1) * 128], sg[:, kc * 128:(kc + 1) * 128], negc[:], start=True, stop=True),
                         reads=[sgb, cmb], writes=[pgb], signal=(i4 == 3))
                S.op('act', lambda e: e.activation(out=gm[:, 4 * q4:4 * q4 + 4].unsqueeze(2), in_=v4(pg[:], 4)[:, :, 0:1], func=AF.Exp), reads=[pgb], writes=[gmb])
            S.dma('sp', lambda e: e.dma_start(out=Z['gam'][ci], in_=gm[:]), reads=[gmb], owner=gmb)
            if PREP_STAGE < 5:
                continue
            for (src, srcb, dst) in ((orr_, orb, 'rT'), (okp, okpb, 'kpT'), (okt, oktb, 'ktT'), (obn, obnb, 'bnT')):
                xt, xtb = xt_r.next()
                for half in range(2):
                    pt, ptb = ptr.next()
                    for j in range(8):
                        kc = half * 8 + j
                        S.op('pe', lambda e: e.transpose(pt[:, j * 128:(j + 1) * 128], src[:, kc * 128:(kc + 1) * 128], CT['identb'][:]),
                             reads=[srcb, cb], writes=[ptb], signal=(j == 7))
                    if half == 0:
                        S.op('act', lambda e: e.copy(out=xt[:, 0:8, :], in_=v4(pt[:], 8)), reads=[ptb], writes=[xtb])
                    else:
                        S.op('dve', lambda e: e.tensor_copy(out=xt[:, 8:16, :], in_=v4(pt[:], 8)), reads=[ptb], writes=[xtb])
                S.dma('sp', lambda e: e.dma_start(out=fm(Z[dst])[:, :, t0:t0 + 128], in_=xt[:]), reads=[xtb], owner=xtb)
        S.barrier()


def phase_rwkv_scan(S, CT, I, layer, Z, T):
    nc = S.nc
    with ExitStack() as es:
        cb = S.buf('c')
        gnw, gnwb = load_bc(S, es, 'sc_gnw', I['rwkv_gn_w'][layer])
        gnb, gnbb = load_bc(S, es, 'sc_gnb', I['rwkv_gn_b'][layer])
        msk = es.enter_context(nc.sbuf_tensor(un('sc_msk'), [128, 3, 128], F32))
        mb = S.buf('msk')
        for i in range(3):
            S.dma('sp', lambda e: e.dma_start(out=msk[:, i, :], in_=I['cmask'][1 + i]), writes=[mb], owner=mb)
        MUS, MUI, MLS = 0, 1, 2

        def mbc(i, n):
            return msk[:, i, :].unsqueeze(1).broadcast_to([128, n, 128])
        A, Ab = tile1(S, es, 'sc_A', [64, RH, 64], F32)
        Abf, Abfb = tile1(S, es, 'sc_Abf', [64, RH, 64], BF16)
        S.op('dve', lambda e: e.memset(A[:], 0.0), writes=[Ab])
        S.op('dve', lambda e: e.memset(Abf[:], 0.0), writes=[Abfb])
        names_cm = ('rT', 'kpT', 'ktT', 'bnT')
        cm_r = {n: Ring(S, es, 'sb', 'sc_' + n, [64, RH, 128], BF16, 1) for n in names_cm}
        tm_r = {n: Ring(S, es, 'sb', 'sc_' + n, [128, D], BF16, 2 if n in ('kt', 'bn', 'vv') else 1) for n in ('kt', 'bn', 'vv', 'bonus', 'g')}
        gm_r = Ring(S, es, 'sb', 'sc_gam', [64, 2, KC], F32, 2)
        sc_t = {n: tile1(S, es, 'sc_s' + n, [128, 16, 128], BF16) for n in ('MkT', 'RKT', 'RBT', 'N0', 'N1', 'NT0', 'NT1', 'Q0', 'Q1')}
        stg = Ring(S, es, 'sb', 'sc_stg', [128, 512], BF16, 2)
        RHS, RHSb = tile1(S, es, 'sc_rhs', [128, 1024], BF16)
        U, Ub = tile1(S, es, 'sc_u', [128, 1024], BF16)
        y, yb = tile1(S, es, 'sc_y', [128, D], F32)
        yt, ytb = tile1(S, es, 'sc_yt', [128, D], F32)
        st, stb = tile1(S, es, 'sc_st', [128, 5, 32], F32)
        z_r = Ring(S, es, 'sb', 'sc_z', [128, D], BF16, 1)
        zT_r = Ring(S, es, 'sb', 'sc_zT', [128, KC, 128], BF16, 2)
        P = Ring(S, es, 'ps', 'sc_p', [128, 512], F32, 6)
        PT = Ring(S, es, 'ps', 'sc_pt', [128, 1024], BF16, 2)
        ecount = [0]

        def cmview(d):
            return d.rearrange("hp (e c) t -> c (hp e) t", e=2)
        for ci, (t0, tn) in enumerate(groups(T, 128)):
            cmt = {}
            for n in names_cm:
                t, b = cm_r[n].next()
                for h0 in range(0, RH, 16):
                    S.dma('sp', lambda e: e.dma_start(out=t[:, h0:h0 + 16, :], in_=cmview(Z[n])[:, h0:h0 + 16, t0:t0 + 128]), writes=[b], owner=b)
                cmt[n] = (t, b)
            tmt = {}
            for n in ('kt', 'bn', 'vv', 'bonus', 'g'):
                t, b = tm_r[n].next()
                src = Z[n] if not (n == 'vv' and layer == 0) else Z['v']
                S.dma('sp', lambda e: e.dma_start(out=t[:], in_=src[t0:t0 + 128, :]), writes=[b], owner=b)
                tmt[n] = (t, b)
            gam, gamb = gm_r.next()
            S.dma('sp', lambda e: e.dma_start(out=gam[:], in_=Z['gam'][ci].rearrange("(e c) hp -> c e hp", e=2)), writes=[gamb], owner=gamb)
            rT, rTb = cmt['rT']
            kpT, kpTb = cmt['kpT']
            ktT, ktTb = cmt['ktT']
            bnT, bnTb = cmt['bnT']
            kt, ktb = tmt['kt']
            bn, bnb = tmt['bn']
            vv, vvb = tmt['vv']
            for hh in range(2):
                H0 = 16 * hh
                kinds = (('MkT', ktT, ktTb, kpT, kpTb, MUS), ('N0', bnT, bnTb, kpT, kpTb, MUS), ('NT0', kpT, kpTb, bnT, bnTb, MLS),
                         ('RKT', ktT, ktTb, rT, rTb, MUI), ('RBT', bnT, bnTb, rT, rTb, MUI))
                for (dn, lt, ltb, rt, rtb, mi) in kinds:
                    dst, dstb = sc_t[dn]
                    for gq in range(4):
                        ps, pb = P.next()
                        for j in range(4):
                            h = H0 + 4 * gq + j
                            S.op('pe', lambda e: e.matmul(ps[:, j * 128:(j + 1) * 128], lt[:, h, :], rt[:, h, :], start=True, stop=True),
                                 reads=[ltb, rtb], writes=[pb], signal=(j == 3))
                        ecount[0] += 1
                        if ecount[0] % 2 == 0:
                            S.op('dve', lambda e: e.tensor_tensor(out=dst[:, 4 * gq:4 * gq + 4, :], in0=v4(ps[:], 4), in1=mbc(mi, 4), op=ALU.mult),
                                 reads=[pb, mb], writes=[dstb])
                        else:
                            sg_, sgb_ = stg.next()
                            S.op('act', lambda e: e.copy(out=sg_[:], in_=ps[:]), reads=[pb], writes=[sgb_])
                            S.op('pool', lambda e: e.tensor_tensor(out=dst[:, 4 * gq:4 * gq + 4, :], in0=v4(sg_[:], 4), in1=mbc(mi, 4), op=ALU.mult),
                                 reads=[sgb_, mb], writes=[dstb])
                if SCAN_STAGE < 2:
                    continue
                Ncur, Ncb = sc_t['N0']
                NTcur, NTcb = sc_t['NT0']
                Nnx, Nnb = sc_t['N1']
                NTnx, NTnb = sc_t['NT1']
                Qc, Qcb = sc_t['Q0']
                Qn, Qnb = sc_t['Q1']
                S.op('pool', lambda e: e.tensor_tensor(out=Qc[:], in0=Ncur[:], in1=CT['identb'][:].unsqueeze(1).broadcast_to([128, 16, 128]), op=ALU.add),
                     reads=[Ncb, cb], writes=[Qcb])
                for lev in range(1, 7):
                    for gq in range(4):
                        ps, pb = P.next()
                        for j in range(4):
                            hx = 4 * gq + j
                            S.op('pe', lambda e: e.matmul(ps[:, j * 128:(j + 1) * 128], Ncur[:, hx, :], NTcur[:, hx, :], start=True, stop=True),
                                 reads=[Ncb, NTcb], writes=[pb], signal=(j == 3))
                        S.op('act', lambda e: e.copy(out=NTnx[:, 4 * gq:4 * gq + 4, :], in_=v4(ps[:], 4)), reads=[pb], writes=[NTnb])
                        if lev < 6:
                            ps2, pb2 = P.next()
                            for j in range(4):
                                hx = 4 * gq + j
                                S.op('pe', lambda e: e.matmul(ps2[:, j * 128:(j + 1) * 128], NTcur[:, hx, :], Ncur[:, hx, :], start=True, stop=True),
                                     reads=[Ncb, NTcb], writes=[pb2], signal=(j == 3))
                            S.op('dve', lambda e: e.tensor_copy(out=Nnx[:, 4 * gq:4 * gq + 4, :], in_=v4(ps2[:], 4)), reads=[pb2], writes=[Nnb])
                    for gq in range(4):
                        ps, pb = P.next()
                        for j in range(4):
                            hx = 4 * gq + j
                            S.op('pe', lambda e: e.matmul(ps[:, j * 128:(j + 1) * 128], NTnx[:, hx, :], Qc[:, hx, :], start=True, stop=True),
                                 reads=[NTnb, Qcb], writes=[pb], signal=(j == 3))
                        S.op('dve', lambda e: e.tensor_tensor(out=Qn[:, 4 * gq:4 * gq + 4, :], in0=v4(ps[:], 4), in1=Qc[:, 4 * gq:4 * gq + 4, :], op=ALU.add),
                             reads=[pb, Qcb], writes=[Qnb])
                    Ncur, Ncb, Nnx, Nnb = Nnx, Nnb, Ncur, Ncb
                    NTcur, NTcb, NTnx, NTnb = NTnx, NTnb, NTcur, NTcb
                    Qc, Qcb, Qn, Qnb = Qn, Qnb, Qc, Qcb
                if SCAN_STAGE < 3:
                    continue
                MkT, MkTb = sc_t['MkT']
                RKT, RKTb = sc_t['RKT']
                RBT, RBTb = sc_t['RBT']
                for g8 in range(2):
                    ps, pb = P.next()
                    for j in range(8):
                        hx = 8 * g8 + j
                        h = H0 + hx
                        S.op('pe', lambda e: e.matmul(ps[:, j * 64:(j + 1) * 64], kpT[:, h, :], Abf[:, h, :], start=True, stop=False),
                             reads=[kpTb, Abfb], writes=[pb], signal=False)
                        S.op('pe', lambda e: e.matmul(ps[:, j * 64:(j + 1) * 64], MkT[:, hx, :], vv[:, h * 64:(h + 1) * 64], start=False, stop=True),
                             reads=[MkTb, vvb], writes=[pb], signal=(j == 7))
                    S.op('act', lambda e: e.copy(out=RHS[:, g8 * 512:(g8 + 1) * 512], in_=ps[:]), reads=[pb], writes=[RHSb])
                for g8 in range(2):
                    ps, pb = P.next()
                    for j in range(8):
                        hx = 8 * g8 + j
                        S.op('pe', lambda e: e.matmul(ps[:, j * 64:(j + 1) * 64], Qc[:, hx, :], RHS[:, hx * 64:(hx + 1) * 64], start=True, stop=True),
                             reads=[Qcb, RHSb], writes=[pb], signal=(j == 7))
                    S.op('act', lambda e: e.copy(out=U[:, g8 * 512:(g8 + 1) * 512], in_=ps[:]), reads=[pb], writes=[Ub])
                for g8 in range(2):
                    ps, pb = P.next()
                    for j in range(8):
                        hx = 8 * g8 + j
                        h = H0 + hx
                        S.op('pe', lambda e: e.matmul(ps[:, j * 64:(j + 1) * 64], rT[:, h, :], Abf[:, h, :], start=True, stop=False),
                             reads=[rTb, Abfb], writes=[pb], signal=False)
                        S.op('pe', lambda e: e.matmul(ps[:, j * 64:(j + 1) * 64], RKT[:, hx, :], vv[:, h * 64:(h + 1) * 64], start=False, stop=False),
                             reads=[RKTb, vvb], writes=[pb], signal=False)
                        S.op('pe', lambda e: e.matmul(ps[:, j * 64:(j + 1) * 64], RBT[:, hx, :], U[:, hx * 64:(hx + 1) * 64], start=False, stop=True),
                             reads=[RBTb, Ub], writes=[pb], signal=(j == 7))
                    c0 = (H0 + 8 * g8) * 64
                    S.op('dve', lambda e: e.tensor_copy(out=y[:, c0:c0 + 512], in_=ps[:]), reads=[pb], writes=[yb])
                if SCAN_STAGE < 4:
                    continue
                for g8 in range(2):
                    ps, pb = P.next()
                    for j in range(8):
                        hx = 8 * g8 + j
                        h = H0 + hx
                        S.op('pe', lambda e: e.matmul(ps[0:64, j * 64:(j + 1) * 64], kt[:, h * 64:(h + 1) * 64], vv[:, h * 64:(h + 1) * 64], start=True, stop=False),
                             reads=[ktb, vvb], writes=[pb], signal=False)
                        S.op('pe', lambda e: e.matmul(ps[0:64, j * 64:(j + 1) * 64], bn[:, h * 64:(h + 1) * 64], U[:, hx * 64:(hx + 1) * 64], start=False, stop=True),
                             reads=[bnb, Ub], writes=[pb], signal=(j == 7))
                    h0 = H0 + 8 * g8
                    hp0 = h0 // 2
                    S.op('dve', lambda e: e.tensor_tensor(out=A[:, h0:h0 + 8, :], in0=ps[0:64, :].rearrange("p (a b) -> p a b", a=8), in1=A[:, h0:h0 + 8, :], op=ALU.add),
                         reads=[pb, Ab], writes=[Ab])
                    S.op('pool', lambda e: e.tensor_tensor(
                        out=A[:, h0:h0 + 8, :].rearrange("c (hp e) v -> c hp e v", e=2),
                        in0=A[:, h0:h0 + 8, :].rearrange("c (hp e) v -> c hp e v", e=2),
                        in1=gam[:].rearrange("c e hp -> c hp e")[:, hp0:hp0 + 4, :].unsqueeze(3).broadcast_to([64, 4, 2, 64]), op=ALU.mult),
                        reads=[gamb, Ab], writes=[Ab])
                    S.op('act', lambda e: e.copy(out=Abf[:, h0:h0 + 8, :], in_=A[:, h0:h0 + 8, :]), reads=[Ab], writes=[Abfb])
            if SCAN_STAGE < 5:
                continue
            bo, bob = tmt['bonus']
            gg, ggb = tmt['g']
            S.op('dve', lambda e: e.tensor_reduce(out=st[:, 0, :], in_=hv3(y[:]), axis=AX.X, op=ALU.add), reads=[yb], writes=[stb])
            S.op('act', lambda e: e.activation(out=yt[:], in_=y[:], func=AF.Square), reads=[yb], writes=[ytb])
            S.op('dve', lambda e: e.tensor_reduce(out=st[:, 1, :], in_=hv3(yt[:]), axis=AX.X, op=ALU.add), reads=[ytb], writes=[stb])
            S.op('dve', lambda e: e.tensor_scalar_mul(out=st[:, 0, :], in0=st[:, 0, :], scalar1=1.0 / 64), reads=[stb], writes=[stb])
            S.op('dve', lambda e: e.tensor_tensor(out=st[:, 2, :], in0=st[:, 0, :], in1=st[:, 0, :], op=ALU.mult), reads=[stb], writes=[stb])
            S.op('dve', lambda e: e.scalar_tensor_tensor(out=st[:, 3, :], in0=st[:, 1, :], scalar=1.0 / 64, in1=st[:, 2, :], op0=ALU.mult, op1=ALU.subtract),
                 reads=[stb], writes=[stb])
            S.op('act', lambda e: e.activation(out=st[:, 3, :], in_=st[:, 3, :], func=AF.Ln, bias=CT['eps'][:, 1:2], scale=1.0), reads=[stb, cb], writes=[stb])
            S.op('act', lambda e: e.activation(out=st[:, 3, :], in_=st[:, 3, :], func=AF.Exp, scale=-0.5), reads=[stb], writes=[stb])
            S.op('dve', lambda e: e.tensor_tensor(out=hv3(yt[:]), in0=hv3(y[:]), in1=bc3(st[:, 0, :], 64), op=ALU.subtract), reads=[yb, stb], writes=[ytb])
            S.op('dve', lambda e: e.tensor_tensor(out=hv3(yt[:]), in0=hv3(yt[:]), in1=bc3(st[:, 3, :], 64), op=ALU.mult), reads=[stb, ytb], writes=[ytb])
            S.op('pool', lambda e: e.tensor_tensor(out=yt[:], in0=yt[:], in1=gnw[:], op=ALU.mult), reads=[gnwb, ytb], writes=[ytb])
            S.op('pool', lambda e: e.tensor_tensor(out=yt[:], in0=yt[:], in1=gnb[:], op=ALU.add), reads=[gnbb, ytb], writes=[ytb])
            S.op('pool', lambda e: e.tensor_tensor(out=yt[:], in0=yt[:], in1=bo[:], op=ALU.add), reads=[bob, ytb], writes=[ytb])
            z, zb = z_r.next()
            S.op('pool', lambda e: e.tensor_tensor(out=z[:], in0=yt[:], in1=gg[:], op=ALU.mult), reads=[ggb, ytb], writes=[zb])
            zT, zTb = zT_r.next()
            for half in range(2):
                pt, ptb = PT.next()
                for j in range(8):
                    kc = half * 8 + j
                    S.op('pe', lambda e: e.transpose(pt[:, j * 128:(j + 1) * 128], z[:, kc * 128:(kc + 1) * 128], CT['identb'][:]),
                         reads=[zb, cb], writes=[ptb], signal=(j == 7))
                S.op('act', lambda e: e.copy(out=zT[:, half * 8:half * 8 + 8, :], in_=v4(pt[:], 8)), reads=[ptb], writes=[zTb])
            S.dma('sp', lambda e: e.dma_start(out=fm(Z['zT'])[:, :, t0:t0 + 128], in_=zT[:]), reads=[zTb], owner=zTb)
        S.barrier()


def phase_proj_fm_res(S, CT, xsrc, W, hres, T, kcn=KC):
    nc = S.nc
    with ExitStack() as es:
        X, Xb = load_resident(S, es, 'po_x', xsrc, kcn, T)
        wr = Ring(S, es, 'sb', 'po_w', [128, kcn, 256], BF16, 2)
        hr = Ring(S, es, 'sb', 'po_h', [128, 512], F32, 4)
        pr = Ring(S, es, 'ps', 'po_p', [128, 512], F32, 4)
        wv = wview(W)
        for og in range(D // 256):
            w, wb = wr.next()
            S.dma('pool', lambda e: e.dma_start(out=w[:], in_=wv[:, :, og * 256:(og + 1) * 256]), writes=[wb], owner=wb)
            for ol in range(2):
                dc = og * 2 + ol
                for (t0, tn) in groups(T, 512):
                    h, hb = hr.next()
                    S.dma('sp', lambda e: e.dma_start(out=h[:, 0:tn], in_=hres[dc, :, t0:t0 + tn]), writes=[hb], owner=hb)
                    ps, pb = pr.next()
                    for kc in range(kcn):
                        S.op('pe', lambda e: e.matmul(ps[:, 0:tn], w[:, kc, ol * 128:(ol + 1) * 128], X[:, kc, t0:t0 + tn],
                                                      start=(kc == 0), stop=(kc == kcn - 1)), reads=[wb, Xb], writes=[pb], signal=(kc == kcn - 1))
                    S.op('dve', lambda e: e.tensor_tensor(out=h[:, 0:tn], in0=h[:, 0:tn], in1=ps[:, 0:tn], op=ALU.add), reads=[pb, hb], writes=[hb])
                    S.dma('sp', lambda e: e.dma_start(out=hres[dc, :, t0:t0 + tn], in_=h[:, 0:tn]), reads=[hb], owner=hb)
        S.barrier()


def rwkv_layer(S, CT, I, layer, Z, hT, T):
    import os
    nph = int(os.environ.get('RWKV_NPH', '99'))
    xm = Z['xmix']
    Zl = dict(Z)
    if layer == 0:
        Zl['v'] = Z['vfirst']
    steps = [
        lambda: phase_rwkv_mix(S, CT, hT, I['cols'], layer, xm, T),
        lambda: phase_proj_tm(S, CT, xm[0], I['rwkv_w_r'][layer], Z['r'], T, BF16),
        lambda: phase_proj_tm(S, CT, xm[2], I['rwkv_w_k'][layer], Z['k'], T, BF16),
        lambda: phase_proj_tm(S, CT, xm[3], I['rwkv_w_v'][layer], Z['v'] if layer == 1 else Z['vfirst'], T, BF16),
        lambda: phase_lora(S, CT, xm[1], I['rwkv_dec_w1'][layer], I['rwkv_dec_w2'][layer], I['rwkv_dec_w0'][layer], 96, AF.Tanh, AF.Sigmoid, Z['sigd'], T, F32),
        lambda: phase_lora(S, CT, xm[4], I['rwkv_a_w1'][layer], I['rwkv_a_w2'][layer], I['rwkv_a_w0'][layer], 96, AF.Copy, AF.Sigmoid, Z['a'], T, BF16),
        lambda: phase_lora(S, CT, xm[5], I['rwkv_g_w1'][layer], I['rwkv_g_w2'][layer], None, 256, AF.Sigmoid, AF.Copy, Z['g'], T, BF16),
    ]
    if layer == 1:
        steps.append(lambda: phase_lora(S, CT, xm[3], I['rwkv_v_w1'][0], I['rwkv_v_w2'][0], I['rwkv_v_w0'][0], 64, AF.Copy, AF.Sigmoid, Z['vg'], T, BF16))
    steps += [
        lambda: phase_rwkv_prep(S, CT, I, layer, Zl, T),
        lambda: phase_rwkv_scan(S, CT, I, layer, Zl, T),
        lambda: phase_proj_fm_res(S, CT, Z['zT'], I['rwkv_w_o'][layer], hT, T),
    ]
    if not RUN_SCAN:
        steps = steps[:-2]
    for i, st_ in enumerate(steps):
        if i < nph:
            st_()


SB_SCALE = 128 ** -0.5


def phase_gather_q(S, CT, hT, hqT, NQB):
    nc = S.nc
    with ExitStack() as es:
        r = Ring(S, es, 'sb', 'gq_t', [128, KC, 128], F32, 3)
        pid = nc.sync.partition_id()
        off = (pid % 2) * 128 + NMETA
        hv = fm(hT)
        qv = fm(hqT)
        for i in range(NQB):
            t, b = r.next()
            S.dma('sp', lambda e: e.dma_start(out=t[:], in_=hv[:, :, bass.ds(off + 256 * i, 128)]), writes=[b], owner=b)
            S.dma('sp', lambda e: e.dma_start(out=qv[:, :, i * 128:(i + 1) * 128], in_=t[:]), reads=[b], owner=b)
        S.barrier()


def phase_headnorm_fm(S, CT, xsrc, W, gain1d, out, Tn):
    nc = S.nc
    with ExitStack() as es:
        cb = S.buf('c')
        X, Xb = load_resident(S, es, 'hn_x', xsrc, KC, Tn)
        gcol = es.enter_context(nc.sbuf_tensor(un('hn_g'), [128, 1], F32))
        gb = S.buf('g')
        S.dma('sp', lambda e: e.dma_start(out=gcol[:], in_=gain1d.rearrange("(p o) -> p o", o=1)), writes=[gb], owner=gb)
        wr = Ring(S, es, 'sb', 'hn_w', [128, KC, 128], BF16, 2)
        sr = Ring(S, es, 'sb', 'hn_sq', [128, 512], F32, 2)
        rr = Ring(S, es, 'sb', 'hn_rs', [128, 512], F32, 2)
        orr = Ring(S, es, 'sb', 'hn_o', [128, Tn], BF16, 2)
        pr = Ring(S, es, 'ps', 'hn_p', [128, 512], F32, 3)
        pr2 = Ring(S, es, 'ps', 'hn_p2', [128, 512], F32, 2)
        wv = wview(W)
        for h in range(SH):
            w, wb = wr.next()
            S.dma('pool', lambda e: e.dma_start(out=w[:], in_=wv[:, :, h * 128:(h + 1) * 128]), writes=[wb], owner=wb)
            o, ob = orr.next()
            for (t0, tn) in groups(Tn, 512):
                ps, pb = pr.next()
                for kc in range(KC):
                    S.op('pe', lambda e: e.matmul(ps[:, 0:tn], w[:, kc, :], X[:, kc, t0:t0 + tn], start=(kc == 0), stop=(kc == KC - 1)),
                         reads=[wb, Xb], writes=[pb], signal=(kc == KC - 1))
                sq, sqb = sr.next()
                S.op('act', lambda e: e.activation(out=sq[:, 0:tn], in_=ps[:, 0:tn], func=AF.Square), reads=[pb], writes=[sqb])
                p2, p2b = pr2.next()
                S.op('pe', lambda e: e.matmul(p2[:, 0:tn], CT['ones'][:], sq[:, 0:tn], start=True, stop=True), reads=[sqb, cb], writes=[p2b])
                rs, rb = rr.next()
                S.op('act', lambda e: e.activation(out=rs[:, 0:tn], in_=p2[:, 0:tn], func=AF.Ln, bias=CT['eps'][:, 0:1], scale=1.0 / 128),
                     reads=[p2b, cb], writes=[rb])
                S.op('act', lambda e: e.activation(out=rs[:, 0:tn], in_=rs[:, 0:tn], func=AF.Exp, scale=-0.5), reads=[rb], writes=[rb])
                S.op('dve', lambda e: e.scalar_tensor_tensor(out=o[:, t0:t0 + tn], in0=ps[:, 0:tn], scalar=gcol[:, 0:1], in1=rs[:, 0:tn],
                                                             op0=ALU.mult, op1=ALU.mult), reads=[pb, rb, gb], writes=[ob])
            S.dma('sp', lambda e: e.dma_start(out=out[h, :, 0:Tn], in_=o[:]), reads=[ob], owner=ob)
        S.barrier()


def phase_attention(S, CT, I, KT, Vtm, QT, OT, NXB):
    nc = S.nc
    NQB = NXB // 2
    NG = NQB // 4
    TQ = NQB * 128
    Tk = NMETA + 128 * NXB
    with ExitStack() as es:
        am = es.enter_context(nc.sbuf_tensor(un('at_am'), [128, 8, 512], F32))
        amb_ = es.enter_context(nc.sbuf_tensor(un('at_amb'), [128, 8, 512], BF16))
        tm = es.enter_context(nc.sbuf_tensor(un('at_tm'), [128, 2, 128], F32))
        mb = S.buf('am')
        S.dma('sp', lambda e: e.dma_start(out=am[:], in_=I['amask'].rearrange("j p q -> p j q")), writes=[mb], owner=mb)
        S.dma('pool', lambda e: e.dma_start(out=amb_[:], in_=I['amask'].rearrange("j p q -> p j q")), writes=[mb], owner=mb)
        S.dma('sp', lambda e: e.dma_start(out=tm[:], in_=I['tmask'].rearrange("j p q -> p j q")), writes=[mb], owner=mb)
        kr = Ring(S, es, 'sb', 'at_k', [128, Tk], BF16, 2)
        vr = Ring(S, es, 'sb', 'at_v', [128, NXB, 128], BF16, 2)
        vmr = Ring(S, es, 'sb', 'at_vm', [NMETA, 128], BF16, 2)
        qr = Ring(S, es, 'sb', 'at_q', [128, TQ], BF16, 2)
        outr = Ring(S, es, 'sb', 'at_o', [128, TQ], BF16, 2)
        Er = Ring(S, es, 'sb', 'at_e', [128, 512], F32, 2)
        SPr = Ring(S, es, 'sb', 'at_sp', [128, 512], F32, 4)
        T1r = Ring(S, es, 'sb', 'at_t1', [128, 512], F32, 2)
        Wr = Ring(S, es, 'sb', 'at_w', [128, 512], BF16, 3)
        Rr = Ring(S, es, 'sb', 'at_r', [128, 512], F32, 4)
        PZ = Ring(S, es, 'ps', 'at_pz', [128, 512], F32, 3)
        PL = Ring(S, es, 'ps', 'at_pl', [128, 512], F32, 2)
        PO = Ring(S, es, 'ps', 'at_po', [128, 512], F32, 2)
        tiles = []
        for h in range(SH):
            for g in range(NG):
                blocks = list(range(8 * g + 7, -1, -1)) + [-1]
                for bi, kb in enumerate(blocks):
                    tiles.append(dict(h=h, g=g, kb=kb, first=(bi == 0), last=(kb < 0), hfirst=(g == 0 and bi == 0), hlast=(g == NG - 1 and kb < 0)))
        hd = {}
        gd = {}

        def stA(t):
            h, g, kb = t['h'], t['g'], t['kb']
            if t['hfirst']:
                k, kb_ = kr.next()
                S.dma('sp', lambda e: e.dma_start(out=k[:], in_=KT[h, :, 0:Tk]), writes=[kb_], owner=kb_)
                v, vb = vr.next()
                S.dma('sp', lambda e: e.dma_start(out=v[:], in_=Vtm[NMETA:NMETA + 128 * NXB, h * 128:(h + 1) * 128].rearrange("(kb p) d -> p kb d", p=128)),
                      writes=[vb], owner=vb)
                vm, vmb = vmr.next()
                S.dma('sp', lambda e: e.dma_start(out=vm[:], in_=Vtm[0:NMETA, h * 128:(h + 1) * 128]), writes=[vmb], owner=vmb)
                q, qb_ = qr.next()
                S.dma('sp', lambda e: e.dma_start(out=q[:], in_=QT[h, :, 0:TQ]), writes=[qb_], owner=qb_)
                o, ob = outr.next()
                hd[h] = (k, kb_, v, vb, vm, vmb, q, qb_, o, ob)
            k, kb_, v, vb, vm, vmb, q, qb_, o, ob = hd[h]
            if t['first']:
                gd[(h, g)] = dict(po=PO.next(), R=None)
            meta = t['last']
            nk = NMETA if meta else 128
            kcols = slice(0, NMETA) if meta else slice(NMETA + 128 * kb, NMETA + 128 * (kb + 1))
            qs = slice(g * 512, (g + 1) * 512)
            masked = (not meta) and kb >= 8 * g
            j = kb - 8 * g
            pz, pzb = PZ.next()
            S.op('pe', lambda e: e.matmul(pz[0:nk, :], k[:, kcols], q[:, qs], start=True, stop=True), reads=[kb_, qb_], writes=[pzb])
            E, Eb = Er.next()
            S.op('act', lambda e: e.activation(out=E[0:nk, :], in_=pz[0:nk, :], func=AF.Exp, scale=SB_SCALE), reads=[pzb], writes=[Eb])
            sp, spb = SPr.next()
            S.op('act', lambda e: e.activation(out=sp[0:nk, :], in_=E[0:nk, :], func=AF.Ln, bias=1.0, scale=1.0), reads=[Eb], writes=[spb])
            if masked:
                S.op('pool', lambda e: e.tensor_tensor(out=sp[:, :], in0=sp[:, :], in1=am[:, j, :], op=ALU.mult), reads=[mb, spb], writes=[spb])
            t.update(nk=nk, pz=pz, pzb=pzb, sp=sp, spb=spb, masked=masked, j=j, meta=meta)

        def stB(t):
            h, g = t['h'], t['g']
            G = gd[(h, g)]
            nk, pz, pzb, sp, spb, first = t['nk'], t['pz'], t['pzb'], t['sp'], t['spb'], t['first']
            pl, plb = PL.next()
            S.op('pe', lambda e: e.matmul(pl[0:nk, :], tm[0:nk, 0, 0:nk], sp[0:nk, :], start=True, stop=first), reads=[mb, spb], writes=[plb], signal=first)
            if not first:
                R, Rb = G['R']
                S.op('pe', lambda e: e.matmul(pl[0:nk, :], tm[:, 1, 0:nk], R[:, :], start=False, stop=True), reads=[mb, Rb], writes=[plb])
            if not t['meta']:
                Rn, Rnb = Rr.next()
                if first:
                    S.op('pool', lambda e: e.tensor_copy(out=Rn[:, :], in_=sp[:, :]), reads=[spb], writes=[Rnb])
                else:
                    R, Rb = G['R']
                    S.op('pool', lambda e: e.tensor_tensor(out=Rn[:, :], in0=R[:, :], in1=sp[:, :], op=ALU.add), reads=[spb, Rb], writes=[Rnb])
                G['R'] = (Rn, Rnb)
            t1, t1b = T1r.next()
            S.op('dve', lambda e: e.scalar_tensor_tensor(out=t1[0:nk, :], in0=pz[0:nk, :], scalar=SB_SCALE, in1=sp[0:nk, :],
                                                         op0=ALU.mult, op1=ALU.subtract), reads=[pzb, spb], writes=[t1b])
            S.op('dve', lambda e: e.tensor_tensor(out=t1[0:nk, :], in0=t1[0:nk, :], in1=pl[0:nk, :], op=ALU.add), reads=[plb, t1b], writes=[t1b])
            w, wb = Wr.next()
            S.op('act', lambda e: e.activation(out=w[0:nk, :], in_=t1[0:nk, :], func=AF.Exp), reads=[t1b], writes=[wb])
            if t['masked']:
                j = t['j']
                S.op('pool', lambda e: e.tensor_tensor(out=w[:, :], in0=w[:, :], in1=amb_[:, j, :], op=ALU.mult), reads=[mb, wb], writes=[wb])
            t.update(w=w, wb=wb)

        def stC(t):
            h, g, kb = t['h'], t['g'], t['kb']
            k, kb_, v, vb, vm, vmb, q, qb_, o, ob = hd[h]
            po, pob = gd[(h, g)]['po']
            w, wb, nk, first = t['w'], t['wb'], t['nk'], t['first']
            if t['meta']:
                S.op('pe', lambda e: e.matmul(po[:, :], vm[:, :], w[0:nk, :], start=first, stop=True), reads=[vmb, wb], writes=[pob])
                qs = slice(g * 512, (g + 1) * 512)
                S.op('act', lambda e: e.copy(out=o[:, qs], in_=po[:, :]), reads=[pob], writes=[ob])
                if t['hlast']:
                    S.dma('sp', lambda e: e.dma_start(out=OT[h, :, 0:TQ], in_=o[:]), reads=[ob], owner=ob)
            else:
                S.op('pe', lambda e: e.matmul(po[:, :], v[:, kb, :], w[:, :], start=first, stop=False), reads=[vb, wb], writes=[pob], signal=False)

        nt = len(tiles)
        for step in range(nt + 2):
            if step < nt:
                stA(tiles[step])
            if 0 <= step - 1 < nt:
                stB(tiles[step - 1])
            if 0 <= step - 2 < nt:
                stC(tiles[step - 2])
        S.barrier()


def att_masks(parity):
    am = np.zeros((8, 128, 512), np.float32)
    p = np.arange(128)
    for j in range(8):
        for i in range(4):
            qb = 2 * i + parity
            if j < qb:
                am[j, :, i * 128:(i + 1) * 128] = 1.0
            elif j == qb:
                am[j, :, i * 128:(i + 1) * 128] = (p[:, None] < p[None, :])
    tmk = np.zeros((2, 128, 128), np.float32)
    tmk[0] = -(p[:, None] > p[None, :]).astype(np.float32)
    tmk[1] = -1.0
    return am, tmk


def const_masks():
    cm = np.zeros((4, 128, 128), np.float32)
    i = np.arange(128)
    cm[0] = np.where(i[:, None] <= i[None, :], -DEC_C, 0.0)
    cm[1] = (i[:, None] < i[None, :])
    cm[2] = (i[:, None] <= i[None, :])
    cm[3] = (i[:, None] > i[None, :])
    return cm


IN_SPECS = [
    ('cols', [128, NCOLS, KC]), ('ident', [128, 128]), ('cmask', [4, 128, 128]),
    ('ffn_w_gate', [4, D, FF]), ('ffn_w_up', [4, D, FF]), ('ffn_w_down', [4, FF, D]),
    ('rwkv_w_r', [2, D, D]), ('rwkv_w_k', [2, D, D]), ('rwkv_w_v', [2, D, D]), ('rwkv_w_o', [2, D, D]),
    ('rwkv_dec_w0', [2, D]), ('rwkv_dec_w1', [2, D, 96]), ('rwkv_dec_w2', [2, 96, D]),
    ('rwkv_a_w0', [2, D]), ('rwkv_a_w1', [2, D, 96]), ('rwkv_a_w2', [2, 96, D]),
    ('rwkv_g_w1', [2, D, 256]), ('rwkv_g_w2', [2, 256, D]),
    ('rwkv_k_k', [2, D]), ('rwkv_k_a', [2, D]), ('rwkv_r_k', [2, 32, 64]), ('rwkv_gn_w', [2, D]), ('rwkv_gn_b', [2, D]),
    ('rwkv_v_w0', [1, D]), ('rwkv_v_w1', [1, D, 64]), ('rwkv_v_w2', [1, 64, D]),
    ('amask', [8, 128, 512]), ('tmask', [2, 128, 128]),
    ('sb_w_k', [D, D]), ('sb_w_v', [D, D]), ('sb_k_gain', [128]), ('sb_w_q', [2, D, D]), ('sb_q_gain', [2, 128]), ('sb_w_o', [2, D, D]),
]


def build(NXB, mode='full', dbg=()):
    T = 128 * (NXB + 1)
    nc = bass.Bass("TRN2", target_bir_lowering=False)
    I = {}
    I['xin'] = nc.dram_tensor('xin', [T, D], F32, kind="ExternalInput").ap()
    for name, shape in IN_SPECS:
        I[name] = nc.dram_tensor(name, list(shape), F32, kind="ExternalInput").ap()

    def scratch(name, shape, dt):
        kind = "ExternalOutput" if name in dbg else "Internal"
        return nc.dram_tensor(name, list(shape), dt, kind=kind).ap()

    hT = scratch('hT', [KC, 128, T], F32)
    xn = scratch('xn', [KC, 128, T], BF16)
    actT = scratch('actT', [FC, 128, T], BF16)
    Z = {'xmix': [scratch('xmix%d' % i, [KC, 128, T], BF16) for i in range(6)]}
    for n in ('r', 'k', 'v', 'vfirst', 'a', 'g', 'vg', 'vv', 'bonus', 'kt', 'bn'):
        Z[n] = scratch('z_' + n, [T, D], BF16)
    Z['sigd'] = scratch('z_sigd', [T, D], F32)
    Z['gam'] = scratch('z_gam', [T // 128, 128, KC], F32)
    for n in ('rT', 'kpT', 'ktT', 'bnT', 'zT'):
        Z[n] = scratch('z_' + n, [KC, 128, T], BF16)
    NQB = NXB // 2
    TQ = NQB * 128
    hqT = scratch('hqT', [KC, 128, TQ], F32)
    KT = scratch('KT', [SH, 128, T], BF16)
    Vtm = scratch('Vtm', [T, D], BF16)
    QT = scratch('QT', [SH, 128, TQ], BF16)
    OT = scratch('OT', [SH, 128, TQ], BF16)
    full = mode in ('full', 'att_test')
    out = nc.dram_tensor('out', [TQ if full else T, D], F32, kind="ExternalOutput").ap()

    def ffn(S, CT, layer, hres, Tn):
        phase_norm(S, CT, hres, I['cols'], COLS[('ffn_norm_g', layer)], xn, Tn)
        phase_ffn_gateup(S, CT, xn, I['ffn_w_gate'][layer], I['ffn_w_up'][layer], actT, Tn)
        phase_ffn_down(S, CT, actT, I['ffn_w_down'][layer], hres, Tn)

    with ExitStack() as es:
        S = Sched(nc, es)
        CT = load_consts(S, es, I)
        phase_in_transpose(S, CT, I['xin'], hT, T)
        if mode == 'ffn_test':
            ffn(S, CT, 0, hT, T)
        if mode == 'rwkv_test':
            rwkv_layer(S, CT, I, 0, Z, hT, T)
        if mode == 'full':
            rwkv_layer(S, CT, I, 0, Z, hT, T)
            ffn(S, CT, 0, hT, T)
            rwkv_layer(S, CT, I, 1, Z, hT, T)
            ffn(S, CT, 1, hT, T)
        if mode == 'rwkv2_test':
            rwkv_layer(S, CT, I, 0, Z, hT, T)
            ffn(S, CT, 0, hT, T)
            rwkv_layer(S, CT, I, 1, Z, hT, T)
        if full:
            phase_norm(S, CT, hT, I['cols'], COLS[('kv_norm_g', 0)], xn, T)
            phase_headnorm_fm(S, CT, xn, I['sb_w_k'], I['sb_k_gain'], KT, T)
            phase_proj_tm(S, CT, xn, I['sb_w_v'], Vtm, T, BF16)
            phase_gather_q(S, CT, hT, hqT, NQB)
            for j in range(2):
                phase_norm(S, CT, hqT, I['cols'], COLS[('mix_norm_g', 2 + j)], xn, TQ)
                phase_headnorm_fm(S, CT, xn, I['sb_w_q'][j], I['sb_q_gain'][j], QT, TQ)
                phase_attention(S, CT, I, KT, Vtm, QT, OT, NXB)
                phase_proj_fm_res(S, CT, OT, I['sb_w_o'][j], hqT, TQ)
                ffn(S, CT, 2 + j, hqT, TQ)
            phase_out_transpose(S, CT, hqT, out, TQ, 0)
        else:
            phase_out_transpose(S, CT, hT, out, T, 0)
        print("instructions:", S.ninst)
    return nc


def host_inputs(inputs):
    d = {k: np.ascontiguousarray(np.asarray(v), dtype=np.float32) for k, v in inputs.items()}
    base = {'cols': pack_cols(d), 'ident': np.eye(128, dtype=np.float32), 'cmask': const_masks(), 'tmask': att_masks(0)[1], 'amask': att_masks(0)[0]}
    for name, shape in IN_SPECS:
        if name not in base:
            base[name] = d[name].reshape(shape)
    return base


def kernel(**inputs):
    NXB = 32
    T = 128 * (NXB + 1)
    base = host_inputs(inputs)
    x = np.asarray(inputs['x'], np.float32)
    meta = np.asarray(inputs['meta_tokens'], np.float32)
    in_maps = []
    for c in range(8):
        b = c // 2
        xin = np.zeros((T, D), np.float32)
        xin[:NMETA] = meta
        xin[NMETA:NMETA + 4096] = x[b]
        m = dict(base)
        m['xin'] = xin
        m['amask'] = att_masks(c % 2)[0]
        in_maps.append(m)
    nc = build(NXB, mode='full')
    res = run_bass_kernel_spmd(nc, in_maps, core_ids=list(range(8)))
    out = np.zeros((4, 4096, D), np.float32)
    for c in range(8):
        o = np.asarray(res.results[c]['out']).reshape(NXB // 2, 128, D)
        out[c // 2].reshape(NXB // 2, 2, 128, D)[:, c % 2] = o
    return out
```

```python
import numpy as np
import ml_dtypes
from contextlib import ExitStack
import concourse.bass as bass
import concourse.mybir as mybir
from concourse.bass_utils import run_bass_kernel_spmd

F32 = mybir.dt.float32
BF16 = mybir.dt.bfloat16
AF = mybir.ActivationFunctionType
ALU = mybir.AluOpType
AX = mybir.AxisListType

D = 2048
KC = 16
FF = 5632
FC = 44
NMETA = 16
RH = 32
SH = 16
RMS_EPS = 1e-6
GN_EPS = 64e-5
ENG = ('pe', 'act', 'dve', 'pool', 'sp')
NDS = 48


_UID = [0]


def un(name):
    _UID[0] += 1
    return '%s_%d' % (name, _UID[0])


class DSem:
    def __init__(self, h):
        self.h = h
        self.count = 0


class Buf:
    __slots__ = ('w', 'r', 'ds', 'name')

    def __init__(self, name=''):
        self.w = None
        self.r = {}
        self.ds = None
        self.name = name


class Sched:
    def __init__(self, nc, es):
        self.nc = nc
        self.eng = {'pe': nc.tensor, 'act': nc.scalar, 'dve': nc.vector, 'pool': nc.gpsimd, 'sp': nc.sync}
        self.sem = {e: es.enter_context(nc.semaphore('s_' + e)) for e in ENG}
        self.cnt = {e: 0 for e in ENG}
        self.seen = {e: {} for e in ENG}
        self.free_ds = [DSem(es.enter_context(nc.semaphore('d%d' % i))) for i in range(NDS)]
        self.used_ds = []
        self.bufs = []
        self.ninst = 0

    def buf(self, name=''):
        b = Buf(name)
        self.bufs.append(b)
        return b

    def _waits(self, e, evs):
        need = {}
        for key, val in evs:
            if key == 'pe' and e == 'pe':
                continue
            if self.seen[e].get(key, 0) >= val:
                continue
            if need.get(key, 0) < val:
                need[key] = val
        for key, val in need.items():
            self.seen[e][key] = val
            h = self.sem[key] if isinstance(key, str) else key.h
            self.eng[e].wait_ge(h, val)
            self.ninst += 1

    @staticmethod
    def _deps(reads, writes):
        evs = []
        for b in reads:
            if b.w is not None:
                evs.append(b.w)
        for b in writes:
            if b.w is not None:
                evs.append(b.w)
            evs.extend(b.r.items())
        return evs

    @staticmethod
    def _mark(ev, reads, writes):
        k, v = ev
        for b in reads:
            if b.r.get(k, 0) < v:
                b.r[k] = v
        for b in writes:
            b.w = ev
            b.r = {}

    def op(self, e, fn, reads=(), writes=(), signal=True):
        self._waits(e, self._deps(reads, writes))
        ins = fn(self.eng[e])
        self.ninst += 1
        if signal:
            self.cnt[e] += 1
            ins.then_inc(self.sem[e], 1)
            ev = (e, self.cnt[e])
        else:
            ev = (e, self.cnt[e] + 1)
        self._mark(ev, reads, writes)
        return ins

    def dma(self, q, fn, reads=(), writes=(), owner=None):
        self._waits(q, self._deps(reads, writes))
        if owner.ds is None:
            owner.ds = self.free_ds.pop()
            self.used_ds.append(owner.ds)
        ds = owner.ds
        ins = fn(self.eng[q])
        self.ninst += 1
        ds.count += 16
        ins.then_inc(ds.h, 16)
        self._mark((ds, ds.count), reads, writes)
        return ins

    def barrier(self):
        evs = [(e, self.cnt[e]) for e in ENG if self.cnt[e] > 0]
        evs += [(ds, ds.count) for ds in self.used_ds]
        for e in ENG:
            self._waits(e, evs)
        for b in self.bufs:
            b.w = None
            b.r = {}
            b.ds = None
        self.free_ds.extend(self.used_ds)
        self.used_ds = []
        self.bufs = []


class Ring:
    def __init__(self, S, es, kind, name, shape, dtype, n):
        nc = S.nc
        self.t = []
        self.b = []
        for i in range(n):
            if kind == 'sb':
                t = es.enter_context(nc.sbuf_tensor(un('%s%d' % (name, i)), shape, dtype))
            else:
                t = es.enter_context(nc.psum_tensor(un('%s%d' % (name, i)), shape, dtype))
            self.t.append(t)
            self.b.append(S.buf(name))
        self.i = 0
        self.n = n

    def next(self):
        i = self.i % self.n
        self.i += 1
        return self.t[i], self.b[i]


def tile1(S, es, name, shape, dtype, kind='sb'):
    r = Ring(S, es, kind, name, shape, dtype, 1)
    return r.t[0], r.b[0]


def groups(T, g):
    out = []
    t0 = 0
    while t0 < T:
        out.append((t0, min(g, T - t0)))
        t0 += g
    return out


def fm(d):
    return d.rearrange("kc p t -> p kc t")


def colvec(w1d, n):
    return w1d.rearrange("(kc p) -> p kc", p=128)


def load_consts(S, es, C):
    nc = S.nc
    t = {}
    t['ident'] = es.enter_context(nc.sbuf_tensor(un('c_ident'), [128, 128], F32))
    t['identb'] = es.enter_context(nc.sbuf_tensor(un('c_identb'), [128, 128], BF16))
    t['ones'] = es.enter_context(nc.sbuf_tensor(un('c_ones'), [128, 128], F32))
    t['eps'] = es.enter_context(nc.sbuf_tensor(un('c_eps'), [128, 2], F32))
    b = S.buf('const')
    S.dma('sp', lambda e: e.dma_start(out=t['ident'][:], in_=C['ident']), writes=[b], owner=b)
    S.dma('pool', lambda e: e.dma_start(out=t['identb'][:], in_=C['ident']), writes=[b], owner=b)
    S.op('dve', lambda e: e.memset(t['ones'][:], 1.0), writes=[b])
    S.op('dve', lambda e: e.memset(t['eps'][:, 0:1], RMS_EPS), writes=[b])
    S.op('dve', lambda e: e.memset(t['eps'][:, 1:2], GN_EPS), writes=[b])
    S.barrier()
    return t


def phase_in_transpose(S, CT, xin, hT, T):
    nc = S.nc
    with ExitStack() as es:
        cb = S.buf('c')
        xr = Ring(S, es, 'sb', 'it_x', [128, D], F32, 3)
        hr = Ring(S, es, 'sb', 'it_h', [128, KC, 512], F32, 2)
        pr = Ring(S, es, 'ps', 'it_p', [128, 512], F32, 4)
        hv = fm(hT)
        for (t0, tn) in groups(T, 512):
            hb, hbb = hr.next()
            for j in range(tn // 128):
                xt, xb = xr.next()
                r0 = t0 + j * 128
                S.dma('sp', lambda e, xt=xt, r0=r0: e.dma_start(out=xt[:], in_=xin[r0:r0 + 128, :]),
                      writes=[xb], owner=xb)
                for q in range(4):
                    ps, pb = pr.next()
                    for i in range(4):
                        kc = 4 * q + i
                        S.op('pe', lambda e, ps=ps, xt=xt, kc=kc, i=i: e.transpose(
                            ps[:, i * 128:(i + 1) * 128], xt[:, kc * 128:(kc + 1) * 128], CT['ident'][:]),
                            reads=[xb, cb], writes=[pb], signal=(i == 3))
                    eng = 'dve' if q % 2 == 0 else 'act'
                    if eng == 'dve':
                        S.op('dve', lambda e, ps=ps, hb=hb, q=q, j=j: e.tensor_copy(
                            out=hb[:, 4 * q:4 * q + 4, j * 128:(j + 1) * 128],
                            in_=ps[:].rearrange("p (a b) -> p a b", a=4)), reads=[pb], writes=[hbb])
                    else:
                        S.op('act', lambda e, ps=ps, hb=hb, q=q, j=j: e.copy(
                            out=hb[:, 4 * q:4 * q + 4, j * 128:(j + 1) * 128],
                            in_=ps[:].rearrange("p (a b) -> p a b", a=4)), reads=[pb], writes=[hbb])
            S.dma('sp', lambda e, hb=hb, t0=t0, tn=tn: e.dma_start(out=hv[:, :, t0:t0 + tn], in_=hb[:, :, 0:tn]),
                  reads=[hbb], owner=hbb)
        S.barrier()


COLS = {}
_ci = 0
for _l in range(4):
    COLS[('ffn_norm_g', _l)] = _ci; _ci += 1
for _l in range(4):
    COLS[('mix_norm_g', _l)] = _ci; _ci += 1
COLS[('kv_norm_g', 0)] = _ci; _ci += 1
for _l in range(2):
    for _i in range(6):
        COLS[('rwkv_mu', _l, _i)] = _ci; _ci += 1
NCOLS = _ci


def pack_cols(inputs):
    out = np.zeros((128, NCOLS, KC), np.float32)
    for key, i in COLS.items():
        a = inputs[key[0]]
        v = a[key[1]] if key[0] != 'kv_norm_g' else a
        if key[0] == 'rwkv_mu':
            v = v[key[2]]
        out[:, i, :] = np.asarray(v).reshape(KC, 128).T
    return out


def load_cols(S, es, name, cols, idxs):
    nc = S.nc
    n = len(idxs)
    t = es.enter_context(nc.sbuf_tensor(un(name), [128, n, KC], F32))
    b = S.buf(name)
    for i, ix in enumerate(idxs):
        S.dma('sp', lambda e, i=i, ix=ix: e.dma_start(out=t[:, i, :], in_=cols[:, ix, :]), writes=[b], owner=b)
    return t, b


def phase_norm(S, CT, hsrc, cols, gidx, xout, T):
    nc = S.nc
    with ExitStack() as es:
        cb = S.buf('c')
        g, gb = load_cols(S, es, 'n_g', cols, [gidx])
        hr = Ring(S, es, 'sb', 'n_h', [128, KC, 512], F32, 2)
        sr = Ring(S, es, 'sb', 'n_sq', [128, 512], F32, 3)
        rr = Ring(S, es, 'sb', 'n_rs', [128, 512], F32, 2)
        orr = Ring(S, es, 'sb', 'n_o', [128, KC, 512], BF16, 2)
        pr = Ring(S, es, 'ps', 'n_p', [128, 512], F32, 2)
        hv = fm(hsrc)
        ov = fm(xout)
        for (t0, tn) in groups(T, 512):
            h, hb = hr.next()
            S.dma('sp', lambda e, h=h, t0=t0, tn=tn: e.dma_start(out=h[:, :, 0:tn], in_=hv[:, :, t0:t0 + tn]),
                  writes=[hb], owner=hb)
            ps, pb = pr.next()
            for kc in range(KC):
                sq, sb_ = sr.next()
                S.op('act', lambda e, sq=sq, h=h, kc=kc, tn=tn: e.activation(
                    out=sq[:, 0:tn], in_=h[:, kc, 0:tn], func=AF.Square), reads=[hb], writes=[sb_])
                S.op('pe', lambda e, ps=ps, sq=sq, kc=kc, tn=tn: e.matmul(
                    ps[:, 0:tn], CT['ones'][:], sq[:, 0:tn], start=(kc == 0), stop=(kc == KC - 1)),
                    reads=[sb_, cb], writes=[pb], signal=True)
            rs, rb = rr.next()
            S.op('act', lambda e, rs=rs, ps=ps, tn=tn: e.activation(
                out=rs[:, 0:tn], in_=ps[:, 0:tn], func=AF.Ln, bias=CT['eps'][:, 0:1], scale=1.0 / D),
                reads=[pb, cb], writes=[rb])
            S.op('act', lambda e, rs=rs, tn=tn: e.activation(
                out=rs[:, 0:tn], in_=rs[:, 0:tn], func=AF.Exp, scale=-0.5), reads=[rb], writes=[rb])
            o, ob = orr.next()
            for kc in range(KC):
                S.op('dve', lambda e, o=o, h=h, rs=rs, kc=kc, tn=tn: e.scalar_tensor_tensor(
                    out=o[:, kc, 0:tn], in0=h[:, kc, 0:tn], scalar=g[:, 0, kc:kc + 1], in1=rs[:, 0:tn],
                    op0=ALU.mult, op1=ALU.mult), reads=[hb, rb, gb], writes=[ob])
            S.dma('sp', lambda e, o=o, t0=t0, tn=tn: e.dma_start(out=ov[:, :, t0:t0 + tn], in_=o[:, :, 0:tn]),
                  reads=[ob], owner=ob)
        S.barrier()


def load_resident(S, es, name, src, kcn, T):
    nc = S.nc
    X = es.enter_context(nc.sbuf_tensor(un(name), [128, kcn, T], BF16))
    Xb = S.buf(name)
    sv = fm(src)
    for k0 in range(0, kcn, 4):
        k1 = min(kcn, k0 + 4)
        S.dma('sp', lambda e, k0=k0, k1=k1: e.dma_start(out=X[:, k0:k1, :], in_=sv[:, k0:k1, 0:T]),
              writes=[Xb], owner=Xb)
    return X, Xb


def wview(W):
    return W.rearrange("(kc p) f -> p kc f", p=128)


def phase_ffn_gateup(S, CT, xn, Wg, Wu, actT, T):
    nc = S.nc
    with ExitStack() as es:
        X, Xb = load_resident(S, es, 'fg_x', xn, KC, T)
        wr = Ring(S, es, 'sb', 'fg_w', [128, 2, KC, 256], BF16, 2)
        ar = Ring(S, es, 'sb', 'fg_a', [128, T], BF16, 2)
        tr = Ring(S, es, 'sb', 'fg_t', [128, 512], F32, 3)
        pg = Ring(S, es, 'ps', 'fg_pg', [128, 512], F32, 3)
        pu = Ring(S, es, 'ps', 'fg_pu', [128, 512], F32, 3)
        wgv = wview(Wg)
        wuv = wview(Wu)
        for og in range(FC // 2):
            w, wb = wr.next()
            c0 = og * 256
            for k0 in range(0, KC, 8):
                S.dma('pool', lambda e, w=w, c0=c0, k0=k0: e.dma_start(out=w[:, 0, k0:k0 + 8, :], in_=wgv[:, k0:k0 + 8, c0:c0 + 256]),
                      writes=[wb], owner=wb)
                S.dma('pool', lambda e, w=w, c0=c0, k0=k0: e.dma_start(out=w[:, 1, k0:k0 + 8, :], in_=wuv[:, k0:k0 + 8, c0:c0 + 256]),
                      writes=[wb], owner=wb)
            for ol in range(2):
                oc = og * 2 + ol
                a, ab = ar.next()
                for (t0, tn) in groups(T, 512):
                    p1, p1b = pg.next()
                    p2, p2b = pu.next()
                    for kc in range(KC):
                        S.op('pe', lambda e, p1=p1, w=w, ol=ol, kc=kc, t0=t0, tn=tn: e.matmul(
                            p1[:, 0:tn], w[:, 0, kc, ol * 128:(ol + 1) * 128], X[:, kc, t0:t0 + tn],
                            start=(kc == 0), stop=(kc == KC - 1)), reads=[wb, Xb], writes=[p1b], signal=(kc == KC - 1))
                    for kc in range(KC):
                        S.op('pe', lambda e, p2=p2, w=w, ol=ol, kc=kc, t0=t0, tn=tn: e.matmul(
                            p2[:, 0:tn], w[:, 1, kc, ol * 128:(ol + 1) * 128], X[:, kc, t0:t0 + tn],
                            start=(kc == 0), stop=(kc == KC - 1)), reads=[wb, Xb], writes=[p2b], signal=(kc == KC - 1))
                    tt, tb = tr.next()
                    S.op('act', lambda e, tt=tt, p1=p1, tn=tn: e.activation(out=tt[:, 0:tn], in_=p1[:, 0:tn], func=AF.Silu),
                         reads=[p1b], writes=[tb])
                    S.op('dve', lambda e, a=a, tt=tt, p2=p2, t0=t0, tn=tn: e.tensor_tensor(
                        out=a[:, t0:t0 + tn], in0=tt[:, 0:tn], in1=p2[:, 0:tn], op=ALU.mult),
                        reads=[tb, p2b], writes=[ab])
                S.dma('sp', lambda e, a=a, oc=oc: e.dma_start(out=actT[oc, :, 0:T], in_=a[:]), reads=[ab], owner=ab)
        S.barrier()


def phase_ffn_down(S, CT, actT, Wd, hT, T):
    nc = S.nc
    with ExitStack() as es:
        wt = es.enter_context(nc.sbuf_tensor(un('fd_w'), [128, FC, 512], BF16))
        wb = S.buf('fd_w')
        ar = Ring(S, es, 'sb', 'fd_a', [128, FC, 512], BF16, 2)
        hr = Ring(S, es, 'sb', 'fd_h', [128, 512], F32, 4)
        pr = Ring(S, es, 'ps', 'fd_p', [128, 512], F32, 4)
        wv = wview(Wd)
        av = fm(actT)
        for q in range(4):
            for k0 in range(0, FC, 11):
                S.dma('pool', lambda e, k0=k0, q=q: e.dma_start(out=wt[:, k0:k0 + 11, :], in_=wv[:, k0:k0 + 11, q * 512:(q + 1) * 512]),
                      writes=[wb], owner=wb)
            for (t0, tn) in groups(T, 512):
                a, ab = ar.next()
                for k0 in range(0, FC, 11):
                    S.dma('sp', lambda e, a=a, k0=k0, t0=t0, tn=tn: e.dma_start(out=a[:, k0:k0 + 11, 0:tn], in_=av[:, k0:k0 + 11, t0:t0 + tn]),
                          writes=[ab], owner=ab)
                for ol in range(4):
                    dc = q * 4 + ol
                    h, hb = hr.next()
                    S.dma('sp', lambda e, h=h, dc=dc, t0=t0, tn=tn: e.dma_start(out=h[:, 0:tn], in_=hT[dc, :, t0:t0 + tn]),
                          writes=[hb], owner=hb)
                    ps, pb = pr.next()
                    for fc in range(FC):
                        S.op('pe', lambda e, ps=ps, a=a, ol=ol, fc=fc, tn=tn: e.matmul(
                            ps[:, 0:tn], wt[:, fc, ol * 128:(ol + 1) * 128], a[:, fc, 0:tn],
                            start=(fc == 0), stop=(fc == FC - 1)), reads=[wb, ab], writes=[pb], signal=(fc == FC - 1))
                    S.op('dve', lambda e, h=h, ps=ps, tn=tn: e.tensor_tensor(
                        out=h[:, 0:tn], in0=h[:, 0:tn], in1=ps[:, 0:tn], op=ALU.add), reads=[pb, hb], writes=[hb])
                    S.dma('sp', lambda e, h=h, dc=dc, t0=t0, tn=tn: e.dma_start(out=hT[dc, :, t0:t0 + tn], in_=h[:, 0:tn]),
                          reads=[hb], owner=hb)
        S.barrier()


def phase_out_transpose(S, CT, hT, out, T, col0):
    nc = S.nc
    with ExitStack() as es:
        cb = S.buf('c')
        hr = Ring(S, es, 'sb', 'ot_h', [128, KC, 128], F32, 3)
        orr = Ring(S, es, 'sb', 'ot_o', [128, D], F32, 2)
        pr = Ring(S, es, 'ps', 'ot_p', [128, 512], F32, 4)
        hv = fm(hT)
        for (t0, tn) in groups(T, 128):
            h, hb = hr.next()
            S.dma('sp', lambda e, h=h, t0=t0: e.dma_start(out=h[:], in_=hv[:, :, col0 + t0:col0 + t0 + 128]),
                  writes=[hb], owner=hb)
            o, ob = orr.next()
            for q in range(4):
                ps, pb = pr.next()
                for i in range(4):
                    kc = 4 * q + i
                    S.op('pe', lambda e, ps=ps, h=h, kc=kc, i=i: e.transpose(
                        ps[:, i * 128:(i + 1) * 128], h[:, kc, :], CT['ident'][:]),
                        reads=[hb, cb], writes=[pb], signal=(i == 3))
                if q % 2 == 0:
                    S.op('dve', lambda e, ps=ps, o=o, q=q: e.tensor_copy(out=o[:, q * 512:(q + 1) * 512], in_=ps[:]),
                         reads=[pb], writes=[ob])
                else:
                    S.op('act', lambda e, ps=ps, o=o, q=q: e.copy(out=o[:, q * 512:(q + 1) * 512], in_=ps[:]),
                         reads=[pb], writes=[ob])
            S.dma('sp', lambda e, o=o, t0=t0: e.dma_start(out=out[t0:t0 + 128, :], in_=o[:]), reads=[ob], owner=ob)
        S.barrier()


def phase_rwkv_mix(S, CT, hT, cols, layer, xmix, T):
    nc = S.nc
    G = 256
    with ExitStack() as es:
        cb = S.buf('c')
        idxs = [COLS[('mix_norm_g', layer)]] + [COLS[('rwkv_mu', layer, i)] for i in range(6)]
        g, gb = load_cols(S, es, 'm_g', cols, idxs)
        hr = Ring(S, es, 'sb', 'm_h', [128, KC, G], F32, 2)
        sr = Ring(S, es, 'sb', 'm_sq', [128, G], F32, 3)
        rr = Ring(S, es, 'sb', 'm_rs', [128, G], F32, 2)
        hnr = Ring(S, es, 'sb', 'm_hn', [128, KC, G + 1], F32, 2)
        xxr = Ring(S, es, 'sb', 'm_xx', [128, KC, G], F32, 1)
        orr = Ring(S, es, 'sb', 'm_o', [128, KC, G], BF16, 6)
        pr = Ring(S, es, 'ps', 'm_p', [128, G], F32, 2)
        hv = fm(hT)
        prev = None
        for (t0, tn) in groups(T, G):
            h, hb = hr.next()
            S.dma('sp', lambda e: e.dma_start(out=h[:, :, 0:tn], in_=hv[:, :, t0:t0 + tn]), writes=[hb], owner=hb)
            ps, pb = pr.next()
            for kc in range(KC):
                sq, sb_ = sr.next()
                S.op('act', lambda e: e.activation(out=sq[:, 0:tn], in_=h[:, kc, 0:tn], func=AF.Square), reads=[hb], writes=[sb_])
                S.op('pe', lambda e: e.matmul(ps[:, 0:tn], CT['ones'][:], sq[:, 0:tn], start=(kc == 0), stop=(kc == KC - 1)),
                     reads=[sb_, cb], writes=[pb], signal=True)
            rs, rb = rr.next()
            S.op('act', lambda e: e.activation(out=rs[:, 0:tn], in_=ps[:, 0:tn], func=AF.Ln, bias=CT['eps'][:, 0:1], scale=1.0 / D),
                 reads=[pb, cb], writes=[rb])
            S.op('act', lambda e: e.activation(out=rs[:, 0:tn], in_=rs[:, 0:tn], func=AF.Exp, scale=-0.5), reads=[rb], writes=[rb])
            hn, hnb = hnr.next()
            if prev is None:
                S.op('pool', lambda e: e.memset(hn[:, :, 0:1], 0.0), writes=[hnb])
            else:
                phn, phnb, ptn = prev
                S.op('pool', lambda e: e.tensor_copy(out=hn[:, :, 0:1], in_=phn[:, :, ptn:ptn + 1]), reads=[phnb], writes=[hnb])
            for kc in range(KC):
                S.op('dve', lambda e: e.scalar_tensor_tensor(
                    out=hn[:, kc, 1:1 + tn], in0=h[:, kc, 0:tn], scalar=g[:, 0, kc:kc + 1], in1=rs[:, 0:tn],
                    op0=ALU.mult, op1=ALU.mult), reads=[hb, rb, gb], writes=[hnb])
            xx, xxb = xxr.next()
            S.op('pool', lambda e: e.tensor_tensor(out=xx[:, :, 0:tn], in0=hn[:, :, 0:tn], in1=hn[:, :, 1:1 + tn], op=ALU.subtract),
                 reads=[hnb], writes=[xxb])
            for i in range(6):
                o, ob = orr.next()
                for kc in range(KC):
                    S.op('dve', lambda e: e.scalar_tensor_tensor(
                        out=o[:, kc, 0:tn], in0=xx[:, kc, 0:tn], scalar=g[:, 1 + i, kc:kc + 1], in1=hn[:, kc, 1:1 + tn],
                        op0=ALU.mult, op1=ALU.add), reads=[xxb, hnb, gb], writes=[ob])
                S.dma('sp', lambda e: e.dma_start(out=fm(xmix[i])[:, :, t0:t0 + tn], in_=o[:, :, 0:tn]), reads=[ob], owner=ob)
            prev = (hn, hnb, tn)
        S.barrier()


def gemm_tm(S, CT, es, X, Xb, T, W, fout, epilogue, wname='gw', pname='gp'):
    wr = Ring(S, es, 'sb', wname, [128, KC, 512], BF16, 2)
    pr = Ring(S, es, 'ps', pname, [128, 512], F32, 4)
    wv = wview(W)
    for og in range(fout // 512):
        w, wb = wr.next()
        for k0 in range(0, KC, 8):
            S.dma('pool', lambda e: e.dma_start(out=w[:, k0:k0 + 8, :], in_=wv[:, k0:k0 + 8, og * 512:(og + 1) * 512]),
                  writes=[wb], owner=wb)
        for (t0, tn) in groups(T, 128):
            ps, pb = pr.next()
            for kc in range(KC):
                S.op('pe', lambda e: e.matmul(ps[:], X[:, kc, t0:t0 + 128], w[:, kc, :], start=(kc == 0), stop=(kc == KC - 1)),
                     reads=[wb, Xb], writes=[pb], signal=(kc == KC - 1))
            epilogue(og, t0, ps, pb)


def phase_proj_tm(S, CT, xsrc, W, out, T, odt):
    nc = S.nc
    with ExitStack() as es:
        X, Xb = load_resident(S, es, 'pj_x', xsrc, KC, T)
        orr = Ring(S, es, 'sb', 'pj_o', [128, 512], odt, 4)
        cnt = [0]

        def epi(og, t0, ps, pb):
            o, ob = orr.next()
            cnt[0] += 1
            if cnt[0] % 2 == 0:
                S.op('dve', lambda e: e.tensor_copy(out=o[:], in_=ps[:]), reads=[pb], writes=[ob])
            else:
                S.op('act', lambda e: e.copy(out=o[:], in_=ps[:]), reads=[pb], writes=[ob])
            S.dma('sp', lambda e: e.dma_start(out=out[t0:t0 + 128, og * 512:(og + 1) * 512], in_=o[:]), reads=[ob], owner=ob)

        gemm_tm(S, CT, es, X, Xb, T, W, D, epi)
        S.barrier()


def phase_lora(S, CT, xsrc, w1, w2, w0row, rdim, hid_func, out_func, out, T, odt):
    nc = S.nc
    nrc = (rdim + 127) // 128
    rp = min(rdim, 128)
    with ExitStack() as es:
        X, Xb = load_resident(S, es, 'lo_x', xsrc, KC, T)
        w1t = es.enter_context(nc.sbuf_tensor(un('lo_w1'), [128, KC, rdim], BF16))
        w1b = S.buf('w1')
        S.dma('pool', lambda e: e.dma_start(out=w1t[:], in_=wview(w1)), writes=[w1b], owner=w1b)
        w2t = es.enter_context(nc.sbuf_tensor(un('lo_w2'), [rp, nrc, D], BF16))
        w2b = S.buf('w2')
        S.dma('pool', lambda e: e.dma_start(out=w2t[:], in_=w2.rearrange("(c p) f -> p c f", p=rp)), writes=[w2b], owner=w2b)
        hid = es.enter_context(nc.sbuf_tensor(un('lo_hid'), [rp, nrc, T], BF16))
        hidb = S.buf('hid')
        if w0row is not None:
            w0t = es.enter_context(nc.sbuf_tensor(un('lo_w0'), [128, D], F32))
            w0b = S.buf('w0')
            S.dma('sp', lambda e: e.dma_start(out=w0t[:], in_=w0row.partition_broadcast(128)), writes=[w0b], owner=w0b)
        pr = Ring(S, es, 'ps', 'lo_p', [128, 512], F32, 3)
        for rc in range(nrc):
            for (t0, tn) in groups(T, 512):
                ps, pb = pr.next()
                for kc in range(KC):
                    S.op('pe', lambda e: e.matmul(ps[0:rp, 0:tn], w1t[:, kc, rc * 128:rc * 128 + rp], X[:, kc, t0:t0 + tn],
                                                  start=(kc == 0), stop=(kc == KC - 1)), reads=[w1b, Xb], writes=[pb], signal=(kc == KC - 1))
                S.op('act', lambda e: e.activation(out=hid[:, rc, t0:t0 + tn], in_=ps[0:rp, 0:tn], func=hid_func), reads=[pb], writes=[hidb])
        orr = Ring(S, es, 'sb', 'lo_o', [128, 512], odt, 4)
        tr = Ring(S, es, 'sb', 'lo_t', [128, 512], F32, 3)
        p2 = Ring(S, es, 'ps', 'lo_p2', [128, 512], F32, 3)
        for og in range(4):
            for (t0, tn) in groups(T, 128):
                ps, pb = p2.next()
                for rc in range(nrc):
                    S.op('pe', lambda e: e.matmul(ps[:], hid[:, rc, t0:t0 + 128], w2t[:, rc, og * 512:(og + 1) * 512],
                                                  start=(rc == 0), stop=(rc == nrc - 1)), reads=[hidb, w2b], writes=[pb], signal=(rc == nrc - 1))
                o, ob = orr.next()
                if w0row is not None:
                    tt, tb = tr.next()
                    S.op('dve', lambda e: e.tensor_tensor(out=tt[:], in0=ps[:], in1=w0t[:, og * 512:(og + 1) * 512], op=ALU.add),
                         reads=[pb, w0b], writes=[tb])
                    S.op('act', lambda e: e.activation(out=o[:], in_=tt[:], func=out_func), reads=[tb], writes=[ob])
                else:
                    S.op('act', lambda e: e.activation(out=o[:], in_=ps[:], func=out_func), reads=[pb], writes=[ob])
                S.dma('sp', lambda e: e.dma_start(out=out[t0:t0 + 128, og * 512:(og + 1) * 512], in_=o[:]), reads=[ob], owner=ob)
        S.barrier()

DEC_C = 0.6065306597126334
import os as _os
PREP_STAGE = int(_os.environ.get('PREP_STAGE', '9'))
SCAN_STAGE = int(_os.environ.get('SCAN_STAGE', '9'))
RUN_SCAN = _os.environ.get('RUN_SCAN', '1') == '1'
SCORE_MODE = int(_os.environ.get('SCORE_MODE', '2'))


def bc3(ap2, n):
    return ap2.unsqueeze(2).broadcast_to([128, ap2.shape[1], n])


def hv3(ap2):
    return ap2.rearrange("p (h c) -> p h c", c=64)


def v4(ap2, a):
    return ap2.rearrange("p (a b) -> p a b", a=a)


def load_bc(S, es, name, row, dt=F32):
    nc = S.nc
    t = es.enter_context(nc.sbuf_tensor(un(name), [128, D], dt))
    b = S.buf(name)
    S.dma('sp', lambda e: e.dma_start(out=t[:], in_=row.partition_broadcast(128)), writes=[b], owner=b)
    return t, b


def phase_rwkv_prep(S, CT, I, layer, Z, T):
    nc = S.nc
    with ExitStack() as es:
        cb = S.buf('c')
        kkbc, kkb = load_bc(S, es, 'pp_kk', I['rwkv_k_k'][layer])
        kabc, kab = load_bc(S, es, 'pp_ka', I['rwkv_k_a'][layer])
        rkbc, rkb = load_bc(S, es, 'pp_rk', I['rwkv_r_k'][layer].rearrange("h c -> (h c)"))
        cm = es.enter_context(nc.sbuf_tensor(un('pp_cm'), [128, 128], F32))
        cmb = S.buf('cm')
        S.dma('sp', lambda e: e.dma_start(out=cm[:], in_=I['cmask'][0]), writes=[cmb], owner=cmb)
        negc = es.enter_context(nc.sbuf_tensor(un('pp_negc'), [128, 128], F32))
        S.op('dve', lambda e: e.memset(negc[:], -DEC_C), writes=[cmb])
        r_in = Ring(S, es, 'sb', 'pp_r', [128, D], BF16, 2)
        k_in = Ring(S, es, 'sb', 'pp_k', [128, D], BF16, 2)
        a_in = Ring(S, es, 'sb', 'pp_a', [128, D], BF16, 2)
        v_in = Ring(S, es, 'sb', 'pp_v', [128, D], BF16, 2)
        s_in = Ring(S, es, 'sb', 'pp_s', [128, D], F32, 2)
        if layer == 1:
            vg_in = Ring(S, es, 'sb', 'pp_vg', [128, D], BF16, 1)
            vf_in = Ring(S, es, 'sb', 'pp_vf', [128, D], BF16, 1)
            o_vv = Ring(S, es, 'sb', 'pp_ovv', [128, D], BF16, 2)
        T1, T1b = tile1(S, es, 'pp_t1', [128, D], F32)
        T2, T2b = tile1(S, es, 'pp_t2', [128, D], F32)
        T3, T3b = tile1(S, es, 'pp_t3', [128, D], F32)
        T4, T4b = tile1(S, es, 'pp_t4', [128, D], F32)
        ss, ssb = tile1(S, es, 'pp_ss', [128, 64], F32)
        Er = Ring(S, es, 'sb', 'pp_e', [128, 512], F32, 4)
        o_r = Ring(S, es, 'sb', 'pp_or', [128, D], BF16, 2)
        o_kp = Ring(S, es, 'sb', 'pp_okp', [128, D], BF16, 2)
        o_kt = Ring(S, es, 'sb', 'pp_okt', [128, D], BF16, 2)
        o_bn = Ring(S, es, 'sb', 'pp_obn', [128, D], BF16, 2)
        o_bo = Ring(S, es, 'sb', 'pp_obo', [128, D], BF16, 2)
        xt_r = Ring(S, es, 'sb', 'pp_xt', [128, KC, 128], BF16, 4)
        gm_r = Ring(S, es, 'sb', 'pp_gm', [128, KC], F32, 2)
        pcw = Ring(S, es, 'ps', 'pp_pcw', [128, 512], F32, 4)
        ptr = Ring(S, es, 'ps', 'pp_ptr', [128, 1024], BF16, 2)
        for ci, (t0, tn) in enumerate(groups(T, 128)):
            def ld(ring, src):
                t, b = ring.next()
                S.dma('sp', lambda e: e.dma_start(out=t[:], in_=src[t0:t0 + 128, :]), writes=[b], owner=b)
                return t, b
            r, rb = ld(r_in, Z['r'])
            k, kb = ld(k_in, Z['k'])
            a, ab = ld(a_in, Z['a'])
            v, vb = ld(v_in, Z['v'])
            sg, sgb = ld(s_in, Z['sigd'])
            if layer == 1:
                vg, vgb = ld(vg_in, Z['vg'])
                vf, vfb = ld(vf_in, Z['vfirst'])
                vv, vvb = o_vv.next()
                S.op('dve', lambda e: e.tensor_tensor(out=T4[:], in0=vf[:], in1=v[:], op=ALU.subtract), reads=[vfb, vb], writes=[T4b])
                S.op('dve', lambda e: e.tensor_tensor(out=T4[:], in0=T4[:], in1=vg[:], op=ALU.mult), reads=[vgb, T4b], writes=[T4b])
                S.op('dve', lambda e: e.tensor_tensor(out=vv[:], in0=T4[:], in1=v[:], op=ALU.add), reads=[vb, T4b], writes=[vvb])
                S.dma('sp', lambda e: e.dma_start(out=Z['vv'][t0:t0 + 128, :], in_=vv[:]), reads=[vvb], owner=vvb)
            else:
                vv, vvb = v, vb
            S.op('dve', lambda e: e.tensor_tensor(out=T1[:], in0=k[:], in1=kkbc[:], op=ALU.mult), reads=[kb, kkb], writes=[T1b])
            S.op('act', lambda e: e.activation(out=T2[:], in_=T1[:], func=AF.Square), reads=[T1b], writes=[T2b])
            S.op('dve', lambda e: e.tensor_reduce(out=ss[:, 0:32], in_=hv3(T2[:]), axis=AX.X, op=ALU.add), reads=[T2b], writes=[ssb])
            S.op('dve', lambda e: e.tensor_scalar_max(out=ss[:, 0:32], in0=ss[:, 0:32], scalar1=1e-24), reads=[ssb], writes=[ssb])
            S.op('act', lambda e: e.activation(out=ss[:, 0:32], in_=ss[:, 0:32], func=AF.Ln), reads=[ssb], writes=[ssb])
            S.op('act', lambda e: e.activation(out=ss[:, 0:32], in_=ss[:, 0:32], func=AF.Exp, scale=-0.5), reads=[ssb], writes=[ssb])
            S.op('dve', lambda e: e.tensor_tensor(out=hv3(T1[:]), in0=hv3(T1[:]), in1=bc3(ss[:, 0:32], 64), op=ALU.mult),
                 reads=[ssb, T1b], writes=[T1b])
            if PREP_STAGE < 2:
                continue
            S.op('dve', lambda e: e.scalar_tensor_tensor(out=T2[:], in0=a[:], scalar=-1.0, in1=kabc[:], op0=ALU.add, op1=ALU.mult),
                 reads=[ab, kab], writes=[T2b])
            S.op('dve', lambda e: e.scalar_tensor_tensor(out=T2[:], in0=T2[:], scalar=1.0, in1=k[:], op0=ALU.add, op1=ALU.mult),
                 reads=[kb, T2b], writes=[T2b])
            S.op('dve', lambda e: e.scalar_tensor_tensor(out=T3[:], in0=T1[:], scalar=-1.0, in1=a[:], op0=ALU.mult, op1=ALU.mult),
                 reads=[T1b, ab], writes=[T3b])
            S.op('dve', lambda e: e.tensor_tensor(out=T4[:], in0=r[:], in1=T2[:], op=ALU.mult), reads=[rb, T2b], writes=[T4b])
            S.op('pool', lambda e: e.tensor_tensor(out=T4[:], in0=T4[:], in1=rkbc[:], op=ALU.mult), reads=[rkb, T4b], writes=[T4b])
            S.op('dve', lambda e: e.tensor_reduce(out=ss[:, 32:64], in_=hv3(T4[:]), axis=AX.X, op=ALU.add), reads=[T4b], writes=[ssb])
            bo, bob = o_bo.next()
            S.op('pool', lambda e: e.tensor_tensor(out=hv3(bo[:]), in0=hv3(vv[:]), in1=bc3(ss[:, 32:64], 64), op=ALU.mult),
                 reads=[ssb, vvb], writes=[bob])
            S.dma('sp', lambda e: e.dma_start(out=Z['bonus'][t0:t0 + 128, :], in_=bo[:]), reads=[bob], owner=bob)
            if PREP_STAGE < 3:
                continue
            orr_, orb = o_r.next()
            okp, okpb = o_kp.next()
            okt, oktb = o_kt.next()
            obn, obnb = o_bn.next()
            for og in range(4):
                sl = slice(og * 512, (og + 1) * 512)
                ps, pb = pcw.next()
                S.op('pe', lambda e: e.matmul(ps[:], cm[:], sg[:, sl], start=True, stop=True), reads=[cmb, sgb], writes=[pb])
                E1, E1b = Er.next()
                S.op('act', lambda e: e.activation(out=E1[:], in_=ps[:], func=AF.Exp), reads=[pb], writes=[E1b])
                S.op('pool', lambda e: e.tensor_tensor(out=orr_[:, sl], in0=r[:, sl], in1=E1[:], op=ALU.mult), reads=[rb, E1b], writes=[orb])
                E2, E2b = Er.next()
                S.op('act', lambda e: e.activation(out=E2[:], in_=ps[:], func=AF.Exp, scale=-1.0), reads=[pb], writes=[E2b])
                S.op('pool', lambda e: e.tensor_tensor(out=okt[:, sl], in0=T2[:, sl], in1=E2[:], op=ALU.mult), reads=[T2b, E2b], writes=[oktb])
                S.op('dve', lambda e: e.tensor_tensor(out=obn[:, sl], in0=T3[:, sl], in1=E2[:], op=ALU.mult), reads=[T3b, E2b], writes=[obnb])
                E3, E3b = Er.next()
                S.op('dve', lambda e: e.scalar_tensor_tensor(out=E3[:], in0=sg[:, sl], scalar=DEC_C, in1=ps[:], op0=ALU.mult, op1=ALU.add),
                     reads=[sgb, pb], writes=[E3b])
                S.op('act', lambda e: e.activation(out=E3[:], in_=E3[:], func=AF.Exp), reads=[E3b], writes=[E3b])
                S.op('pool', lambda e: e.tensor_tensor(out=okp[:, sl], in0=T1[:, sl], in1=E3[:], op=ALU.mult), reads=[T1b, E3b], writes=[okpb])
            S.dma('sp', lambda e: e.dma_start(out=Z['kt'][t0:t0 + 128, :], in_=okt[:]), reads=[oktb], owner=oktb)
            S.dma('sp', lambda e: e.dma_start(out=Z['bn'][t0:t0 + 128, :], in_=obn[:]), reads=[obnb], owner=obnb)
            if PREP_STAGE < 4:
                continue
            gm, gmb = gm_r.next()
            for q4 in range(4):
                pg, pgb = pcw.next()
                for i4 in range(4):
                    kc = 4 * q4 + i4
                    S.op('pe', lambda e: e.matmul(pg[:, i4 * 128:(i4 + 1) * 128], sg[:, kc * 128:(kc + 1) * 128], negc[:], start=True, stop=True),
                         reads=[sgb, cmb], writes=[pgb], signal=(i4 == 3))
                S.op('act', lambda e: e.activation(out=gm[:, 4 * q4:4 * q4 + 4].unsqueeze(2), in_=v4(pg[:], 4)[:, :, 0:1], func=AF.Exp), reads=[pgb], writes=[gmb])
            S.dma('sp', lambda e: e.dma_start(out=Z['gam'][ci], in_=gm[:]), reads=[gmb], owner=gmb)
            if PREP_STAGE < 5:
                continue
            for (src, srcb, dst) in ((orr_, orb, 'rT'), (okp, okpb, 'kpT'), (okt, oktb, 'ktT'), (obn, obnb, 'bnT')):
                xt, xtb = xt_r.next()
                for half in range(2):
                    pt, ptb = ptr.next()
                    for j in range(8):
                        kc = half * 8 + j
                        S.op('pe', lambda e: e.transpose(pt[:, j * 128:(j + 1) * 128], src[:, kc * 128:(kc + 1) * 128], CT['identb'][:]),
                             reads=[srcb, cb], writes=[ptb], signal=(j == 7))
                    if half == 0:
                        S.op('act', lambda e: e.copy(out=xt[:, 0:8, :], in_=v4(pt[:], 8)), reads=[ptb], writes=[xtb])
                    else:
                        S.op('dve', lambda e: e.tensor_copy(out=xt[:, 8:16, :], in_=v4(pt[:], 8)), reads=[ptb], writes=[xtb])
                for e2 in range(2):
                    S.dma('sp', lambda e: e.dma_start(out=Z[dst][ci].rearrange("c (hp e) t -> c hp e t", e=2)[:, :, e2, :],
                                                      in_=xt[e2 * 64:(e2 + 1) * 64, :, :]), reads=[xtb], owner=xtb)
        S.barrier()


def phase_rwkv_scan(S, CT, I, layer, Z, T):
    nc = S.nc
    with ExitStack() as es:
        cb = S.buf('c')
        gnw, gnwb = load_bc(S, es, 'sc_gnw', I['rwkv_gn_w'][layer])
        gnb, gnbb = load_bc(S, es, 'sc_gnb', I['rwkv_gn_b'][layer])
        msk = es.enter_context(nc.sbuf_tensor(un('sc_msk'), [128, 3, 128], F32))
        mb = S.buf('msk')
        for i in range(3):
            S.dma('sp', lambda e: e.dma_start(out=msk[:, i, :], in_=I['cmask'][1 + i]), writes=[mb], owner=mb)
        MUS, MUI, MLS = 0, 1, 2

        def mbc(i, n):
            return msk[:, i, :].unsqueeze(1).broadcast_to([128, n, 128])
        A, Ab = tile1(S, es, 'sc_A', [64, RH, 64], F32)
        Abf, Abfb = tile1(S, es, 'sc_Abf', [64, RH, 64], BF16)
        S.op('dve', lambda e: e.memset(A[:], 0.0), writes=[Ab])
        S.op('dve', lambda e: e.memset(Abf[:], 0.0), writes=[Abfb])
        names_cm = ('rT', 'kpT', 'ktT', 'bnT')
        cm_r = {n: Ring(S, es, 'sb', 'sc_' + n, [64, RH, 128], BF16, 1) for n in names_cm}
        tm_r = {n: Ring(S, es, 'sb', 'sc_' + n, [128, D], BF16, 2) for n in ('kt', 'bn', 'vv', 'bonus', 'g')}
        gm_r = Ring(S, es, 'sb', 'sc_gam', [64, 2, KC], F32, 2)
        sc_t = {n: tile1(S, es, 'sc_s' + n, [128, 16, 128], BF16) for n in ('MkT', 'RKT', 'RBT', 'N0', 'N1', 'NT0', 'NT1', 'Q0', 'Q1')}
        stg = Ring(S, es, 'sb', 'sc_stg', [128, 512], BF16, 2)
        RHS, RHSb = tile1(S, es, 'sc_rhs', [128, 1024], BF16)
        U, Ub = tile1(S, es, 'sc_u', [128, 1024], BF16)
        y_r = Ring(S, es, 'sb', 'sc_y', [128, D], F32, 2)
        yt, ytb = tile1(S, es, 'sc_yt', [128, D], F32)
        st, stb = tile1(S, es, 'sc_st', [128, 5, 32], F32)
        z_r = Ring(S, es, 'sb', 'sc_z', [128, D], BF16, 1)
        zT_r = Ring(S, es, 'sb', 'sc_zT', [128, KC, 128], BF16, 2)
        P = Ring(S, es, 'ps', 'sc_p', [128, 512], F32, 6)
        PT = Ring(S, es, 'ps', 'sc_pt', [128, 1024], BF16, 2)
        ecount = [0]

        def cmview(d):
            return d.rearrange("hp (e c) t -> c (hp e) t", e=2)
        def output_stage(y, yb, bo, bob, gg, ggb, t0):
            S.op('dve', lambda e: e.tensor_reduce(out=st[:, 0, :], in_=hv3(y[:]), axis=AX.X, op=ALU.add), reads=[yb], writes=[stb])
            S.op('act', lambda e: e.activation(out=yt[:], in_=y[:], func=AF.Square), reads=[yb], writes=[ytb])
            S.op('dve', lambda e: e.tensor_reduce(out=st[:, 1, :], in_=hv3(yt[:]), axis=AX.X, op=ALU.add), reads=[ytb], writes=[stb])
            S.op('dve', lambda e: e.tensor_scalar_mul(out=st[:, 0, :], in0=st[:, 0, :], scalar1=1.0 / 64), reads=[stb], writes=[stb])
            S.op('dve', lambda e: e.tensor_tensor(out=st[:, 2, :], in0=st[:, 0, :], in1=st[:, 0, :], op=ALU.mult), reads=[stb], writes=[stb])
            S.op('dve', lambda e: e.scalar_tensor_tensor(out=st[:, 3, :], in0=st[:, 1, :], scalar=1.0 / 64, in1=st[:, 2, :], op0=ALU.mult, op1=ALU.subtract),
                 reads=[stb], writes=[stb])
            S.op('act', lambda e: e.activation(out=st[:, 3, :], in_=st[:, 3, :], func=AF.Ln, bias=CT['eps'][:, 1:2], scale=1.0), reads=[stb, cb], writes=[stb])
            S.op('act', lambda e: e.activation(out=st[:, 3, :], in_=st[:, 3, :], func=AF.Exp, scale=-0.5), reads=[stb], writes=[stb])
            S.op('dve', lambda e: e.tensor_tensor(out=hv3(yt[:]), in0=hv3(y[:]), in1=bc3(st[:, 0, :], 64), op=ALU.subtract), reads=[yb, stb], writes=[ytb])
            S.op('dve', lambda e: e.tensor_tensor(out=hv3(yt[:]), in0=hv3(yt[:]), in1=bc3(st[:, 3, :], 64), op=ALU.mult), reads=[stb, ytb], writes=[ytb])
            S.op('pool', lambda e: e.tensor_tensor(out=yt[:], in0=yt[:], in1=gnw[:], op=ALU.mult), reads=[gnwb, ytb], writes=[ytb])
            S.op('pool', lambda e: e.tensor_tensor(out=yt[:], in0=yt[:], in1=gnb[:], op=ALU.add), reads=[gnbb, ytb], writes=[ytb])
            S.op('pool', lambda e: e.tensor_tensor(out=yt[:], in0=yt[:], in1=bo[:], op=ALU.add), reads=[bob, ytb], writes=[ytb])
            z, zb = z_r.next()
            S.op('pool', lambda e: e.tensor_tensor(out=z[:], in0=yt[:], in1=gg[:], op=ALU.mult), reads=[ggb, ytb], writes=[zb])
            return (z, zb, t0)

        def output_tr(z, zb, t0):
            zT, zTb = zT_r.next()
            for half in range(2):
                pt, ptb = PT.next()
                for j in range(8):
                    kc = half * 8 + j
                    S.op('pe', lambda e: e.transpose(pt[:, j * 128:(j + 1) * 128], z[:, kc * 128:(kc + 1) * 128], CT['identb'][:]),
                         reads=[zb, cb], writes=[ptb], signal=(j == 7))
                S.op('act', lambda e: e.copy(out=zT[:, half * 8:half * 8 + 8, :], in_=v4(pt[:], 8)), reads=[ptb], writes=[zTb])
            S.dma('sp', lambda e: e.dma_start(out=fm(Z['zT'])[:, :, t0:t0 + 128], in_=zT[:]), reads=[zTb], owner=zTb)

        prev_out = None
        prev_z = None
        for ci, (t0, tn) in enumerate(groups(T, 128)):
            cmt = {}
            for n in names_cm:
                t, b = cm_r[n].next()
                S.dma('sp', lambda e: e.dma_start(out=t[:], in_=Z[n][ci]), writes=[b], owner=b)
                cmt[n] = (t, b)
            tmt = {}
            for n in ('kt', 'bn', 'vv', 'bonus', 'g'):
                t, b = tm_r[n].next()
                src = Z[n] if not (n == 'vv' and layer == 0) else Z['v']
                S.dma('sp', lambda e: e.dma_start(out=t[:], in_=src[t0:t0 + 128, :]), writes=[b], owner=b)
                tmt[n] = (t, b)
            gam, gamb = gm_r.next()
            S.dma('sp', lambda e: e.dma_start(out=gam[:], in_=Z['gam'][ci].rearrange("(e c) hp -> c e hp", e=2)), writes=[gamb], owner=gamb)
            rT, rTb = cmt['rT']
            kpT, kpTb = cmt['kpT']
            ktT, ktTb = cmt['ktT']
            bnT, bnTb = cmt['bnT']
            kt, ktb = tmt['kt']
            bn, bnb = tmt['bn']
            vv, vvb = tmt['vv']
            y, yb = y_r.next()
            for hh in range(2):
                H0 = 16 * hh
                kinds = (('MkT', ktT, ktTb, kpT, kpTb, MUS), ('N0', bnT, bnTb, kpT, kpTb, MUS), ('NT0', kpT, kpTb, bnT, bnTb, MLS),
                         ('RKT', ktT, ktTb, rT, rTb, MUI), ('RBT', bnT, bnTb, rT, rTb, MUI))
                for (dn, lt, ltb, rt, rtb, mi) in kinds:
                    dst, dstb = sc_t[dn]
                    for gq in range(4):
                        ps, pb = P.next()
                        for j in range(4):
                            h = H0 + 4 * gq + j
                            S.op('pe', lambda e: e.matmul(ps[:, j * 128:(j + 1) * 128], lt[:, h, :], rt[:, h, :], start=True, stop=True),
                                 reads=[ltb, rtb], writes=[pb], signal=(j == 3))
                        ecount[0] += 1
                        if ecount[0] % 2 == 0:
                            S.op('dve', lambda e: e.tensor_tensor(out=dst[:, 4 * gq:4 * gq + 4, :], in0=v4(ps[:], 4), in1=mbc(mi, 4), op=ALU.mult),
                                 reads=[pb, mb], writes=[dstb])
                        else:
                            sg_, sgb_ = stg.next()
                            S.op('act', lambda e: e.copy(out=sg_[:], in_=ps[:]), reads=[pb], writes=[sgb_])
                            S.op('pool', lambda e: e.tensor_tensor(out=dst[:, 4 * gq:4 * gq + 4, :], in0=v4(sg_[:], 4), in1=mbc(mi, 4), op=ALU.mult),
                                 reads=[sgb_, mb], writes=[dstb])
                if SCAN_STAGE < 2:
                    continue
                Ncur, Ncb = sc_t['N0']
                NTcur, NTcb = sc_t['NT0']
                Nnx, Nnb = sc_t['N1']
                NTnx, NTnb = sc_t['NT1']
                Qc, Qcb = sc_t['Q0']
                Qn, Qnb = sc_t['Q1']
                S.op('pool', lambda e: e.tensor_tensor(out=Qc[:], in0=Ncur[:], in1=CT['identb'][:].unsqueeze(1).broadcast_to([128, 16, 128]), op=ALU.add),
                     reads=[Ncb, cb], writes=[Qcb])
                for lev in range(1, 7):
                    for gq in range(4):
                        ps, pb = P.next()
                        for j in range(4):
                            hx = 4 * gq + j
                            S.op('pe', lambda e: e.matmul(ps[:, j * 128:(j + 1) * 128], Ncur[:, hx, :], NTcur[:, hx, :], start=True, stop=True),
                                 reads=[Ncb, NTcb], writes=[pb], signal=(j == 3))
                        S.op('act', lambda e: e.copy(out=NTnx[:, 4 * gq:4 * gq + 4, :], in_=v4(ps[:], 4)), reads=[pb], writes=[NTnb])
                        if lev < 6:
                            ps2, pb2 = P.next()
                            for j in range(4):
                                hx = 4 * gq + j
                                S.op('pe', lambda e: e.matmul(ps2[:, j * 128:(j + 1) * 128], NTcur[:, hx, :], Ncur[:, hx, :], start=True, stop=True),
                                     reads=[Ncb, NTcb], writes=[pb2], signal=(j == 3))
                            S.op('dve', lambda e: e.tensor_copy(out=Nnx[:, 4 * gq:4 * gq + 4, :], in_=v4(ps2[:], 4)), reads=[pb2], writes=[Nnb])
                    for gq in range(4):
                        ps, pb = P.next()
                        for j in range(4):
                            hx = 4 * gq + j
                            S.op('pe', lambda e: e.matmul(ps[:, j * 128:(j + 1) * 128], NTnx[:, hx, :], Qc[:, hx, :], start=True, stop=True),
                                 reads=[NTnb, Qcb], writes=[pb], signal=(j == 3))
                        S.op('dve', lambda e: e.tensor_tensor(out=Qn[:, 4 * gq:4 * gq + 4, :], in0=v4(ps[:], 4), in1=Qc[:, 4 * gq:4 * gq + 4, :], op=ALU.add),
                             reads=[pb, Qcb], writes=[Qnb])
                    Ncur, Ncb, Nnx, Nnb = Nnx, Nnb, Ncur, Ncb
                    NTcur, NTcb, NTnx, NTnb = NTnx, NTnb, NTcur, NTcb
                    Qc, Qcb, Qn, Qnb = Qn, Qnb, Qc, Qcb
                if hh == 0 and prev_out is not None:
                    prev_z = output_stage(*prev_out)
                    prev_out = None
                if hh == 1 and prev_z is not None:
                    output_tr(*prev_z)
                    prev_z = None
                MkT, MkTb = sc_t['MkT']
                RKT, RKTb = sc_t['RKT']
                RBT, RBTb = sc_t['RBT']
                for g8 in range(2):
                    ps, pb = P.next()
                    for j in range(8):
                        hx = 8 * g8 + j
                        h = H0 + hx
                        S.op('pe', lambda e: e.matmul(ps[:, j * 64:(j + 1) * 64], kpT[:, h, :], Abf[:, h, :], start=True, stop=False),
                             reads=[kpTb, Abfb], writes=[pb], signal=False)
                        S.op('pe', lambda e: e.matmul(ps[:, j * 64:(j + 1) * 64], MkT[:, hx, :], vv[:, h * 64:(h + 1) * 64], start=False, stop=True),
                             reads=[MkTb, vvb], writes=[pb], signal=(j == 7))
                    S.op('act', lambda e: e.copy(out=RHS[:, g8 * 512:(g8 + 1) * 512], in_=ps[:]), reads=[pb], writes=[RHSb])
                for g8 in range(2):
                    ps, pb = P.next()
                    for j in range(8):
                        hx = 8 * g8 + j
                        S.op('pe', lambda e: e.matmul(ps[:, j * 64:(j + 1) * 64], Qc[:, hx, :], RHS[:, hx * 64:(hx + 1) * 64], start=True, stop=True),
                             reads=[Qcb, RHSb], writes=[pb], signal=(j == 7))
                    S.op('act', lambda e: e.copy(out=U[:, g8 * 512:(g8 + 1) * 512], in_=ps[:]), reads=[pb], writes=[Ub])
                for g8 in range(2):
                    ps, pb = P.next()
                    for j in range(8):
                        hx = 8 * g8 + j
                        h = H0 + hx
                        S.op('pe', lambda e: e.matmul(ps[:, j * 64:(j + 1) * 64], rT[:, h, :], Abf[:, h, :], start=True, stop=False),
                             reads=[rTb, Abfb], writes=[pb], signal=False)
                        S.op('pe', lambda e: e.matmul(ps[:, j * 64:(j + 1) * 64], RKT[:, hx, :], vv[:, h * 64:(h + 1) * 64], start=False, stop=False),
                             reads=[RKTb, vvb], writes=[pb], signal=False)
                        S.op('pe', lambda e: e.matmul(ps[:, j * 64:(j + 1) * 64], RBT[:, hx, :], U[:, hx * 64:(hx + 1) * 64], start=False, stop=True),
                             reads=[RBTb, Ub], writes=[pb], signal=(j == 7))
                    c0 = (H0 + 8 * g8) * 64
                    S.op('dve', lambda e: e.tensor_copy(out=y[:, c0:c0 + 512], in_=ps[:]), reads=[pb], writes=[yb])
                if SCAN_STAGE < 4:
                    continue
                for g8 in range(2):
                    ps, pb = P.next()
                    for j in range(8):
                        hx = 8 * g8 + j
                        h = H0 + hx
                        S.op('pe', lambda e: e.matmul(ps[0:64, j * 64:(j + 1) * 64], kt[:, h * 64:(h + 1) * 64], vv[:, h * 64:(h + 1) * 64], start=True, stop=False),
                             reads=[ktb, vvb], writes=[pb], signal=False)
                        S.op('pe', lambda e: e.matmul(ps[0:64, j * 64:(j + 1) * 64], bn[:, h * 64:(h + 1) * 64], U[:, hx * 64:(hx + 1) * 64], start=False, stop=True),
                             reads=[bnb, Ub], writes=[pb], signal=(j == 7))
                    h0 = H0 + 8 * g8
                    hp0 = h0 // 2
                    S.op('dve', lambda e: e.tensor_tensor(out=A[:, h0:h0 + 8, :], in0=ps[0:64, :].rearrange("p (a b) -> p a b", a=8), in1=A[:, h0:h0 + 8, :], op=ALU.add),
                         reads=[pb, Ab], writes=[Ab])
                    S.op('pool', lambda e: e.tensor_tensor(
                        out=A[:, h0:h0 + 8, :].rearrange("c (hp e) v -> c hp e v", e=2),
                        in0=A[:, h0:h0 + 8, :].rearrange("c (hp e) v -> c hp e v", e=2),
                        in1=gam[:].rearrange("c e hp -> c hp e")[:, hp0:hp0 + 4, :].unsqueeze(3).broadcast_to([64, 4, 2, 64]), op=ALU.mult),
                        reads=[gamb, Ab], writes=[Ab])
                    S.op('act', lambda e: e.copy(out=Abf[:, h0:h0 + 8, :], in_=A[:, h0:h0 + 8, :]), reads=[Ab], writes=[Abfb])
            prev_out = (y, yb, tmt['bonus'][0], tmt['bonus'][1], tmt['g'][0], tmt['g'][1], t0)
        output_tr(*output_stage(*prev_out))
        S.barrier()


def phase_proj_fm_res(S, CT, xsrc, W, hres, T, kcn=KC):
    nc = S.nc
    with ExitStack() as es:
        X, Xb = load_resident(S, es, 'po_x', xsrc, kcn, T)
        wr = Ring(S, es, 'sb', 'po_w', [128, kcn, 256], BF16, 2)
        hr = Ring(S, es, 'sb', 'po_h', [128, 512], F32, 4)
        pr = Ring(S, es, 'ps', 'po_p', [128, 512], F32, 4)
        wv = wview(W)
        for og in range(D // 256):
            w, wb = wr.next()
            S.dma('pool', lambda e: e.dma_start(out=w[:], in_=wv[:, :, og * 256:(og + 1) * 256]), writes=[wb], owner=wb)
            for ol in range(2):
                dc = og * 2 + ol
                for (t0, tn) in groups(T, 512):
                    h, hb = hr.next()
                    S.dma('sp', lambda e: e.dma_start(out=h[:, 0:tn], in_=hres[dc, :, t0:t0 + tn]), writes=[hb], owner=hb)
                    ps, pb = pr.next()
                    for kc in range(kcn):
                        S.op('pe', lambda e: e.matmul(ps[:, 0:tn], w[:, kc, ol * 128:(ol + 1) * 128], X[:, kc, t0:t0 + tn],
                                                      start=(kc == 0), stop=(kc == kcn - 1)), reads=[wb, Xb], writes=[pb], signal=(kc == kcn - 1))
                    S.op('dve', lambda e: e.tensor_tensor(out=h[:, 0:tn], in0=h[:, 0:tn], in1=ps[:, 0:tn], op=ALU.add), reads=[pb, hb], writes=[hb])
                    S.dma('sp', lambda e: e.dma_start(out=hres[dc, :, t0:t0 + tn], in_=h[:, 0:tn]), reads=[hb], owner=hb)
        S.barrier()


def rwkv_layer(S, CT, I, layer, Z, hT, T):
    import os
    nph = int(os.environ.get('RWKV_NPH', '99'))
    xm = Z['xmix']
    Zl = dict(Z)
    if layer == 0:
        Zl['v'] = Z['vfirst']
    steps = [
        lambda: phase_rwkv_mix(S, CT, hT, I['cols'], layer, xm, T),
        lambda: phase_proj_tm(S, CT, xm[0], I['rwkv_w_r'][layer], Z['r'], T, BF16),
        lambda: phase_proj_tm(S, CT, xm[2], I['rwkv_w_k'][layer], Z['k'], T, BF16),
        lambda: phase_proj_tm(S, CT, xm[3], I['rwkv_w_v'][layer], Z['v'] if layer == 1 else Z['vfirst'], T, BF16),
        lambda: phase_lora(S, CT, xm[1], I['rwkv_dec_w1'][layer], I['rwkv_dec_w2'][layer], I['rwkv_dec_w0'][layer], 96, AF.Tanh, AF.Sigmoid, Z['sigd'], T, F32),
        lambda: phase_lora(S, CT, xm[4], I['rwkv_a_w1'][layer], I['rwkv_a_w2'][layer], I['rwkv_a_w0'][layer], 96, AF.Copy, AF.Sigmoid, Z['a'], T, BF16),
        lambda: phase_lora(S, CT, xm[5], I['rwkv_g_w1'][layer], I['rwkv_g_w2'][layer], None, 256, AF.Sigmoid, AF.Copy, Z['g'], T, BF16),
    ]
    if layer == 1:
        steps.append(lambda: phase_lora(S, CT, xm[3], I['rwkv_v_w1'][0], I['rwkv_v_w2'][0], I['rwkv_v_w0'][0], 64, AF.Copy, AF.Sigmoid, Z['vg'], T, BF16))
    steps += [
        lambda: phase_rwkv_prep(S, CT, I, layer, Zl, T),
        lambda: phase_rwkv_scan(S, CT, I, layer, Zl, T),
        lambda: phase_proj_fm_res(S, CT, Z['zT'], I['rwkv_w_o'][layer], hT, T),
    ]
    if not RUN_SCAN:
        steps = steps[:-2]
    for i, st_ in enumerate(steps):
        if i < nph:
            st_()


SB_SCALE = 128 ** -0.5


def phase_gather_q(S, CT, hT, hqT, NQB):
    nc = S.nc
    with ExitStack() as es:
        r = Ring(S, es, 'sb', 'gq_t', [128, KC, 128], F32, 3)
        pid = nc.sync.partition_id()
        off = (pid % 2) * 128 + NMETA
        hv = fm(hT)
        qv = fm(hqT)
        for i in range(NQB):
            t, b = r.next()
            S.dma('sp', lambda e: e.dma_start(out=t[:], in_=hv[:, :, bass.ds(off + 256 * i, 128)]), writes=[b], owner=b)
            S.dma('sp', lambda e: e.dma_start(out=qv[:, :, i * 128:(i + 1) * 128], in_=t[:]), reads=[b], owner=b)
        S.barrier()


def phase_headnorm_fm(S, CT, xsrc, W, gain1d, out, Tn):
    nc = S.nc
    with ExitStack() as es:
        cb = S.buf('c')
        X, Xb = load_resident(S, es, 'hn_x', xsrc, KC, Tn)
        gcol = es.enter_context(nc.sbuf_tensor(un('hn_g'), [128, 1], F32))
        gb = S.buf('g')
        S.dma('sp', lambda e: e.dma_start(out=gcol[:], in_=gain1d.rearrange("(p o) -> p o", o=1)), writes=[gb], owner=gb)
        wr = Ring(S, es, 'sb', 'hn_w', [128, KC, 128], BF16, 2)
        sr = Ring(S, es, 'sb', 'hn_sq', [128, 512], F32, 2)
        rr = Ring(S, es, 'sb', 'hn_rs', [128, 512], F32, 2)
        orr = Ring(S, es, 'sb', 'hn_o', [128, Tn], BF16, 2)
        pr = Ring(S, es, 'ps', 'hn_p', [128, 512], F32, 3)
        pr2 = Ring(S, es, 'ps', 'hn_p2', [128, 512], F32, 2)
        wv = wview(W)
        for h in range(SH):
            w, wb = wr.next()
            S.dma('pool', lambda e: e.dma_start(out=w[:], in_=wv[:, :, h * 128:(h + 1) * 128]), writes=[wb], owner=wb)
            o, ob = orr.next()
            for (t0, tn) in groups(Tn, 512):
                ps, pb = pr.next()
                for kc in range(KC):
                    S.op('pe', lambda e: e.matmul(ps[:, 0:tn], w[:, kc, :], X[:, kc, t0:t0 + tn], start=(kc == 0), stop=(kc == KC - 1)),
                         reads=[wb, Xb], writes=[pb], signal=(kc == KC - 1))
                sq, sqb = sr.next()
                S.op('act', lambda e: e.activation(out=sq[:, 0:tn], in_=ps[:, 0:tn], func=AF.Square), reads=[pb], writes=[sqb])
                p2, p2b = pr2.next()
                S.op('pe', lambda e: e.matmul(p2[:, 0:tn], CT['ones'][:], sq[:, 0:tn], start=True, stop=True), reads=[sqb, cb], writes=[p2b])
                rs, rb = rr.next()
                S.op('act', lambda e: e.activation(out=rs[:, 0:tn], in_=p2[:, 0:tn], func=AF.Ln, bias=CT['eps'][:, 0:1], scale=1.0 / 128),
                     reads=[p2b, cb], writes=[rb])
                S.op('act', lambda e: e.activation(out=rs[:, 0:tn], in_=rs[:, 0:tn], func=AF.Exp, scale=-0.5), reads=[rb], writes=[rb])
                S.op('dve', lambda e: e.scalar_tensor_tensor(out=o[:, t0:t0 + tn], in0=ps[:, 0:tn], scalar=gcol[:, 0:1], in1=rs[:, 0:tn],
                                                             op0=ALU.mult, op1=ALU.mult), reads=[pb, rb, gb], writes=[ob])
            S.dma('sp', lambda e: e.dma_start(out=out[h, :, 0:Tn], in_=o[:]), reads=[ob], owner=ob)
        S.barrier()


def phase_attention(S, CT, I, KT, Vtm, QT, OT, NXB):
    nc = S.nc
    NQB = NXB // 2
    NG = NQB // 4
    TQ = NQB * 128
    Tk = NMETA + 128 * NXB
    with ExitStack() as es:
        am = es.enter_context(nc.sbuf_tensor(un('at_am'), [128, 8, 512], F32))
        amb_ = es.enter_context(nc.sbuf_tensor(un('at_amb'), [128, 8, 512], BF16))
        tm = es.enter_context(nc.sbuf_tensor(un('at_tm'), [128, 2, 128], F32))
        mb = S.buf('am')
        S.dma('sp', lambda e: e.dma_start(out=am[:], in_=I['amask'].rearrange("j p q -> p j q")), writes=[mb], owner=mb)
        S.dma('pool', lambda e: e.dma_start(out=amb_[:], in_=I['amask'].rearrange("j p q -> p j q")), writes=[mb], owner=mb)
        S.dma('sp', lambda e: e.dma_start(out=tm[:], in_=I['tmask'].rearrange("j p q -> p j q")), writes=[mb], owner=mb)
        kr = Ring(S, es, 'sb', 'at_k', [128, Tk], BF16, 2)
        vr = Ring(S, es, 'sb', 'at_v', [128, NXB, 128], BF16, 2)
        vmr = Ring(S, es, 'sb', 'at_vm', [NMETA, 128], BF16, 2)
        qr = Ring(S, es, 'sb', 'at_q', [128, TQ], BF16, 2)
        outr = Ring(S, es, 'sb', 'at_o', [128, TQ], BF16, 2)
        Er = Ring(S, es, 'sb', 'at_e', [128, 512], F32, 2)
        SPr = Ring(S, es, 'sb', 'at_sp', [128, 512], F32, 4)
        T1r = Ring(S, es, 'sb', 'at_t1', [128, 512], F32, 2)
        Wr = Ring(S, es, 'sb', 'at_w', [128, 512], BF16, 3)
        Rr = Ring(S, es, 'sb', 'at_r', [128, 512], F32, 4)
        PZ = Ring(S, es, 'ps', 'at_pz', [128, 512], F32, 3)
        PL = Ring(S, es, 'ps', 'at_pl', [128, 512], F32, 2)
        PO = Ring(S, es, 'ps', 'at_po', [128, 512], F32, 2)
        tiles = []
        for h in range(SH):
            for g in range(NG):
                blocks = list(range(8 * g + 7, -1, -1)) + [-1]
                for bi, kb in enumerate(blocks):
                    tiles.append(dict(h=h, g=g, kb=kb, first=(bi == 0), last=(kb < 0), hfirst=(g == 0 and bi == 0), hlast=(g == NG - 1 and kb < 0)))
        hd = {}
        gd = {}

        def stA(t):
            h, g, kb = t['h'], t['g'], t['kb']
            if t['hfirst']:
                k, kb_ = kr.next()
                S.dma('sp', lambda e: e.dma_start(out=k[:], in_=KT[h, :, 0:Tk]), writes=[kb_], owner=kb_)
                v, vb = vr.next()
                S.dma('sp', lambda e: e.dma_start(out=v[:], in_=Vtm[NMETA:NMETA + 128 * NXB, h * 128:(h + 1) * 128].rearrange("(kb p) d -> p kb d", p=128)),
                      writes=[vb], owner=vb)
                vm, vmb = vmr.next()
                S.dma('sp', lambda e: e.dma_start(out=vm[:], in_=Vtm[0:NMETA, h * 128:(h + 1) * 128]), writes=[vmb], owner=vmb)
                q, qb_ = qr.next()
                S.dma('sp', lambda e: e.dma_start(out=q[:], in_=QT[h, :, 0:TQ]), writes=[qb_], owner=qb_)
                o, ob = outr.next()
                hd[h] = (k, kb_, v, vb, vm, vmb, q, qb_, o, ob)
            k, kb_, v, vb, vm, vmb, q, qb_, o, ob = hd[h]
            if t['first']:
                gd[(h, g)] = dict(po=PO.next(), R=None)
            meta = t['last']
            nk = NMETA if meta else 128
            kcols = slice(0, NMETA) if meta else slice(NMETA + 128 * kb, NMETA + 128 * (kb + 1))
            qs = slice(g * 512, (g + 1) * 512)
            masked = (not meta) and kb >= 8 * g
            j = kb - 8 * g
            pz, pzb = PZ.next()
            S.op('pe', lambda e: e.matmul(pz[0:nk, :], k[:, kcols], q[:, qs], start=True, stop=True), reads=[kb_, qb_], writes=[pzb])
            E, Eb = Er.next()
            S.op('act', lambda e: e.activation(out=E[0:nk, :], in_=pz[0:nk, :], func=AF.Exp, scale=SB_SCALE), reads=[pzb], writes=[Eb])
            sp, spb = SPr.next()
            S.op('act', lambda e: e.activation(out=sp[0:nk, :], in_=E[0:nk, :], func=AF.Ln, bias=1.0, scale=1.0), reads=[Eb], writes=[spb])
            if masked:
                S.op('pool', lambda e: e.tensor_tensor(out=sp[:, :], in0=sp[:, :], in1=am[:, j, :], op=ALU.mult), reads=[mb, spb], writes=[spb])
            t.update(nk=nk, pz=pz, pzb=pzb, sp=sp, spb=spb, masked=masked, j=j, meta=meta)

        def stB(t):
            h, g = t['h'], t['g']
            G = gd[(h, g)]
            nk, pz, pzb, sp, spb, first = t['nk'], t['pz'], t['pzb'], t['sp'], t['spb'], t['first']
            pl, plb = PL.next()
            S.op('pe', lambda e: e.matmul(pl[0:nk, :], tm[0:nk, 0, 0:nk], sp[0:nk, :], start=True, stop=first), reads=[mb, spb], writes=[plb], signal=first)
            if not first:
                R, Rb = G['R']
                S.op('pe', lambda e: e.matmul(pl[0:nk, :], tm[:, 1, 0:nk], R[:, :], start=False, stop=True), reads=[mb, Rb], writes=[plb])
            if not t['meta']:
                Rn, Rnb = Rr.next()
                if first:
                    S.op('pool', lambda e: e.tensor_copy(out=Rn[:, :], in_=sp[:, :]), reads=[spb], writes=[Rnb])
                else:
                    R, Rb = G['R']
                    S.op('pool', lambda e: e.tensor_tensor(out=Rn[:, :], in0=R[:, :], in1=sp[:, :], op=ALU.add), reads=[spb, Rb], writes=[Rnb])
                G['R'] = (Rn, Rnb)
            t1, t1b = T1r.next()
            S.op('dve', lambda e: e.scalar_tensor_tensor(out=t1[0:nk, :], in0=pz[0:nk, :], scalar=SB_SCALE, in1=sp[0:nk, :],
                                                         op0=ALU.mult, op1=ALU.subtract), reads=[pzb, spb], writes=[t1b])
            S.op('dve', lambda e: e.tensor_tensor(out=t1[0:nk, :], in0=t1[0:nk, :], in1=pl[0:nk, :], op=ALU.add), reads=[plb, t1b], writes=[t1b])
            w, wb = Wr.next()
            S.op('act', lambda e: e.activation(out=w[0:nk, :], in_=t1[0:nk, :], func=AF.Exp), reads=[t1b], writes=[wb])
            if t['masked']:
                j = t['j']
                S.op('pool', lambda e: e.tensor_tensor(out=w[:, :], in0=w[:, :], in1=amb_[:, j, :], op=ALU.mult), reads=[mb, wb], writes=[wb])
            t.update(w=w, wb=wb)

        def stC(t):
            h, g, kb = t['h'], t['g'], t['kb']
            k, kb_, v, vb, vm, vmb, q, qb_, o, ob = hd[h]
            po, pob = gd[(h, g)]['po']
            w, wb, nk, first = t['w'], t['wb'], t['nk'], t['first']
            if t['meta']:
                S.op('pe', lambda e: e.matmul(po[:, :], vm[:, :], w[0:nk, :], start=first, stop=True), reads=[vmb, wb], writes=[pob])
                qs = slice(g * 512, (g + 1) * 512)
                S.op('act', lambda e: e.copy(out=o[:, qs], in_=po[:, :]), reads=[pob], writes=[ob])
                if t['hlast']:
                    S.dma('sp', lambda e: e.dma_start(out=OT[h, :, 0:TQ], in_=o[:]), reads=[ob], owner=ob)
            else:
                S.op('pe', lambda e: e.matmul(po[:, :], v[:, kb, :], w[:, :], start=first, stop=False), reads=[vb, wb], writes=[pob], signal=False)

        nt = len(tiles)
        for step in range(nt + 2):
            if step < nt:
                stA(tiles[step])
            if 0 <= step - 1 < nt:
                stB(tiles[step - 1])
            if 0 <= step - 2 < nt:
                stC(tiles[step - 2])
        S.barrier()


def att_masks(parity):
    am = np.zeros((8, 128, 512), np.float32)
    p = np.arange(128)
    for j in range(8):
        for i in range(4):
            qb = 2 * i + parity
            if j < qb:
                am[j, :, i * 128:(i + 1) * 128] = 1.0
            elif j == qb:
                am[j, :, i * 128:(i + 1) * 128] = (p[:, None] < p[None, :])
    tmk = np.zeros((2, 128, 128), np.float32)
    tmk[0] = -(p[:, None] > p[None, :]).astype(np.float32)
    tmk[1] = -1.0
    return am, tmk


def const_masks():
    cm = np.zeros((4, 128, 128), np.float32)
    i = np.arange(128)
    cm[0] = np.where(i[:, None] <= i[None, :], -DEC_C, 0.0)
    cm[1] = (i[:, None] < i[None, :])
    cm[2] = (i[:, None] <= i[None, :])
    cm[3] = (i[:, None] > i[None, :])
    return cm


IN_SPECS = [
    ('cols', [128, NCOLS, KC]), ('ident', [128, 128]), ('cmask', [4, 128, 128]),
    ('ffn_w_gate', [4, D, FF]), ('ffn_w_up', [4, D, FF]), ('ffn_w_down', [4, FF, D]),
    ('rwkv_w_r', [2, D, D]), ('rwkv_w_k', [2, D, D]), ('rwkv_w_v', [2, D, D]), ('rwkv_w_o', [2, D, D]),
    ('rwkv_dec_w0', [2, D]), ('rwkv_dec_w1', [2, D, 96]), ('rwkv_dec_w2', [2, 96, D]),
    ('rwkv_a_w0', [2, D]), ('rwkv_a_w1', [2, D, 96]), ('rwkv_a_w2', [2, 96, D]),
    ('rwkv_g_w1', [2, D, 256]), ('rwkv_g_w2', [2, 256, D]),
    ('rwkv_k_k', [2, D]), ('rwkv_k_a', [2, D]), ('rwkv_r_k', [2, 32, 64]), ('rwkv_gn_w', [2, D]), ('rwkv_gn_b', [2, D]),
    ('rwkv_v_w0', [1, D]), ('rwkv_v_w1', [1, D, 64]), ('rwkv_v_w2', [1, 64, D]),
    ('amask', [8, 128, 512]), ('tmask', [2, 128, 128]),
    ('sb_w_k', [D, D]), ('sb_w_v', [D, D]), ('sb_k_gain', [128]), ('sb_w_q', [2, D, D]), ('sb_q_gain', [2, 128]), ('sb_w_o', [2, D, D]),
]


def build(NXB, mode='full', dbg=()):
    T = 128 * (NXB + 1)
    nc = bass.Bass("TRN2", target_bir_lowering=False)
    I = {}
    I['xin'] = nc.dram_tensor('xin', [T, D], F32, kind="ExternalInput").ap()
    for name, shape in IN_SPECS:
        I[name] = nc.dram_tensor(name, list(shape), F32, kind="ExternalInput").ap()

    def scratch(name, shape, dt):
        kind = "ExternalOutput" if name in dbg else "Internal"
        return nc.dram_tensor(name, list(shape), dt, kind=kind).ap()

    hT = scratch('hT', [KC, 128, T], F32)
    xn = scratch('xn', [KC, 128, T], BF16)
    actT = scratch('actT', [FC, 128, T], BF16)
    Z = {'xmix': [scratch('xmix%d' % i, [KC, 128, T], BF16) for i in range(6)]}
    for n in ('r', 'k', 'v', 'vfirst', 'a', 'g', 'vg', 'vv', 'bonus', 'kt', 'bn'):
        Z[n] = scratch('z_' + n, [T, D], BF16)
    Z['sigd'] = scratch('z_sigd', [T, D], F32)
    Z['gam'] = scratch('z_gam', [T // 128, 128, KC], F32)
    Z['zT'] = scratch('z_zT', [KC, 128, T], BF16)
    for n in ('rT', 'kpT', 'ktT', 'bnT'):
        Z[n] = scratch('z_' + n, [T // 128, 64, RH, 128], BF16)
    NQB = NXB // 2
    TQ = NQB * 128
    hqT = scratch('hqT', [KC, 128, TQ], F32)
    KT = scratch('KT', [SH, 128, T], BF16)
    Vtm = scratch('Vtm', [T, D], BF16)
    QT = scratch('QT', [SH, 128, TQ], BF16)
    OT = scratch('OT', [SH, 128, TQ], BF16)
    full = mode in ('full', 'att_test')
    out = nc.dram_tensor('out', [TQ if full else T, D], F32, kind="ExternalOutput").ap()

    def ffn(S, CT, layer, hres, Tn):
        phase_norm(S, CT, hres, I['cols'], COLS[('ffn_norm_g', layer)], xn, Tn)
        phase_ffn_gateup(S, CT, xn, I['ffn_w_gate'][layer], I['ffn_w_up'][layer], actT, Tn)
        phase_ffn_down(S, CT, actT, I['ffn_w_down'][layer], hres, Tn)

    with ExitStack() as es:
        S = Sched(nc, es)
        CT = load_consts(S, es, I)
        phase_in_transpose(S, CT, I['xin'], hT, T)
        if mode == 'ffn_test':
            ffn(S, CT, 0, hT, T)
        if mode == 'rwkv_test':
            rwkv_layer(S, CT, I, 0, Z, hT, T)
        if mode == 'full':
            rwkv_layer(S, CT, I, 0, Z, hT, T)
            ffn(S, CT, 0, hT, T)
            rwkv_layer(S, CT, I, 1, Z, hT, T)
            ffn(S, CT, 1, hT, T)
        if mode == 'rwkv2_test':
            rwkv_layer(S, CT, I, 0, Z, hT, T)
            ffn(S, CT, 0, hT, T)
            rwkv_layer(S, CT, I, 1, Z, hT, T)
        if full:
            phase_norm(S, CT, hT, I['cols'], COLS[('kv_norm_g', 0)], xn, T)
            phase_headnorm_fm(S, CT, xn, I['sb_w_k'], I['sb_k_gain'], KT, T)
            phase_proj_tm(S, CT, xn, I['sb_w_v'], Vtm, T, BF16)
            phase_gather_q(S, CT, hT, hqT, NQB)
            for j in range(2):
                phase_norm(S, CT, hqT, I['cols'], COLS[('mix_norm_g', 2 + j)], xn, TQ)
                phase_headnorm_fm(S, CT, xn, I['sb_w_q'][j], I['sb_q_gain'][j], QT, TQ)
                phase_attention(S, CT, I, KT, Vtm, QT, OT, NXB)
                phase_proj_fm_res(S, CT, OT, I['sb_w_o'][j], hqT, TQ)
                ffn(S, CT, 2 + j, hqT, TQ)
            phase_out_transpose(S, CT, hqT, out, TQ, 0)
        else:
            phase_out_transpose(S, CT, hT, out, T, 0)
        print("instructions:", S.ninst)
    return nc


def host_inputs(inputs):
    d = {k: np.ascontiguousarray(np.asarray(v), dtype=np.float32) for k, v in inputs.items()}
    base = {'cols': pack_cols(d), 'ident': np.eye(128, dtype=np.float32), 'cmask': const_masks(), 'tmask': att_masks(0)[1], 'amask': att_masks(0)[0]}
    for name, shape in IN_SPECS:
        if name not in base:
            base[name] = d[name].reshape(shape)
    return base


def kernel(**inputs):
    NXB = 32
    T = 128 * (NXB + 1)
    base = host_inputs(inputs)
    x = np.asarray(inputs['x'], np.float32)
    meta = np.asarray(inputs['meta_tokens'], np.float32)
    in_maps = []
    for c in range(8):
        b = c // 2
        xin = np.zeros((T, D), np.float32)
        xin[:NMETA] = meta
        xin[NMETA:NMETA + 4096] = x[b]
        m = dict(base)
        m['xin'] = xin
        m['amask'] = att_masks(c % 2)[0]
        in_maps.append(m)
    nc = build(NXB, mode='full')
    res = run_bass_kernel_spmd(nc, in_maps, core_ids=list(range(8)))
    out = np.zeros((4, 4096, D), np.float32)
    for c in range(8):
        o = np.asarray(res.results[c]['out']).reshape(NXB // 2, 128, D)
        out[c // 2].reshape(NXB // 2, 2, 128, D)[:, c % 2] = o
    return out
```

```python
import numpy as np
import ml_dtypes
from contextlib import ExitStack
import concourse.bass as bass
import concourse.mybir as mybir
from concourse.bass_utils import run_bass_kernel_spmd

F32 = mybir.dt.float32
BF16 = mybir.dt.bfloat16
AF = mybir.ActivationFunctionType
ALU = mybir.AluOpType
AX = mybir.AxisListType

D = 2048
KC = 16
FF = 5632
FC = 44
NMETA = 16
RH = 32
SH = 16
RMS_EPS = 1e-6
GN_EPS = 64e-5
ENG = ('pe', 'act', 'dve', 'pool', 'sp')
NDS = 48


_UID = [0]


def un(name):
    _UID[0] += 1
    return '%s_%d' % (name, _UID[0])


class DSem:
    def __init__(self, h):
        self.h = h
        self.count = 0


class Buf:
    __slots__ = ('w', 'r', 'ds', 'name')

    def __init__(self, name=''):
        self.w = None
        self.r = {}
        self.ds = None
        self.name = name


class Sched:
    def __init__(self, nc, es):
        self.nc = nc
        self.eng = {'pe': nc.tensor, 'act': nc.scalar, 'dve': nc.vector, 'pool': nc.gpsimd, 'sp': nc.sync}
        self.sem = {e: es.enter_context(nc.semaphore('s_' + e)) for e in ENG}
        self.cnt = {e: 0 for e in ENG}
        self.seen = {e: {} for e in ENG}
        self.free_ds = [DSem(es.enter_context(nc.semaphore('d%d' % i))) for i in range(NDS)]
        self.used_ds = []
        self.bufs = []
        self.ninst = 0

    def buf(self, name=''):
        b = Buf(name)
        self.bufs.append(b)
        return b

    def _waits(self, e, evs):
        need = {}
        for key, val in evs:
            if key == 'pe' and e == 'pe':
                continue
            if self.seen[e].get(key, 0) >= val:
                continue
            if need.get(key, 0) < val:
                need[key] = val
        for key, val in need.items():
            self.seen[e][key] = val
            h = self.sem[key] if isinstance(key, str) else key.h
            self.eng[e].wait_ge(h, val)
            self.ninst += 1

    @staticmethod
    def _deps(reads, writes):
        evs = []
        for b in reads:
            if b.w is not None:
                evs.append(b.w)
        for b in writes:
            if b.w is not None:
                evs.append(b.w)
            evs.extend(b.r.items())
        return evs

    @staticmethod
    def _mark(ev, reads, writes):
        k, v = ev
        for b in reads:
            if b.r.get(k, 0) < v:
                b.r[k] = v
        for b in writes:
            b.w = ev
            b.r = {}

    def op(self, e, fn, reads=(), writes=(), signal=True, disjoint=False):
        deps = self._deps(reads, writes)
        if disjoint:
            own = [b.w for b in writes if b.w is not None and b.w[0] == e]
            deps = [d for d in deps if d not in own]
        self._waits(e, deps)
        ins = fn(self.eng[e])
        self.ninst += 1
        if signal:
            self.cnt[e] += 1
            ins.then_inc(self.sem[e], 1)
            ev = (e, self.cnt[e])
        else:
            ev = (e, self.cnt[e] + 1)
        self._mark(ev, reads, writes)
        return ins

    def dma(self, q, fn, reads=(), writes=(), owner=None):
        self._waits(q, self._deps(reads, writes))
        if owner.ds is None:
            owner.ds = self.free_ds.pop()
            self.used_ds.append(owner.ds)
        ds = owner.ds
        ins = fn(self.eng[q])
        self.ninst += 1
        ds.count += 16
        ins.then_inc(ds.h, 16)
        self._mark((ds, ds.count), reads, writes)
        return ins

    def barrier(self):
        evs = [(e, self.cnt[e]) for e in ENG if self.cnt[e] > 0]
        evs += [(ds, ds.count) for ds in self.used_ds]
        for e in ENG:
            self._waits(e, evs)
        for b in self.bufs:
            b.w = None
            b.r = {}
            b.ds = None
        self.free_ds.extend(self.used_ds)
        self.used_ds = []
        self.bufs = []


class Ring:
    def __init__(self, S, es, kind, name, shape, dtype, n):
        nc = S.nc
        self.t = []
        self.b = []
        for i in range(n):
            if kind == 'sb':
                t = es.enter_context(nc.sbuf_tensor(un('%s%d' % (name, i)), shape, dtype))
            else:
                t = es.enter_context(nc.psum_tensor(un('%s%d' % (name, i)), shape, dtype))
            self.t.append(t)
            self.b.append(S.buf(name))
        self.i = 0
        self.n = n

    def next(self):
        i = self.i % self.n
        self.i += 1
        return self.t[i], self.b[i]


def tile1(S, es, name, shape, dtype, kind='sb'):
    r = Ring(S, es, kind, name, shape, dtype, 1)
    return r.t[0], r.b[0]


def groups(T, g):
    out = []
    t0 = 0
    while t0 < T:
        out.append((t0, min(g, T - t0)))
        t0 += g
    return out


def fm(d):
    return d.rearrange("kc p t -> p kc t")


def colvec(w1d, n):
    return w1d.rearrange("(kc p) -> p kc", p=128)


def load_consts(S, es, C):
    nc = S.nc
    t = {}
    t['ident'] = es.enter_context(nc.sbuf_tensor(un('c_ident'), [128, 128], F32))
    t['identb'] = es.enter_context(nc.sbuf_tensor(un('c_identb'), [128, 128], BF16))
    t['ones'] = es.enter_context(nc.sbuf_tensor(un('c_ones'), [128, 128], F32))
    t['eps'] = es.enter_context(nc.sbuf_tensor(un('c_eps'), [128, 2], F32))
    b = S.buf('const')
    S.dma('sp', lambda e: e.dma_start(out=t['ident'][:], in_=C['ident']), writes=[b], owner=b)
    S.dma('pool', lambda e: e.dma_start(out=t['identb'][:], in_=C['ident']), writes=[b], owner=b)
    S.op('dve', lambda e: e.memset(t['ones'][:], 1.0), writes=[b])
    S.op('dve', lambda e: e.memset(t['eps'][:, 0:1], RMS_EPS), writes=[b])
    S.op('dve', lambda e: e.memset(t['eps'][:, 1:2], GN_EPS), writes=[b])
    S.barrier()
    return t


def phase_in_transpose(S, CT, xin, hT, T):
    nc = S.nc
    with ExitStack() as es:
        cb = S.buf('c')
        xr = Ring(S, es, 'sb', 'it_x', [128, D], F32, 3)
        hr = Ring(S, es, 'sb', 'it_h', [128, KC, 512], F32, 2)
        pr = Ring(S, es, 'ps', 'it_p', [128, 512], F32, 4)
        hv = fm(hT)
        for (t0, tn) in groups(T, 512):
            hb, hbb = hr.next()
            for j in range(tn // 128):
                xt, xb = xr.next()
                r0 = t0 + j * 128
                S.dma('sp', lambda e, xt=xt, r0=r0: e.dma_start(out=xt[:], in_=xin[r0:r0 + 128, :]),
                      writes=[xb], owner=xb)
                for q in range(4):
                    ps, pb = pr.next()
                    for i in range(4):
                        kc = 4 * q + i
                        S.op('pe', lambda e, ps=ps, xt=xt, kc=kc, i=i: e.transpose(
                            ps[:, i * 128:(i + 1) * 128], xt[:, kc * 128:(kc + 1) * 128], CT['ident'][:]),
                            reads=[xb, cb], writes=[pb], signal=(i == 3))
                    eng = 'dve' if q % 2 == 0 else 'act'
                    if eng == 'dve':
                        S.op('dve', lambda e, ps=ps, hb=hb, q=q, j=j: e.tensor_copy(
                            out=hb[:, 4 * q:4 * q + 4, j * 128:(j + 1) * 128],
                            in_=ps[:].rearrange("p (a b) -> p a b", a=4)), reads=[pb], writes=[hbb])
                    else:
                        S.op('act', lambda e, ps=ps, hb=hb, q=q, j=j: e.copy(
                            out=hb[:, 4 * q:4 * q + 4, j * 128:(j + 1) * 128],
                            in_=ps[:].rearrange("p (a b) -> p a b", a=4)), reads=[pb], writes=[hbb])
            S.dma('sp', lambda e, hb=hb, t0=t0, tn=tn: e.dma_start(out=hv[:, :, t0:t0 + tn], in_=hb[:, :, 0:tn]),
                  reads=[hbb], owner=hbb)
        S.barrier()


COLS = {}
_ci = 0
for _l in range(4):
    COLS[('ffn_norm_g', _l)] = _ci; _ci += 1
for _l in range(4):
    COLS[('mix_norm_g', _l)] = _ci; _ci += 1
COLS[('kv_norm_g', 0)] = _ci; _ci += 1
for _l in range(2):
    for _i in range(6):
        COLS[('rwkv_mu', _l, _i)] = _ci; _ci += 1
NCOLS = _ci


def pack_cols(inputs):
    out = np.zeros((128, NCOLS, KC), np.float32)
    for key, i in COLS.items():
        a = inputs[key[0]]
        v = a[key[1]] if key[0] != 'kv_norm_g' else a
        if key[0] == 'rwkv_mu':
            v = v[key[2]]
        out[:, i, :] = np.asarray(v).reshape(KC, 128).T
    return out


def load_cols(S, es, name, cols, idxs):
    nc = S.nc
    n = len(idxs)
    t = es.enter_context(nc.sbuf_tensor(un(name), [128, n, KC], F32))
    b = S.buf(name)
    for i, ix in enumerate(idxs):
        S.dma('sp', lambda e, i=i, ix=ix: e.dma_start(out=t[:, i, :], in_=cols[:, ix, :]), writes=[b], owner=b)
    return t, b


def phase_norm(S, CT, hsrc, cols, gidx, xout, T):
    nc = S.nc
    with ExitStack() as es:
        cb = S.buf('c')
        g, gb = load_cols(S, es, 'n_g', cols, [gidx])
        hr = Ring(S, es, 'sb', 'n_h', [128, KC, 512], F32, 2)
        sr = Ring(S, es, 'sb', 'n_sq', [128, 512], F32, 3)
        rr = Ring(S, es, 'sb', 'n_rs', [128, 512], F32, 2)
        orr = Ring(S, es, 'sb', 'n_o', [128, KC, 512], BF16, 2)
        pr = Ring(S, es, 'ps', 'n_p', [128, 512], F32, 2)
        hv = fm(hsrc)
        ov = fm(xout)
        for (t0, tn) in groups(T, 512):
            h, hb = hr.next()
            S.dma('sp', lambda e, h=h, t0=t0, tn=tn: e.dma_start(out=h[:, :, 0:tn], in_=hv[:, :, t0:t0 + tn]),
                  writes=[hb], owner=hb)
            ps, pb = pr.next()
            for kc in range(KC):
                sq, sb_ = sr.next()
                S.op('act', lambda e, sq=sq, h=h, kc=kc, tn=tn: e.activation(
                    out=sq[:, 0:tn], in_=h[:, kc, 0:tn], func=AF.Square), reads=[hb], writes=[sb_])
                S.op('pe', lambda e, ps=ps, sq=sq, kc=kc, tn=tn: e.matmul(
                    ps[:, 0:tn], CT['ones'][:], sq[:, 0:tn], start=(kc == 0), stop=(kc == KC - 1)),
                    reads=[sb_, cb], writes=[pb], signal=True)
            rs, rb = rr.next()
            S.op('act', lambda e, rs=rs, ps=ps, tn=tn: e.activation(
                out=rs[:, 0:tn], in_=ps[:, 0:tn], func=AF.Ln, bias=CT['eps'][:, 0:1], scale=1.0 / D),
                reads=[pb, cb], writes=[rb])
            S.op('act', lambda e, rs=rs, tn=tn: e.activation(
                out=rs[:, 0:tn], in_=rs[:, 0:tn], func=AF.Exp, scale=-0.5), reads=[rb], writes=[rb])
            o, ob = orr.next()
            for kc in range(KC):
                S.op('dve', lambda e, o=o, h=h, rs=rs, kc=kc, tn=tn: e.scalar_tensor_tensor(
                    out=o[:, kc, 0:tn], in0=h[:, kc, 0:tn], scalar=g[:, 0, kc:kc + 1], in1=rs[:, 0:tn],
                    op0=ALU.mult, op1=ALU.mult), reads=[hb, rb, gb], writes=[ob], disjoint=(kc > 0))
            S.dma('sp', lambda e, o=o, t0=t0, tn=tn: e.dma_start(out=ov[:, :, t0:t0 + tn], in_=o[:, :, 0:tn]),
                  reads=[ob], owner=ob)
        S.barrier()


def load_resident(S, es, name, src, kcn, T):
    nc = S.nc
    X = es.enter_context(nc.sbuf_tensor(un(name), [128, kcn, T], BF16))
    Xb = S.buf(name)
    sv = fm(src)
    for k0 in range(0, kcn, 4):
        k1 = min(kcn, k0 + 4)
        S.dma('sp', lambda e, k0=k0, k1=k1: e.dma_start(out=X[:, k0:k1, :], in_=sv[:, k0:k1, 0:T]),
              writes=[Xb], owner=Xb)
    return X, Xb


def wview(W):
    return W.rearrange("(kc p) f -> p kc f", p=128)


def phase_ffn_gateup(S, CT, xn, Wg, Wu, actT, T):
    nc = S.nc
    with ExitStack() as es:
        X, Xb = load_resident(S, es, 'fg_x', xn, KC, T)
        wr = Ring(S, es, 'sb', 'fg_w', [128, 2, KC, 256], BF16, 2)
        ar = Ring(S, es, 'sb', 'fg_a', [128, T], BF16, 2)
        tr = Ring(S, es, 'sb', 'fg_t', [128, 512], F32, 3)
        pg = Ring(S, es, 'ps', 'fg_pg', [128, 512], F32, 3)
        pu = Ring(S, es, 'ps', 'fg_pu', [128, 512], F32, 3)
        wgv = wview(Wg)
        wuv = wview(Wu)
        for og in range(FC // 2):
            w, wb = wr.next()
            c0 = og * 256
            for k0 in range(0, KC, 8):
                S.dma('pool', lambda e, w=w, c0=c0, k0=k0: e.dma_start(out=w[:, 0, k0:k0 + 8, :], in_=wgv[:, k0:k0 + 8, c0:c0 + 256]),
                      writes=[wb], owner=wb)
                S.dma('pool', lambda e, w=w, c0=c0, k0=k0: e.dma_start(out=w[:, 1, k0:k0 + 8, :], in_=wuv[:, k0:k0 + 8, c0:c0 + 256]),
                      writes=[wb], owner=wb)
            for ol in range(2):
                oc = og * 2 + ol
                a, ab = ar.next()
                for (t0, tn) in groups(T, 512):
                    p1, p1b = pg.next()
                    p2, p2b = pu.next()
                    for kc in range(KC):
                        S.op('pe', lambda e, p1=p1, w=w, ol=ol, kc=kc, t0=t0, tn=tn: e.matmul(
                            p1[:, 0:tn], w[:, 0, kc, ol * 128:(ol + 1) * 128], X[:, kc, t0:t0 + tn],
                            start=(kc == 0), stop=(kc == KC - 1)), reads=[wb, Xb], writes=[p1b], signal=(kc == KC - 1))
                    for kc in range(KC):
                        S.op('pe', lambda e, p2=p2, w=w, ol=ol, kc=kc, t0=t0, tn=tn: e.matmul(
                            p2[:, 0:tn], w[:, 1, kc, ol * 128:(ol + 1) * 128], X[:, kc, t0:t0 + tn],
                            start=(kc == 0), stop=(kc == KC - 1)), reads=[wb, Xb], writes=[p2b], signal=(kc == KC - 1))
                    tt, tb = tr.next()
                    S.op('act', lambda e, tt=tt, p1=p1, tn=tn: e.activation(out=tt[:, 0:tn], in_=p1[:, 0:tn], func=AF.Silu),
                         reads=[p1b], writes=[tb])
                    S.op('dve', lambda e, a=a, tt=tt, p2=p2, t0=t0, tn=tn: e.tensor_tensor(
                        out=a[:, t0:t0 + tn], in0=tt[:, 0:tn], in1=p2[:, 0:tn], op=ALU.mult),
                        reads=[tb, p2b], writes=[ab], disjoint=(t0 > 0))
                S.dma('sp', lambda e, a=a, oc=oc: e.dma_start(out=actT[oc, :, 0:T], in_=a[:]), reads=[ab], owner=ab)
        S.barrier()


def phase_ffn_down(S, CT, actT, Wd, hT, T):
    nc = S.nc
    with ExitStack() as es:
        wt = es.enter_context(nc.sbuf_tensor(un('fd_w'), [128, FC, 512], BF16))
        wb = S.buf('fd_w')
        ar = Ring(S, es, 'sb', 'fd_a', [128, FC, 512], BF16, 2)
        hr = Ring(S, es, 'sb', 'fd_h', [128, 512], F32, 4)
        pr = Ring(S, es, 'ps', 'fd_p', [128, 512], F32, 4)
        wv = wview(Wd)
        av = fm(actT)
        for q in range(4):
            for k0 in range(0, FC, 11):
                S.dma('pool', lambda e, k0=k0, q=q: e.dma_start(out=wt[:, k0:k0 + 11, :], in_=wv[:, k0:k0 + 11, q * 512:(q + 1) * 512]),
                      writes=[wb], owner=wb)
            for (t0, tn) in groups(T, 512):
                a, ab = ar.next()
                for k0 in range(0, FC, 11):
                    S.dma('sp', lambda e, a=a, k0=k0, t0=t0, tn=tn: e.dma_start(out=a[:, k0:k0 + 11, 0:tn], in_=av[:, k0:k0 + 11, t0:t0 + tn]),
                          writes=[ab], owner=ab)
                for ol in range(4):
                    dc = q * 4 + ol
                    h, hb = hr.next()
                    S.dma('sp', lambda e, h=h, dc=dc, t0=t0, tn=tn: e.dma_start(out=h[:, 0:tn], in_=hT[dc, :, t0:t0 + tn]),
                          writes=[hb], owner=hb)
                    ps, pb = pr.next()
                    for fc in range(FC):
                        S.op('pe', lambda e, ps=ps, a=a, ol=ol, fc=fc, tn=tn: e.matmul(
                            ps[:, 0:tn], wt[:, fc, ol * 128:(ol + 1) * 128], a[:, fc, 0:tn],
                            start=(fc == 0), stop=(fc == FC - 1)), reads=[wb, ab], writes=[pb], signal=(fc == FC - 1))
                    S.op('dve', lambda e, h=h, ps=ps, tn=tn: e.tensor_tensor(
                        out=h[:, 0:tn], in0=h[:, 0:tn], in1=ps[:, 0:tn], op=ALU.add), reads=[pb, hb], writes=[hb])
                    S.dma('sp', lambda e, h=h, dc=dc, t0=t0, tn=tn: e.dma_start(out=hT[dc, :, t0:t0 + tn], in_=h[:, 0:tn]),
                          reads=[hb], owner=hb)
        S.barrier()


def phase_out_transpose(S, CT, hT, out, T, col0):
    nc = S.nc
    with ExitStack() as es:
        cb = S.buf('c')
        hr = Ring(S, es, 'sb', 'ot_h', [128, KC, 128], F32, 3)
        orr = Ring(S, es, 'sb', 'ot_o', [128, D], F32, 2)
        pr = Ring(S, es, 'ps', 'ot_p', [128, 512], F32, 4)
        hv = fm(hT)
        for (t0, tn) in groups(T, 128):
            h, hb = hr.next()
            S.dma('sp', lambda e, h=h, t0=t0: e.dma_start(out=h[:], in_=hv[:, :, col0 + t0:col0 + t0 + 128]),
                  writes=[hb], owner=hb)
            o, ob = orr.next()
            for q in range(4):
                ps, pb = pr.next()
                for i in range(4):
                    kc = 4 * q + i
                    S.op('pe', lambda e, ps=ps, h=h, kc=kc, i=i: e.transpose(
                        ps[:, i * 128:(i + 1) * 128], h[:, kc, :], CT['ident'][:]),
                        reads=[hb, cb], writes=[pb], signal=(i == 3))
                if q % 2 == 0:
                    S.op('dve', lambda e, ps=ps, o=o, q=q: e.tensor_copy(out=o[:, q * 512:(q + 1) * 512], in_=ps[:]),
                         reads=[pb], writes=[ob])
                else:
                    S.op('act', lambda e, ps=ps, o=o, q=q: e.copy(out=o[:, q * 512:(q + 1) * 512], in_=ps[:]),
                         reads=[pb], writes=[ob])
            S.dma('sp', lambda e, o=o, t0=t0: e.dma_start(out=out[t0:t0 + 128, :], in_=o[:]), reads=[ob], owner=ob)
        S.barrier()


def phase_rwkv_mix(S, CT, hT, cols, layer, xmix, T):
    nc = S.nc
    G = 256
    with ExitStack() as es:
        cb = S.buf('c')
        idxs = [COLS[('mix_norm_g', layer)]] + [COLS[('rwkv_mu', layer, i)] for i in range(6)]
        g, gb = load_cols(S, es, 'm_g', cols, idxs)
        hr = Ring(S, es, 'sb', 'm_h', [128, KC, G], F32, 2)
        sr = Ring(S, es, 'sb', 'm_sq', [128, G], F32, 3)
        rr = Ring(S, es, 'sb', 'm_rs', [128, G], F32, 2)
        hnr = Ring(S, es, 'sb', 'm_hn', [128, KC, G + 1], F32, 2)
        xxr = Ring(S, es, 'sb', 'm_xx', [128, KC, G], F32, 1)
        orr = Ring(S, es, 'sb', 'm_o', [128, KC, G], BF16, 6)
        pr = Ring(S, es, 'ps', 'm_p', [128, G], F32, 2)
        hv = fm(hT)
        prev = None
        for (t0, tn) in groups(T, G):
            h, hb = hr.next()
            S.dma('sp', lambda e: e.dma_start(out=h[:, :, 0:tn], in_=hv[:, :, t0:t0 + tn]), writes=[hb], owner=hb)
            ps, pb = pr.next()
            for kc in range(KC):
                sq, sb_ = sr.next()
                S.op('act', lambda e: e.activation(out=sq[:, 0:tn], in_=h[:, kc, 0:tn], func=AF.Square), reads=[hb], writes=[sb_])
                S.op('pe', lambda e: e.matmul(ps[:, 0:tn], CT['ones'][:], sq[:, 0:tn], start=(kc == 0), stop=(kc == KC - 1)),
                     reads=[sb_, cb], writes=[pb], signal=True)
            rs, rb = rr.next()
            S.op('act', lambda e: e.activation(out=rs[:, 0:tn], in_=ps[:, 0:tn], func=AF.Ln, bias=CT['eps'][:, 0:1], scale=1.0 / D),
                 reads=[pb, cb], writes=[rb])
            S.op('act', lambda e: e.activation(out=rs[:, 0:tn], in_=rs[:, 0:tn], func=AF.Exp, scale=-0.5), reads=[rb], writes=[rb])
            hn, hnb = hnr.next()
            if prev is None:
                S.op('pool', lambda e: e.memset(hn[:, :, 0:1], 0.0), writes=[hnb])
            else:
                phn, phnb, ptn = prev
                S.op('pool', lambda e: e.tensor_copy(out=hn[:, :, 0:1], in_=phn[:, :, ptn:ptn + 1]), reads=[phnb], writes=[hnb])
            for kc in range(KC):
                S.op('dve', lambda e: e.scalar_tensor_tensor(
                    out=hn[:, kc, 1:1 + tn], in0=h[:, kc, 0:tn], scalar=g[:, 0, kc:kc + 1], in1=rs[:, 0:tn],
                    op0=ALU.mult, op1=ALU.mult), reads=[hb, rb, gb], writes=[hnb], disjoint=(kc > 0))
            xx, xxb = xxr.next()
            S.op('pool', lambda e: e.tensor_tensor(out=xx[:, :, 0:tn], in0=hn[:, :, 0:tn], in1=hn[:, :, 1:1 + tn], op=ALU.subtract),
                 reads=[hnb], writes=[xxb])
            for i in range(6):
                o, ob = orr.next()
                for kc in range(KC):
                    S.op('dve', lambda e: e.scalar_tensor_tensor(
                        out=o[:, kc, 0:tn], in0=xx[:, kc, 0:tn], scalar=g[:, 1 + i, kc:kc + 1], in1=hn[:, kc, 1:1 + tn],
                        op0=ALU.mult, op1=ALU.add), reads=[xxb, hnb, gb], writes=[ob], disjoint=(kc > 0))
                S.dma('sp', lambda e: e.dma_start(out=fm(xmix[i])[:, :, t0:t0 + tn], in_=o[:, :, 0:tn]), reads=[ob], owner=ob)
            prev = (hn, hnb, tn)
        S.barrier()


def gemm_tm(S, CT, es, X, Xb, T, W, fout, epilogue, wname='gw', pname='gp'):
    wr = Ring(S, es, 'sb', wname, [128, KC, 512], BF16, 2)
    pr = Ring(S, es, 'ps', pname, [128, 512], F32, 4)
    wv = wview(W)
    for og in range(fout // 512):
        w, wb = wr.next()
        for k0 in range(0, KC, 8):
            S.dma('pool', lambda e: e.dma_start(out=w[:, k0:k0 + 8, :], in_=wv[:, k0:k0 + 8, og * 512:(og + 1) * 512]),
                  writes=[wb], owner=wb)
        for (t0, tn) in groups(T, 128):
            ps, pb = pr.next()
            for kc in range(KC):
                S.op('pe', lambda e: e.matmul(ps[:], X[:, kc, t0:t0 + 128], w[:, kc, :], start=(kc == 0), stop=(kc == KC - 1)),
                     reads=[wb, Xb], writes=[pb], signal=(kc == KC - 1))
            epilogue(og, t0, ps, pb)


def phase_proj_tm(S, CT, xsrc, W, out, T, odt):
    nc = S.nc
    with ExitStack() as es:
        X, Xb = load_resident(S, es, 'pj_x', xsrc, KC, T)
        orr = Ring(S, es, 'sb', 'pj_o', [128, 512], odt, 4)
        cnt = [0]

        def epi(og, t0, ps, pb):
            o, ob = orr.next()
            cnt[0] += 1
            if cnt[0] % 2 == 0:
                S.op('dve', lambda e: e.tensor_copy(out=o[:], in_=ps[:]), reads=[pb], writes=[ob])
            else:
                S.op('act', lambda e: e.copy(out=o[:], in_=ps[:]), reads=[pb], writes=[ob])
            S.dma('sp', lambda e: e.dma_start(out=out[t0:t0 + 128, og * 512:(og + 1) * 512], in_=o[:]), reads=[ob], owner=ob)

        gemm_tm(S, CT, es, X, Xb, T, W, D, epi)
        S.barrier()


def phase_lora(S, CT, xsrc, w1, w2, w0row, rdim, hid_func, out_func, out, T, odt):
    nc = S.nc
    nrc = (rdim + 127) // 128
    rp = min(rdim, 128)
    with ExitStack() as es:
        X, Xb = load_resident(S, es, 'lo_x', xsrc, KC, T)
        w1t = es.enter_context(nc.sbuf_tensor(un('lo_w1'), [128, KC, rdim], BF16))
        w1b = S.buf('w1')
        S.dma('pool', lambda e: e.dma_start(out=w1t[:], in_=wview(w1)), writes=[w1b], owner=w1b)
        w2t = es.enter_context(nc.sbuf_tensor(un('lo_w2'), [rp, nrc, D], BF16))
        w2b = S.buf('w2')
        S.dma('pool', lambda e: e.dma_start(out=w2t[:], in_=w2.rearrange("(c p) f -> p c f", p=rp)), writes=[w2b], owner=w2b)
        hid = es.enter_context(nc.sbuf_tensor(un('lo_hid'), [rp, nrc, T], BF16))
        hidb = S.buf('hid')
        if w0row is not None:
            w0t = es.enter_context(nc.sbuf_tensor(un('lo_w0'), [128, D], F32))
            w0b = S.buf('w0')
            S.dma('sp', lambda e: e.dma_start(out=w0t[:], in_=w0row.partition_broadcast(128)), writes=[w0b], owner=w0b)
        pr = Ring(S, es, 'ps', 'lo_p', [128, 512], F32, 3)
        for rc in range(nrc):
            for (t0, tn) in groups(T, 512):
                ps, pb = pr.next()
                for kc in range(KC):
                    S.op('pe', lambda e: e.matmul(ps[0:rp, 0:tn], w1t[:, kc, rc * 128:rc * 128 + rp], X[:, kc, t0:t0 + tn],
                                                  start=(kc == 0), stop=(kc == KC - 1)), reads=[w1b, Xb], writes=[pb], signal=(kc == KC - 1))
                S.op('act', lambda e: e.activation(out=hid[:, rc, t0:t0 + tn], in_=ps[0:rp, 0:tn], func=hid_func), reads=[pb], writes=[hidb])
        orr = Ring(S, es, 'sb', 'lo_o', [128, 512], odt, 4)
        tr = Ring(S, es, 'sb', 'lo_t', [128, 512], F32, 3)
        p2 = Ring(S, es, 'ps', 'lo_p2', [128, 512], F32, 3)
        for og in range(4):
            for (t0, tn) in groups(T, 128):
                ps, pb = p2.next()
                for rc in range(nrc):
                    S.op('pe', lambda e: e.matmul(ps[:], hid[:, rc, t0:t0 + 128], w2t[:, rc, og * 512:(og + 1) * 512],
                                                  start=(rc == 0), stop=(rc == nrc - 1)), reads=[hidb, w2b], writes=[pb], signal=(rc == nrc - 1))
                o, ob = orr.next()
                if w0row is not None:
                    tt, tb = tr.next()
                    S.op('dve', lambda e: e.tensor_tensor(out=tt[:], in0=ps[:], in1=w0t[:, og * 512:(og + 1) * 512], op=ALU.add),
                         reads=[pb, w0b], writes=[tb])
                    S.op('act', lambda e: e.activation(out=o[:], in_=tt[:], func=out_func), reads=[tb], writes=[ob])
                else:
                    S.op('act', lambda e: e.activation(out=o[:], in_=ps[:], func=out_func), reads=[pb], writes=[ob])
                S.dma('sp', lambda e: e.dma_start(out=out[t0:t0 + 128, og * 512:(og + 1) * 512], in_=o[:]), reads=[ob], owner=ob)
        S.barrier()

DEC_C = 0.6065306597126334
import os as _os
PREP_STAGE = int(_os.environ.get('PREP_STAGE', '9'))
SCAN_STAGE = int(_os.environ.get('SCAN_STAGE', '9'))
RUN_SCAN = _os.environ.get('RUN_SCAN', '1') == '1'
SCORE_MODE = int(_os.environ.get('SCORE_MODE', '2'))


def bc3(ap2, n):
    return ap2.unsqueeze(2).broadcast_to([128, ap2.shape[1], n])


def hv3(ap2):
    return ap2.rearrange("p (h c) -> p h c", c=64)


def v4(ap2, a):
    return ap2.rearrange("p (a b) -> p a b", a=a)


def load_bc(S, es, name, row, dt=F32):
    nc = S.nc
    t = es.enter_context(nc.sbuf_tensor(un(name), [128, D], dt))
    b = S.buf(name)
    S.dma('sp', lambda e: e.dma_start(out=t[:], in_=row.partition_broadcast(128)), writes=[b], owner=b)
    return t, b


def phase_rwkv_prep(S, CT, I, layer, Z, T):
    nc = S.nc
    with ExitStack() as es:
        cb = S.buf('c')
        kkbc, kkb = load_bc(S, es, 'pp_kk', I['rwkv_k_k'][layer])
        kabc, kab = load_bc(S, es, 'pp_ka', I['rwkv_k_a'][layer])
        rkbc, rkb = load_bc(S, es, 'pp_rk', I['rwkv_r_k'][layer].rearrange("h c -> (h c)"))
        cm = es.enter_context(nc.sbuf_tensor(un('pp_cm'), [128, 128], F32))
        cmb = S.buf('cm')
        S.dma('sp', lambda e: e.dma_start(out=cm[:], in_=I['cmask'][0]), writes=[cmb], owner=cmb)
        negc = es.enter_context(nc.sbuf_tensor(un('pp_negc'), [128, 128], F32))
        S.op('dve', lambda e: e.memset(negc[:], -DEC_C), writes=[cmb])
        r_in = Ring(S, es, 'sb', 'pp_r', [128, D], BF16, 2)
        k_in = Ring(S, es, 'sb', 'pp_k', [128, D], BF16, 2)
        a_in = Ring(S, es, 'sb', 'pp_a', [128, D], BF16, 2)
        v_in = Ring(S, es, 'sb', 'pp_v', [128, D], BF16, 2)
        s_in = Ring(S, es, 'sb', 'pp_s', [128, D], F32, 2)
        if layer == 1:
            vg_in = Ring(S, es, 'sb', 'pp_vg', [128, D], BF16, 1)
            vf_in = Ring(S, es, 'sb', 'pp_vf', [128, D], BF16, 1)
            o_vv = Ring(S, es, 'sb', 'pp_ovv', [128, D], BF16, 2)
        T1, T1b = tile1(S, es, 'pp_t1', [128, D], F32)
        T2, T2b = tile1(S, es, 'pp_t2', [128, D], F32)
        T3, T3b = tile1(S, es, 'pp_t3', [128, D], F32)
        T4, T4b = tile1(S, es, 'pp_t4', [128, D], F32)
        ss, ssb = tile1(S, es, 'pp_ss', [128, 64], F32)
        Er = Ring(S, es, 'sb', 'pp_e', [128, 512], F32, 4)
        o_r = Ring(S, es, 'sb', 'pp_or', [128, D], BF16, 2)
        o_kp = Ring(S, es, 'sb', 'pp_okp', [128, D], BF16, 2)
        o_kt = Ring(S, es, 'sb', 'pp_okt', [128, D], BF16, 2)
        o_bn = Ring(S, es, 'sb', 'pp_obn', [128, D], BF16, 2)
        o_bo = Ring(S, es, 'sb', 'pp_obo', [128, D], BF16, 2)
        xt_r = Ring(S, es, 'sb', 'pp_xt', [128, KC, 128], BF16, 4)
        gm_r = Ring(S, es, 'sb', 'pp_gm', [128, KC], F32, 2)
        pcw = Ring(S, es, 'ps', 'pp_pcw', [128, 512], F32, 4)
        ptr = Ring(S, es, 'ps', 'pp_ptr', [128, 1024], BF16, 2)
        for ci, (t0, tn) in enumerate(groups(T, 128)):
            def ld(ring, src):
                t, b = ring.next()
                S.dma('sp', lambda e: e.dma_start(out=t[:], in_=src[t0:t0 + 128, :]), writes=[b], owner=b)
                return t, b
            r, rb = ld(r_in, Z['r'])
            k, kb = ld(k_in, Z['k'])
            a, ab = ld(a_in, Z['a'])
            v, vb = ld(v_in, Z['v'])
            sg, sgb = ld(s_in, Z['sigd'])
            if layer == 1:
                vg, vgb = ld(vg_in, Z['vg'])
                vf, vfb = ld(vf_in, Z['vfirst'])
                vv, vvb = o_vv.next()
                S.op('dve', lambda e: e.tensor_tensor(out=T4[:], in0=vf[:], in1=v[:], op=ALU.subtract), reads=[vfb, vb], writes=[T4b])
                S.op('dve', lambda e: e.tensor_tensor(out=T4[:], in0=T4[:], in1=vg[:], op=ALU.mult), reads=[vgb, T4b], writes=[T4b])
                S.op('dve', lambda e: e.tensor_tensor(out=vv[:], in0=T4[:], in1=v[:], op=ALU.add), reads=[vb, T4b], writes=[vvb])
                S.dma('sp', lambda e: e.dma_start(out=Z['vv'][t0:t0 + 128, :], in_=vv[:]), reads=[vvb], owner=vvb)
            else:
                vv, vvb = v, vb
            S.op('dve', lambda e: e.tensor_tensor(out=T1[:], in0=k[:], in1=kkbc[:], op=ALU.mult), reads=[kb, kkb], writes=[T1b])
            S.op('act', lambda e: e.activation(out=T2[:], in_=T1[:], func=AF.Square), reads=[T1b], writes=[T2b])
            S.op('dve', lambda e: e.tensor_reduce(out=ss[:, 0:32], in_=hv3(T2[:]), axis=AX.X, op=ALU.add), reads=[T2b], writes=[ssb])
            S.op('dve', lambda e: e.tensor_scalar_max(out=ss[:, 0:32], in0=ss[:, 0:32], scalar1=1e-24), reads=[ssb], writes=[ssb])
            S.op('act', lambda e: e.activation(out=ss[:, 0:32], in_=ss[:, 0:32], func=AF.Ln), reads=[ssb], writes=[ssb])
            S.op('act', lambda e: e.activation(out=ss[:, 0:32], in_=ss[:, 0:32], func=AF.Exp, scale=-0.5), reads=[ssb], writes=[ssb])
            S.op('dve', lambda e: e.tensor_tensor(out=hv3(T1[:]), in0=hv3(T1[:]), in1=bc3(ss[:, 0:32], 64), op=ALU.mult),
                 reads=[ssb, T1b], writes=[T1b])
            if PREP_STAGE < 2:
                continue
            S.op('dve', lambda e: e.scalar_tensor_tensor(out=T2[:], in0=a[:], scalar=-1.0, in1=kabc[:], op0=ALU.add, op1=ALU.mult),
                 reads=[ab, kab], writes=[T2b])
            S.op('dve', lambda e: e.scalar_tensor_tensor(out=T2[:], in0=T2[:], scalar=1.0, in1=k[:], op0=ALU.add, op1=ALU.mult),
                 reads=[kb, T2b], writes=[T2b])
            S.op('dve', lambda e: e.scalar_tensor_tensor(out=T3[:], in0=T1[:], scalar=-1.0, in1=a[:], op0=ALU.mult, op1=ALU.mult),
                 reads=[T1b, ab], writes=[T3b])
            S.op('dve', lambda e: e.tensor_tensor(out=T4[:], in0=r[:], in1=T2[:], op=ALU.mult), reads=[rb, T2b], writes=[T4b])
            S.op('pool', lambda e: e.tensor_tensor(out=T4[:], in0=T4[:], in1=rkbc[:], op=ALU.mult), reads=[rkb, T4b], writes=[T4b])
            S.op('dve', lambda e: e.tensor_reduce(out=ss[:, 32:64], in_=hv3(T4[:]), axis=AX.X, op=ALU.add), reads=[T4b], writes=[ssb])
            bo, bob = o_bo.next()
            S.op('pool', lambda e: e.tensor_tensor(out=hv3(bo[:]), in0=hv3(vv[:]), in1=bc3(ss[:, 32:64], 64), op=ALU.mult),
                 reads=[ssb, vvb], writes=[bob])
            S.dma('sp', lambda e: e.dma_start(out=Z['bonus'][t0:t0 + 128, :], in_=bo[:]), reads=[bob], owner=bob)
            if PREP_STAGE < 3:
                continue
            orr_, orb = o_r.next()
            okp, okpb = o_kp.next()
            okt, oktb = o_kt.next()
            obn, obnb = o_bn.next()
            for og in range(4):
                sl = slice(og * 512, (og + 1) * 512)
                ps, pb = pcw.next()
                S.op('pe', lambda e: e.matmul(ps[:], cm[:], sg[:, sl], start=True, stop=True), reads=[cmb, sgb], writes=[pb])
                E1, E1b = Er.next()
                S.op('act', lambda e: e.activation(out=E1[:], in_=ps[:], func=AF.Exp), reads=[pb], writes=[E1b])
                S.op('pool', lambda e: e.tensor_tensor(out=orr_[:, sl], in0=r[:, sl], in1=E1[:], op=ALU.mult), reads=[rb, E1b], writes=[orb], disjoint=(og > 0))
                E2, E2b = Er.next()
                S.op('act', lambda e: e.activation(out=E2[:], in_=ps[:], func=AF.Exp, scale=-1.0), reads=[pb], writes=[E2b])
                S.op('pool', lambda e: e.tensor_tensor(out=okt[:, sl], in0=T2[:, sl], in1=E2[:], op=ALU.mult), reads=[T2b, E2b], writes=[oktb], disjoint=(og > 0))
                S.op('dve', lambda e: e.tensor_tensor(out=obn[:, sl], in0=T3[:, sl], in1=E2[:], op=ALU.mult), reads=[T3b, E2b], writes=[obnb], disjoint=(og > 0))
                E3, E3b = Er.next()
                S.op('dve', lambda e: e.scalar_tensor_tensor(out=E3[:], in0=sg[:, sl], scalar=DEC_C, in1=ps[:], op0=ALU.mult, op1=ALU.add),
                     reads=[sgb, pb], writes=[E3b])
                S.op('act', lambda e: e.activation(out=E3[:], in_=E3[:], func=AF.Exp), reads=[E3b], writes=[E3b])
                S.op('pool', lambda e: e.tensor_tensor(out=okp[:, sl], in0=T1[:, sl], in1=E3[:], op=ALU.mult), reads=[T1b, E3b], writes=[okpb], disjoint=(og > 0))
            S.dma('sp', lambda e: e.dma_start(out=Z['kt'][t0:t0 + 128, :], in_=okt[:]), reads=[oktb], owner=oktb)
            S.dma('sp', lambda e: e.dma_start(out=Z['bn'][t0:t0 + 128, :], in_=obn[:]), reads=[obnb], owner=obnb)
            if PREP_STAGE < 4:
                continue
            gm, gmb = gm_r.next()
            for q4 in range(4):
                pg, pgb = pcw.next()
                for i4 in range(4):
                    kc = 4 * q4 + i4
                    S.op('pe', lambda e: e.matmul(pg[:, i4 * 128:(i4 + 1) * 128], sg[:, kc * 128:(kc + 1) * 128], negc[:], start=True, stop=True),
                         reads=[sgb, cmb], writes=[pgb], signal=(i4 == 3))
                S.op('act', lambda e: e.activation(out=gm[:, 4 * q4:4 * q4 + 4].unsqueeze(2), in_=v4(pg[:], 4)[:, :, 0:1], func=AF.Exp), reads=[pgb], writes=[gmb])
            S.dma('sp', lambda e: e.dma_start(out=Z['gam'][ci], in_=gm[:]), reads=[gmb], owner=gmb)
            if PREP_STAGE < 5:
                continue
            for (src, srcb, dst) in ((orr_, orb, 'rT'), (okp, okpb, 'kpT'), (okt, oktb, 'ktT'), (obn, obnb, 'bnT')):
                xt, xtb = xt_r.next()
                for half in range(2):
                    pt, ptb = ptr.next()
                    for j in range(8):
                        kc = half * 8 + j
                        S.op('pe', lambda e: e.transpose(pt[:, j * 128:(j + 1) * 128], src[:, kc * 128:(kc + 1) * 128], CT['identb'][:]),
                             reads=[srcb, cb], writes=[ptb], signal=(j == 7))
                    if half == 0:
                        S.op('act', lambda e: e.copy(out=xt[:, 0:8, :], in_=v4(pt[:], 8)), reads=[ptb], writes=[xtb])
                    else:
                        S.op('dve', lambda e: e.tensor_copy(out=xt[:, 8:16, :], in_=v4(pt[:], 8)), reads=[ptb], writes=[xtb])
                for e2 in range(2):
                    S.dma('sp', lambda e: e.dma_start(out=Z[dst][ci].rearrange("c (hp e) t -> c hp e t", e=2)[:, :, e2, :],
                                                      in_=xt[e2 * 64:(e2 + 1) * 64, :, :]), reads=[xtb], owner=xtb)
        S.barrier()


def phase_rwkv_scan(S, CT, I, layer, Z, T):
    nc = S.nc
    with ExitStack() as es:
        cb = S.buf('c')
        gnw, gnwb = load_bc(S, es, 'sc_gnw', I['rwkv_gn_w'][layer])
        gnb, gnbb = load_bc(S, es, 'sc_gnb', I['rwkv_gn_b'][layer])
        msk = es.enter_context(nc.sbuf_tensor(un('sc_msk'), [128, 3, 128], F32))
        mb = S.buf('msk')
        for i in range(3):
            S.dma('sp', lambda e: e.dma_start(out=msk[:, i, :], in_=I['cmask'][1 + i]), writes=[mb], owner=mb)
        MUS, MUI, MLS = 0, 1, 2

        def mbc(i, n):
            return msk[:, i, :].unsqueeze(1).broadcast_to([128, n, 128])
        A, Ab = tile1(S, es, 'sc_A', [64, RH, 64], F32)
        Abf, Abfb = tile1(S, es, 'sc_Abf', [64, RH, 64], BF16)
        S.op('dve', lambda e: e.memset(A[:], 0.0), writes=[Ab])
        S.op('dve', lambda e: e.memset(Abf[:], 0.0), writes=[Abfb])
        names_cm = ('rT', 'kpT', 'ktT', 'bnT')
        cm_r = {n: Ring(S, es, 'sb', 'sc_' + n, [64, RH, 128], BF16, 1) for n in names_cm}
        tm_r = {n: Ring(S, es, 'sb', 'sc_' + n, [128, D], BF16, 2) for n in ('kt', 'bn', 'vv', 'bonus', 'g')}
        gm_r = Ring(S, es, 'sb', 'sc_gam', [64, 2, KC], F32, 2)
        sc_t = {n: tile1(S, es, 'sc_s' + n, [128, 16, 128], BF16) for n in ('MkT', 'RKT', 'RBT', 'N0', 'N1', 'NT0', 'NT1', 'Q0', 'Q1')}
        stg = Ring(S, es, 'sb', 'sc_stg', [128, 512], BF16, 2)
        RHS, RHSb = tile1(S, es, 'sc_rhs', [128, 1024], BF16)
        U, Ub = tile1(S, es, 'sc_u', [128, 1024], BF16)
        y_r = Ring(S, es, 'sb', 'sc_y', [128, D], F32, 2)
        yt, ytb = tile1(S, es, 'sc_yt', [128, D], F32)
        st, stb = tile1(S, es, 'sc_st', [128, 5, 32], F32)
        z_r = Ring(S, es, 'sb', 'sc_z', [128, D], BF16, 1)
        zT_r = Ring(S, es, 'sb', 'sc_zT', [128, KC, 128], BF16, 2)
        P = Ring(S, es, 'ps', 'sc_p', [128, 512], F32, 6)
        PT = Ring(S, es, 'ps', 'sc_pt', [128, 1024], BF16, 2)
        ecount = [0]

        def cmview(d):
            return d.rearrange("hp (e c) t -> c (hp e) t", e=2)
        def output_stage(y, yb, bo, bob, gg, ggb, t0):
            S.op('dve', lambda e: e.tensor_reduce(out=st[:, 0, :], in_=hv3(y[:]), axis=AX.X, op=ALU.add), reads=[yb], writes=[stb])
            S.op('act', lambda e: e.activation(out=yt[:], in_=y[:], func=AF.Square), reads=[yb], writes=[ytb])
            S.op('dve', lambda e: e.tensor_reduce(out=st[:, 1, :], in_=hv3(yt[:]), axis=AX.X, op=ALU.add), reads=[ytb], writes=[stb])
            S.op('dve', lambda e: e.tensor_scalar_mul(out=st[:, 0, :], in0=st[:, 0, :], scalar1=1.0 / 64), reads=[stb], writes=[stb])
            S.op('dve', lambda e: e.tensor_tensor(out=st[:, 2, :], in0=st[:, 0, :], in1=st[:, 0, :], op=ALU.mult), reads=[stb], writes=[stb])
            S.op('dve', lambda e: e.scalar_tensor_tensor(out=st[:, 3, :], in0=st[:, 1, :], scalar=1.0 / 64, in1=st[:, 2, :], op0=ALU.mult, op1=ALU.subtract),
                 reads=[stb], writes=[stb])
            S.op('act', lambda e: e.activation(out=st[:, 3, :], in_=st[:, 3, :], func=AF.Ln, bias=CT['eps'][:, 1:2], scale=1.0), reads=[stb, cb], writes=[stb])
            S.op('act', lambda e: e.activation(out=st[:, 3, :], in_=st[:, 3, :], func=AF.Exp, scale=-0.5), reads=[stb], writes=[stb])
            S.op('dve', lambda e: e.tensor_tensor(out=hv3(yt[:]), in0=hv3(y[:]), in1=bc3(st[:, 0, :], 64), op=ALU.subtract), reads=[yb, stb], writes=[ytb])
            S.op('dve', lambda e: e.tensor_tensor(out=hv3(yt[:]), in0=hv3(yt[:]), in1=bc3(st[:, 3, :], 64), op=ALU.mult), reads=[stb, ytb], writes=[ytb])
            S.op('pool', lambda e: e.tensor_tensor(out=yt[:], in0=yt[:], in1=gnw[:], op=ALU.mult), reads=[gnwb, ytb], writes=[ytb])
            S.op('pool', lambda e: e.tensor_tensor(out=yt[:], in0=yt[:], in1=gnb[:], op=ALU.add), reads=[gnbb, ytb], writes=[ytb])
            S.op('pool', lambda e: e.tensor_tensor(out=yt[:], in0=yt[:], in1=bo[:], op=ALU.add), reads=[bob, ytb], writes=[ytb])
            z, zb = z_r.next()
            S.op('pool', lambda e: e.tensor_tensor(out=z[:], in0=yt[:], in1=gg[:], op=ALU.mult), reads=[ggb, ytb], writes=[zb])
            return (z, zb, t0)

        def output_tr(z, zb, t0):
            zT, zTb = zT_r.next()
            for half in range(2):
                pt, ptb = PT.next()
                for j in range(8):
                    kc = half * 8 + j
                    S.op('pe', lambda e: e.transpose(pt[:, j * 128:(j + 1) * 128], z[:, kc * 128:(kc + 1) * 128], CT['identb'][:]),
                         reads=[zb, cb], writes=[ptb], signal=(j == 7))
                S.op('act', lambda e: e.copy(out=zT[:, half * 8:half * 8 + 8, :], in_=v4(pt[:], 8)), reads=[ptb], writes=[zTb])
            S.dma('sp', lambda e: e.dma_start(out=fm(Z['zT'])[:, :, t0:t0 + 128], in_=zT[:]), reads=[zTb], owner=zTb)

        prev_out = None
        prev_z = None
        for ci, (t0, tn) in enumerate(groups(T, 128)):
            cmt = {}
            for n in names_cm:
                t, b = cm_r[n].next()
                S.dma('sp', lambda e: e.dma_start(out=t[:], in_=Z[n][ci]), writes=[b], owner=b)
                cmt[n] = (t, b)
            tmt = {}
            for n in ('kt', 'bn', 'vv', 'bonus', 'g'):
                t, b = tm_r[n].next()
                src = Z[n] if not (n == 'vv' and layer == 0) else Z['v']
                S.dma('sp', lambda e: e.dma_start(out=t[:], in_=src[t0:t0 + 128, :]), writes=[b], owner=b)
                tmt[n] = (t, b)
            gam, gamb = gm_r.next()
            S.dma('sp', lambda e: e.dma_start(out=gam[:], in_=Z['gam'][ci].rearrange("(e c) hp -> c e hp", e=2)), writes=[gamb], owner=gamb)
            rT, rTb = cmt['rT']
            kpT, kpTb = cmt['kpT']
            ktT, ktTb = cmt['ktT']
            bnT, bnTb = cmt['bnT']
            kt, ktb = tmt['kt']
            bn, bnb = tmt['bn']
            vv, vvb = tmt['vv']
            y, yb = y_r.next()
            for hh in range(2):
                H0 = 16 * hh
                kinds = (('MkT', ktT, ktTb, kpT, kpTb, MUS), ('N0', bnT, bnTb, kpT, kpTb, MUS), ('NT0', kpT, kpTb, bnT, bnTb, MLS),
                         ('RKT', ktT, ktTb, rT, rTb, MUI), ('RBT', bnT, bnTb, rT, rTb, MUI))
                for (dn, lt, ltb, rt, rtb, mi) in kinds:
                    dst, dstb = sc_t[dn]
                    for gq in range(4):
                        ps, pb = P.next()
                        for j in range(4):
                            h = H0 + 4 * gq + j
                            S.op('pe', lambda e: e.matmul(ps[:, j * 128:(j + 1) * 128], lt[:, h, :], rt[:, h, :], start=True, stop=True),
                                 reads=[ltb, rtb], writes=[pb], signal=(j == 3))
                        ecount[0] += 1
                        if ecount[0] % 2 == 0:
                            S.op('dve', lambda e: e.tensor_tensor(out=dst[:, 4 * gq:4 * gq + 4, :], in0=v4(ps[:], 4), in1=mbc(mi, 4), op=ALU.mult),
                                 reads=[pb, mb], writes=[dstb])
                        else:
                            sg_, sgb_ = stg.next()
                            S.op('act', lambda e: e.copy(out=sg_[:], in_=ps[:]), reads=[pb], writes=[sgb_])
                            S.op('pool', lambda e: e.tensor_tensor(out=dst[:, 4 * gq:4 * gq + 4, :], in0=v4(sg_[:], 4), in1=mbc(mi, 4), op=ALU.mult),
                                 reads=[sgb_, mb], writes=[dstb])
                if SCAN_STAGE < 2:
                    continue
                Ncur, Ncb = sc_t['N0']
                NTcur, NTcb = sc_t['NT0']
                Nnx, Nnb = sc_t['N1']
                NTnx, NTnb = sc_t['NT1']
                Qc, Qcb = sc_t['Q0']
                Qn, Qnb = sc_t['Q1']
                S.op('pool', lambda e: e.tensor_tensor(out=Qc[:], in0=Ncur[:], in1=CT['identb'][:].unsqueeze(1).broadcast_to([128, 16, 128]), op=ALU.add),
                     reads=[Ncb, cb], writes=[Qcb])
                for lev in range(1, 7):
                    for gq in range(4):
                        ps, pb = P.next()
                        for j in range(4):
                            hx = 4 * gq + j
                            S.op('pe', lambda e: e.matmul(ps[:, j * 128:(j + 1) * 128], Ncur[:, hx, :], NTcur[:, hx, :], start=True, stop=True),
                                 reads=[Ncb, NTcb], writes=[pb], signal=(j == 3))
                        S.op('act', lambda e: e.copy(out=NTnx[:, 4 * gq:4 * gq + 4, :], in_=v4(ps[:], 4)), reads=[pb], writes=[NTnb], disjoint=(gq > 0))
                        if lev < 6:
                            ps2, pb2 = P.next()
                            for j in range(4):
                                hx = 4 * gq + j
                                S.op('pe', lambda e: e.matmul(ps2[:, j * 128:(j + 1) * 128], NTcur[:, hx, :], Ncur[:, hx, :], start=True, stop=True),
                                     reads=[Ncb, NTcb], writes=[pb2], signal=(j == 3))
                            S.op('dve', lambda e: e.tensor_copy(out=Nnx[:, 4 * gq:4 * gq + 4, :], in_=v4(ps2[:], 4)), reads=[pb2], writes=[Nnb], disjoint=(gq > 0))
                    for gq in range(4):
                        ps, pb = P.next()
                        for j in range(4):
                            hx = 4 * gq + j
                            S.op('pe', lambda e: e.matmul(ps[:, j * 128:(j + 1) * 128], NTnx[:, hx, :], Qc[:, hx, :], start=True, stop=True),
                                 reads=[NTnb, Qcb], writes=[pb], signal=(j == 3))
                        S.op('dve', lambda e: e.tensor_tensor(out=Qn[:, 4 * gq:4 * gq + 4, :], in0=v4(ps[:], 4), in1=Qc[:, 4 * gq:4 * gq + 4, :], op=ALU.add),
                             reads=[pb, Qcb], writes=[Qnb], disjoint=(gq > 0))
                    Ncur, Ncb, Nnx, Nnb = Nnx, Nnb, Ncur, Ncb
                    NTcur, NTcb, NTnx, NTnb = NTnx, NTnb, NTcur, NTcb
                    Qc, Qcb, Qn, Qnb = Qn, Qnb, Qc, Qcb
                if hh == 0 and prev_out is not None:
                    prev_z = output_stage(*prev_out)
                    prev_out = None
                if hh == 1 and prev_z is not None:
                    output_tr(*prev_z)
                    prev_z = None
                MkT, MkTb = sc_t['MkT']
                RKT, RKTb = sc_t['RKT']
                RBT, RBTb = sc_t['RBT']
                for g8 in range(2):
                    ps, pb = P.next()
                    for j in range(8):
                        hx = 8 * g8 + j
                        h = H0 + hx
                        S.op('pe', lambda e: e.matmul(ps[:, j * 64:(j + 1) * 64], kpT[:, h, :], Abf[:, h, :], start=True, stop=False),
                             reads=[kpTb, Abfb], writes=[pb], signal=False)
                        S.op('pe', lambda e: e.matmul(ps[:, j * 64:(j + 1) * 64], MkT[:, hx, :], vv[:, h * 64:(h + 1) * 64], start=False, stop=True),
                             reads=[MkTb, vvb], writes=[pb], signal=(j == 7))
                    S.op('act', lambda e: e.copy(out=RHS[:, g8 * 512:(g8 + 1) * 512], in_=ps[:]), reads=[pb], writes=[RHSb], disjoint=(g8 > 0))
                for g8 in range(2):
                    ps, pb = P.next()
                    for j in range(8):
                        hx = 8 * g8 + j
                        S.op('pe', lambda e: e.matmul(ps[:, j * 64:(j + 1) * 64], Qc[:, hx, :], RHS[:, hx * 64:(hx + 1) * 64], start=True, stop=True),
                             reads=[Qcb, RHSb], writes=[pb], signal=(j == 7))
                    S.op('act', lambda e: e.copy(out=U[:, g8 * 512:(g8 + 1) * 512], in_=ps[:]), reads=[pb], writes=[Ub], disjoint=(g8 > 0))
                for g8 in range(2):
                    ps, pb = P.next()
                    for j in range(8):
                        hx = 8 * g8 + j
                        h = H0 + hx
                        S.op('pe', lambda e: e.matmul(ps[:, j * 64:(j + 1) * 64], rT[:, h, :], Abf[:, h, :], start=True, stop=False),
                             reads=[rTb, Abfb], writes=[pb], signal=False)
                        S.op('pe', lambda e: e.matmul(ps[:, j * 64:(j + 1) * 64], RKT[:, hx, :], vv[:, h * 64:(h + 1) * 64], start=False, stop=False),
                             reads=[RKTb, vvb], writes=[pb], signal=False)
                        S.op('pe', lambda e: e.matmul(ps[:, j * 64:(j + 1) * 64], RBT[:, hx, :], U[:, hx * 64:(hx + 1) * 64], start=False, stop=True),
                             reads=[RBTb, Ub], writes=[pb], signal=(j == 7))
                    c0 = (H0 + 8 * g8) * 64
                    S.op('dve', lambda e: e.tensor_copy(out=y[:, c0:c0 + 512], in_=ps[:]), reads=[pb], writes=[yb])
                if SCAN_STAGE < 4:
                    continue
                for g8 in range(2):
                    ps, pb = P.next()
                    for j in range(8):
                        hx = 8 * g8 + j
                        h = H0 + hx
                        S.op('pe', lambda e: e.matmul(ps[0:64, j * 64:(j + 1) * 64], kt[:, h * 64:(h + 1) * 64], vv[:, h * 64:(h + 1) * 64], start=True, stop=False),
                             reads=[ktb, vvb], writes=[pb], signal=False)
                        S.op('pe', lambda e: e.matmul(ps[0:64, j * 64:(j + 1) * 64], bn[:, h * 64:(h + 1) * 64], U[:, hx * 64:(hx + 1) * 64], start=False, stop=True),
                             reads=[bnb, Ub], writes=[pb], signal=(j == 7))
                    h0 = H0 + 8 * g8
                    hp0 = h0 // 2
                    S.op('dve', lambda e: e.tensor_tensor(out=A[:, h0:h0 + 8, :], in0=ps[0:64, :].rearrange("p (a b) -> p a b", a=8), in1=A[:, h0:h0 + 8, :], op=ALU.add),
                         reads=[pb, Ab], writes=[Ab])
                    S.op('pool', lambda e: e.tensor_tensor(
                        out=A[:, h0:h0 + 8, :].rearrange("c (hp e) v -> c hp e v", e=2),
                        in0=A[:, h0:h0 + 8, :].rearrange("c (hp e) v -> c hp e v", e=2),
                        in1=gam[:].rearrange("c e hp -> c hp e")[:, hp0:hp0 + 4, :].unsqueeze(3).broadcast_to([64, 4, 2, 64]), op=ALU.mult),
                        reads=[gamb, Ab], writes=[Ab])
                    S.op('act', lambda e: e.copy(out=Abf[:, h0:h0 + 8, :], in_=A[:, h0:h0 + 8, :]), reads=[Ab], writes=[Abfb])
            prev_out = (y, yb, tmt['bonus'][0], tmt['bonus'][1], tmt['g'][0], tmt['g'][1], t0)
        output_tr(*output_stage(*prev_out))
        S.barrier()


def phase_proj_fm_res(S, CT, xsrc, W, hres, T, kcn=KC):
    nc = S.nc
    with ExitStack() as es:
        X, Xb = load_resident(S, es, 'po_x', xsrc, kcn, T)
        wr = Ring(S, es, 'sb', 'po_w', [128, kcn, 256], BF16, 2)
        hr = Ring(S, es, 'sb', 'po_h', [128, 512], F32, 4)
        pr = Ring(S, es, 'ps', 'po_p', [128, 512], F32, 4)
        wv = wview(W)
        for og in range(D // 256):
            w, wb = wr.next()
            S.dma('pool', lambda e: e.dma_start(out=w[:], in_=wv[:, :, og * 256:(og + 1) * 256]), writes=[wb], owner=wb)
            for ol in range(2):
                dc = og * 2 + ol
                for (t0, tn) in groups(T, 512):
                    h, hb = hr.next()
                    S.dma('sp', lambda e: e.dma_start(out=h[:, 0:tn], in_=hres[dc, :, t0:t0 + tn]), writes=[hb], owner=hb)
                    ps, pb = pr.next()
                    for kc in range(kcn):
                        S.op('pe', lambda e: e.matmul(ps[:, 0:tn], w[:, kc, ol * 128:(ol + 1) * 128], X[:, kc, t0:t0 + tn],
                                                      start=(kc == 0), stop=(kc == kcn - 1)), reads=[wb, Xb], writes=[pb], signal=(kc == kcn - 1))
                    S.op('dve', lambda e: e.tensor_tensor(out=h[:, 0:tn], in0=h[:, 0:tn], in1=ps[:, 0:tn], op=ALU.add), reads=[pb, hb], writes=[hb])
                    S.dma('sp', lambda e: e.dma_start(out=hres[dc, :, t0:t0 + tn], in_=h[:, 0:tn]), reads=[hb], owner=hb)
        S.barrier()


def rwkv_layer(S, CT, I, layer, Z, hT, T):
    import os
    nph = int(os.environ.get('RWKV_NPH', '99'))
    xm = Z['xmix']
    Zl = dict(Z)
    if layer == 0:
        Zl['v'] = Z['vfirst']
    steps = [
        lambda: phase_rwkv_mix(S, CT, hT, I['cols'], layer, xm, T),
        lambda: phase_proj_tm(S, CT, xm[0], I['rwkv_w_r'][layer], Z['r'], T, BF16),
        lambda: phase_proj_tm(S, CT, xm[2], I['rwkv_w_k'][layer], Z['k'], T, BF16),
        lambda: phase_proj_tm(S, CT, xm[3], I['rwkv_w_v'][layer], Z['v'] if layer == 1 else Z['vfirst'], T, BF16),
        lambda: phase_lora(S, CT, xm[1], I['rwkv_dec_w1'][layer], I['rwkv_dec_w2'][layer], I['rwkv_dec_w0'][layer], 96, AF.Tanh, AF.Sigmoid, Z['sigd'], T, F32),
        lambda: phase_lora(S, CT, xm[4], I['rwkv_a_w1'][layer], I['rwkv_a_w2'][layer], I['rwkv_a_w0'][layer], 96, AF.Copy, AF.Sigmoid, Z['a'], T, BF16),
        lambda: phase_lora(S, CT, xm[5], I['rwkv_g_w1'][layer], I['rwkv_g_w2'][layer], None, 256, AF.Sigmoid, AF.Copy, Z['g'], T, BF16),
    ]
    if layer == 1:
        steps.append(lambda: phase_lora(S, CT, xm[3], I['rwkv_v_w1'][0], I['rwkv_v_w2'][0], I['rwkv_v_w0'][0], 64, AF.Copy, AF.Sigmoid, Z['vg'], T, BF16))
    steps += [
        lambda: phase_rwkv_prep(S, CT, I, layer, Zl, T),
        lambda: phase_rwkv_scan(S, CT, I, layer, Zl, T),
        lambda: phase_proj_fm_res(S, CT, Z['zT'], I['rwkv_w_o'][layer], hT, T),
    ]
    if not RUN_SCAN:
        steps = steps[:-2]
    for i, st_ in enumerate(steps):
        if i < nph:
            st_()


SB_SCALE = 128 ** -0.5


def phase_gather_q(S, CT, hT, hqT, NQB):
    nc = S.nc
    with ExitStack() as es:
        r = Ring(S, es, 'sb', 'gq_t', [128, KC, 128], F32, 3)
        pid = nc.sync.partition_id()
        off = (pid % 2) * 128 + NMETA
        hv = fm(hT)
        qv = fm(hqT)
        for i in range(NQB):
            t, b = r.next()
            S.dma('sp', lambda e: e.dma_start(out=t[:], in_=hv[:, :, bass.ds(off + 256 * i, 128)]), writes=[b], owner=b)
            S.dma('sp', lambda e: e.dma_start(out=qv[:, :, i * 128:(i + 1) * 128], in_=t[:]), reads=[b], owner=b)
        S.barrier()


def phase_headnorm_fm(S, CT, xsrc, W, gain1d, out, Tn):
    nc = S.nc
    with ExitStack() as es:
        cb = S.buf('c')
        X, Xb = load_resident(S, es, 'hn_x', xsrc, KC, Tn)
        gcol = es.enter_context(nc.sbuf_tensor(un('hn_g'), [128, 1], F32))
        gb = S.buf('g')
        S.dma('sp', lambda e: e.dma_start(out=gcol[:], in_=gain1d.rearrange("(p o) -> p o", o=1)), writes=[gb], owner=gb)
        wr = Ring(S, es, 'sb', 'hn_w', [128, KC, 128], BF16, 2)
        sr = Ring(S, es, 'sb', 'hn_sq', [128, 512], F32, 2)
        rr = Ring(S, es, 'sb', 'hn_rs', [128, 512], F32, 2)
        orr = Ring(S, es, 'sb', 'hn_o', [128, Tn], BF16, 2)
        pr = Ring(S, es, 'ps', 'hn_p', [128, 512], F32, 3)
        pr2 = Ring(S, es, 'ps', 'hn_p2', [128, 512], F32, 2)
        wv = wview(W)
        for h in range(SH):
            w, wb = wr.next()
            S.dma('pool', lambda e: e.dma_start(out=w[:], in_=wv[:, :, h * 128:(h + 1) * 128]), writes=[wb], owner=wb)
            o, ob = orr.next()
            for (t0, tn) in groups(Tn, 512):
                ps, pb = pr.next()
                for kc in range(KC):
                    S.op('pe', lambda e: e.matmul(ps[:, 0:tn], w[:, kc, :], X[:, kc, t0:t0 + tn], start=(kc == 0), stop=(kc == KC - 1)),
                         reads=[wb, Xb], writes=[pb], signal=(kc == KC - 1))
                sq, sqb = sr.next()
                S.op('act', lambda e: e.activation(out=sq[:, 0:tn], in_=ps[:, 0:tn], func=AF.Square), reads=[pb], writes=[sqb])
                p2, p2b = pr2.next()
                S.op('pe', lambda e: e.matmul(p2[:, 0:tn], CT['ones'][:], sq[:, 0:tn], start=True, stop=True), reads=[sqb, cb], writes=[p2b])
                rs, rb = rr.next()
                S.op('act', lambda e: e.activation(out=rs[:, 0:tn], in_=p2[:, 0:tn], func=AF.Ln, bias=CT['eps'][:, 0:1], scale=1.0 / 128),
                     reads=[p2b, cb], writes=[rb])
                S.op('act', lambda e: e.activation(out=rs[:, 0:tn], in_=rs[:, 0:tn], func=AF.Exp, scale=-0.5), reads=[rb], writes=[rb])
                S.op('dve', lambda e: e.scalar_tensor_tensor(out=o[:, t0:t0 + tn], in0=ps[:, 0:tn], scalar=gcol[:, 0:1], in1=rs[:, 0:tn],
                                                             op0=ALU.mult, op1=ALU.mult), reads=[pb, rb, gb], writes=[ob], disjoint=(t0 > 0))
            S.dma('sp', lambda e: e.dma_start(out=out[h, :, 0:Tn], in_=o[:]), reads=[ob], owner=ob)
        S.barrier()


def phase_attention(S, CT, I, KT, Vtm, QT, OT, NXB):
    nc = S.nc
    NQB = NXB // 2
    NG = NQB // 4
    TQ = NQB * 128
    Tk = NMETA + 128 * NXB
    with ExitStack() as es:
        am = es.enter_context(nc.sbuf_tensor(un('at_am'), [128, 8, 512], F32))
        amb_ = es.enter_context(nc.sbuf_tensor(un('at_amb'), [128, 8, 512], BF16))
        tm = es.enter_context(nc.sbuf_tensor(un('at_tm'), [128, 2, 128], F32))
        mb = S.buf('am')
        S.dma('sp', lambda e: e.dma_start(out=am[:], in_=I['amask'].rearrange("j p q -> p j q")), writes=[mb], owner=mb)
        S.dma('pool', lambda e: e.dma_start(out=amb_[:], in_=I['amask'].rearrange("j p q -> p j q")), writes=[mb], owner=mb)
        S.dma('sp', lambda e: e.dma_start(out=tm[:], in_=I['tmask'].rearrange("j p q -> p j q")), writes=[mb], owner=mb)
        kr = Ring(S, es, 'sb', 'at_k', [128, Tk], BF16, 2)
        vr = Ring(S, es, 'sb', 'at_v', [128, NXB, 128], BF16, 2)
        vmr = Ring(S, es, 'sb', 'at_vm', [NMETA, 128], BF16, 2)
        qr = Ring(S, es, 'sb', 'at_q', [128, TQ], BF16, 2)
        outr = Ring(S, es, 'sb', 'at_o', [128, TQ], BF16, 2)
        Er = Ring(S, es, 'sb', 'at_e', [128, 512], F32, 2)
        SPr = Ring(S, es, 'sb', 'at_sp', [128, 512], F32, 4)
        T1r = Ring(S, es, 'sb', 'at_t1', [128, 512], F32, 2)
        Wr = Ring(S, es, 'sb', 'at_w', [128, 512], BF16, 3)
        Rr = Ring(S, es, 'sb', 'at_r', [128, 512], F32, 4)
        PZ = Ring(S, es, 'ps', 'at_pz', [128, 512], F32, 3)
        PL = Ring(S, es, 'ps', 'at_pl', [128, 512], F32, 2)
        PO = Ring(S, es, 'ps', 'at_po', [128, 512], F32, 2)
        tiles = []
        for h in range(SH):
            for g in range(NG):
                blocks = list(range(8 * g + 7, -1, -1)) + [-1]
                for bi, kb in enumerate(blocks):
                    tiles.append(dict(h=h, g=g, kb=kb, first=(bi == 0), last=(kb < 0), hfirst=(g == 0 and bi == 0), hlast=(g == NG - 1 and kb < 0)))
        hd = {}
        gd = {}

        def stA(t):
            h, g, kb = t['h'], t['g'], t['kb']
            if t['hfirst']:
                k, kb_ = kr.next()
                S.dma('sp', lambda e: e.dma_start(out=k[:], in_=KT[h, :, 0:Tk]), writes=[kb_], owner=kb_)
                v, vb = vr.next()
                S.dma('sp', lambda e: e.dma_start(out=v[:], in_=Vtm[NMETA:NMETA + 128 * NXB, h * 128:(h + 1) * 128].rearrange("(kb p) d -> p kb d", p=128)),
                      writes=[vb], owner=vb)
                vm, vmb = vmr.next()
                S.dma('sp', lambda e: e.dma_start(out=vm[:], in_=Vtm[0:NMETA, h * 128:(h + 1) * 128]), writes=[vmb], owner=vmb)
                q, qb_ = qr.next()
                S.dma('sp', lambda e: e.dma_start(out=q[:], in_=QT[h, :, 0:TQ]), writes=[qb_], owner=qb_)
                o, ob = outr.next()
                hd[h] = (k, kb_, v, vb, vm, vmb, q, qb_, o, ob)
            k, kb_, v, vb, vm, vmb, q, qb_, o, ob = hd[h]
            if t['first']:
                gd[(h, g)] = dict(po=PO.next(), R=None)
            meta = t['last']
            nk = NMETA if meta else 128
            kcols = slice(0, NMETA) if meta else slice(NMETA + 128 * kb, NMETA + 128 * (kb + 1))
            qs = slice(g * 512, (g + 1) * 512)
            masked = (not meta) and kb >= 8 * g
            j = kb - 8 * g
            pz, pzb = PZ.next()
            S.op('pe', lambda e: e.matmul(pz[0:nk, :], k[:, kcols], q[:, qs], start=True, stop=True), reads=[kb_, qb_], writes=[pzb])
            E, Eb = Er.next()
            S.op('act', lambda e: e.activation(out=E[0:nk, :], in_=pz[0:nk, :], func=AF.Exp, scale=SB_SCALE), reads=[pzb], writes=[Eb])
            sp, spb = SPr.next()
            S.op('act', lambda e: e.activation(out=sp[0:nk, :], in_=E[0:nk, :], func=AF.Ln, bias=1.0, scale=1.0), reads=[Eb], writes=[spb])
            if masked:
                S.op('pool', lambda e: e.tensor_tensor(out=sp[:, :], in0=sp[:, :], in1=am[:, j, :], op=ALU.mult), reads=[mb, spb], writes=[spb])
            t.update(nk=nk, pz=pz, pzb=pzb, sp=sp, spb=spb, masked=masked, j=j, meta=meta)

        def stB(t):
            h, g = t['h'], t['g']
            G = gd[(h, g)]
            nk, pz, pzb, sp, spb, first = t['nk'], t['pz'], t['pzb'], t['sp'], t['spb'], t['first']
            pl, plb = PL.next()
            S.op('pe', lambda e: e.matmul(pl[0:nk, :], tm[0:nk, 0, 0:nk], sp[0:nk, :], start=True, stop=first), reads=[mb, spb], writes=[plb], signal=first)
            if not first:
                R, Rb = G['R']
                S.op('pe', lambda e: e.matmul(pl[0:nk, :], tm[:, 1, 0:nk], R[:, :], start=False, stop=True), reads=[mb, Rb], writes=[plb])
            if not t['meta']:
                Rn, Rnb = Rr.next()
                if first:
                    S.op('pool', lambda e: e.tensor_copy(out=Rn[:, :], in_=sp[:, :]), reads=[spb], writes=[Rnb])
                else:
                    R, Rb = G['R']
                    S.op('pool', lambda e: e.tensor_tensor(out=Rn[:, :], in0=R[:, :], in1=sp[:, :], op=ALU.add), reads=[spb, Rb], writes=[Rnb])
                G['R'] = (Rn, Rnb)
            t1, t1b = T1r.next()
            S.op('dve', lambda e: e.scalar_tensor_tensor(out=t1[0:nk, :], in0=pz[0:nk, :], scalar=SB_SCALE, in1=sp[0:nk, :],
                                                         op0=ALU.mult, op1=ALU.subtract), reads=[pzb, spb], writes=[t1b])
            S.op('dve', lambda e: e.tensor_tensor(out=t1[0:nk, :], in0=t1[0:nk, :], in1=pl[0:nk, :], op=ALU.add), reads=[plb, t1b], writes=[t1b])
            w, wb = Wr.next()
            S.op('act', lambda e: e.activation(out=w[0:nk, :], in_=t1[0:nk, :], func=AF.Exp), reads=[t1b], writes=[wb])
            if t['masked']:
                j = t['j']
                S.op('pool', lambda e: e.tensor_tensor(out=w[:, :], in0=w[:, :], in1=amb_[:, j, :], op=ALU.mult), reads=[mb, wb], writes=[wb])
            t.update(w=w, wb=wb)

        def stC(t):
            h, g, kb = t['h'], t['g'], t['kb']
            k, kb_, v, vb, vm, vmb, q, qb_, o, ob = hd[h]
            po, pob = gd[(h, g)]['po']
            w, wb, nk, first = t['w'], t['wb'], t['nk'], t['first']
            if t['meta']:
                S.op('pe', lambda e: e.matmul(po[:, :], vm[:, :], w[0:nk, :], start=first, stop=True), reads=[vmb, wb], writes=[pob])
                qs = slice(g * 512, (g + 1) * 512)
                S.op('act', lambda e: e.copy(out=o[:, qs], in_=po[:, :]), reads=[pob], writes=[ob])
                if t['hlast']:
                    S.dma('sp', lambda e: e.dma_start(out=OT[h, :, 0:TQ], in_=o[:]), reads=[ob], owner=ob)
            else:
                S.op('pe', lambda e: e.matmul(po[:, :], v[:, kb, :], w[:, :], start=first, stop=False), reads=[vb, wb], writes=[pob], signal=False)

        nt = len(tiles)
        for step in range(nt + 2):
            if step < nt:
                stA(tiles[step])
            if 0 <= step - 1 < nt:
                stB(tiles[step - 1])
            if 0 <= step - 2 < nt:
                stC(tiles[step - 2])
        S.barrier()


def att_masks(parity):
    am = np.zeros((8, 128, 512), np.float32)
    p = np.arange(128)
    for j in range(8):
        for i in range(4):
            qb = 2 * i + parity
            if j < qb:
                am[j, :, i * 128:(i + 1) * 128] = 1.0
            elif j == qb:
                am[j, :, i * 128:(i + 1) * 128] = (p[:, None] < p[None, :])
    tmk = np.zeros((2, 128, 128), np.float32)
    tmk[0] = -(p[:, None] > p[None, :]).astype(np.float32)
    tmk[1] = -1.0
    return am, tmk


def const_masks():
    cm = np.zeros((4, 128, 128), np.float32)
    i = np.arange(128)
    cm[0] = np.where(i[:, None] <= i[None, :], -DEC_C, 0.0)
    cm[1] = (i[:, None] < i[None, :])
    cm[2] = (i[:, None] <= i[None, :])
    cm[3] = (i[:, None] > i[None, :])
    return cm


IN_SPECS = [
    ('cols', [128, NCOLS, KC]), ('ident', [128, 128]), ('cmask', [4, 128, 128]),
    ('ffn_w_gate', [4, D, FF]), ('ffn_w_up', [4, D, FF]), ('ffn_w_down', [4, FF, D]),
    ('rwkv_w_r', [2, D, D]), ('rwkv_w_k', [2, D, D]), ('rwkv_w_v', [2, D, D]), ('rwkv_w_o', [2, D, D]),
    ('rwkv_dec_w0', [2, D]), ('rwkv_dec_w1', [2, D, 96]), ('rwkv_dec_w2', [2, 96, D]),
    ('rwkv_a_w0', [2, D]), ('rwkv_a_w1', [2, D, 96]), ('rwkv_a_w2', [2, 96, D]),
    ('rwkv_g_w1', [2, D, 256]), ('rwkv_g_w2', [2, 256, D]),
    ('rwkv_k_k', [2, D]), ('rwkv_k_a', [2, D]), ('rwkv_r_k', [2, 32, 64]), ('rwkv_gn_w', [2, D]), ('rwkv_gn_b', [2, D]),
    ('rwkv_v_w0', [1, D]), ('rwkv_v_w1', [1, D, 64]), ('rwkv_v_w2', [1, 64, D]),
    ('amask', [8, 128, 512]), ('tmask', [2, 128, 128]),
    ('sb_w_k', [D, D]), ('sb_w_v', [D, D]), ('sb_k_gain', [128]), ('sb_w_q', [2, D, D]), ('sb_q_gain', [2, 128]), ('sb_w_o', [2, D, D]),
]


def build(NXB, mode='full', dbg=()):
    T = 128 * (NXB + 1)
    nc = bass.Bass("TRN2", target_bir_lowering=False)
    I = {}
    I['xin'] = nc.dram_tensor('xin', [T, D], F32, kind="ExternalInput").ap()
    for name, shape in IN_SPECS:
        I[name] = nc.dram_tensor(name, list(shape), F32, kind="ExternalInput").ap()

    def scratch(name, shape, dt):
        kind = "ExternalOutput" if name in dbg else "Internal"
        return nc.dram_tensor(name, list(shape), dt, kind=kind).ap()

    hT = scratch('hT', [KC, 128, T], F32)
    xn = scratch('xn', [KC, 128, T], BF16)
    actT = scratch('actT', [FC, 128, T], BF16)
    Z = {'xmix': [scratch('xmix%d' % i, [KC, 128, T], BF16) for i in range(6)]}
    for n in ('r', 'k', 'v', 'vfirst', 'a', 'g', 'vg', 'vv', 'bonus', 'kt', 'bn'):
        Z[n] = scratch('z_' + n, [T, D], BF16)
    Z['sigd'] = scratch('z_sigd', [T, D], F32)
    Z['gam'] = scratch('z_gam', [T // 128, 128, KC], F32)
    Z['zT'] = scratch('z_zT', [KC, 128, T], BF16)
    for n in ('rT', 'kpT', 'ktT', 'bnT'):
        Z[n] = scratch('z_' + n, [T // 128, 64, RH, 128], BF16)
    NQB = NXB // 2
    TQ = NQB * 128
    hqT = scratch('hqT', [KC, 128, TQ], F32)
    KT = scratch('KT', [SH, 128, T], BF16)
    Vtm = scratch('Vtm', [T, D], BF16)
    QT = scratch('QT', [SH, 128, TQ], BF16)
    OT = scratch('OT', [SH, 128, TQ], BF16)
    full = mode in ('full', 'att_test')
    out = nc.dram_tensor('out', [TQ if full else T, D], F32, kind="ExternalOutput").ap()

    def ffn(S, CT, layer, hres, Tn):
        phase_norm(S, CT, hres, I['cols'], COLS[('ffn_norm_g', layer)], xn, Tn)
        phase_ffn_gateup(S, CT, xn, I['ffn_w_gate'][layer], I['ffn_w_up'][layer], actT, Tn)
        phase_ffn_down(S, CT, actT, I['ffn_w_down'][layer], hres, Tn)

    with ExitStack() as es:
        S = Sched(nc, es)
        CT = load_consts(S, es, I)
        phase_in_transpose(S, CT, I['xin'], hT, T)
        if mode == 'ffn_test':
            ffn(S, CT, 0, hT, T)
        if mode == 'rwkv_test':
            rwkv_layer(S, CT, I, 0, Z, hT, T)
        if mode == 'full':
            rwkv_layer(S, CT, I, 0, Z, hT, T)
            ffn(S, CT, 0, hT, T)
            rwkv_layer(S, CT, I, 1, Z, hT, T)
            ffn(S, CT, 1, hT, T)
        if mode == 'rwkv2_test':
            rwkv_layer(S, CT, I, 0, Z, hT, T)
            ffn(S, CT, 0, hT, T)
            rwkv_layer(S, CT, I, 1, Z, hT, T)
        if full:
            phase_norm(S, CT, hT, I['cols'], COLS[('kv_norm_g', 0)], xn, T)
            phase_headnorm_fm(S, CT, xn, I['sb_w_k'], I['sb_k_gain'], KT, T)
            phase_proj_tm(S, CT, xn, I['sb_w_v'], Vtm, T, BF16)
            phase_gather_q(S, CT, hT, hqT, NQB)
            for j in range(2):
                phase_norm(S, CT, hqT, I['cols'], COLS[('mix_norm_g', 2 + j)], xn, TQ)
                phase_headnorm_fm(S, CT, xn, I['sb_w_q'][j], I['sb_q_gain'][j], QT, TQ)
                phase_attention(S, CT, I, KT, Vtm, QT, OT, NXB)
                phase_proj_fm_res(S, CT, OT, I['sb_w_o'][j], hqT, TQ)
                ffn(S, CT, 2 + j, hqT, TQ)
            phase_out_transpose(S, CT, hqT, out, TQ, 0)
        else:
            phase_out_transpose(S, CT, hT, out, T, 0)
        print("instructions:", S.ninst)
    return nc


def host_inputs(inputs):
    d = {k: np.ascontiguousarray(np.asarray(v), dtype=np.float32) for k, v in inputs.items()}
    base = {'cols': pack_cols(d), 'ident': np.eye(128, dtype=np.float32), 'cmask': const_masks(), 'tmask': att_masks(0)[1], 'amask': att_masks(0)[0]}
    for name, shape in IN_SPECS:
        if name not in base:
            base[name] = d[name].reshape(shape)
    return base


def kernel(**inputs):
    NXB = 32
    T = 128 * (NXB + 1)
    base = host_inputs(inputs)
    x = np.asarray(inputs['x'], np.float32)
    meta = np.asarray(inputs['meta_tokens'], np.float32)
    in_maps = []
    for c in range(8):
        b = c // 2
        xin = np.zeros((T, D), np.float32)
        xin[:NMETA] = meta
        xin[NMETA:NMETA + 4096] = x[b]
        m = dict(base)
        m['xin'] = xin
        m['amask'] = att_masks(c % 2)[0]
        in_maps.append(m)
    nc = build(NXB, mode='full')
    res = run_bass_kernel_spmd(nc, in_maps, core_ids=list(range(8)))
    out = np.zeros((4, 4096, D), np.float32)
    for c in range(8):
        o = np.asarray(res.results[c]['out']).reshape(NXB // 2, 128, D)
        out[c // 2].reshape(NXB // 2, 2, 128, D)[:, c % 2] = o
    return out
```

```python
import numpy as np
import ml_dtypes
from contextlib import ExitStack
import concourse.bass as bass
import concourse.mybir as mybir
from concourse.bass_utils import run_bass_kernel_spmd

F32 = mybir.dt.float32
BF16 = mybir.dt.bfloat16
AF = mybir.ActivationFunctionType
ALU = mybir.AluOpType
AX = mybir.AxisListType

D = 2048
KC = 16
FF = 5632
FC = 44
NMETA = 16
RH = 32
SH = 16
RMS_EPS = 1e-6
GN_EPS = 64e-5
ENG = ('pe', 'act', 'dve', 'pool', 'sp')
NDS = 48


_UID = [0]


def un(name):
    _UID[0] += 1
    return '%s_%d' % (name, _UID[0])


class DSem:
    def __init__(self, h):
        self.h = h
        self.count = 0


class Buf:
    __slots__ = ('w', 'r', 'ds', 'name')

    def __init__(self, name=''):
        self.w = None
        self.r = {}
        self.ds = None
        self.name = name


class Sched:
    def __init__(self, nc, es):
        self.nc = nc
        self.eng = {'pe': nc.tensor, 'act': nc.scalar, 'dve': nc.vector, 'pool': nc.gpsimd, 'sp': nc.sync}
        self.sem = {e: es.enter_context(nc.semaphore('s_' + e)) for e in ENG}
        self.cnt = {e: 0 for e in ENG}
        self.seen = {e: {} for e in ENG}
        self.free_ds = [DSem(es.enter_context(nc.semaphore('d%d' % i))) for i in range(NDS)]
        self.used_ds = []
        self.bufs = []
        self.ninst = 0

    def buf(self, name=''):
        b = Buf(name)
        self.bufs.append(b)
        return b

    def _waits(self, e, evs):
        need = {}
        for key, val in evs:
            if key == 'pe' and e == 'pe':
                continue
            if self.seen[e].get(key, 0) >= val:
                continue
            if need.get(key, 0) < val:
                need[key] = val
        for key, val in need.items():
            self.seen[e][key] = val
            h = self.sem[key] if isinstance(key, str) else key.h
            self.eng[e].wait_ge(h, val)
            self.ninst += 1

    @staticmethod
    def _deps(reads, writes):
        evs = []
        for b in reads:
            if b.w is not None:
                evs.append(b.w)
        for b in writes:
            if b.w is not None:
                evs.append(b.w)
            evs.extend(b.r.items())
        return evs

    @staticmethod
    def _mark(ev, reads, writes):
        k, v = ev
        for b in reads:
            if b.r.get(k, 0) < v:
                b.r[k] = v
        for b in writes:
            b.w = ev
            b.r = {}

    def op(self, e, fn, reads=(), writes=(), signal=True, disjoint=False):
        deps = self._deps(reads, writes)
        if disjoint:
            own = [b.w for b in writes if b.w is not None and b.w[0] == e]
            deps = [d for d in deps if d not in own]
        self._waits(e, deps)
        ins = fn(self.eng[e])
        self.ninst += 1
        if signal:
            self.cnt[e] += 1
            ins.then_inc(self.sem[e], 1)
            ev = (e, self.cnt[e])
        else:
            ev = (e, self.cnt[e] + 1)
        self._mark(ev, reads, writes)
        return ins

    def dma(self, q, fn, reads=(), writes=(), owner=None):
        self._waits(q, self._deps(reads, writes))
        if owner.ds is None:
            owner.ds = self.free_ds.pop()
            self.used_ds.append(owner.ds)
        ds = owner.ds
        ins = fn(self.eng[q])
        self.ninst += 1
        ds.count += 16
        ins.then_inc(ds.h, 16)
        self._mark((ds, ds.count), reads, writes)
        return ins

    def barrier(self):
        evs = [(e, self.cnt[e]) for e in ENG if self.cnt[e] > 0]
        evs += [(ds, ds.count) for ds in self.used_ds]
        for e in ENG:
            self._waits(e, evs)
        for b in self.bufs:
            b.w = None
            b.r = {}
            b.ds = None
        self.free_ds.extend(self.used_ds)
        self.used_ds = []
        self.bufs = []


class Ring:
    def __init__(self, S, es, kind, name, shape, dtype, n):
        nc = S.nc
        self.t = []
        self.b = []
        for i in range(n):
            if kind == 'sb':
                t = es.enter_context(nc.sbuf_tensor(un('%s%d' % (name, i)), shape, dtype))
            else:
                t = es.enter_context(nc.psum_tensor(un('%s%d' % (name, i)), shape, dtype))
            self.t.append(t)
            self.b.append(S.buf(name))
        self.i = 0
        self.n = n

    def next(self):
        i = self.i % self.n
        self.i += 1
        return self.t[i], self.b[i]


def tile1(S, es, name, shape, dtype, kind='sb'):
    r = Ring(S, es, kind, name, shape, dtype, 1)
    return r.t[0], r.b[0]


def groups(T, g):
    out = []
    t0 = 0
    while t0 < T:
        out.append((t0, min(g, T - t0)))
        t0 += g
    return out


def fm(d):
    return d.rearrange("kc p t -> p kc t")


def colvec(w1d, n):
    return w1d.rearrange("(kc p) -> p kc", p=128)


def load_consts(S, es, C):
    nc = S.nc
    t = {}
    t['ident'] = es.enter_context(nc.sbuf_tensor(un('c_ident'), [128, 128], F32))
    t['identb'] = es.enter_context(nc.sbuf_tensor(un('c_identb'), [128, 128], BF16))
    t['ones'] = es.enter_context(nc.sbuf_tensor(un('c_ones'), [128, 128], F32))
    t['eps'] = es.enter_context(nc.sbuf_tensor(un('c_eps'), [128, 2], F32))
    b = S.buf('const')
    S.dma('sp', lambda e: e.dma_start(out=t['ident'][:], in_=C['ident']), writes=[b], owner=b)
    S.dma('pool', lambda e: e.dma_start(out=t['identb'][:], in_=C['ident']), writes=[b], owner=b)
    S.op('dve', lambda e: e.memset(t['ones'][:], 1.0), writes=[b])
    S.op('dve', lambda e: e.memset(t['eps'][:, 0:1], RMS_EPS), writes=[b])
    S.op('dve', lambda e: e.memset(t['eps'][:, 1:2], GN_EPS), writes=[b])
    S.barrier()
    return t


def phase_in_transpose(S, CT, xin, hT, T):
    nc = S.nc
    with ExitStack() as es:
        cb = S.buf('c')
        xr = Ring(S, es, 'sb', 'it_x', [128, D], F32, 3)
        hr = Ring(S, es, 'sb', 'it_h', [128, KC, 512], F32, 2)
        pr = Ring(S, es, 'ps', 'it_p', [128, 512], F32, 4)
        hv = fm(hT)
        for (t0, tn) in groups(T, 512):
            hb, hbb = hr.next()
            for j in range(tn // 128):
                xt, xb = xr.next()
                r0 = t0 + j * 128
                S.dma('sp', lambda e, xt=xt, r0=r0: e.dma_start(out=xt[:], in_=xin[r0:r0 + 128, :]),
                      writes=[xb], owner=xb)
                for q in range(4):
                    ps, pb = pr.next()
                    for i in range(4):
                        kc = 4 * q + i
                        S.op('pe', lambda e, ps=ps, xt=xt, kc=kc, i=i: e.transpose(
                            ps[:, i * 128:(i + 1) * 128], xt[:, kc * 128:(kc + 1) * 128], CT['ident'][:]),
                            reads=[xb, cb], writes=[pb], signal=(i == 3))
                    eng = 'dve' if q % 2 == 0 else 'act'
                    if eng == 'dve':
                        S.op('dve', lambda e, ps=ps, hb=hb, q=q, j=j: e.tensor_copy(
                            out=hb[:, 4 * q:4 * q + 4, j * 128:(j + 1) * 128],
                            in_=ps[:].rearrange("p (a b) -> p a b", a=4)), reads=[pb], writes=[hbb])
                    else:
                        S.op('act', lambda e, ps=ps, hb=hb, q=q, j=j: e.copy(
                            out=hb[:, 4 * q:4 * q + 4, j * 128:(j + 1) * 128],
                            in_=ps[:].rearrange("p (a b) -> p a b", a=4)), reads=[pb], writes=[hbb])
            S.dma('sp', lambda e, hb=hb, t0=t0, tn=tn: e.dma_start(out=hv[:, :, t0:t0 + tn], in_=hb[:, :, 0:tn]),
                  reads=[hbb], owner=hbb)
        S.barrier()


COLS = {}
_ci = 0
for _l in range(4):
    COLS[('ffn_norm_g', _l)] = _ci; _ci += 1
for _l in range(4):
    COLS[('mix_norm_g', _l)] = _ci; _ci += 1
COLS[('kv_norm_g', 0)] = _ci; _ci += 1
for _l in range(2):
    for _i in range(6):
        COLS[('rwkv_mu', _l, _i)] = _ci; _ci += 1
NCOLS = _ci


def pack_cols(inputs):
    out = np.zeros((128, NCOLS, KC), np.float32)
    for key, i in COLS.items():
        a = inputs[key[0]]
        v = a[key[1]] if key[0] != 'kv_norm_g' else a
        if key[0] == 'rwkv_mu':
            v = v[key[2]]
        out[:, i, :] = np.asarray(v).reshape(KC, 128).T
    return out


def load_cols(S, es, name, cols, idxs):
    nc = S.nc
    n = len(idxs)
    t = es.enter_context(nc.sbuf_tensor(un(name), [128, n, KC], F32))
    b = S.buf(name)
    for i, ix in enumerate(idxs):
        S.dma('sp', lambda e, i=i, ix=ix: e.dma_start(out=t[:, i, :], in_=cols[:, ix, :]), writes=[b], owner=b)
    return t, b


def phase_norm(S, CT, hsrc, cols, gidx, xout, T):
    nc = S.nc
    with ExitStack() as es:
        cb = S.buf('c')
        g, gb = load_cols(S, es, 'n_g', cols, [gidx])
        hr = Ring(S, es, 'sb', 'n_h', [128, KC, 512], F32, 2)
        sr = Ring(S, es, 'sb', 'n_sq', [128, 512], F32, 3)
        rr = Ring(S, es, 'sb', 'n_rs', [128, 512], F32, 2)
        orr = Ring(S, es, 'sb', 'n_o', [128, KC, 512], BF16, 2)
        pr = Ring(S, es, 'ps', 'n_p', [128, 512], F32, 2)
        hv = fm(hsrc)
        ov = fm(xout)
        for (t0, tn) in groups(T, 512):
            h, hb = hr.next()
            S.dma('sp', lambda e, h=h, t0=t0, tn=tn: e.dma_start(out=h[:, :, 0:tn], in_=hv[:, :, t0:t0 + tn]),
                  writes=[hb], owner=hb)
            ps, pb = pr.next()
            for kc in range(KC):
                sq, sb_ = sr.next()
                S.op('act', lambda e, sq=sq, h=h, kc=kc, tn=tn: e.activation(
                    out=sq[:, 0:tn], in_=h[:, kc, 0:tn], func=AF.Square), reads=[hb], writes=[sb_])
                S.op('pe', lambda e, ps=ps, sq=sq, kc=kc, tn=tn: e.matmul(
                    ps[:, 0:tn], CT['ones'][:], sq[:, 0:tn], start=(kc == 0), stop=(kc == KC - 1)),
                    reads=[sb_, cb], writes=[pb], signal=True)
            rs, rb = rr.next()
            S.op('act', lambda e, rs=rs, ps=ps, tn=tn: e.activation(
                out=rs[:, 0:tn], in_=ps[:, 0:tn], func=AF.Ln, bias=CT['eps'][:, 0:1], scale=1.0 / D),
                reads=[pb, cb], writes=[rb])
            S.op('act', lambda e, rs=rs, tn=tn: e.activation(
                out=rs[:, 0:tn], in_=rs[:, 0:tn], func=AF.Exp, scale=-0.5), reads=[rb], writes=[rb])
            o, ob = orr.next()
            for kc in range(KC):
                S.op('dve', lambda e, o=o, h=h, rs=rs, kc=kc, tn=tn: e.scalar_tensor_tensor(
                    out=o[:, kc, 0:tn], in0=h[:, kc, 0:tn], scalar=g[:, 0, kc:kc + 1], in1=rs[:, 0:tn],
                    op0=ALU.mult, op1=ALU.mult), reads=[hb, rb, gb], writes=[ob], disjoint=(kc > 0))
            S.dma('sp', lambda e, o=o, t0=t0, tn=tn: e.dma_start(out=ov[:, :, t0:t0 + tn], in_=o[:, :, 0:tn]),
                  reads=[ob], owner=ob)
        S.barrier()


def load_resident(S, es, name, src, kcn, T):
    nc = S.nc
    X = es.enter_context(nc.sbuf_tensor(un(name), [128, kcn, T], BF16))
    Xb = S.buf(name)
    sv = fm(src)
    for k0 in range(0, kcn, 4):
        k1 = min(kcn, k0 + 4)
        S.dma('sp', lambda e, k0=k0, k1=k1: e.dma_start(out=X[:, k0:k1, :], in_=sv[:, k0:k1, 0:T]),
              writes=[Xb], owner=Xb)
    return X, Xb


def wview(W):
    return W.rearrange("(kc p) f -> p kc f", p=128)


def phase_ffn_gateup(S, CT, xn, Wg, Wu, actT, T):
    nc = S.nc
    with ExitStack() as es:
        X, Xb = load_resident(S, es, 'fg_x', xn, KC, T)
        wr = Ring(S, es, 'sb', 'fg_w', [128, 2, KC, 256], BF16, 2)
        ar = Ring(S, es, 'sb', 'fg_a', [128, T], BF16, 2)
        tr = Ring(S, es, 'sb', 'fg_t', [128, 512], F32, 3)
        pg = Ring(S, es, 'ps', 'fg_pg', [128, 512], F32, 3)
        pu = Ring(S, es, 'ps', 'fg_pu', [128, 512], F32, 3)
        wgv = wview(Wg)
        wuv = wview(Wu)
        for og in range(FC // 2):
            w, wb = wr.next()
            c0 = og * 256
            for k0 in range(0, KC, 8):
                S.dma('pool', lambda e, w=w, c0=c0, k0=k0: e.dma_start(out=w[:, 0, k0:k0 + 8, :], in_=wgv[:, k0:k0 + 8, c0:c0 + 256]),
                      writes=[wb], owner=wb)
                S.dma('pool', lambda e, w=w, c0=c0, k0=k0: e.dma_start(out=w[:, 1, k0:k0 + 8, :], in_=wuv[:, k0:k0 + 8, c0:c0 + 256]),
                      writes=[wb], owner=wb)
            for ol in range(2):
                oc = og * 2 + ol
                a, ab = ar.next()
                for (t0, tn) in groups(T, 512):
                    p1, p1b = pg.next()
                    p2, p2b = pu.next()
                    for kc in range(KC):
                        S.op('pe', lambda e, p1=p1, w=w, ol=ol, kc=kc, t0=t0, tn=tn: e.matmul(
                            p1[:, 0:tn], w[:, 0, kc, ol * 128:(ol + 1) * 128], X[:, kc, t0:t0 + tn],
                            start=(kc == 0), stop=(kc == KC - 1)), reads=[wb, Xb], writes=[p1b], signal=(kc == KC - 1))
                    for kc in range(KC):
                        S.op('pe', lambda e, p2=p2, w=w, ol=ol, kc=kc, t0=t0, tn=tn: e.matmul(
                            p2[:, 0:tn], w[:, 1, kc, ol * 128:(ol + 1) * 128], X[:, kc, t0:t0 + tn],
                            start=(kc == 0), stop=(kc == KC - 1)), reads=[wb, Xb], writes=[p2b], signal=(kc == KC - 1))
                    tt, tb = tr.next()
                    S.op('act', lambda e, tt=tt, p1=p1, tn=tn: e.activation(out=tt[:, 0:tn], in_=p1[:, 0:tn], func=AF.Silu),
                         reads=[p1b], writes=[tb])
                    S.op('dve', lambda e, a=a, tt=tt, p2=p2, t0=t0, tn=tn: e.tensor_tensor(
                        out=a[:, t0:t0 + tn], in0=tt[:, 0:tn], in1=p2[:, 0:tn], op=ALU.mult),
                        reads=[tb, p2b], writes=[ab], disjoint=(t0 > 0))
                S.dma('sp', lambda e, a=a, oc=oc: e.dma_start(out=actT[oc, :, 0:T], in_=a[:]), reads=[ab], owner=ab)
        S.barrier()


def phase_ffn_down(S, CT, actT, Wd, hT, T):
    nc = S.nc
    with ExitStack() as es:
        wt = es.enter_context(nc.sbuf_tensor(un('fd_w'), [128, FC, 512], BF16))
        wb = S.buf('fd_w')
        ar = Ring(S, es, 'sb', 'fd_a', [128, FC, 512], BF16, 2)
        hr = Ring(S, es, 'sb', 'fd_h', [128, 512], F32, 4)
        pr = Ring(S, es, 'ps', 'fd_p', [128, 512], F32, 4)
        wv = wview(Wd)
        av = fm(actT)
        for q in range(4):
            for k0 in range(0, FC, 11):
                S.dma('pool', lambda e, k0=k0, q=q: e.dma_start(out=wt[:, k0:k0 + 11, :], in_=wv[:, k0:k0 + 11, q * 512:(q + 1) * 512]),
                      writes=[wb], owner=wb)
            for (t0, tn) in groups(T, 512):
                a, ab = ar.next()
                for k0 in range(0, FC, 11):
                    S.dma('sp', lambda e, a=a, k0=k0, t0=t0, tn=tn: e.dma_start(out=a[:, k0:k0 + 11, 0:tn], in_=av[:, k0:k0 + 11, t0:t0 + tn]),
                          writes=[ab], owner=ab)
                for ol in range(4):
                    dc = q * 4 + ol
                    h, hb = hr.next()
                    S.dma('sp', lambda e, h=h, dc=dc, t0=t0, tn=tn: e.dma_start(out=h[:, 0:tn], in_=hT[dc, :, t0:t0 + tn]),
                          writes=[hb], owner=hb)
                    ps, pb = pr.next()
                    for fc in range(FC):
                        S.op('pe', lambda e, ps=ps, a=a, ol=ol, fc=fc, tn=tn: e.matmul(
                            ps[:, 0:tn], wt[:, fc, ol * 128:(ol + 1) * 128], a[:, fc, 0:tn],
                            start=(fc == 0), stop=(fc == FC - 1)), reads=[wb, ab], writes=[pb], signal=(fc == FC - 1))
                    S.op('dve', lambda e, h=h, ps=ps, tn=tn: e.tensor_tensor(
                        out=h[:, 0:tn], in0=h[:, 0:tn], in1=ps[:, 0:tn], op=ALU.add), reads=[pb, hb], writes=[hb])
                    S.dma('sp', lambda e, h=h, dc=dc, t0=t0, tn=tn: e.dma_start(out=hT[dc, :, t0:t0 + tn], in_=h[:, 0:tn]),
                          reads=[hb], owner=hb)
        S.barrier()


def phase_out_transpose(S, CT, hT, out, T, col0):
    nc = S.nc
    with ExitStack() as es:
        cb = S.buf('c')
        hr = Ring(S, es, 'sb', 'ot_h', [128, KC, 128], F32, 3)
        orr = Ring(S, es, 'sb', 'ot_o', [128, D], F32, 2)
        pr = Ring(S, es, 'ps', 'ot_p', [128, 512], F32, 4)
        hv = fm(hT)
        for (t0, tn) in groups(T, 128):
            h, hb = hr.next()
            S.dma('sp', lambda e, h=h, t0=t0: e.dma_start(out=h[:], in_=hv[:, :, col0 + t0:col0 + t0 + 128]),
                  writes=[hb], owner=hb)
            o, ob = orr.next()
            for q in range(4):
                ps, pb = pr.next()
                for i in range(4):
                    kc = 4 * q + i
                    S.op('pe', lambda e, ps=ps, h=h, kc=kc, i=i: e.transpose(
                        ps[:, i * 128:(i + 1) * 128], h[:, kc, :], CT['ident'][:]),
                        reads=[hb, cb], writes=[pb], signal=(i == 3))
                if q % 2 == 0:
                    S.op('dve', lambda e, ps=ps, o=o, q=q: e.tensor_copy(out=o[:, q * 512:(q + 1) * 512], in_=ps[:]),
                         reads=[pb], writes=[ob])
                else:
                    S.op('act', lambda e, ps=ps, o=o, q=q: e.copy(out=o[:, q * 512:(q + 1) * 512], in_=ps[:]),
                         reads=[pb], writes=[ob])
            S.dma('sp', lambda e, o=o, t0=t0: e.dma_start(out=out[t0:t0 + 128, :], in_=o[:]), reads=[ob], owner=ob)
        S.barrier()


def phase_rwkv_mix(S, CT, hT, cols, layer, xmix, T):
    nc = S.nc
    G = 256
    with ExitStack() as es:
        cb = S.buf('c')
        idxs = [COLS[('mix_norm_g', layer)]] + [COLS[('rwkv_mu', layer, i)] for i in range(6)]
        g, gb = load_cols(S, es, 'm_g', cols, idxs)
        hr = Ring(S, es, 'sb', 'm_h', [128, KC, G], F32, 2)
        sr = Ring(S, es, 'sb', 'm_sq', [128, G], F32, 3)
        rr = Ring(S, es, 'sb', 'm_rs', [128, G], F32, 2)
        hnr = Ring(S, es, 'sb', 'm_hn', [128, KC, G + 1], F32, 2)
        xxr = Ring(S, es, 'sb', 'm_xx', [128, KC, G], F32, 1)
        orr = Ring(S, es, 'sb', 'm_o', [128, KC, G], BF16, 6)
        pr = Ring(S, es, 'ps', 'm_p', [128, G], F32, 2)
        hv = fm(hT)
        prev = None
        for (t0, tn) in groups(T, G):
            h, hb = hr.next()
            S.dma('sp', lambda e: e.dma_start(out=h[:, :, 0:tn], in_=hv[:, :, t0:t0 + tn]), writes=[hb], owner=hb)
            ps, pb = pr.next()
            for kc in range(KC):
                sq, sb_ = sr.next()
                S.op('act', lambda e: e.activation(out=sq[:, 0:tn], in_=h[:, kc, 0:tn], func=AF.Square), reads=[hb], writes=[sb_])
                S.op('pe', lambda e: e.matmul(ps[:, 0:tn], CT['ones'][:], sq[:, 0:tn], start=(kc == 0), stop=(kc == KC - 1)),
                     reads=[sb_, cb], writes=[pb], signal=True)
            rs, rb = rr.next()
            S.op('act', lambda e: e.activation(out=rs[:, 0:tn], in_=ps[:, 0:tn], func=AF.Ln, bias=CT['eps'][:, 0:1], scale=1.0 / D),
                 reads=[pb, cb], writes=[rb])
            S.op('act', lambda e: e.activation(out=rs[:, 0:tn], in_=rs[:, 0:tn], func=AF.Exp, scale=-0.5), reads=[rb], writes=[rb])
            hn, hnb = hnr.next()
            if prev is None:
                S.op('pool', lambda e: e.memset(hn[:, :, 0:1], 0.0), writes=[hnb])
            else:
                phn, phnb, ptn = prev
                S.op('pool', lambda e: e.tensor_copy(out=hn[:, :, 0:1], in_=phn[:, :, ptn:ptn + 1]), reads=[phnb], writes=[hnb])
            for kc in range(KC):
                S.op('dve', lambda e: e.scalar_tensor_tensor(
                    out=hn[:, kc, 1:1 + tn], in0=h[:, kc, 0:tn], scalar=g[:, 0, kc:kc + 1], in1=rs[:, 0:tn],
                    op0=ALU.mult, op1=ALU.mult), reads=[hb, rb, gb], writes=[hnb], disjoint=(kc > 0))
            xx, xxb = xxr.next()
            S.op('pool', lambda e: e.tensor_tensor(out=xx[:, :, 0:tn], in0=hn[:, :, 0:tn], in1=hn[:, :, 1:1 + tn], op=ALU.subtract),
                 reads=[hnb], writes=[xxb])
            for i in range(6):
                o, ob = orr.next()
                for kc in range(KC):
                    S.op('dve', lambda e: e.scalar_tensor_tensor(
                        out=o[:, kc, 0:tn], in0=xx[:, kc, 0:tn], scalar=g[:, 1 + i, kc:kc + 1], in1=hn[:, kc, 1:1 + tn],
                        op0=ALU.mult, op1=ALU.add), reads=[xxb, hnb, gb], writes=[ob], disjoint=(kc > 0))
                S.dma('sp', lambda e: e.dma_start(out=fm(xmix[i])[:, :, t0:t0 + tn], in_=o[:, :, 0:tn]), reads=[ob], owner=ob)
            prev = (hn, hnb, tn)
        S.barrier()


def gemm_tm(S, CT, es, X, Xb, T, W, fout, epilogue, wname='gw', pname='gp'):
    wr = Ring(S, es, 'sb', wname, [128, KC, 512], BF16, 2)
    pr = Ring(S, es, 'ps', pname, [128, 512], F32, 4)
    wv = wview(W)
    for og in range(fout // 512):
        w, wb = wr.next()
        for k0 in range(0, KC, 8):
            S.dma('pool', lambda e: e.dma_start(out=w[:, k0:k0 + 8, :], in_=wv[:, k0:k0 + 8, og * 512:(og + 1) * 512]),
                  writes=[wb], owner=wb)
        for (t0, tn) in groups(T, 128):
            ps, pb = pr.next()
            for kc in range(KC):
                S.op('pe', lambda e: e.matmul(ps[:], X[:, kc, t0:t0 + 128], w[:, kc, :], start=(kc == 0), stop=(kc == KC - 1)),
                     reads=[wb, Xb], writes=[pb], signal=(kc == KC - 1))
            epilogue(og, t0, ps, pb)


def phase_proj_tm(S, CT, xsrc, W, out, T, odt):
    nc = S.nc
    with ExitStack() as es:
        X, Xb = load_resident(S, es, 'pj_x', xsrc, KC, T)
        orr = Ring(S, es, 'sb', 'pj_o', [128, 512], odt, 4)
        cnt = [0]

        def epi(og, t0, ps, pb):
            o, ob = orr.next()
            cnt[0] += 1
            if cnt[0] % 2 == 0:
                S.op('dve', lambda e: e.tensor_copy(out=o[:], in_=ps[:]), reads=[pb], writes=[ob])
            else:
                S.op('act', lambda e: e.copy(out=o[:], in_=ps[:]), reads=[pb], writes=[ob])
            S.dma('sp', lambda e: e.dma_start(out=out[t0:t0 + 128, og * 512:(og + 1) * 512], in_=o[:]), reads=[ob], owner=ob)

        gemm_tm(S, CT, es, X, Xb, T, W, D, epi)
        S.barrier()


def phase_lora(S, CT, xsrc, w1, w2, w0row, rdim, hid_func, out_func, out, T, odt):
    nc = S.nc
    nrc = (rdim + 127) // 128
    rp = min(rdim, 128)
    with ExitStack() as es:
        X, Xb = load_resident(S, es, 'lo_x', xsrc, KC, T)
        w1t = es.enter_context(nc.sbuf_tensor(un('lo_w1'), [128, KC, rdim], BF16))
        w1b = S.buf('w1')
        S.dma('pool', lambda e: e.dma_start(out=w1t[:], in_=wview(w1)), writes=[w1b], owner=w1b)
        w2t = es.enter_context(nc.sbuf_tensor(un('lo_w2'), [rp, nrc, D], BF16))
        w2b = S.buf('w2')
        S.dma('pool', lambda e: e.dma_start(out=w2t[:], in_=w2.rearrange("(c p) f -> p c f", p=rp)), writes=[w2b], owner=w2b)
        hid = es.enter_context(nc.sbuf_tensor(un('lo_hid'), [rp, nrc, T], BF16))
        hidb = S.buf('hid')
        if w0row is not None:
            w0t = es.enter_context(nc.sbuf_tensor(un('lo_w0'), [128, D], F32))
            w0b = S.buf('w0')
            S.dma('sp', lambda e: e.dma_start(out=w0t[:], in_=w0row.partition_broadcast(128)), writes=[w0b], owner=w0b)
        pr = Ring(S, es, 'ps', 'lo_p', [128, 512], F32, 3)
        for rc in range(nrc):
            for (t0, tn) in groups(T, 512):
                ps, pb = pr.next()
                for kc in range(KC):
                    S.op('pe', lambda e: e.matmul(ps[0:rp, 0:tn], w1t[:, kc, rc * 128:rc * 128 + rp], X[:, kc, t0:t0 + tn],
                                                  start=(kc == 0), stop=(kc == KC - 1)), reads=[w1b, Xb], writes=[pb], signal=(kc == KC - 1))
                S.op('act', lambda e: e.activation(out=hid[:, rc, t0:t0 + tn], in_=ps[0:rp, 0:tn], func=hid_func), reads=[pb], writes=[hidb])
        orr = Ring(S, es, 'sb', 'lo_o', [128, 512], odt, 4)
        tr = Ring(S, es, 'sb', 'lo_t', [128, 512], F32, 3)
        p2 = Ring(S, es, 'ps', 'lo_p2', [128, 512], F32, 3)
        for og in range(4):
            for (t0, tn) in groups(T, 128):
                ps, pb = p2.next()
                for rc in range(nrc):
                    S.op('pe', lambda e: e.matmul(ps[:], hid[:, rc, t0:t0 + 128], w2t[:, rc, og * 512:(og + 1) * 512],
                                                  start=(rc == 0), stop=(rc == nrc - 1)), reads=[hidb, w2b], writes=[pb], signal=(rc == nrc - 1))
                o, ob = orr.next()
                if w0row is not None:
                    tt, tb = tr.next()
                    S.op('dve', lambda e: e.tensor_tensor(out=tt[:], in0=ps[:], in1=w0t[:, og * 512:(og + 1) * 512], op=ALU.add),
                         reads=[pb, w0b], writes=[tb])
                    S.op('act', lambda e: e.activation(out=o[:], in_=tt[:], func=out_func), reads=[tb], writes=[ob])
                else:
                    S.op('act', lambda e: e.activation(out=o[:], in_=ps[:], func=out_func), reads=[pb], writes=[ob])
                S.dma('sp', lambda e: e.dma_start(out=out[t0:t0 + 128, og * 512:(og + 1) * 512], in_=o[:]), reads=[ob], owner=ob)
        S.barrier()

DEC_C = 0.6065306597126334
import os as _os
PREP_STAGE = int(_os.environ.get('PREP_STAGE', '9'))
SCAN_STAGE = int(_os.environ.get('SCAN_STAGE', '9'))
RUN_SCAN = _os.environ.get('RUN_SCAN', '1') == '1'
SCORE_MODE = int(_os.environ.get('SCORE_MODE', '2'))


def bc3(ap2, n):
    return ap2.unsqueeze(2).broadcast_to([128, ap2.shape[1], n])


def hv3(ap2):
    return ap2.rearrange("p (h c) -> p h c", c=64)


def v4(ap2, a):
    return ap2.rearrange("p (a b) -> p a b", a=a)


def load_bc(S, es, name, row, dt=F32):
    nc = S.nc
    t = es.enter_context(nc.sbuf_tensor(un(name), [128, D], dt))
    b = S.buf(name)
    S.dma('sp', lambda e: e.dma_start(out=t[:], in_=row.partition_broadcast(128)), writes=[b], owner=b)
    return t, b


def phase_rwkv_prep(S, CT, I, layer, Z, T):
    nc = S.nc
    with ExitStack() as es:
        cb = S.buf('c')
        kkbc, kkb = load_bc(S, es, 'pp_kk', I['rwkv_k_k'][layer])
        kabc, kab = load_bc(S, es, 'pp_ka', I['rwkv_k_a'][layer])
        rkbc, rkb = load_bc(S, es, 'pp_rk', I['rwkv_r_k'][layer].rearrange("h c -> (h c)"))
        cm = es.enter_context(nc.sbuf_tensor(un('pp_cm'), [128, 128], F32))
        cmb = S.buf('cm')
        S.dma('sp', lambda e: e.dma_start(out=cm[:], in_=I['cmask'][0]), writes=[cmb], owner=cmb)
        negc = es.enter_context(nc.sbuf_tensor(un('pp_negc'), [128, 128], F32))
        S.op('dve', lambda e: e.memset(negc[:], -DEC_C), writes=[cmb])
        r_in = Ring(S, es, 'sb', 'pp_r', [128, D], BF16, 2)
        k_in = Ring(S, es, 'sb', 'pp_k', [128, D], BF16, 2)
        a_in = Ring(S, es, 'sb', 'pp_a', [128, D], BF16, 2)
        v_in = Ring(S, es, 'sb', 'pp_v', [128, D], BF16, 2)
        s_in = Ring(S, es, 'sb', 'pp_s', [128, D], F32, 2)
        if layer == 1:
            vg_in = Ring(S, es, 'sb', 'pp_vg', [128, D], BF16, 1)
            vf_in = Ring(S, es, 'sb', 'pp_vf', [128, D], BF16, 1)
            o_vv = Ring(S, es, 'sb', 'pp_ovv', [128, D], BF16, 2)
        T1, T1b = tile1(S, es, 'pp_t1', [128, D], F32)
        T2, T2b = tile1(S, es, 'pp_t2', [128, D], F32)
        T3, T3b = tile1(S, es, 'pp_t3', [128, D], F32)
        T4, T4b = tile1(S, es, 'pp_t4', [128, D], F32)
        ss, ssb = tile1(S, es, 'pp_ss', [128, 64], F32)
        Er = Ring(S, es, 'sb', 'pp_e', [128, 512], F32, 4)
        o_r = Ring(S, es, 'sb', 'pp_or', [128, D], BF16, 2)
        o_kp = Ring(S, es, 'sb', 'pp_okp', [128, D], BF16, 2)
        o_kt = Ring(S, es, 'sb', 'pp_okt', [128, D], BF16, 2)
        o_bn = Ring(S, es, 'sb', 'pp_obn', [128, D], BF16, 2)
        o_bo = Ring(S, es, 'sb', 'pp_obo', [128, D], BF16, 2)
        xt_r = Ring(S, es, 'sb', 'pp_xt', [128, KC, 128], BF16, 4)
        gm_r = Ring(S, es, 'sb', 'pp_gm', [128, KC], F32, 2)
        pcw = Ring(S, es, 'ps', 'pp_pcw', [128, 512], F32, 4)
        ptr = Ring(S, es, 'ps', 'pp_ptr', [128, 1024], BF16, 2)
        for ci, (t0, tn) in enumerate(groups(T, 128)):
            def ld(ring, src):
                t, b = ring.next()
                S.dma('sp', lambda e: e.dma_start(out=t[:], in_=src[t0:t0 + 128, :]), writes=[b], owner=b)
                return t, b
            r, rb = ld(r_in, Z['r'])
            k, kb = ld(k_in, Z['k'])
            a, ab = ld(a_in, Z['a'])
            v, vb = ld(v_in, Z['v'])
            sg, sgb = ld(s_in, Z['sigd'])
            if layer == 1:
                vg, vgb = ld(vg_in, Z['vg'])
                vf, vfb = ld(vf_in, Z['vfirst'])
                vv, vvb = o_vv.next()
                S.op('dve', lambda e: e.tensor_tensor(out=T4[:], in0=vf[:], in1=v[:], op=ALU.subtract), reads=[vfb, vb], writes=[T4b])
                S.op('dve', lambda e: e.tensor_tensor(out=T4[:], in0=T4[:], in1=vg[:], op=ALU.mult), reads=[vgb, T4b], writes=[T4b])
                S.op('dve', lambda e: e.tensor_tensor(out=vv[:], in0=T4[:], in1=v[:], op=ALU.add), reads=[vb, T4b], writes=[vvb])
                S.dma('sp', lambda e: e.dma_start(out=Z['vv'][t0:t0 + 128, :], in_=vv[:]), reads=[vvb], owner=vvb)
            else:
                vv, vvb = v, vb
            S.op('dve', lambda e: e.tensor_tensor(out=T1[:], in0=k[:], in1=kkbc[:], op=ALU.mult), reads=[kb, kkb], writes=[T1b])
            S.op('act', lambda e: e.activation(out=T2[:], in_=T1[:], func=AF.Square), reads=[T1b], writes=[T2b])
            S.op('dve', lambda e: e.tensor_reduce(out=ss[:, 0:32], in_=hv3(T2[:]), axis=AX.X, op=ALU.add), reads=[T2b], writes=[ssb])
            S.op('dve', lambda e: e.tensor_scalar_max(out=ss[:, 0:32], in0=ss[:, 0:32], scalar1=1e-24), reads=[ssb], writes=[ssb])
            S.op('act', lambda e: e.activation(out=ss[:, 0:32], in_=ss[:, 0:32], func=AF.Ln), reads=[ssb], writes=[ssb])
            S.op('act', lambda e: e.activation(out=ss[:, 0:32], in_=ss[:, 0:32], func=AF.Exp, scale=-0.5), reads=[ssb], writes=[ssb])
            S.op('dve', lambda e: e.tensor_tensor(out=hv3(T1[:]), in0=hv3(T1[:]), in1=bc3(ss[:, 0:32], 64), op=ALU.mult),
                 reads=[ssb, T1b], writes=[T1b])
            if PREP_STAGE < 2:
                continue
            S.op('dve', lambda e: e.scalar_tensor_tensor(out=T2[:], in0=a[:], scalar=-1.0, in1=kabc[:], op0=ALU.add, op1=ALU.mult),
                 reads=[ab, kab], writes=[T2b])
            S.op('dve', lambda e: e.scalar_tensor_tensor(out=T2[:], in0=T2[:], scalar=1.0, in1=k[:], op0=ALU.add, op1=ALU.mult),
                 reads=[kb, T2b], writes=[T2b])
            S.op('dve', lambda e: e.scalar_tensor_tensor(out=T3[:], in0=T1[:], scalar=-1.0, in1=a[:], op0=ALU.mult, op1=ALU.mult),
                 reads=[T1b, ab], writes=[T3b])
            S.op('dve', lambda e: e.tensor_tensor(out=T4[:], in0=r[:], in1=T2[:], op=ALU.mult), reads=[rb, T2b], writes=[T4b])
            S.op('pool', lambda e: e.tensor_tensor(out=T4[:], in0=T4[:], in1=rkbc[:], op=ALU.mult), reads=[rkb, T4b], writes=[T4b])
            S.op('dve', lambda e: e.tensor_reduce(out=ss[:, 32:64], in_=hv3(T4[:]), axis=AX.X, op=ALU.add), reads=[T4b], writes=[ssb])
            bo, bob = o_bo.next()
            S.op('pool', lambda e: e.tensor_tensor(out=hv3(bo[:]), in0=hv3(vv[:]), in1=bc3(ss[:, 32:64], 64), op=ALU.mult),
                 reads=[ssb, vvb], writes=[bob])
            S.dma('sp', lambda e: e.dma_start(out=Z['bonus'][t0:t0 + 128, :], in_=bo[:]), reads=[bob], owner=bob)
            if PREP_STAGE < 3:
                continue
            orr_, orb = o_r.next()
            okp, okpb = o_kp.next()
            okt, oktb = o_kt.next()
            obn, obnb = o_bn.next()
            for og in range(4):
                sl = slice(og * 512, (og + 1) * 512)
                ps, pb = pcw.next()
                S.op('pe', lambda e: e.matmul(ps[:], cm[:], sg[:, sl], start=True, stop=True), reads=[cmb, sgb], writes=[pb])
                E1, E1b = Er.next()
                S.op('act', lambda e: e.activation(out=E1[:], in_=ps[:], func=AF.Exp), reads=[pb], writes=[E1b])
                S.op('pool', lambda e: e.tensor_tensor(out=orr_[:, sl], in0=r[:, sl], in1=E1[:], op=ALU.mult), reads=[rb, E1b], writes=[orb], disjoint=(og > 0))
                E2, E2b = Er.next()
                S.op('act', lambda e: e.activation(out=E2[:], in_=ps[:], func=AF.Exp, scale=-1.0), reads=[pb], writes=[E2b])
                S.op('pool', lambda e: e.tensor_tensor(out=okt[:, sl], in0=T2[:, sl], in1=E2[:], op=ALU.mult), reads=[T2b, E2b], writes=[oktb], disjoint=(og > 0))
                S.op('dve', lambda e: e.tensor_tensor(out=obn[:, sl], in0=T3[:, sl], in1=E2[:], op=ALU.mult), reads=[T3b, E2b], writes=[obnb], disjoint=(og > 0))
                E3, E3b = Er.next()
                S.op('dve', lambda e: e.scalar_tensor_tensor(out=E3[:], in0=sg[:, sl], scalar=DEC_C, in1=ps[:], op0=ALU.mult, op1=ALU.add),
                     reads=[sgb, pb], writes=[E3b])
                S.op('act', lambda e: e.activation(out=E3[:], in_=E3[:], func=AF.Exp), reads=[E3b], writes=[E3b])
                S.op('pool', lambda e: e.tensor_tensor(out=okp[:, sl], in0=T1[:, sl], in1=E3[:], op=ALU.mult), reads=[T1b, E3b], writes=[okpb], disjoint=(og > 0))
            S.dma('sp', lambda e: e.dma_start(out=Z['kt'][t0:t0 + 128, :], in_=okt[:]), reads=[oktb], owner=oktb)
            S.dma('sp', lambda e: e.dma_start(out=Z['bn'][t0:t0 + 128, :], in_=obn[:]), reads=[obnb], owner=obnb)
            if PREP_STAGE < 4:
                continue
            gm, gmb = gm_r.next()
            for q4 in range(4):
                pg, pgb = pcw.next()
                for i4 in range(4):
                    kc = 4 * q4 + i4
                    S.op('pe', lambda e: e.matmul(pg[:, i4 * 128:(i4 + 1) * 128], sg[:, kc * 128:(kc + 1) * 128], negc[:], start=True, stop=True),
                         reads=[sgb, cmb], writes=[pgb], signal=(i4 == 3))
                S.op('act', lambda e: e.activation(out=gm[:, 4 * q4:4 * q4 + 4].unsqueeze(2), in_=v4(pg[:], 4)[:, :, 0:1], func=AF.Exp), reads=[pgb], writes=[gmb])
            S.dma('sp', lambda e: e.dma_start(out=Z['gam'][ci], in_=gm[:]), reads=[gmb], owner=gmb)
            if PREP_STAGE < 5:
                continue
            for (src, srcb, dst) in ((orr_, orb, 'rT'), (okp, okpb, 'kpT'), (okt, oktb, 'ktT'), (obn, obnb, 'bnT')):
                xt, xtb = xt_r.next()
                for half in range(2):
                    pt, ptb = ptr.next()
                    for j in range(8):
                        kc = half * 8 + j
                        S.op('pe', lambda e: e.transpose(pt[:, j * 128:(j + 1) * 128], src[:, kc * 128:(kc + 1) * 128], CT['identb'][:]),
                             reads=[srcb, cb], writes=[ptb], signal=(j == 7))
                    if half == 0:
                        S.op('act', lambda e: e.copy(out=xt[:, 0:8, :], in_=v4(pt[:], 8)), reads=[ptb], writes=[xtb])
                    else:
                        S.op('dve', lambda e: e.tensor_copy(out=xt[:, 8:16, :], in_=v4(pt[:], 8)), reads=[ptb], writes=[xtb])
                for e2 in range(2):
                    S.dma('sp', lambda e: e.dma_start(out=Z[dst][ci].rearrange("c (hp e) t -> c hp e t", e=2)[:, :, e2, :],
                                                      in_=xt[e2 * 64:(e2 + 1) * 64, :, :]), reads=[xtb], owner=xtb)
        S.barrier()


def phase_rwkv_scan(S, CT, I, layer, Z, T):
    nc = S.nc
    with ExitStack() as es:
        cb = S.buf('c')
        gnw, gnwb = load_bc(S, es, 'sc_gnw', I['rwkv_gn_w'][layer])
        gnb, gnbb = load_bc(S, es, 'sc_gnb', I['rwkv_gn_b'][layer])
        msk = es.enter_context(nc.sbuf_tensor(un('sc_msk'), [128, 3, 128], F32))
        mb = S.buf('msk')
        for i in range(3):
            S.dma('sp', lambda e: e.dma_start(out=msk[:, i, :], in_=I['cmask'][1 + i]), writes=[mb], owner=mb)
        MUS, MUI, MLS = 0, 1, 2

        def mbc(i, n):
            return msk[:, i, :].unsqueeze(1).broadcast_to([128, n, 128])
        A, Ab = tile1(S, es, 'sc_A', [64, RH, 64], F32)
        Abf, Abfb = tile1(S, es, 'sc_Abf', [64, RH, 64], BF16)
        S.op('dve', lambda e: e.memset(A[:], 0.0), writes=[Ab])
        S.op('dve', lambda e: e.memset(Abf[:], 0.0), writes=[Abfb])
        names_cm = ('rT', 'kpT', 'ktT', 'bnT')
        cm_r = {n: Ring(S, es, 'sb', 'sc_' + n, [64, RH, 128], BF16, 1) for n in names_cm}
        tm_r = {n: Ring(S, es, 'sb', 'sc_' + n, [128, D], BF16, 2) for n in ('kt', 'bn', 'vv', 'bonus', 'g')}
        gm_r = Ring(S, es, 'sb', 'sc_gam', [64, 2, KC], F32, 2)
        sc_t = {n: (es.enter_context(nc.sbuf_tensor(un('sc_s' + n), [128, 16, 128], BF16)), [S.buf(n) for _ in range(4)])
                for n in ('MkT', 'RKT', 'RBT', 'N0', 'N1', 'NT0', 'NT1', 'Q0', 'Q1')}
        stg = Ring(S, es, 'sb', 'sc_stg', [128, 512], BF16, 2)
        RHS, RHSb = tile1(S, es, 'sc_rhs', [128, 1024], BF16)
        U, Ub = tile1(S, es, 'sc_u', [128, 1024], BF16)
        y_r = Ring(S, es, 'sb', 'sc_y', [128, D], F32, 2)
        yt, ytb = tile1(S, es, 'sc_yt', [128, D], F32)
        st, stb = tile1(S, es, 'sc_st', [128, 5, 32], F32)
        z_r = Ring(S, es, 'sb', 'sc_z', [128, D], BF16, 1)
        zT_r = Ring(S, es, 'sb', 'sc_zT', [128, KC, 128], BF16, 2)
        P = Ring(S, es, 'ps', 'sc_p', [128, 512], F32, 6)
        PT = Ring(S, es, 'ps', 'sc_pt', [128, 1024], BF16, 2)
        ecount = [0]

        def cmview(d):
            return d.rearrange("hp (e c) t -> c (hp e) t", e=2)
        def output_stage(y, yb, bo, bob, gg, ggb, t0):
            S.op('dve', lambda e: e.tensor_reduce(out=st[:, 0, :], in_=hv3(y[:]), axis=AX.X, op=ALU.add), reads=[yb], writes=[stb])
            S.op('act', lambda e: e.activation(out=yt[:], in_=y[:], func=AF.Square), reads=[yb], writes=[ytb])
            S.op('dve', lambda e: e.tensor_reduce(out=st[:, 1, :], in_=hv3(yt[:]), axis=AX.X, op=ALU.add), reads=[ytb], writes=[stb])
            S.op('dve', lambda e: e.tensor_scalar_mul(out=st[:, 0, :], in0=st[:, 0, :], scalar1=1.0 / 64), reads=[stb], writes=[stb])
            S.op('dve', lambda e: e.tensor_tensor(out=st[:, 2, :], in0=st[:, 0, :], in1=st[:, 0, :], op=ALU.mult), reads=[stb], writes=[stb])
            S.op('dve', lambda e: e.scalar_tensor_tensor(out=st[:, 3, :], in0=st[:, 1, :], scalar=1.0 / 64, in1=st[:, 2, :], op0=ALU.mult, op1=ALU.subtract),
                 reads=[stb], writes=[stb])
            S.op('act', lambda e: e.activation(out=st[:, 3, :], in_=st[:, 3, :], func=AF.Ln, bias=CT['eps'][:, 1:2], scale=1.0), reads=[stb, cb], writes=[stb])
            S.op('act', lambda e: e.activation(out=st[:, 3, :], in_=st[:, 3, :], func=AF.Exp, scale=-0.5), reads=[stb], writes=[stb])
            S.op('dve', lambda e: e.tensor_tensor(out=hv3(yt[:]), in0=hv3(y[:]), in1=bc3(st[:, 0, :], 64), op=ALU.subtract), reads=[yb, stb], writes=[ytb])
            S.op('dve', lambda e: e.tensor_tensor(out=hv3(yt[:]), in0=hv3(yt[:]), in1=bc3(st[:, 3, :], 64), op=ALU.mult), reads=[stb, ytb], writes=[ytb])
            S.op('pool', lambda e: e.tensor_tensor(out=yt[:], in0=yt[:], in1=gnw[:], op=ALU.mult), reads=[gnwb, ytb], writes=[ytb])
            S.op('pool', lambda e: e.tensor_tensor(out=yt[:], in0=yt[:], in1=gnb[:], op=ALU.add), reads=[gnbb, ytb], writes=[ytb])
            S.op('pool', lambda e: e.tensor_tensor(out=yt[:], in0=yt[:], in1=bo[:], op=ALU.add), reads=[bob, ytb], writes=[ytb])
            z, zb = z_r.next()
            S.op('pool', lambda e: e.tensor_tensor(out=z[:], in0=yt[:], in1=gg[:], op=ALU.mult), reads=[ggb, ytb], writes=[zb])
            return (z, zb, t0)

        def output_tr(z, zb, t0):
            zT, zTb = zT_r.next()
            for half in range(2):
                pt, ptb = PT.next()
                for j in range(8):
                    kc = half * 8 + j
                    S.op('pe', lambda e: e.transpose(pt[:, j * 128:(j + 1) * 128], z[:, kc * 128:(kc + 1) * 128], CT['identb'][:]),
                         reads=[zb, cb], writes=[ptb], signal=(j == 7))
                S.op('act', lambda e: e.copy(out=zT[:, half * 8:half * 8 + 8, :], in_=v4(pt[:], 8)), reads=[ptb], writes=[zTb])
            S.dma('sp', lambda e: e.dma_start(out=fm(Z['zT'])[:, :, t0:t0 + 128], in_=zT[:]), reads=[zTb], owner=zTb)

        prev_out = None
        prev_z = None
        for ci, (t0, tn) in enumerate(groups(T, 128)):
            cmt = {}
            for n in names_cm:
                t, b = cm_r[n].next()
                S.dma('sp', lambda e: e.dma_start(out=t[:], in_=Z[n][ci]), writes=[b], owner=b)
                cmt[n] = (t, b)
            tmt = {}
            for n in ('kt', 'bn', 'vv', 'bonus', 'g'):
                t, b = tm_r[n].next()
                src = Z[n] if not (n == 'vv' and layer == 0) else Z['v']
                S.dma('sp', lambda e: e.dma_start(out=t[:], in_=src[t0:t0 + 128, :]), writes=[b], owner=b)
                tmt[n] = (t, b)
            gam, gamb = gm_r.next()
            S.dma('sp', lambda e: e.dma_start(out=gam[:], in_=Z['gam'][ci].rearrange("(e c) hp -> c e hp", e=2)), writes=[gamb], owner=gamb)
            rT, rTb = cmt['rT']
            kpT, kpTb = cmt['kpT']
            ktT, ktTb = cmt['ktT']
            bnT, bnTb = cmt['bnT']
            kt, ktb = tmt['kt']
            bn, bnb = tmt['bn']
            vv, vvb = tmt['vv']
            y, yb = y_r.next()
            for hh in range(2):
                H0 = 16 * hh
                kinds = (('MkT', ktT, ktTb, kpT, kpTb, MUS), ('N0', bnT, bnTb, kpT, kpTb, MUS), ('NT0', kpT, kpTb, bnT, bnTb, MLS),
                         ('RKT', ktT, ktTb, rT, rTb, MUI), ('RBT', bnT, bnTb, rT, rTb, MUI))
                for (dn, lt, ltb, rt, rtb, mi) in kinds:
                    dst, dstb = sc_t[dn]
                    for gq in range(4):
                        ps, pb = P.next()
                        for j in range(4):
                            h = H0 + 4 * gq + j
                            S.op('pe', lambda e: e.matmul(ps[:, j * 128:(j + 1) * 128], lt[:, h, :], rt[:, h, :], start=True, stop=True),
                                 reads=[ltb, rtb], writes=[pb], signal=(j == 3))
                        ecount[0] += 1
                        if ecount[0] % 2 == 0:
                            S.op('dve', lambda e: e.tensor_tensor(out=dst[:, 4 * gq:4 * gq + 4, :], in0=v4(ps[:], 4), in1=mbc(mi, 4), op=ALU.mult),
                                 reads=[pb, mb], writes=[dstb[gq]])
                        else:
                            sg_, sgb_ = stg.next()
                            S.op('act', lambda e: e.copy(out=sg_[:], in_=ps[:]), reads=[pb], writes=[sgb_])
                            S.op('pool', lambda e: e.tensor_tensor(out=dst[:, 4 * gq:4 * gq + 4, :], in0=v4(sg_[:], 4), in1=mbc(mi, 4), op=ALU.mult),
                                 reads=[sgb_, mb], writes=[dstb[gq]])
                if SCAN_STAGE < 2:
                    continue
                Ncur, Ncb = sc_t['N0']
                NTcur, NTcb = sc_t['NT0']
                Nnx, Nnb = sc_t['N1']
                NTnx, NTnb = sc_t['NT1']
                Qc, Qcb = sc_t['Q0']
                Qn, Qnb = sc_t['Q1']
                S.op('pool', lambda e: e.tensor_tensor(out=Qc[:], in0=Ncur[:], in1=CT['identb'][:].unsqueeze(1).broadcast_to([128, 16, 128]), op=ALU.add),
                     reads=Ncb + [cb], writes=Qcb)
                for lev in range(1, 7):
                    for gq in range(4):
                        ps, pb = P.next()
                        for j in range(4):
                            hx = 4 * gq + j
                            S.op('pe', lambda e: e.matmul(ps[:, j * 128:(j + 1) * 128], Ncur[:, hx, :], NTcur[:, hx, :], start=True, stop=True),
                                 reads=[Ncb[gq], NTcb[gq]], writes=[pb], signal=(j == 3))
                        S.op('act', lambda e: e.copy(out=NTnx[:, 4 * gq:4 * gq + 4, :], in_=v4(ps[:], 4)), reads=[pb], writes=[NTnb[gq]])
                        if lev < 6:
                            ps2, pb2 = P.next()
                            for j in range(4):
                                hx = 4 * gq + j
                                S.op('pe', lambda e: e.matmul(ps2[:, j * 128:(j + 1) * 128], NTcur[:, hx, :], Ncur[:, hx, :], start=True, stop=True),
                                     reads=[Ncb[gq], NTcb[gq]], writes=[pb2], signal=(j == 3))
                            S.op('dve', lambda e: e.tensor_copy(out=Nnx[:, 4 * gq:4 * gq + 4, :], in_=v4(ps2[:], 4)), reads=[pb2], writes=[Nnb[gq]])
                    for gq in range(4):
                        ps, pb = P.next()
                        for j in range(4):
                            hx = 4 * gq + j
                            S.op('pe', lambda e: e.matmul(ps[:, j * 128:(j + 1) * 128], NTnx[:, hx, :], Qc[:, hx, :], start=True, stop=True),
                                 reads=[NTnb[gq], Qcb[gq]], writes=[pb], signal=(j == 3))
                        S.op('dve', lambda e: e.tensor_tensor(out=Qn[:, 4 * gq:4 * gq + 4, :], in0=v4(ps[:], 4), in1=Qc[:, 4 * gq:4 * gq + 4, :], op=ALU.add),
                             reads=[pb, Qcb[gq]], writes=[Qnb[gq]])
                    Ncur, Ncb, Nnx, Nnb = Nnx, Nnb, Ncur, Ncb
                    NTcur, NTcb, NTnx, NTnb = NTnx, NTnb, NTcur, NTcb
                    Qc, Qcb, Qn, Qnb = Qn, Qnb, Qc, Qcb
                if hh == 0 and prev_out is not None:
                    prev_z = output_stage(*prev_out)
                    prev_out = None
                if hh == 1 and prev_z is not None:
                    output_tr(*prev_z)
                    prev_z = None
                MkT, MkTb = sc_t['MkT']
                RKT, RKTb = sc_t['RKT']
                RBT, RBTb = sc_t['RBT']
                for g8 in range(2):
                    ps, pb = P.next()
                    for j in range(8):
                        hx = 8 * g8 + j
                        h = H0 + hx
                        S.op('pe', lambda e: e.matmul(ps[:, j * 64:(j + 1) * 64], kpT[:, h, :], Abf[:, h, :], start=True, stop=False),
                             reads=[kpTb, Abfb], writes=[pb], signal=False)
                        S.op('pe', lambda e: e.matmul(ps[:, j * 64:(j + 1) * 64], MkT[:, hx, :], vv[:, h * 64:(h + 1) * 64], start=False, stop=True),
                             reads=[MkTb[hx // 4], vvb], writes=[pb], signal=(j == 7))
                    S.op('act', lambda e: e.copy(out=RHS[:, g8 * 512:(g8 + 1) * 512], in_=ps[:]), reads=[pb], writes=[RHSb], disjoint=(g8 > 0))
                for g8 in range(2):
                    ps, pb = P.next()
                    for j in range(8):
                        hx = 8 * g8 + j
                        S.op('pe', lambda e: e.matmul(ps[:, j * 64:(j + 1) * 64], Qc[:, hx, :], RHS[:, hx * 64:(hx + 1) * 64], start=True, stop=True),
                             reads=[Qcb[hx // 4], RHSb], writes=[pb], signal=(j == 7))
                    S.op('act', lambda e: e.copy(out=U[:, g8 * 512:(g8 + 1) * 512], in_=ps[:]), reads=[pb], writes=[Ub], disjoint=(g8 > 0))
                for g8 in range(2):
                    ps, pb = P.next()
                    for j in range(8):
                        hx = 8 * g8 + j
                        h = H0 + hx
                        S.op('pe', lambda e: e.matmul(ps[:, j * 64:(j + 1) * 64], rT[:, h, :], Abf[:, h, :], start=True, stop=False),
                             reads=[rTb, Abfb], writes=[pb], signal=False)
                        S.op('pe', lambda e: e.matmul(ps[:, j * 64:(j + 1) * 64], RKT[:, hx, :], vv[:, h * 64:(h + 1) * 64], start=False, stop=False),
                             reads=[RKTb[hx // 4], vvb], writes=[pb], signal=False)
                        S.op('pe', lambda e: e.matmul(ps[:, j * 64:(j + 1) * 64], RBT[:, hx, :], U[:, hx * 64:(hx + 1) * 64], start=False, stop=True),
                             reads=[RBTb[hx // 4], Ub], writes=[pb], signal=(j == 7))
                    c0 = (H0 + 8 * g8) * 64
                    S.op('dve', lambda e: e.tensor_copy(out=y[:, c0:c0 + 512], in_=ps[:]), reads=[pb], writes=[yb])
                if SCAN_STAGE < 4:
                    continue
                for g8 in range(2):
                    ps, pb = P.next()
                    for j in range(8):
                        hx = 8 * g8 + j
                        h = H0 + hx
                        S.op('pe', lambda e: e.matmul(ps[0:64, j * 64:(j + 1) * 64], kt[:, h * 64:(h + 1) * 64], vv[:, h * 64:(h + 1) * 64], start=True, stop=False),
                             reads=[ktb, vvb], writes=[pb], signal=False)
                        S.op('pe', lambda e: e.matmul(ps[0:64, j * 64:(j + 1) * 64], bn[:, h * 64:(h + 1) * 64], U[:, hx * 64:(hx + 1) * 64], start=False, stop=True),
                             reads=[bnb, Ub], writes=[pb], signal=(j == 7))
                    h0 = H0 + 8 * g8
                    hp0 = h0 // 2
                    S.op('dve', lambda e: e.tensor_tensor(out=A[:, h0:h0 + 8, :], in0=ps[0:64, :].rearrange("p (a b) -> p a b", a=8), in1=A[:, h0:h0 + 8, :], op=ALU.add),
                         reads=[pb, Ab], writes=[Ab])
                    S.op('pool', lambda e: e.tensor_tensor(
                        out=A[:, h0:h0 + 8, :].rearrange("c (hp e) v -> c hp e v", e=2),
                        in0=A[:, h0:h0 + 8, :].rearrange("c (hp e) v -> c hp e v", e=2),
                        in1=gam[:].rearrange("c e hp -> c hp e")[:, hp0:hp0 + 4, :].unsqueeze(3).broadcast_to([64, 4, 2, 64]), op=ALU.mult),
                        reads=[gamb, Ab], writes=[Ab])
                    S.op('act', lambda e: e.copy(out=Abf[:, h0:h0 + 8, :], in_=A[:, h0:h0 + 8, :]), reads=[Ab], writes=[Abfb])
            prev_out = (y, yb, tmt['bonus'][0], tmt['bonus'][1], tmt['g'][0], tmt['g'][1], t0)
        output_tr(*output_stage(*prev_out))
        S.barrier()


def phase_proj_fm_res(S, CT, xsrc, W, hres, T, kcn=KC):
    nc = S.nc
    with ExitStack() as es:
        X, Xb = load_resident(S, es, 'po_x', xsrc, kcn, T)
        wr = Ring(S, es, 'sb', 'po_w', [128, kcn, 256], BF16, 2)
        hr = Ring(S, es, 'sb', 'po_h', [128, 512], F32, 4)
        pr = Ring(S, es, 'ps', 'po_p', [128, 512], F32, 4)
        wv = wview(W)
        for og in range(D // 256):
            w, wb = wr.next()
            S.dma('pool', lambda e: e.dma_start(out=w[:], in_=wv[:, :, og * 256:(og + 1) * 256]), writes=[wb], owner=wb)
            for ol in range(2):
                dc = og * 2 + ol
                for (t0, tn) in groups(T, 512):
                    h, hb = hr.next()
                    S.dma('sp', lambda e: e.dma_start(out=h[:, 0:tn], in_=hres[dc, :, t0:t0 + tn]), writes=[hb], owner=hb)
                    ps, pb = pr.next()
                    for kc in range(kcn):
                        S.op('pe', lambda e: e.matmul(ps[:, 0:tn], w[:, kc, ol * 128:(ol + 1) * 128], X[:, kc, t0:t0 + tn],
                                                      start=(kc == 0), stop=(kc == kcn - 1)), reads=[wb, Xb], writes=[pb], signal=(kc == kcn - 1))
                    S.op('dve', lambda e: e.tensor_tensor(out=h[:, 0:tn], in0=h[:, 0:tn], in1=ps[:, 0:tn], op=ALU.add), reads=[pb, hb], writes=[hb])
                    S.dma('sp', lambda e: e.dma_start(out=hres[dc, :, t0:t0 + tn], in_=h[:, 0:tn]), reads=[hb], owner=hb)
        S.barrier()


def rwkv_layer(S, CT, I, layer, Z, hT, T):
    import os
    nph = int(os.environ.get('RWKV_NPH', '99'))
    xm = Z['xmix']
    Zl = dict(Z)
    if layer == 0:
        Zl['v'] = Z['vfirst']
    steps = [
        lambda: phase_rwkv_mix(S, CT, hT, I['cols'], layer, xm, T),
        lambda: phase_proj_tm(S, CT, xm[0], I['rwkv_w_r'][layer], Z['r'], T, BF16),
        lambda: phase_proj_tm(S, CT, xm[2], I['rwkv_w_k'][layer], Z['k'], T, BF16),
        lambda: phase_proj_tm(S, CT, xm[3], I['rwkv_w_v'][layer], Z['v'] if layer == 1 else Z['vfirst'], T, BF16),
        lambda: phase_lora(S, CT, xm[1], I['rwkv_dec_w1'][layer], I['rwkv_dec_w2'][layer], I['rwkv_dec_w0'][layer], 96, AF.Tanh, AF.Sigmoid, Z['sigd'], T, F32),
        lambda: phase_lora(S, CT, xm[4], I['rwkv_a_w1'][layer], I['rwkv_a_w2'][layer], I['rwkv_a_w0'][layer], 96, AF.Copy, AF.Sigmoid, Z['a'], T, BF16),
        lambda: phase_lora(S, CT, xm[5], I['rwkv_g_w1'][layer], I['rwkv_g_w2'][layer], None, 256, AF.Sigmoid, AF.Copy, Z['g'], T, BF16),
    ]
    if layer == 1:
        steps.append(lambda: phase_lora(S, CT, xm[3], I['rwkv_v_w1'][0], I['rwkv_v_w2'][0], I['rwkv_v_w0'][0], 64, AF.Copy, AF.Sigmoid, Z['vg'], T, BF16))
    steps += [
        lambda: phase_rwkv_prep(S, CT, I, layer, Zl, T),
        lambda: phase_rwkv_scan(S, CT, I, layer, Zl, T),
        lambda: phase_proj_fm_res(S, CT, Z['zT'], I['rwkv_w_o'][layer], hT, T),
    ]
    if not RUN_SCAN:
        steps = steps[:-2]
    for i, st_ in enumerate(steps):
        if i < nph:
            st_()


SB_SCALE = 128 ** -0.5


def phase_gather_q(S, CT, hT, hqT, NQB):
    nc = S.nc
    with ExitStack() as es:
        r = Ring(S, es, 'sb', 'gq_t', [128, KC, 128], F32, 3)
        pid = nc.sync.partition_id()
        off = (pid % 2) * 128 + NMETA
        hv = fm(hT)
        qv = fm(hqT)
        for i in range(NQB):
            t, b = r.next()
            S.dma('sp', lambda e: e.dma_start(out=t[:], in_=hv[:, :, bass.ds(off + 256 * i, 128)]), writes=[b], owner=b)
            S.dma('sp', lambda e: e.dma_start(out=qv[:, :, i * 128:(i + 1) * 128], in_=t[:]), reads=[b], owner=b)
        S.barrier()


def phase_headnorm_fm(S, CT, xsrc, W, gain1d, out, Tn):
    nc = S.nc
    with ExitStack() as es:
        cb = S.buf('c')
        X, Xb = load_resident(S, es, 'hn_x', xsrc, KC, Tn)
        gcol = es.enter_context(nc.sbuf_tensor(un('hn_g'), [128, 1], F32))
        gb = S.buf('g')
        S.dma('sp', lambda e: e.dma_start(out=gcol[:], in_=gain1d.rearrange("(p o) -> p o", o=1)), writes=[gb], owner=gb)
        wr = Ring(S, es, 'sb', 'hn_w', [128, KC, 128], BF16, 2)
        sr = Ring(S, es, 'sb', 'hn_sq', [128, 512], F32, 2)
        rr = Ring(S, es, 'sb', 'hn_rs', [128, 512], F32, 2)
        orr = Ring(S, es, 'sb', 'hn_o', [128, Tn], BF16, 2)
        pr = Ring(S, es, 'ps', 'hn_p', [128, 512], F32, 3)
        pr2 = Ring(S, es, 'ps', 'hn_p2', [128, 512], F32, 2)
        wv = wview(W)
        for h in range(SH):
            w, wb = wr.next()
            S.dma('pool', lambda e: e.dma_start(out=w[:], in_=wv[:, :, h * 128:(h + 1) * 128]), writes=[wb], owner=wb)
            o, ob = orr.next()
            for (t0, tn) in groups(Tn, 512):
                ps, pb = pr.next()
                for kc in range(KC):
                    S.op('pe', lambda e: e.matmul(ps[:, 0:tn], w[:, kc, :], X[:, kc, t0:t0 + tn], start=(kc == 0), stop=(kc == KC - 1)),
                         reads=[wb, Xb], writes=[pb], signal=(kc == KC - 1))
                sq, sqb = sr.next()
                S.op('act', lambda e: e.activation(out=sq[:, 0:tn], in_=ps[:, 0:tn], func=AF.Square), reads=[pb], writes=[sqb])
                p2, p2b = pr2.next()
                S.op('pe', lambda e: e.matmul(p2[:, 0:tn], CT['ones'][:], sq[:, 0:tn], start=True, stop=True), reads=[sqb, cb], writes=[p2b])
                rs, rb = rr.next()
                S.op('act', lambda e: e.activation(out=rs[:, 0:tn], in_=p2[:, 0:tn], func=AF.Ln, bias=CT['eps'][:, 0:1], scale=1.0 / 128),
                     reads=[p2b, cb], writes=[rb])
                S.op('act', lambda e: e.activation(out=rs[:, 0:tn], in_=rs[:, 0:tn], func=AF.Exp, scale=-0.5), reads=[rb], writes=[rb])
                S.op('dve', lambda e: e.scalar_tensor_tensor(out=o[:, t0:t0 + tn], in0=ps[:, 0:tn], scalar=gcol[:, 0:1], in1=rs[:, 0:tn],
                                                             op0=ALU.mult, op1=ALU.mult), reads=[pb, rb, gb], writes=[ob], disjoint=(t0 > 0))
            S.dma('sp', lambda e: e.dma_start(out=out[h, :, 0:Tn], in_=o[:]), reads=[ob], owner=ob)
        S.barrier()


def phase_attention(S, CT, I, KT, Vtm, QT, OT, NXB):
    nc = S.nc
    NQB = NXB // 2
    NG = NQB // 4
    TQ = NQB * 128
    Tk = NMETA + 128 * NXB
    with ExitStack() as es:
        am = es.enter_context(nc.sbuf_tensor(un('at_am'), [128, 8, 512], F32))
        amb_ = es.enter_context(nc.sbuf_tensor(un('at_amb'), [128, 8, 512], BF16))
        tm = es.enter_context(nc.sbuf_tensor(un('at_tm'), [128, 2, 128], F32))
        mb = S.buf('am')
        S.dma('sp', lambda e: e.dma_start(out=am[:], in_=I['amask'].rearrange("j p q -> p j q")), writes=[mb], owner=mb)
        S.dma('pool', lambda e: e.dma_start(out=amb_[:], in_=I['amask'].rearrange("j p q -> p j q")), writes=[mb], owner=mb)
        S.dma('sp', lambda e: e.dma_start(out=tm[:], in_=I['tmask'].rearrange("j p q -> p j q")), writes=[mb], owner=mb)
        kr = Ring(S, es, 'sb', 'at_k', [128, Tk], BF16, 2)
        vr = Ring(S, es, 'sb', 'at_v', [128, NXB, 128], BF16, 2)
        vmr = Ring(S, es, 'sb', 'at_vm', [NMETA, 128], BF16, 2)
        qr = Ring(S, es, 'sb', 'at_q', [128, TQ], BF16, 2)
        outr = Ring(S, es, 'sb', 'at_o', [128, TQ], BF16, 2)
        Er = Ring(S, es, 'sb', 'at_e', [128, 512], F32, 2)
        SPr = Ring(S, es, 'sb', 'at_sp', [128, 512], F32, 4)
        T1r = Ring(S, es, 'sb', 'at_t1', [128, 512], F32, 2)
        Wr = Ring(S, es, 'sb', 'at_w', [128, 512], BF16, 3)
        Rr = Ring(S, es, 'sb', 'at_r', [128, 512], F32, 4)
        PZ = Ring(S, es, 'ps', 'at_pz', [128, 512], F32, 3)
        PL = Ring(S, es, 'ps', 'at_pl', [128, 512], F32, 2)
        PO = Ring(S, es, 'ps', 'at_po', [128, 512], F32, 2)
        tiles = []
        for h in range(SH):
            for g in range(NG):
                blocks = list(range(8 * g + 7, -1, -1)) + [-1]
                for bi, kb in enumerate(blocks):
                    tiles.append(dict(h=h, g=g, kb=kb, first=(bi == 0), last=(kb < 0), hfirst=(g == 0 and bi == 0), hlast=(g == NG - 1 and kb < 0)))
        hd = {}
        gd = {}

        def stA(t):
            h, g, kb = t['h'], t['g'], t['kb']
            if t['hfirst']:
                k, kb_ = kr.next()
                S.dma('sp', lambda e: e.dma_start(out=k[:], in_=KT[h, :, 0:Tk]), writes=[kb_], owner=kb_)
                v, vb = vr.next()
                S.dma('sp', lambda e: e.dma_start(out=v[:], in_=Vtm[NMETA:NMETA + 128 * NXB, h * 128:(h + 1) * 128].rearrange("(kb p) d -> p kb d", p=128)),
                      writes=[vb], owner=vb)
                vm, vmb = vmr.next()
                S.dma('sp', lambda e: e.dma_start(out=vm[:], in_=Vtm[0:NMETA, h * 128:(h + 1) * 128]), writes=[vmb], owner=vmb)
                q, qb_ = qr.next()
                S.dma('sp', lambda e: e.dma_start(out=q[:], in_=QT[h, :, 0:TQ]), writes=[qb_], owner=qb_)
                o, ob = outr.next()
                hd[h] = (k, kb_, v, vb, vm, vmb, q, qb_, o, ob)
            k, kb_, v, vb, vm, vmb, q, qb_, o, ob = hd[h]
            if t['first']:
                gd[(h, g)] = dict(po=PO.next(), R=None)
            meta = t['last']
            nk = NMETA if meta else 128
            kcols = slice(0, NMETA) if meta else slice(NMETA + 128 * kb, NMETA + 128 * (kb + 1))
            qs = slice(g * 512, (g + 1) * 512)
            masked = (not meta) and kb >= 8 * g
            j = kb - 8 * g
            pz, pzb = PZ.next()
            S.op('pe', lambda e: e.matmul(pz[0:nk, :], k[:, kcols], q[:, qs], start=True, stop=True), reads=[kb_, qb_], writes=[pzb])
            E, Eb = Er.next()
            S.op('act', lambda e: e.activation(out=E[0:nk, :], in_=pz[0:nk, :], func=AF.Exp, scale=SB_SCALE), reads=[pzb], writes=[Eb])
            sp, spb = SPr.next()
            S.op('act', lambda e: e.activation(out=sp[0:nk, :], in_=E[0:nk, :], func=AF.Ln, bias=1.0, scale=1.0), reads=[Eb], writes=[spb])
            if masked:
                S.op('pool', lambda e: e.tensor_tensor(out=sp[:, :], in0=sp[:, :], in1=am[:, j, :], op=ALU.mult), reads=[mb, spb], writes=[spb])
            t.update(nk=nk, pz=pz, pzb=pzb, sp=sp, spb=spb, masked=masked, j=j, meta=meta)

        def stB(t):
            h, g = t['h'], t['g']
            G = gd[(h, g)]
            nk, pz, pzb, sp, spb, first = t['nk'], t['pz'], t['pzb'], t['sp'], t['spb'], t['first']
            pl, plb = PL.next()
            S.op('pe', lambda e: e.matmul(pl[0:nk, :], tm[0:nk, 0, 0:nk], sp[0:nk, :], start=True, stop=first), reads=[mb, spb], writes=[plb], signal=first)
            if not first:
                R, Rb = G['R']
                S.op('pe', lambda e: e.matmul(pl[0:nk, :], tm[:, 1, 0:nk], R[:, :], start=False, stop=True), reads=[mb, Rb], writes=[plb])
            if not t['meta']:
                Rn, Rnb = Rr.next()
                if first:
                    S.op('pool', lambda e: e.tensor_copy(out=Rn[:, :], in_=sp[:, :]), reads=[spb], writes=[Rnb])
                else:
                    R, Rb = G['R']
                    S.op('pool', lambda e: e.tensor_tensor(out=Rn[:, :], in0=R[:, :], in1=sp[:, :], op=ALU.add), reads=[spb, Rb], writes=[Rnb])
                G['R'] = (Rn, Rnb)
            t1, t1b = T1r.next()
            S.op('dve', lambda e: e.scalar_tensor_tensor(out=t1[0:nk, :], in0=pz[0:nk, :], scalar=SB_SCALE, in1=sp[0:nk, :],
                                                         op0=ALU.mult, op1=ALU.subtract), reads=[pzb, spb], writes=[t1b])
            S.op('dve', lambda e: e.tensor_tensor(out=t1[0:nk, :], in0=t1[0:nk, :], in1=pl[0:nk, :], op=ALU.add), reads=[plb, t1b], writes=[t1b])
            w, wb = Wr.next()
            S.op('act', lambda e: e.activation(out=w[0:nk, :], in_=t1[0:nk, :], func=AF.Exp), reads=[t1b], writes=[wb])
            if t['masked']:
                j = t['j']
                S.op('pool', lambda e: e.tensor_tensor(out=w[:, :], in0=w[:, :], in1=amb_[:, j, :], op=ALU.mult), reads=[mb, wb], writes=[wb])
            t.update(w=w, wb=wb)

        def stC(t):
            h, g, kb = t['h'], t['g'], t['kb']
            k, kb_, v, vb, vm, vmb, q, qb_, o, ob = hd[h]
            po, pob = gd[(h, g)]['po']
            w, wb, nk, first = t['w'], t['wb'], t['nk'], t['first']
            if t['meta']:
                S.op('pe', lambda e: e.matmul(po[:, :], vm[:, :], w[0:nk, :], start=first, stop=True), reads=[vmb, wb], writes=[pob])
                qs = slice(g * 512, (g + 1) * 512)
                S.op('act', lambda e: e.copy(out=o[:, qs], in_=po[:, :]), reads=[pob], writes=[ob])
                if t['hlast']:
                    S.dma('sp', lambda e: e.dma_start(out=OT[h, :, 0:TQ], in_=o[:]), reads=[ob], owner=ob)
            else:
                S.op('pe', lambda e: e.matmul(po[:, :], v[:, kb, :], w[:, :], start=first, stop=False), reads=[vb, wb], writes=[pob], signal=False)

        nt = len(tiles)
        for step in range(nt + 2):
            if step < nt:
                stA(tiles[step])
            if 0 <= step - 1 < nt:
                stB(tiles[step - 1])
            if 0 <= step - 2 < nt:
                stC(tiles[step - 2])
        S.barrier()


def att_masks(parity):
    am = np.zeros((8, 128, 512), np.float32)
    p = np.arange(128)
    for j in range(8):
        for i in range(4):
            qb = 2 * i + parity
            if j < qb:
                am[j, :, i * 128:(i + 1) * 128] = 1.0
            elif j == qb:
                am[j, :, i * 128:(i + 1) * 128] = (p[:, None] < p[None, :])
    tmk = np.zeros((2, 128, 128), np.float32)
    tmk[0] = -(p[:, None] > p[None, :]).astype(np.float32)
    tmk[1] = -1.0
    return am, tmk


def const_masks():
    cm = np.zeros((4, 128, 128), np.float32)
    i = np.arange(128)
    cm[0] = np.where(i[:, None] <= i[None, :], -DEC_C, 0.0)
    cm[1] = (i[:, None] < i[None, :])
    cm[2] = (i[:, None] <= i[None, :])
    cm[3] = (i[:, None] > i[None, :])
    return cm


IN_SPECS = [
    ('cols', [128, NCOLS, KC]), ('ident', [128, 128]), ('cmask', [4, 128, 128]),
    ('ffn_w_gate', [4, D, FF]), ('ffn_w_up', [4, D, FF]), ('ffn_w_down', [4, FF, D]),
    ('rwkv_w_r', [2, D, D]), ('rwkv_w_k', [2, D, D]), ('rwkv_w_v', [2, D, D]), ('rwkv_w_o', [2, D, D]),
    ('rwkv_dec_w0', [2, D]), ('rwkv_dec_w1', [2, D, 96]), ('rwkv_dec_w2', [2, 96, D]),
    ('rwkv_a_w0', [2, D]), ('rwkv_a_w1', [2, D, 96]), ('rwkv_a_w2', [2, 96, D]),
    ('rwkv_g_w1', [2, D, 256]), ('rwkv_g_w2', [2, 256, D]),
    ('rwkv_k_k', [2, D]), ('rwkv_k_a', [2, D]), ('rwkv_r_k', [2, 32, 64]), ('rwkv_gn_w', [2, D]), ('rwkv_gn_b', [2, D]),
    ('rwkv_v_w0', [1, D]), ('rwkv_v_w1', [1, D, 64]), ('rwkv_v_w2', [1, 64, D]),
    ('amask', [8, 128, 512]), ('tmask', [2, 128, 128]),
    ('sb_w_k', [D, D]), ('sb_w_v', [D, D]), ('sb_k_gain', [128]), ('sb_w_q', [2, D, D]), ('sb_q_gain', [2, 128]), ('sb_w_o', [2, D, D]),
]


def build(NXB, mode='full', dbg=()):
    T = 128 * (NXB + 1)
    nc = bass.Bass("TRN2", target_bir_lowering=False)
    I = {}
    I['xin'] = nc.dram_tensor('xin', [T, D], F32, kind="ExternalInput").ap()
    for name, shape in IN_SPECS:
        I[name] = nc.dram_tensor(name, list(shape), F32, kind="ExternalInput").ap()

    def scratch(name, shape, dt):
        kind = "ExternalOutput" if name in dbg else "Internal"
        return nc.dram_tensor(name, list(shape), dt, kind=kind).ap()

    hT = scratch('hT', [KC, 128, T], F32)
    xn = scratch('xn', [KC, 128, T], BF16)
    actT = scratch('actT', [FC, 128, T], BF16)
    Z = {'xmix': [scratch('xmix%d' % i, [KC, 128, T], BF16) for i in range(6)]}
    for n in ('r', 'k', 'v', 'vfirst', 'a', 'g', 'vg', 'vv', 'bonus', 'kt', 'bn'):
        Z[n] = scratch('z_' + n, [T, D], BF16)
    Z['sigd'] = scratch('z_sigd', [T, D], F32)
    Z['gam'] = scratch('z_gam', [T // 128, 128, KC], F32)
    Z['zT'] = scratch('z_zT', [KC, 128, T], BF16)
    for n in ('rT', 'kpT', 'ktT', 'bnT'):
        Z[n] = scratch('z_' + n, [T // 128, 64, RH, 128], BF16)
    NQB = NXB // 2
    TQ = NQB * 128
    hqT = scratch('hqT', [KC, 128, TQ], F32)
    KT = scratch('KT', [SH, 128, T], BF16)
    Vtm = scratch('Vtm', [T, D], BF16)
    QT = scratch('QT', [SH, 128, TQ], BF16)
    OT = scratch('OT', [SH, 128, TQ], BF16)
    full = mode in ('full', 'att_test')
    out = nc.dram_tensor('out', [TQ if full else T, D], F32, kind="ExternalOutput").ap()

    def ffn(S, CT, layer, hres, Tn):
        phase_norm(S, CT, hres, I['cols'], COLS[('ffn_norm_g', layer)], xn, Tn)
        phase_ffn_gateup(S, CT, xn, I['ffn_w_gate'][layer], I['ffn_w_up'][layer], actT, Tn)
        phase_ffn_down(S, CT, actT, I['ffn_w_down'][layer], hres, Tn)

    with ExitStack() as es:
        S = Sched(nc, es)
        CT = load_consts(S, es, I)
        phase_in_transpose(S, CT, I['xin'], hT, T)
        if mode == 'ffn_test':
            ffn(S, CT, 0, hT, T)
        if mode == 'rwkv_test':
            rwkv_layer(S, CT, I, 0, Z, hT, T)
        if mode == 'full':
            rwkv_layer(S, CT, I, 0, Z, hT, T)
            ffn(S, CT, 0, hT, T)
            rwkv_layer(S, CT, I, 1, Z, hT, T)
            ffn(S, CT, 1, hT, T)
        if mode == 'rwkv2_test':
            rwkv_layer(S, CT, I, 0, Z, hT, T)
            ffn(S, CT, 0, hT, T)
            rwkv_layer(S, CT, I, 1, Z, hT, T)
        if full:
            phase_norm(S, CT, hT, I['cols'], COLS[('kv_norm_g', 0)], xn, T)
            phase_headnorm_fm(S, CT, xn, I['sb_w_k'], I['sb_k_gain'], KT, T)
            phase_proj_tm(S, CT, xn, I['sb_w_v'], Vtm, T, BF16)
            phase_gather_q(S, CT, hT, hqT, NQB)
            for j in range(2):
                phase_norm(S, CT, hqT, I['cols'], COLS[('mix_norm_g', 2 + j)], xn, TQ)
                phase_headnorm_fm(S, CT, xn, I['sb_w_q'][j], I['sb_q_gain'][j], QT, TQ)
                phase_attention(S, CT, I, KT, Vtm, QT, OT, NXB)
                phase_proj_fm_res(S, CT, OT, I['sb_w_o'][j], hqT, TQ)
                ffn(S, CT, 2 + j, hqT, TQ)
            phase_out_transpose(S, CT, hqT, out, TQ, 0)
        else:
            phase_out_transpose(S, CT, hT, out, T, 0)
        print("instructions:", S.ninst)
    return nc


def host_inputs(inputs):
    d = {k: np.ascontiguousarray(np.asarray(v), dtype=np.float32) for k, v in inputs.items()}
    base = {'cols': pack_cols(d), 'ident': np.eye(128, dtype=np.float32), 'cmask': const_masks(), 'tmask': att_masks(0)[1], 'amask': att_masks(0)[0]}
    for name, shape in IN_SPECS:
        if name not in base:
            base[name] = d[name].reshape(shape)
    return base


def kernel(**inputs):
    NXB = 32
    T = 128 * (NXB + 1)
    base = host_inputs(inputs)
    x = np.asarray(inputs['x'], np.float32)
    meta = np.asarray(inputs['meta_tokens'], np.float32)
    in_maps = []
    for c in range(8):
        b = c // 2
        xin = np.zeros((T, D), np.float32)
        xin[:NMETA] = meta
        xin[NMETA:NMETA + 4096] = x[b]
        m = dict(base)
        m['xin'] = xin
        m['amask'] = att_masks(c % 2)[0]
        in_maps.append(m)
    nc = build(NXB, mode='full')
    res = run_bass_kernel_spmd(nc, in_maps, core_ids=list(range(8)))
    out = np.zeros((4, 4096, D), np.float32)
    for c in range(8):
        o = np.asarray(res.results[c]['out']).reshape(NXB // 2, 128, D)
        out[c // 2].reshape(NXB // 2, 2, 128, D)[:, c % 2] = o
    return out
```

```python
import numpy as np
import ml_dtypes
from contextlib import ExitStack
import concourse.bass as bass
import concourse.mybir as mybir
from concourse.bass_utils import run_bass_kernel_spmd

F32 = mybir.dt.float32
BF16 = mybir.dt.bfloat16
AF = mybir.ActivationFunctionType
ALU = mybir.AluOpType
AX = mybir.AxisListType

D = 2048
KC = 16
FF = 5632
FC = 44
NMETA = 16
RH = 32
SH = 16
RMS_EPS = 1e-6
GN_EPS = 64e-5
ENG = ('pe', 'act', 'dve', 'pool', 'sp')
NDS = 48


_UID = [0]


def un(name):
    _UID[0] += 1
    return '%s_%d' % (name, _UID[0])


class DSem:
    def __init__(self, h):
        self.h = h
        self.count = 0


class Buf:
    __slots__ = ('w', 'r', 'ds', 'name')

    def __init__(self, name=''):
        self.w = None
        self.r = {}
        self.ds = None
        self.name = name


class Sched:
    def __init__(self, nc, es):
        self.nc = nc
        self.eng = {'pe': nc.tensor, 'act': nc.scalar, 'dve': nc.vector, 'pool': nc.gpsimd, 'sp': nc.sync}
        self.sem = {e: es.enter_context(nc.semaphore('s_' + e)) for e in ENG}
        self.cnt = {e: 0 for e in ENG}
        self.seen = {e: {} for e in ENG}
        self.free_ds = [DSem(es.enter_context(nc.semaphore('d%d' % i))) for i in range(NDS)]
        self.used_ds = []
        self.bufs = []
        self.ninst = 0

    def buf(self, name=''):
        b = Buf(name)
        self.bufs.append(b)
        return b

    def _waits(self, e, evs):
        need = {}
        for key, val in evs:
            if key == 'pe' and e == 'pe':
                continue
            if self.seen[e].get(key, 0) >= val:
                continue
            if need.get(key, 0) < val:
                need[key] = val
        for key, val in need.items():
            self.seen[e][key] = val
            h = self.sem[key] if isinstance(key, str) else key.h
            self.eng[e].wait_ge(h, val)
            self.ninst += 1

    @staticmethod
    def _deps(reads, writes):
        evs = []
        for b in reads:
            if b.w is not None:
                evs.append(b.w)
        for b in writes:
            if b.w is not None:
                evs.append(b.w)
            evs.extend(b.r.items())
        return evs

    @staticmethod
    def _mark(ev, reads, writes):
        k, v = ev
        for b in reads:
            if b.r.get(k, 0) < v:
                b.r[k] = v
        for b in writes:
            b.w = ev
            b.r = {}

    def op(self, e, fn, reads=(), writes=(), signal=True, disjoint=False):
        deps = self._deps(reads, writes)
        if disjoint:
            own = [b.w for b in writes if b.w is not None and b.w[0] == e]
            deps = [d for d in deps if d not in own]
        self._waits(e, deps)
        ins = fn(self.eng[e])
        self.ninst += 1
        if signal:
            self.cnt[e] += 1
            ins.then_inc(self.sem[e], 1)
            ev = (e, self.cnt[e])
        else:
            ev = (e, self.cnt[e] + 1)
        self._mark(ev, reads, writes)
        return ins

    def dma(self, q, fn, reads=(), writes=(), owner=None):
        self._waits(q, self._deps(reads, writes))
        if owner.ds is None:
            owner.ds = self.free_ds.pop()
            self.used_ds.append(owner.ds)
        ds = owner.ds
        ins = fn(self.eng[q])
        self.ninst += 1
        ds.count += 16
        ins.then_inc(ds.h, 16)
        self._mark((ds, ds.count), reads, writes)
        return ins

    def barrier(self):
        evs = [(e, self.cnt[e]) for e in ENG if self.cnt[e] > 0]
        evs += [(ds, ds.count) for ds in self.used_ds]
        for e in ENG:
            self._waits(e, evs)
        for b in self.bufs:
            b.w = None
            b.r = {}
            b.ds = None
        self.free_ds.extend(self.used_ds)
        self.used_ds = []
        self.bufs = []


class Ring:
    def __init__(self, S, es, kind, name, shape, dtype, n):
        nc = S.nc
        self.t = []
        self.b = []
        for i in range(n):
            if kind == 'sb':
                t = es.enter_context(nc.sbuf_tensor(un('%s%d' % (name, i)), shape, dtype))
            else:
                t = es.enter_context(nc.psum_tensor(un('%s%d' % (name, i)), shape, dtype))
            self.t.append(t)
            self.b.append(S.buf(name))
        self.i = 0
        self.n = n

    def next(self):
        i = self.i % self.n
        self.i += 1
        return self.t[i], self.b[i]


def tile1(S, es, name, shape, dtype, kind='sb'):
    r = Ring(S, es, kind, name, shape, dtype, 1)
    return r.t[0], r.b[0]


def groups(T, g):
    out = []
    t0 = 0
    while t0 < T:
        out.append((t0, min(g, T - t0)))
        t0 += g
    return out


def fm(d):
    return d.rearrange("kc p t -> p kc t")


def colvec(w1d, n):
    return w1d.rearrange("(kc p) -> p kc", p=128)


def load_consts(S, es, C):
    nc = S.nc
    t = {}
    t['ident'] = es.enter_context(nc.sbuf_tensor(un('c_ident'), [128, 128], F32))
    t['identb'] = es.enter_context(nc.sbuf_tensor(un('c_identb'), [128, 128], BF16))
    t['ones'] = es.enter_context(nc.sbuf_tensor(un('c_ones'), [128, 128], F32))
    t['eps'] = es.enter_context(nc.sbuf_tensor(un('c_eps'), [128, 2], F32))
    b = S.buf('const')
    S.dma('sp', lambda e: e.dma_start(out=t['ident'][:], in_=C['ident']), writes=[b], owner=b)
    S.dma('pool', lambda e: e.dma_start(out=t['identb'][:], in_=C['ident']), writes=[b], owner=b)
    S.op('dve', lambda e: e.memset(t['ones'][:], 1.0), writes=[b])
    S.op('dve', lambda e: e.memset(t['eps'][:, 0:1], RMS_EPS), writes=[b])
    S.op('dve', lambda e: e.memset(t['eps'][:, 1:2], GN_EPS), writes=[b])
    S.barrier()
    return t


def phase_in_transpose(S, CT, xin, hT, T):
    nc = S.nc
    with ExitStack() as es:
        cb = S.buf('c')
        xr = Ring(S, es, 'sb', 'it_x', [128, D], F32, 3)
        hr = Ring(S, es, 'sb', 'it_h', [128, KC, 512], F32, 2)
        pr = Ring(S, es, 'ps', 'it_p', [128, 512], F32, 4)
        hv = fm(hT)
        for (t0, tn) in groups(T, 512):
            hb, hbb = hr.next()
            for j in range(tn // 128):
                xt, xb = xr.next()
                r0 = t0 + j * 128
                S.dma('sp', lambda e, xt=xt, r0=r0: e.dma_start(out=xt[:], in_=xin[r0:r0 + 128, :]),
                      writes=[xb], owner=xb)
                for q in range(4):
                    ps, pb = pr.next()
                    for i in range(4):
                        kc = 4 * q + i
                        S.op('pe', lambda e, ps=ps, xt=xt, kc=kc, i=i: e.transpose(
                            ps[:, i * 128:(i + 1) * 128], xt[:, kc * 128:(kc + 1) * 128], CT['ident'][:]),
                            reads=[xb, cb], writes=[pb], signal=(i == 3))
                    eng = 'dve' if q % 2 == 0 else 'act'
                    if eng == 'dve':
                        S.op('dve', lambda e, ps=ps, hb=hb, q=q, j=j: e.tensor_copy(
                            out=hb[:, 4 * q:4 * q + 4, j * 128:(j + 1) * 128],
                            in_=ps[:].rearrange("p (a b) -> p a b", a=4)), reads=[pb], writes=[hbb])
                    else:
                        S.op('act', lambda e, ps=ps, hb=hb, q=q, j=j: e.copy(
                            out=hb[:, 4 * q:4 * q + 4, j * 128:(j + 1) * 128],
                            in_=ps[:].rearrange("p (a b) -> p a b", a=4)), reads=[pb], writes=[hbb])
            S.dma('sp', lambda e, hb=hb, t0=t0, tn=tn: e.dma_start(out=hv[:, :, t0:t0 + tn], in_=hb[:, :, 0:tn]),
                  reads=[hbb], owner=hbb)
        S.barrier()


COLS = {}
_ci = 0
for _l in range(4):
    COLS[('ffn_norm_g', _l)] = _ci; _ci += 1
for _l in range(4):
    COLS[('mix_norm_g', _l)] = _ci; _ci += 1
COLS[('kv_norm_g', 0)] = _ci; _ci += 1
for _l in range(2):
    for _i in range(6):
        COLS[('rwkv_mu', _l, _i)] = _ci; _ci += 1
NCOLS = _ci


def pack_cols(inputs):
    out = np.zeros((128, NCOLS, KC), np.float32)
    for key, i in COLS.items():
        a = inputs[key[0]]
        v = a[key[1]] if key[0] != 'kv_norm_g' else a
        if key[0] == 'rwkv_mu':
            v = v[key[2]]
        out[:, i, :] = np.asarray(v).reshape(KC, 128).T
    return out


def load_cols(S, es, name, cols, idxs):
    nc = S.nc
    n = len(idxs)
    t = es.enter_context(nc.sbuf_tensor(un(name), [128, n, KC], F32))
    b = S.buf(name)
    for i, ix in enumerate(idxs):
        S.dma('sp', lambda e, i=i, ix=ix: e.dma_start(out=t[:, i, :], in_=cols[:, ix, :]), writes=[b], owner=b)
    return t, b


def phase_norm(S, CT, hsrc, cols, gidx, xout, T):
    nc = S.nc
    with ExitStack() as es:
        cb = S.buf('c')
        g, gb = load_cols(S, es, 'n_g', cols, [gidx])
        hr = Ring(S, es, 'sb', 'n_h', [128, KC, 512], F32, 2)
        sr = Ring(S, es, 'sb', 'n_sq', [128, 512], F32, 3)
        rr = Ring(S, es, 'sb', 'n_rs', [128, 512], F32, 2)
        orr = Ring(S, es, 'sb', 'n_o', [128, KC, 512], BF16, 2)
        pr = Ring(S, es, 'ps', 'n_p', [128, 512], F32, 2)
        hv = fm(hsrc)
        ov = fm(xout)
        for (t0, tn) in groups(T, 512):
            h, hb = hr.next()
            S.dma('sp', lambda e, h=h, t0=t0, tn=tn: e.dma_start(out=h[:, :, 0:tn], in_=hv[:, :, t0:t0 + tn]),
                  writes=[hb], owner=hb)
            ps, pb = pr.next()
            for kc in range(KC):
                sq, sb_ = sr.next()
                S.op('act', lambda e, sq=sq, h=h, kc=kc, tn=tn: e.activation(
                    out=sq[:, 0:tn], in_=h[:, kc, 0:tn], func=AF.Square), reads=[hb], writes=[sb_])
                S.op('pe', lambda e, ps=ps, sq=sq, kc=kc, tn=tn: e.matmul(
                    ps[:, 0:tn], CT['ones'][:], sq[:, 0:tn], start=(kc == 0), stop=(kc == KC - 1)),
                    reads=[sb_, cb], writes=[pb], signal=True)
            rs, rb = rr.next()
            S.op('act', lambda e, rs=rs, ps=ps, tn=tn: e.activation(
                out=rs[:, 0:tn], in_=ps[:, 0:tn], func=AF.Ln, bias=CT['eps'][:, 0:1], scale=1.0 / D),
                reads=[pb, cb], writes=[rb])
            S.op('act', lambda e, rs=rs, tn=tn: e.activation(
                out=rs[:, 0:tn], in_=rs[:, 0:tn], func=AF.Exp, scale=-0.5), reads=[rb], writes=[rb])
            o, ob = orr.next()
            for kc in range(KC):
                S.op('dve', lambda e, o=o, h=h, rs=rs, kc=kc, tn=tn: e.scalar_tensor_tensor(
                    out=o[:, kc, 0:tn], in0=h[:, kc, 0:tn], scalar=g[:, 0, kc:kc + 1], in1=rs[:, 0:tn],
                    op0=ALU.mult, op1=ALU.mult), reads=[hb, rb, gb], writes=[ob], disjoint=(kc > 0))
            S.dma('sp', lambda e, o=o, t0=t0, tn=tn: e.dma_start(out=ov[:, :, t0:t0 + tn], in_=o[:, :, 0:tn]),
                  reads=[ob], owner=ob)
        S.barrier()


def load_resident(S, es, name, src, kcn, T):
    nc = S.nc
    X = es.enter_context(nc.sbuf_tensor(un(name), [128, kcn, T], BF16))
    Xb = S.buf(name)
    sv = fm(src)
    for k0 in range(0, kcn, 4):
        k1 = min(kcn, k0 + 4)
        S.dma('sp', lambda e, k0=k0, k1=k1: e.dma_start(out=X[:, k0:k1, :], in_=sv[:, k0:k1, 0:T]),
              writes=[Xb], owner=Xb)
    return X, Xb


def wview(W):
    return W.rearrange("(kc p) f -> p kc f", p=128)


def phase_ffn_gateup(S, CT, xn, Wg, Wu, actT, T):
    nc = S.nc
    with ExitStack() as es:
        X, Xb = load_resident(S, es, 'fg_x', xn, KC, T)
        wr = Ring(S, es, 'sb', 'fg_w', [128, 2, KC, 256], BF16, 2)
        ar = Ring(S, es, 'sb', 'fg_a', [128, T], BF16, 2)
        tr = Ring(S, es, 'sb', 'fg_t', [128, 512], F32, 3)
        pg = Ring(S, es, 'ps', 'fg_pg', [128, 512], F32, 3)
        pu = Ring(S, es, 'ps', 'fg_pu', [128, 512], F32, 3)
        wgv = wview(Wg)
        wuv = wview(Wu)
        for og in range(FC // 2):
            w, wb = wr.next()
            c0 = og * 256
            for k0 in range(0, KC, 8):
                S.dma('pool', lambda e, w=w, c0=c0, k0=k0: e.dma_start(out=w[:, 0, k0:k0 + 8, :], in_=wgv[:, k0:k0 + 8, c0:c0 + 256]),
                      writes=[wb], owner=wb)
                S.dma('pool', lambda e, w=w, c0=c0, k0=k0: e.dma_start(out=w[:, 1, k0:k0 + 8, :], in_=wuv[:, k0:k0 + 8, c0:c0 + 256]),
                      writes=[wb], owner=wb)
            for ol in range(2):
                oc = og * 2 + ol
                a, ab = ar.next()
                for (t0, tn) in groups(T, 512):
                    p1, p1b = pg.next()
                    p2, p2b = pu.next()
                    for kc in range(KC):
                        S.op('pe', lambda e, p1=p1, w=w, ol=ol, kc=kc, t0=t0, tn=tn: e.matmul(
                            p1[:, 0:tn], w[:, 0, kc, ol * 128:(ol + 1) * 128], X[:, kc, t0:t0 + tn],
                            start=(kc == 0), stop=(kc == KC - 1)), reads=[wb, Xb], writes=[p1b], signal=(kc == KC - 1))
                    for kc in range(KC):
                        S.op('pe', lambda e, p2=p2, w=w, ol=ol, kc=kc, t0=t0, tn=tn: e.matmul(
                            p2[:, 0:tn], w[:, 1, kc, ol * 128:(ol + 1) * 128], X[:, kc, t0:t0 + tn],
                            start=(kc == 0), stop=(kc == KC - 1)), reads=[wb, Xb], writes=[p2b], signal=(kc == KC - 1))
                    tt, tb = tr.next()
                    S.op('act', lambda e, tt=tt, p1=p1, tn=tn: e.activation(out=tt[:, 0:tn], in_=p1[:, 0:tn], func=AF.Silu),
                         reads=[p1b], writes=[tb])
                    S.op('dve', lambda e, a=a, tt=tt, p2=p2, t0=t0, tn=tn: e.tensor_tensor(
                        out=a[:, t0:t0 + tn], in0=tt[:, 0:tn], in1=p2[:, 0:tn], op=ALU.mult),
                        reads=[tb, p2b], writes=[ab], disjoint=(t0 > 0))
                S.dma('sp', lambda e, a=a, oc=oc: e.dma_start(out=actT[oc, :, 0:T], in_=a[:]), reads=[ab], owner=ab)
        S.barrier()


def phase_ffn_down(S, CT, actT, Wd, hT, T):
    nc = S.nc
    with ExitStack() as es:
        wt = es.enter_context(nc.sbuf_tensor(un('fd_w'), [128, FC, 512], BF16))
        wb = S.buf('fd_w')
        ar = Ring(S, es, 'sb', 'fd_a', [128, FC, 512], BF16, 2)
        hr = Ring(S, es, 'sb', 'fd_h', [128, 512], F32, 4)
        pr = Ring(S, es, 'ps', 'fd_p', [128, 512], F32, 4)
        wv = wview(Wd)
        av = fm(actT)
        for q in range(4):
            for k0 in range(0, FC, 11):
                S.dma('pool', lambda e, k0=k0, q=q: e.dma_start(out=wt[:, k0:k0 + 11, :], in_=wv[:, k0:k0 + 11, q * 512:(q + 1) * 512]),
                      writes=[wb], owner=wb)
            for (t0, tn) in groups(T, 512):
                a, ab = ar.next()
                for k0 in range(0, FC, 11):
                    S.dma('sp', lambda e, a=a, k0=k0, t0=t0, tn=tn: e.dma_start(out=a[:, k0:k0 + 11, 0:tn], in_=av[:, k0:k0 + 11, t0:t0 + tn]),
                          writes=[ab], owner=ab)
                for ol in range(4):
                    dc = q * 4 + ol
                    h, hb = hr.next()
                    S.dma('sp', lambda e, h=h, dc=dc, t0=t0, tn=tn: e.dma_start(out=h[:, 0:tn], in_=hT[dc, :, t0:t0 + tn]),
                          writes=[hb], owner=hb)
                    ps, pb = pr.next()
                    for fc in range(FC):
                        S.op('pe', lambda e, ps=ps, a=a, ol=ol, fc=fc, tn=tn: e.matmul(
                            ps[:, 0:tn], wt[:, fc, ol * 128:(ol + 1) * 128], a[:, fc, 0:tn],
                            start=(fc == 0), stop=(fc == FC - 1)), reads=[wb, ab], writes=[pb], signal=(fc == FC - 1))
                    S.op('dve', lambda e, h=h, ps=ps, tn=tn: e.tensor_tensor(
                        out=h[:, 0:tn], in0=h[:, 0:tn], in1=ps[:, 0:tn], op=ALU.add), reads=[pb, hb], writes=[hb])
                    S.dma('sp', lambda e, h=h, dc=dc, t0=t0, tn=tn: e.dma_start(out=hT[dc, :, t0:t0 + tn], in_=h[:, 0:tn]),
                          reads=[hb], owner=hb)
        S.barrier()


def phase_out_transpose(S, CT, hT, out, T, col0):
    nc = S.nc
    with ExitStack() as es:
        cb = S.buf('c')
        hr = Ring(S, es, 'sb', 'ot_h', [128, KC, 128], F32, 3)
        orr = Ring(S, es, 'sb', 'ot_o', [128, D], F32, 2)
        pr = Ring(S, es, 'ps', 'ot_p', [128, 512], F32, 4)
        hv = fm(hT)
        for (t0, tn) in groups(T, 128):
            h, hb = hr.next()
            S.dma('sp', lambda e, h=h, t0=t0: e.dma_start(out=h[:], in_=hv[:, :, col0 + t0:col0 + t0 + 128]),
                  writes=[hb], owner=hb)
            o, ob = orr.next()
            for q in range(4):
                ps, pb = pr.next()
                for i in range(4):
                    kc = 4 * q + i
                    S.op('pe', lambda e, ps=ps, h=h, kc=kc, i=i: e.transpose(
                        ps[:, i * 128:(i + 1) * 128], h[:, kc, :], CT['ident'][:]),
                        reads=[hb, cb], writes=[pb], signal=(i == 3))
                if q % 2 == 0:
                    S.op('dve', lambda e, ps=ps, o=o, q=q: e.tensor_copy(out=o[:, q * 512:(q + 1) * 512], in_=ps[:]),
                         reads=[pb], writes=[ob])
                else:
                    S.op('act', lambda e, ps=ps, o=o, q=q: e.copy(out=o[:, q * 512:(q + 1) * 512], in_=ps[:]),
                         reads=[pb], writes=[ob])
            S.dma('sp', lambda e, o=o, t0=t0: e.dma_start(out=out[t0:t0 + 128, :], in_=o[:]), reads=[ob], owner=ob)
        S.barrier()


def phase_rwkv_mix(S, CT, hT, cols, layer, xmix, T):
    nc = S.nc
    G = 256
    with ExitStack() as es:
        cb = S.buf('c')
        idxs = [COLS[('mix_norm_g', layer)]] + [COLS[('rwkv_mu', layer, i)] for i in range(6)]
        g, gb = load_cols(S, es, 'm_g', cols, idxs)
        hr = Ring(S, es, 'sb', 'm_h', [128, KC, G], F32, 2)
        sr = Ring(S, es, 'sb', 'm_sq', [128, G], F32, 3)
        rr = Ring(S, es, 'sb', 'm_rs', [128, G], F32, 2)
        hnr = Ring(S, es, 'sb', 'm_hn', [128, KC, G + 1], F32, 2)
        xxr = Ring(S, es, 'sb', 'm_xx', [128, KC, G], F32, 1)
        orr = Ring(S, es, 'sb', 'm_o', [128, KC, G], BF16, 6)
        pr = Ring(S, es, 'ps', 'm_p', [128, G], F32, 2)
        hv = fm(hT)
        prev = None
        for (t0, tn) in groups(T, G):
            h, hb = hr.next()
            S.dma('sp', lambda e: e.dma_start(out=h[:, :, 0:tn], in_=hv[:, :, t0:t0 + tn]), writes=[hb], owner=hb)
            ps, pb = pr.next()
            for kc in range(KC):
                sq, sb_ = sr.next()
                S.op('act', lambda e: e.activation(out=sq[:, 0:tn], in_=h[:, kc, 0:tn], func=AF.Square), reads=[hb], writes=[sb_])
                S.op('pe', lambda e: e.matmul(ps[:, 0:tn], CT['ones'][:], sq[:, 0:tn], start=(kc == 0), stop=(kc == KC - 1)),
                     reads=[sb_, cb], writes=[pb], signal=True)
            rs, rb = rr.next()
            S.op('act', lambda e: e.activation(out=rs[:, 0:tn], in_=ps[:, 0:tn], func=AF.Ln, bias=CT['eps'][:, 0:1], scale=1.0 / D),
                 reads=[pb, cb], writes=[rb])
            S.op('act', lambda e: e.activation(out=rs[:, 0:tn], in_=rs[:, 0:tn], func=AF.Exp, scale=-0.5), reads=[rb], writes=[rb])
            hn, hnb = hnr.next()
            if prev is None:
                S.op('pool', lambda e: e.memset(hn[:, :, 0:1], 0.0), writes=[hnb])
            else:
                phn, phnb, ptn = prev
                S.op('pool', lambda e: e.tensor_copy(out=hn[:, :, 0:1], in_=phn[:, :, ptn:ptn + 1]), reads=[phnb], writes=[hnb])
            for kc in range(KC):
                S.op('dve', lambda e: e.scalar_tensor_tensor(
                    out=hn[:, kc, 1:1 + tn], in0=h[:, kc, 0:tn], scalar=g[:, 0, kc:kc + 1], in1=rs[:, 0:tn],
                    op0=ALU.mult, op1=ALU.mult), reads=[hb, rb, gb], writes=[hnb], disjoint=(kc > 0))
            xx, xxb = xxr.next()
            S.op('pool', lambda e: e.tensor_tensor(out=xx[:, :, 0:tn], in0=hn[:, :, 0:tn], in1=hn[:, :, 1:1 + tn], op=ALU.subtract),
                 reads=[hnb], writes=[xxb])
            for i in range(6):
                o, ob = orr.next()
                for kc in range(KC):
                    S.op('dve', lambda e: e.scalar_tensor_tensor(
                        out=o[:, kc, 0:tn], in0=xx[:, kc, 0:tn], scalar=g[:, 1 + i, kc:kc + 1], in1=hn[:, kc, 1:1 + tn],
                        op0=ALU.mult, op1=ALU.add), reads=[xxb, hnb, gb], writes=[ob], disjoint=(kc > 0))
                S.dma('sp', lambda e: e.dma_start(out=fm(xmix[i])[:, :, t0:t0 + tn], in_=o[:, :, 0:tn]), reads=[ob], owner=ob)
            prev = (hn, hnb, tn)
        S.barrier()


def gemm_tm(S, CT, es, X, Xb, T, W, fout, epilogue, wname='gw', pname='gp'):
    wr = Ring(S, es, 'sb', wname, [128, KC, 512], BF16, 2)
    pr = Ring(S, es, 'ps', pname, [128, 512], F32, 4)
    wv = wview(W)
    for og in range(fout // 512):
        w, wb = wr.next()
        for k0 in range(0, KC, 8):
            S.dma('pool', lambda e: e.dma_start(out=w[:, k0:k0 + 8, :], in_=wv[:, k0:k0 + 8, og * 512:(og + 1) * 512]),
                  writes=[wb], owner=wb)
        for (t0, tn) in groups(T, 128):
            ps, pb = pr.next()
            for kc in range(KC):
                S.op('pe', lambda e: e.matmul(ps[:], X[:, kc, t0:t0 + 128], w[:, kc, :], start=(kc == 0), stop=(kc == KC - 1)),
                     reads=[wb, Xb], writes=[pb], signal=(kc == KC - 1))
            epilogue(og, t0, ps, pb)


def phase_proj_tm(S, CT, xsrc, W, out, T, odt):
    nc = S.nc
    with ExitStack() as es:
        X, Xb = load_resident(S, es, 'pj_x', xsrc, KC, T)
        orr = Ring(S, es, 'sb', 'pj_o', [128, 512], odt, 4)
        cnt = [0]

        def epi(og, t0, ps, pb):
            o, ob = orr.next()
            cnt[0] += 1
            if cnt[0] % 2 == 0:
                S.op('dve', lambda e: e.tensor_copy(out=o[:], in_=ps[:]), reads=[pb], writes=[ob])
            else:
                S.op('act', lambda e: e.copy(out=o[:], in_=ps[:]), reads=[pb], writes=[ob])
            S.dma('sp', lambda e: e.dma_start(out=out[t0:t0 + 128, og * 512:(og + 1) * 512], in_=o[:]), reads=[ob], owner=ob)

        gemm_tm(S, CT, es, X, Xb, T, W, D, epi)
        S.barrier()


def phase_lora(S, CT, xsrc, w1, w2, w0row, rdim, hid_func, out_func, out, T, odt):
    nc = S.nc
    nrc = (rdim + 127) // 128
    rp = min(rdim, 128)
    with ExitStack() as es:
        X, Xb = load_resident(S, es, 'lo_x', xsrc, KC, T)
        w1t = es.enter_context(nc.sbuf_tensor(un('lo_w1'), [128, KC, rdim], BF16))
        w1b = S.buf('w1')
        S.dma('pool', lambda e: e.dma_start(out=w1t[:], in_=wview(w1)), writes=[w1b], owner=w1b)
        w2t = es.enter_context(nc.sbuf_tensor(un('lo_w2'), [rp, nrc, D], BF16))
        w2b = S.buf('w2')
        S.dma('pool', lambda e: e.dma_start(out=w2t[:], in_=w2.rearrange("(c p) f -> p c f", p=rp)), writes=[w2b], owner=w2b)
        hid = es.enter_context(nc.sbuf_tensor(un('lo_hid'), [rp, nrc, T], BF16))
        hidb = S.buf('hid')
        if w0row is not None:
            w0t = es.enter_context(nc.sbuf_tensor(un('lo_w0'), [128, D], F32))
            w0b = S.buf('w0')
            S.dma('sp', lambda e: e.dma_start(out=w0t[:], in_=w0row.partition_broadcast(128)), writes=[w0b], owner=w0b)
        pr = Ring(S, es, 'ps', 'lo_p', [128, 512], F32, 3)
        for rc in range(nrc):
            for (t0, tn) in groups(T, 512):
                ps, pb = pr.next()
                for kc in range(KC):
                    S.op('pe', lambda e: e.matmul(ps[0:rp, 0:tn], w1t[:, kc, rc * 128:rc * 128 + rp], X[:, kc, t0:t0 + tn],
                                                  start=(kc == 0), stop=(kc == KC - 1)), reads=[w1b, Xb], writes=[pb], signal=(kc == KC - 1))
                S.op('act', lambda e: e.activation(out=hid[:, rc, t0:t0 + tn], in_=ps[0:rp, 0:tn], func=hid_func), reads=[pb], writes=[hidb])
        orr = Ring(S, es, 'sb', 'lo_o', [128, 512], odt, 4)
        tr = Ring(S, es, 'sb', 'lo_t', [128, 512], F32, 3)
        p2 = Ring(S, es, 'ps', 'lo_p2', [128, 512], F32, 3)
        for og in range(4):
            for (t0, tn) in groups(T, 128):
                ps, pb = p2.next()
                for rc in range(nrc):
                    S.op('pe', lambda e: e.matmul(ps[:], hid[:, rc, t0:t0 + 128], w2t[:, rc, og * 512:(og + 1) * 512],
                                                  start=(rc == 0), stop=(rc == nrc - 1)), reads=[hidb, w2b], writes=[pb], signal=(rc == nrc - 1))
                o, ob = orr.next()
                if w0row is not None:
                    tt, tb = tr.next()
                    S.op('dve', lambda e: e.tensor_tensor(out=tt[:], in0=ps[:], in1=w0t[:, og * 512:(og + 1) * 512], op=ALU.add),
                         reads=[pb, w0b], writes=[tb])
                    S.op('act', lambda e: e.activation(out=o[:], in_=tt[:], func=out_func), reads=[tb], writes=[ob])
                else:
                    S.op('act', lambda e: e.activation(out=o[:], in_=ps[:], func=out_func), reads=[pb], writes=[ob])
                S.dma('sp', lambda e: e.dma_start(out=out[t0:t0 + 128, og * 512:(og + 1) * 512], in_=o[:]), reads=[ob], owner=ob)
        S.barrier()

DEC_C = 0.6065306597126334
import os as _os
PREP_STAGE = int(_os.environ.get('PREP_STAGE', '9'))
SCAN_STAGE = int(_os.environ.get('SCAN_STAGE', '9'))
RUN_SCAN = _os.environ.get('RUN_SCAN', '1') == '1'
SCORE_MODE = int(_os.environ.get('SCORE_MODE', '2'))


def bc3(ap2, n):
    return ap2.unsqueeze(2).broadcast_to([128, ap2.shape[1], n])


def hv3(ap2):
    return ap2.rearrange("p (h c) -> p h c", c=64)


def v4(ap2, a):
    return ap2.rearrange("p (a b) -> p a b", a=a)


def load_bc(S, es, name, row, dt=F32):
    nc = S.nc
    t = es.enter_context(nc.sbuf_tensor(un(name), [128, D], dt))
    b = S.buf(name)
    S.dma('sp', lambda e: e.dma_start(out=t[:], in_=row.partition_broadcast(128)), writes=[b], owner=b)
    return t, b


def phase_rwkv_prep(S, CT, I, layer, Z, T):
    nc = S.nc
    with ExitStack() as es:
        cb = S.buf('c')
        kkbc, kkb = load_bc(S, es, 'pp_kk', I['rwkv_k_k'][layer])
        kabc, kab = load_bc(S, es, 'pp_ka', I['rwkv_k_a'][layer])
        rkbc, rkb = load_bc(S, es, 'pp_rk', I['rwkv_r_k'][layer].rearrange("h c -> (h c)"))
        cm = es.enter_context(nc.sbuf_tensor(un('pp_cm'), [128, 128], F32))
        cmb = S.buf('cm')
        S.dma('sp', lambda e: e.dma_start(out=cm[:], in_=I['cmask'][0]), writes=[cmb], owner=cmb)
        negc = es.enter_context(nc.sbuf_tensor(un('pp_negc'), [128, 128], F32))
        S.op('dve', lambda e: e.memset(negc[:], -DEC_C), writes=[cmb])
        r_in = Ring(S, es, 'sb', 'pp_r', [128, D], BF16, 2)
        k_in = Ring(S, es, 'sb', 'pp_k', [128, D], BF16, 2)
        a_in = Ring(S, es, 'sb', 'pp_a', [128, D], BF16, 2)
        v_in = Ring(S, es, 'sb', 'pp_v', [128, D], BF16, 2)
        s_in = Ring(S, es, 'sb', 'pp_s', [128, D], F32, 2)
        if layer == 1:
            vg_in = Ring(S, es, 'sb', 'pp_vg', [128, D], BF16, 1)
            vf_in = Ring(S, es, 'sb', 'pp_vf', [128, D], BF16, 1)
            o_vv = Ring(S, es, 'sb', 'pp_ovv', [128, D], BF16, 2)
        T1, T1b = tile1(S, es, 'pp_t1', [128, D], F32)
        T2, T2b = tile1(S, es, 'pp_t2', [128, D], F32)
        T3, T3b = tile1(S, es, 'pp_t3', [128, D], F32)
        T4, T4b = tile1(S, es, 'pp_t4', [128, D], F32)
        ss, ssb = tile1(S, es, 'pp_ss', [128, 64], F32)
        Er = Ring(S, es, 'sb', 'pp_e', [128, 512], F32, 4)
        o_r = Ring(S, es, 'sb', 'pp_or', [128, D], BF16, 2)
        o_kp = Ring(S, es, 'sb', 'pp_okp', [128, D], BF16, 2)
        o_kt = Ring(S, es, 'sb', 'pp_okt', [128, D], BF16, 2)
        o_bn = Ring(S, es, 'sb', 'pp_obn', [128, D], BF16, 2)
        o_bo = Ring(S, es, 'sb', 'pp_obo', [128, D], BF16, 2)
        xt_r = Ring(S, es, 'sb', 'pp_xt', [128, KC, 128], BF16, 4)
        gm_r = Ring(S, es, 'sb', 'pp_gm', [128, KC], F32, 2)
        pcw = Ring(S, es, 'ps', 'pp_pcw', [128, 512], F32, 4)
        ptr = Ring(S, es, 'ps', 'pp_ptr', [128, 1024], BF16, 2)
        for ci, (t0, tn) in enumerate(groups(T, 128)):
            def ld(ring, src):
                t, b = ring.next()
                S.dma('sp', lambda e: e.dma_start(out=t[:], in_=src[t0:t0 + 128, :]), writes=[b], owner=b)
                return t, b
            r, rb = ld(r_in, Z['r'])
            k, kb = ld(k_in, Z['k'])
            a, ab = ld(a_in, Z['a'])
            v, vb = ld(v_in, Z['v'])
            sg, sgb = ld(s_in, Z['sigd'])
            if layer == 1:
                vg, vgb = ld(vg_in, Z['vg'])
                vf, vfb = ld(vf_in, Z['vfirst'])
                vv, vvb = o_vv.next()
                S.op('dve', lambda e: e.tensor_tensor(out=T4[:], in0=vf[:], in1=v[:], op=ALU.subtract), reads=[vfb, vb], writes=[T4b])
                S.op('dve', lambda e: e.tensor_tensor(out=T4[:], in0=T4[:], in1=vg[:], op=ALU.mult), reads=[vgb, T4b], writes=[T4b])
                S.op('dve', lambda e: e.tensor_tensor(out=vv[:], in0=T4[:], in1=v[:], op=ALU.add), reads=[vb, T4b], writes=[vvb])
                S.dma('sp', lambda e: e.dma_start(out=Z['vv'][t0:t0 + 128, :], in_=vv[:]), reads=[vvb], owner=vvb)
            else:
                vv, vvb = v, vb
            S.op('dve', lambda e: e.tensor_tensor(out=T1[:], in0=k[:], in1=kkbc[:], op=ALU.mult), reads=[kb, kkb], writes=[T1b])
            S.op('act', lambda e: e.activation(out=T2[:], in_=T1[:], func=AF.Square), reads=[T1b], writes=[T2b])
            S.op('dve', lambda e: e.tensor_reduce(out=ss[:, 0:32], in_=hv3(T2[:]), axis=AX.X, op=ALU.add), reads=[T2b], writes=[ssb])
            S.op('dve', lambda e: e.tensor_scalar_max(out=ss[:, 0:32], in0=ss[:, 0:32], scalar1=1e-24), reads=[ssb], writes=[ssb])
            S.op('act', lambda e: e.activation(out=ss[:, 0:32], in_=ss[:, 0:32], func=AF.Ln), reads=[ssb], writes=[ssb])
            S.op('act', lambda e: e.activation(out=ss[:, 0:32], in_=ss[:, 0:32], func=AF.Exp, scale=-0.5), reads=[ssb], writes=[ssb])
            S.op('dve', lambda e: e.tensor_tensor(out=hv3(T1[:]), in0=hv3(T1[:]), in1=bc3(ss[:, 0:32], 64), op=ALU.mult),
                 reads=[ssb, T1b], writes=[T1b])
            if PREP_STAGE < 2:
                continue
            S.op('dve', lambda e: e.scalar_tensor_tensor(out=T2[:], in0=a[:], scalar=-1.0, in1=kabc[:], op0=ALU.add, op1=ALU.mult),
                 reads=[ab, kab], writes=[T2b])
            S.op('dve', lambda e: e.scalar_tensor_tensor(out=T2[:], in0=T2[:], scalar=1.0, in1=k[:], op0=ALU.add, op1=ALU.mult),
                 reads=[kb, T2b], writes=[T2b])
            S.op('dve', lambda e: e.scalar_tensor_tensor(out=T3[:], in0=T1[:], scalar=-1.0, in1=a[:], op0=ALU.mult, op1=ALU.mult),
                 reads=[T1b, ab], writes=[T3b])
            S.op('dve', lambda e: e.tensor_tensor(out=T4[:], in0=r[:], in1=T2[:], op=ALU.mult), reads=[rb, T2b], writes=[T4b])
            S.op('pool', lambda e: e.tensor_tensor(out=T4[:], in0=T4[:], in1=rkbc[:], op=ALU.mult), reads=[rkb, T4b], writes=[T4b])
            S.op('dve', lambda e: e.tensor_reduce(out=ss[:, 32:64], in_=hv3(T4[:]), axis=AX.X, op=ALU.add), reads=[T4b], writes=[ssb])
            bo, bob = o_bo.next()
            S.op('pool', lambda e: e.tensor_tensor(out=hv3(bo[:]), in0=hv3(vv[:]), in1=bc3(ss[:, 32:64], 64), op=ALU.mult),
                 reads=[ssb, vvb], writes=[bob])
            S.dma('sp', lambda e: e.dma_start(out=Z['bonus'][t0:t0 + 128, :], in_=bo[:]), reads=[bob], owner=bob)
            if PREP_STAGE < 3:
                continue
            orr_, orb = o_r.next()
            okp, okpb = o_kp.next()
            okt, oktb = o_kt.next()
            obn, obnb = o_bn.next()
            for og in range(4):
                sl = slice(og * 512, (og + 1) * 512)
                ps, pb = pcw.next()
                S.op('pe', lambda e: e.matmul(ps[:], cm[:], sg[:, sl], start=True, stop=True), reads=[cmb, sgb], writes=[pb])
                E1, E1b = Er.next()
                S.op('act', lambda e: e.activation(out=E1[:], in_=ps[:], func=AF.Exp), reads=[pb], writes=[E1b])
                S.op('pool', lambda e: e.tensor_tensor(out=orr_[:, sl], in0=r[:, sl], in1=E1[:], op=ALU.mult), reads=[rb, E1b], writes=[orb], disjoint=(og > 0))
                E2, E2b = Er.next()
                S.op('act', lambda e: e.activation(out=E2[:], in_=ps[:], func=AF.Exp, scale=-1.0), reads=[pb], writes=[E2b])
                S.op('pool', lambda e: e.tensor_tensor(out=okt[:, sl], in0=T2[:, sl], in1=E2[:], op=ALU.mult), reads=[T2b, E2b], writes=[oktb], disjoint=(og > 0))
                S.op('dve', lambda e: e.tensor_tensor(out=obn[:, sl], in0=T3[:, sl], in1=E2[:], op=ALU.mult), reads=[T3b, E2b], writes=[obnb], disjoint=(og > 0))
                E3, E3b = Er.next()
                S.op('dve', lambda e: e.scalar_tensor_tensor(out=E3[:], in0=sg[:, sl], scalar=DEC_C, in1=ps[:], op0=ALU.mult, op1=ALU.add),
                     reads=[sgb, pb], writes=[E3b])
                S.op('act', lambda e: e.activation(out=E3[:], in_=E3[:], func=AF.Exp), reads=[E3b], writes=[E3b])
                S.op('pool', lambda e: e.tensor_tensor(out=okp[:, sl], in0=T1[:, sl], in1=E3[:], op=ALU.mult), reads=[T1b, E3b], writes=[okpb], disjoint=(og > 0))
            S.dma('sp', lambda e: e.dma_start(out=Z['kt'][t0:t0 + 128, :], in_=okt[:]), reads=[oktb], owner=oktb)
            S.dma('sp', lambda e: e.dma_start(out=Z['bn'][t0:t0 + 128, :], in_=obn[:]), reads=[obnb], owner=obnb)
            if PREP_STAGE < 4:
                continue
            gm, gmb = gm_r.next()
            for q4 in range(4):
                pg, pgb = pcw.next()
                for i4 in range(4):
                    kc = 4 * q4 + i4
                    S.op('pe', lambda e: e.matmul(pg[:, i4 * 128:(i4 + 1) * 128], sg[:, kc * 128:(kc + 1) * 128], negc[:], start=True, stop=True),
                         reads=[sgb, cmb], writes=[pgb], signal=(i4 == 3))
                S.op('act', lambda e: e.activation(out=gm[:, 4 * q4:4 * q4 + 4].unsqueeze(2), in_=v4(pg[:], 4)[:, :, 0:1], func=AF.Exp), reads=[pgb], writes=[gmb])
            S.dma('sp', lambda e: e.dma_start(out=Z['gam'][ci], in_=gm[:]), reads=[gmb], owner=gmb)
            if PREP_STAGE < 5:
                continue
            for (src, srcb, dst) in ((orr_, orb, 'rT'), (okp, okpb, 'kpT'), (okt, oktb, 'ktT'), (obn, obnb, 'bnT')):
                xt, xtb = xt_r.next()
                for half in range(2):
                    pt, ptb = ptr.next()
                    for j in range(8):
                        kc = half * 8 + j
                        S.op('pe', lambda e: e.transpose(pt[:, j * 128:(j + 1) * 128], src[:, kc * 128:(kc + 1) * 128], CT['identb'][:]),
                             reads=[srcb, cb], writes=[ptb], signal=(j == 7))
                    if half == 0:
                        S.op('act', lambda e: e.copy(out=xt[:, 0:8, :], in_=v4(pt[:], 8)), reads=[ptb], writes=[xtb])
                    else:
                        S.op('dve', lambda e: e.tensor_copy(out=xt[:, 8:16, :], in_=v4(pt[:], 8)), reads=[ptb], writes=[xtb])
                for e2 in range(2):
                    S.dma('sp', lambda e: e.dma_start(out=Z[dst][ci].rearrange("c (hp e) t -> c hp e t", e=2)[:, :, e2, :],
                                                      in_=xt[e2 * 64:(e2 + 1) * 64, :, :]), reads=[xtb], owner=xtb)
        S.barrier()


def phase_rwkv_scan(S, CT, I, layer, Z, T):
    nc = S.nc
    with ExitStack() as es:
        cb = S.buf('c')
        gnw, gnwb = load_bc(S, es, 'sc_gnw', I['rwkv_gn_w'][layer])
        gnb, gnbb = load_bc(S, es, 'sc_gnb', I['rwkv_gn_b'][layer])
        msk = es.enter_context(nc.sbuf_tensor(un('sc_msk'), [128, 3, 128], F32))
        mb = S.buf('msk')
        for i in range(3):
            S.dma('sp', lambda e: e.dma_start(out=msk[:, i, :], in_=I['cmask'][1 + i]), writes=[mb], owner=mb)
        MUS, MUI, MLS = 0, 1, 2

        def mbc(i, n):
            return msk[:, i, :].unsqueeze(1).broadcast_to([128, n, 128])
        A, Ab = tile1(S, es, 'sc_A', [64, RH, 64], F32)
        Abf, Abfb = tile1(S, es, 'sc_Abf', [64, RH, 64], BF16)
        S.op('dve', lambda e: e.memset(A[:], 0.0), writes=[Ab])
        S.op('dve', lambda e: e.memset(Abf[:], 0.0), writes=[Abfb])
        names_cm = ('rT', 'kpT', 'ktT', 'bnT')
        cm_r = {n: Ring(S, es, 'sb', 'sc_' + n, [64, RH, 128], BF16, 1) for n in names_cm}
        tm_r = {n: Ring(S, es, 'sb', 'sc_' + n, [128, D], BF16, 2) for n in ('kt', 'bn', 'vv', 'bonus', 'g')}
        gm_r = Ring(S, es, 'sb', 'sc_gam', [64, 2, KC], F32, 2)
        sc_t = {n: (es.enter_context(nc.sbuf_tensor(un('sc_s' + n), [128, 16, 128], BF16)), [S.buf(n) for _ in range(4)])
                for n in ('MkT', 'RKT', 'RBT', 'N0', 'N1', 'NT0', 'NT1', 'Q0', 'Q1')}
        stg = Ring(S, es, 'sb', 'sc_stg', [128, 512], BF16, 2)
        RHS, RHSb = tile1(S, es, 'sc_rhs', [128, 1024], BF16)
        U, Ub = tile1(S, es, 'sc_u', [128, 1024], BF16)
        y_r = Ring(S, es, 'sb', 'sc_y', [128, D], F32, 2)
        yt, ytb = tile1(S, es, 'sc_yt', [128, D], F32)
        st, stb = tile1(S, es, 'sc_st', [128, 5, 32], F32)
        z_r = Ring(S, es, 'sb', 'sc_z', [128, D], BF16, 1)
        zT_r = Ring(S, es, 'sb', 'sc_zT', [128, KC, 128], BF16, 2)
        P = Ring(S, es, 'ps', 'sc_p', [128, 512], F32, 6)
        PT = Ring(S, es, 'ps', 'sc_pt', [128, 1024], BF16, 2)
        ecount = [0]

        def cmview(d):
            return d.rearrange("hp (e c) t -> c (hp e) t", e=2)
        def output_stage(y, yb, bo, bob, gg, ggb, t0):
            S.op('dve', lambda e: e.tensor_reduce(out=st[:, 0, :], in_=hv3(y[:]), axis=AX.X, op=ALU.add), reads=[yb], writes=[stb])
            S.op('act', lambda e: e.activation(out=yt[:], in_=y[:], func=AF.Square), reads=[yb], writes=[ytb])
            S.op('dve', lambda e: e.tensor_reduce(out=st[:, 1, :], in_=hv3(yt[:]), axis=AX.X, op=ALU.add), reads=[ytb], writes=[stb])
            S.op('dve', lambda e: e.tensor_scalar_mul(out=st[:, 0, :], in0=st[:, 0, :], scalar1=1.0 / 64), reads=[stb], writes=[stb])
            S.op('dve', lambda e: e.tensor_tensor(out=st[:, 2, :], in0=st[:, 0, :], in1=st[:, 0, :], op=ALU.mult), reads=[stb], writes=[stb])
            S.op('dve', lambda e: e.scalar_tensor_tensor(out=st[:, 3, :], in0=st[:, 1, :], scalar=1.0 / 64, in1=st[:, 2, :], op0=ALU.mult, op1=ALU.subtract),
                 reads=[stb], writes=[stb])
            S.op('act', lambda e: e.activation(out=st[:, 3, :], in_=st[:, 3, :], func=AF.Ln, bias=CT['eps'][:, 1:2], scale=1.0), reads=[stb, cb], writes=[stb])
            S.op('act', lambda e: e.activation(out=st[:, 3, :], in_=st[:, 3, :], func=AF.Exp, scale=-0.5), reads=[stb], writes=[stb])
            S.op('dve', lambda e: e.tensor_tensor(out=hv3(yt[:]), in0=hv3(y[:]), in1=bc3(st[:, 0, :], 64), op=ALU.subtract), reads=[yb, stb], writes=[ytb])
            S.op('dve', lambda e: e.tensor_tensor(out=hv3(yt[:]), in0=hv3(yt[:]), in1=bc3(st[:, 3, :], 64), op=ALU.mult), reads=[stb, ytb], writes=[ytb])
            S.op('pool', lambda e: e.tensor_tensor(out=yt[:], in0=yt[:], in1=gnw[:], op=ALU.mult), reads=[gnwb, ytb], writes=[ytb])
            S.op('pool', lambda e: e.tensor_tensor(out=yt[:], in0=yt[:], in1=gnb[:], op=ALU.add), reads=[gnbb, ytb], writes=[ytb])
            S.op('pool', lambda e: e.tensor_tensor(out=yt[:], in0=yt[:], in1=bo[:], op=ALU.add), reads=[bob, ytb], writes=[ytb])
            z, zb = z_r.next()
            S.op('pool', lambda e: e.tensor_tensor(out=z[:], in0=yt[:], in1=gg[:], op=ALU.mult), reads=[ggb, ytb], writes=[zb])
            return (z, zb, t0)

        def output_tr(z, zb, t0):
            zT, zTb = zT_r.next()
            for half in range(2):
                pt, ptb = PT.next()
                for j in range(8):
                    kc = half * 8 + j
                    S.op('pe', lambda e: e.transpose(pt[:, j * 128:(j + 1) * 128], z[:, kc * 128:(kc + 1) * 128], CT['identb'][:]),
                         reads=[zb, cb], writes=[ptb], signal=(j == 7))
                S.op('act', lambda e: e.copy(out=zT[:, half * 8:half * 8 + 8, :], in_=v4(pt[:], 8)), reads=[ptb], writes=[zTb])
            S.dma('sp', lambda e: e.dma_start(out=fm(Z['zT'])[:, :, t0:t0 + 128], in_=zT[:]), reads=[zTb], owner=zTb)

        prev_out = None
        prev_z = None
        for ci, (t0, tn) in enumerate(groups(T, 128)):
            cmt = {}
            for n in names_cm:
                t, b = cm_r[n].next()
                S.dma('sp', lambda e: e.dma_start(out=t[:], in_=Z[n][ci]), writes=[b], owner=b)
                cmt[n] = (t, b)
            tmt = {}
            for n in ('kt', 'bn', 'vv', 'bonus', 'g'):
                t, b = tm_r[n].next()
                src = Z[n] if not (n == 'vv' and layer == 0) else Z['v']
                S.dma('sp', lambda e: e.dma_start(out=t[:], in_=src[t0:t0 + 128, :]), writes=[b], owner=b)
                tmt[n] = (t, b)
            gam, gamb = gm_r.next()
            S.dma('sp', lambda e: e.dma_start(out=gam[:], in_=Z['gam'][ci].rearrange("(e c) hp -> c e hp", e=2)), writes=[gamb], owner=gamb)
            rT, rTb = cmt['rT']
            kpT, kpTb = cmt['kpT']
            ktT, ktTb = cmt['ktT']
            bnT, bnTb = cmt['bnT']
            kt, ktb = tmt['kt']
            bn, bnb = tmt['bn']
            vv, vvb = tmt['vv']
            y, yb = y_r.next()
            for hh in range(2):
                H0 = 16 * hh
                kinds = (('MkT', ktT, ktTb, kpT, kpTb, MUS), ('N0', bnT, bnTb, kpT, kpTb, MUS), ('NT0', kpT, kpTb, bnT, bnTb, MLS),
                         ('RKT', ktT, ktTb, rT, rTb, MUI), ('RBT', bnT, bnTb, rT, rTb, MUI))
                for (dn, lt, ltb, rt, rtb, mi) in kinds:
                    dst, dstb = sc_t[dn]
                    for gq in range(4):
                        ps, pb = P.next()
                        for j in range(4):
                            h = H0 + 4 * gq + j
                            S.op('pe', lambda e: e.matmul(ps[:, j * 128:(j + 1) * 128], lt[:, h, :], rt[:, h, :], start=True, stop=True),
                                 reads=[ltb, rtb], writes=[pb], signal=(j == 3))
                        ecount[0] += 1
                        if ecount[0] % 2 == 0:
                            S.op('dve', lambda e: e.tensor_tensor(out=dst[:, 4 * gq:4 * gq + 4, :], in0=v4(ps[:], 4), in1=mbc(mi, 4), op=ALU.mult),
                                 reads=[pb, mb], writes=[dstb[gq]])
                        else:
                            sg_, sgb_ = stg.next()
                            S.op('act', lambda e: e.copy(out=sg_[:], in_=ps[:]), reads=[pb], writes=[sgb_])
                            S.op('pool', lambda e: e.tensor_tensor(out=dst[:, 4 * gq:4 * gq + 4, :], in0=v4(sg_[:], 4), in1=mbc(mi, 4), op=ALU.mult),
                                 reads=[sgb_, mb], writes=[dstb[gq]])
                if SCAN_STAGE < 2:
                    continue
                Ncur, Ncb = sc_t['N0']
                NTcur, NTcb = sc_t['NT0']
                Nnx, Nnb = sc_t['N1']
                NTnx, NTnb = sc_t['NT1']
                Qc, Qcb = sc_t['Q0']
                Qn, Qnb = sc_t['Q1']
                S.op('pool', lambda e: e.tensor_tensor(out=Qc[:], in0=Ncur[:], in1=CT['identb'][:].unsqueeze(1).broadcast_to([128, 16, 128]), op=ALU.add),
                     reads=Ncb + [cb], writes=Qcb)
                for lev in range(1, 7):
                    for gq in range(4):
                        ps, pb = P.next()
                        for j in range(4):
                            hx = 4 * gq + j
                            S.op('pe', lambda e: e.matmul(ps[:, j * 128:(j + 1) * 128], Ncur[:, hx, :], NTcur[:, hx, :], start=True, stop=True),
                                 reads=[Ncb[gq], NTcb[gq]], writes=[pb], signal=(j == 3))
                        S.op('act', lambda e: e.copy(out=NTnx[:, 4 * gq:4 * gq + 4, :], in_=v4(ps[:], 4)), reads=[pb], writes=[NTnb[gq]])
                        if lev < 6:
                            ps2, pb2 = P.next()
                            for j in range(4):
                                hx = 4 * gq + j
                                S.op('pe', lambda e: e.matmul(ps2[:, j * 128:(j + 1) * 128], NTcur[:, hx, :], Ncur[:, hx, :], start=True, stop=True),
                                     reads=[Ncb[gq], NTcb[gq]], writes=[pb2], signal=(j == 3))
                            S.op('dve', lambda e: e.tensor_copy(out=Nnx[:, 4 * gq:4 * gq + 4, :], in_=v4(ps2[:], 4)), reads=[pb2], writes=[Nnb[gq]])
                    for gq in range(4):
                        ps, pb = P.next()
                        for j in range(4):
                            hx = 4 * gq + j
                            S.op('pe', lambda e: e.matmul(ps[:, j * 128:(j + 1) * 128], NTnx[:, hx, :], Qc[:, hx, :], start=True, stop=True),
                                 reads=[NTnb[gq], Qcb[gq]], writes=[pb], signal=(j == 3))
                        S.op('dve', lambda e: e.tensor_tensor(out=Qn[:, 4 * gq:4 * gq + 4, :], in0=v4(ps[:], 4), in1=Qc[:, 4 * gq:4 * gq + 4, :], op=ALU.add),
                             reads=[pb, Qcb[gq]], writes=[Qnb[gq]])
                    Ncur, Ncb, Nnx, Nnb = Nnx, Nnb, Ncur, Ncb
                    NTcur, NTcb, NTnx, NTnb = NTnx, NTnb, NTcur, NTcb
                    Qc, Qcb, Qn, Qnb = Qn, Qnb, Qc, Qcb
                if hh == 0 and prev_out is not None:
                    prev_z = output_stage(*prev_out)
                    prev_out = None
                if hh == 1 and prev_z is not None:
                    output_tr(*prev_z)
                    prev_z = None
                MkT, MkTb = sc_t['MkT']
                RKT, RKTb = sc_t['RKT']
                RBT, RBTb = sc_t['RBT']
                for g8 in range(2):
                    ps, pb = P.next()
                    for j in range(8):
                        hx = 8 * g8 + j
                        h = H0 + hx
                        S.op('pe', lambda e: e.matmul(ps[:, j * 64:(j + 1) * 64], kpT[:, h, :], Abf[:, h, :], start=True, stop=False),
                             reads=[kpTb, Abfb], writes=[pb], signal=False)
                        S.op('pe', lambda e: e.matmul(ps[:, j * 64:(j + 1) * 64], MkT[:, hx, :], vv[:, h * 64:(h + 1) * 64], start=False, stop=True),
                             reads=[MkTb[hx // 4], vvb], writes=[pb], signal=(j == 7))
                    S.op('act', lambda e: e.copy(out=RHS[:, g8 * 512:(g8 + 1) * 512], in_=ps[:]), reads=[pb], writes=[RHSb], disjoint=(g8 > 0))
                for g8 in range(2):
                    ps, pb = P.next()
                    for j in range(8):
                        hx = 8 * g8 + j
                        S.op('pe', lambda e: e.matmul(ps[:, j * 64:(j + 1) * 64], Qc[:, hx, :], RHS[:, hx * 64:(hx + 1) * 64], start=True, stop=True),
                             reads=[Qcb[hx // 4], RHSb], writes=[pb], signal=(j == 7))
                    S.op('act', lambda e: e.copy(out=U[:, g8 * 512:(g8 + 1) * 512], in_=ps[:]), reads=[pb], writes=[Ub], disjoint=(g8 > 0))
                for g8 in range(2):
                    ps, pb = P.next()
                    for j in range(8):
                        hx = 8 * g8 + j
                        h = H0 + hx
                        S.op('pe', lambda e: e.matmul(ps[:, j * 64:(j + 1) * 64], rT[:, h, :], Abf[:, h, :], start=True, stop=False),
                             reads=[rTb, Abfb], writes=[pb], signal=False)
                        S.op('pe', lambda e: e.matmul(ps[:, j * 64:(j + 1) * 64], RKT[:, hx, :], vv[:, h * 64:(h + 1) * 64], start=False, stop=False),
                             reads=[RKTb[hx // 4], vvb], writes=[pb], signal=False)
                        S.op('pe', lambda e: e.matmul(ps[:, j * 64:(j + 1) * 64], RBT[:, hx, :], U[:, hx * 64:(hx + 1) * 64], start=False, stop=True),
                             reads=[RBTb[hx // 4], Ub], writes=[pb], signal=(j == 7))
                    c0 = (H0 + 8 * g8) * 64
                    S.op('dve', lambda e: e.tensor_copy(out=y[:, c0:c0 + 512], in_=ps[:]), reads=[pb], writes=[yb])
                if SCAN_STAGE < 4:
                    continue
                for g8 in range(2):
                    ps, pb = P.next()
                    for j in range(8):
                        hx = 8 * g8 + j
                        h = H0 + hx
                        S.op('pe', lambda e: e.matmul(ps[0:64, j * 64:(j + 1) * 64], kt[:, h * 64:(h + 1) * 64], vv[:, h * 64:(h + 1) * 64], start=True, stop=False),
                             reads=[ktb, vvb], writes=[pb], signal=False)
                        S.op('pe', lambda e: e.matmul(ps[0:64, j * 64:(j + 1) * 64], bn[:, h * 64:(h + 1) * 64], U[:, hx * 64:(hx + 1) * 64], start=False, stop=True),
                             reads=[bnb, Ub], writes=[pb], signal=(j == 7))
                    h0 = H0 + 8 * g8
                    hp0 = h0 // 2
                    S.op('dve', lambda e: e.tensor_tensor(out=A[:, h0:h0 + 8, :], in0=ps[0:64, :].rearrange("p (a b) -> p a b", a=8), in1=A[:, h0:h0 + 8, :], op=ALU.add),
                         reads=[pb, Ab], writes=[Ab])
                    S.op('pool', lambda e: e.tensor_tensor(
                        out=A[:, h0:h0 + 8, :].rearrange("c (hp e) v -> c hp e v", e=2),
                        in0=A[:, h0:h0 + 8, :].rearrange("c (hp e) v -> c hp e v", e=2),
                        in1=gam[:].rearrange("c e hp -> c hp e")[:, hp0:hp0 + 4, :].unsqueeze(3).broadcast_to([64, 4, 2, 64]), op=ALU.mult),
                        reads=[gamb, Ab], writes=[Ab])
                    S.op('act', lambda e: e.copy(out=Abf[:, h0:h0 + 8, :], in_=A[:, h0:h0 + 8, :]), reads=[Ab], writes=[Abfb])
            prev_out = (y, yb, tmt['bonus'][0], tmt['bonus'][1], tmt['g'][0], tmt['g'][1], t0)
        output_tr(*output_stage(*prev_out))
        S.barrier()


def phase_proj_fm_res(S, CT, xsrc, W, hres, T, kcn=KC):
    nc = S.nc
    with ExitStack() as es:
        X, Xb = load_resident(S, es, 'po_x', xsrc, kcn, T)
        wr = Ring(S, es, 'sb', 'po_w', [128, kcn, 256], BF16, 2)
        hr = Ring(S, es, 'sb', 'po_h', [128, 512], F32, 4)
        pr = Ring(S, es, 'ps', 'po_p', [128, 512], F32, 4)
        wv = wview(W)
        for og in range(D // 256):
            w, wb = wr.next()
            S.dma('pool', lambda e: e.dma_start(out=w[:], in_=wv[:, :, og * 256:(og + 1) * 256]), writes=[wb], owner=wb)
            for ol in range(2):
                dc = og * 2 + ol
                for (t0, tn) in groups(T, 512):
                    h, hb = hr.next()
                    S.dma('sp', lambda e: e.dma_start(out=h[:, 0:tn], in_=hres[dc, :, t0:t0 + tn]), writes=[hb], owner=hb)
                    ps, pb = pr.next()
                    for kc in range(kcn):
                        S.op('pe', lambda e: e.matmul(ps[:, 0:tn], w[:, kc, ol * 128:(ol + 1) * 128], X[:, kc, t0:t0 + tn],
                                                      start=(kc == 0), stop=(kc == kcn - 1)), reads=[wb, Xb], writes=[pb], signal=(kc == kcn - 1))
                    S.op('dve', lambda e: e.tensor_tensor(out=h[:, 0:tn], in0=h[:, 0:tn], in1=ps[:, 0:tn], op=ALU.add), reads=[pb, hb], writes=[hb])
                    S.dma('sp', lambda e: e.dma_start(out=hres[dc, :, t0:t0 + tn], in_=h[:, 0:tn]), reads=[hb], owner=hb)
        S.barrier()


def rwkv_layer(S, CT, I, layer, Z, hT, T):
    import os
    nph = int(os.environ.get('RWKV_NPH', '99'))
    xm = Z['xmix']
    Zl = dict(Z)
    if layer == 0:
        Zl['v'] = Z['vfirst']
    steps = [
        lambda: phase_rwkv_mix(S, CT, hT, I['cols'], layer, xm, T),
        lambda: phase_proj_tm(S, CT, xm[0], I['rwkv_w_r'][layer], Z['r'], T, BF16),
        lambda: phase_proj_tm(S, CT, xm[2], I['rwkv_w_k'][layer], Z['k'], T, BF16),
        lambda: phase_proj_tm(S, CT, xm[3], I['rwkv_w_v'][layer], Z['v'] if layer == 1 else Z['vfirst'], T, BF16),
        lambda: phase_lora(S, CT, xm[1], I['rwkv_dec_w1'][layer], I['rwkv_dec_w2'][layer], I['rwkv_dec_w0'][layer], 96, AF.Tanh, AF.Sigmoid, Z['sigd'], T, F32),
        lambda: phase_lora(S, CT, xm[4], I['rwkv_a_w1'][layer], I['rwkv_a_w2'][layer], I['rwkv_a_w0'][layer], 96, AF.Copy, AF.Sigmoid, Z['a'], T, BF16),
        lambda: phase_lora(S, CT, xm[5], I['rwkv_g_w1'][layer], I['rwkv_g_w2'][layer], None, 256, AF.Sigmoid, AF.Copy, Z['g'], T, BF16),
    ]
    if layer == 1:
        steps.append(lambda: phase_lora(S, CT, xm[3], I['rwkv_v_w1'][0], I['rwkv_v_w2'][0], I['rwkv_v_w0'][0], 64, AF.Copy, AF.Sigmoid, Z['vg'], T, BF16))
    steps += [
        lambda: phase_rwkv_prep(S, CT, I, layer, Zl, T),
        lambda: phase_rwkv_scan(S, CT, I, layer, Zl, T),
        lambda: phase_proj_fm_res(S, CT, Z['zT'], I['rwkv_w_o'][layer], hT, T),
    ]
    if not RUN_SCAN:
        steps = steps[:-2]
    for i, st_ in enumerate(steps):
        if i < nph:
            st_()


SB_SCALE = 128 ** -0.5


def phase_gather_q(S, CT, hT, hqT, NQB):
    nc = S.nc
    with ExitStack() as es:
        r = Ring(S, es, 'sb', 'gq_t', [128, KC, 128], F32, 3)
        pid = nc.sync.partition_id()
        off = (pid % 2) * 128 + NMETA
        hv = fm(hT)
        qv = fm(hqT)
        for i in range(NQB):
            t, b = r.next()
            S.dma('sp', lambda e: e.dma_start(out=t[:], in_=hv[:, :, bass.ds(off + 256 * i, 128)]), writes=[b], owner=b)
            S.dma('sp', lambda e: e.dma_start(out=qv[:, :, i * 128:(i + 1) * 128], in_=t[:]), reads=[b], owner=b)
        S.barrier()


def phase_headnorm_fm(S, CT, xsrc, W, gain1d, out, Tn):
    nc = S.nc
    with ExitStack() as es:
        cb = S.buf('c')
        X, Xb = load_resident(S, es, 'hn_x', xsrc, KC, Tn)
        gcol = es.enter_context(nc.sbuf_tensor(un('hn_g'), [128, 1], F32))
        gb = S.buf('g')
        S.dma('sp', lambda e: e.dma_start(out=gcol[:], in_=gain1d.rearrange("(p o) -> p o", o=1)), writes=[gb], owner=gb)
        wr = Ring(S, es, 'sb', 'hn_w', [128, KC, 128], BF16, 2)
        sr = Ring(S, es, 'sb', 'hn_sq', [128, 512], F32, 2)
        rr = Ring(S, es, 'sb', 'hn_rs', [128, 512], F32, 2)
        orr = Ring(S, es, 'sb', 'hn_o', [128, Tn], BF16, 2)
        pr = Ring(S, es, 'ps', 'hn_p', [128, 512], F32, 3)
        pr2 = Ring(S, es, 'ps', 'hn_p2', [128, 512], F32, 2)
        wv = wview(W)
        for h in range(SH):
            w, wb = wr.next()
            S.dma('pool', lambda e: e.dma_start(out=w[:], in_=wv[:, :, h * 128:(h + 1) * 128]), writes=[wb], owner=wb)
            o, ob = orr.next()
            pend = None
            for (t0, tn) in groups(Tn, 512):
                ps, pb = pr.next()
                for kc in range(KC):
                    S.op('pe', lambda e: e.matmul(ps[:, 0:tn], w[:, kc, :], X[:, kc, t0:t0 + tn], start=(kc == 0), stop=(kc == KC - 1)),
                         reads=[wb, Xb], writes=[pb], signal=(kc == KC - 1))

                def epi(ps=ps, pb=pb, t0=t0, tn=tn, o=o, ob=ob):
                    sq, sqb = sr.next()
                    S.op('act', lambda e: e.activation(out=sq[:, 0:tn], in_=ps[:, 0:tn], func=AF.Square), reads=[pb], writes=[sqb])
                    p2, p2b = pr2.next()
                    S.op('pe', lambda e: e.matmul(p2[:, 0:tn], CT['ones'][:], sq[:, 0:tn], start=True, stop=True), reads=[sqb, cb], writes=[p2b])
                    rs, rb = rr.next()
                    S.op('act', lambda e: e.activation(out=rs[:, 0:tn], in_=p2[:, 0:tn], func=AF.Ln, bias=CT['eps'][:, 0:1], scale=1.0 / 128),
                         reads=[p2b, cb], writes=[rb])
                    S.op('act', lambda e: e.activation(out=rs[:, 0:tn], in_=rs[:, 0:tn], func=AF.Exp, scale=-0.5), reads=[rb], writes=[rb])
                    S.op('dve', lambda e: e.scalar_tensor_tensor(out=o[:, t0:t0 + tn], in0=ps[:, 0:tn], scalar=gcol[:, 0:1], in1=rs[:, 0:tn],
                                                                 op0=ALU.mult, op1=ALU.mult), reads=[pb, rb, gb], writes=[ob], disjoint=(t0 > 0))
                if pend is not None:
                    pend()
                pend = epi
            pend()
            S.dma('sp', lambda e: e.dma_start(out=out[h, :, 0:Tn], in_=o[:]), reads=[ob], owner=ob)
        S.barrier()


def phase_attention(S, CT, I, KT, Vtm, QT, OT, NXB):
    nc = S.nc
    NQB = NXB // 2
    NG = NQB // 4
    TQ = NQB * 128
    Tk = NMETA + 128 * NXB
    with ExitStack() as es:
        am = es.enter_context(nc.sbuf_tensor(un('at_am'), [128, 8, 512], F32))
        amb_ = es.enter_context(nc.sbuf_tensor(un('at_amb'), [128, 8, 512], BF16))
        tm = es.enter_context(nc.sbuf_tensor(un('at_tm'), [128, 2, 128], F32))
        mb = S.buf('am')
        S.dma('sp', lambda e: e.dma_start(out=am[:], in_=I['amask'].rearrange("j p q -> p j q")), writes=[mb], owner=mb)
        S.dma('pool', lambda e: e.dma_start(out=amb_[:], in_=I['amask'].rearrange("j p q -> p j q")), writes=[mb], owner=mb)
        S.dma('sp', lambda e: e.dma_start(out=tm[:], in_=I['tmask'].rearrange("j p q -> p j q")), writes=[mb], owner=mb)
        kr = Ring(S, es, 'sb', 'at_k', [128, Tk], BF16, 2)
        vr = Ring(S, es, 'sb', 'at_v', [128, NXB, 128], BF16, 2)
        vmr = Ring(S, es, 'sb', 'at_vm', [NMETA, 128], BF16, 2)
        qr = Ring(S, es, 'sb', 'at_q', [128, TQ], BF16, 2)
        outr = Ring(S, es, 'sb', 'at_o', [128, TQ], BF16, 2)
        Er = Ring(S, es, 'sb', 'at_e', [128, 512], F32, 2)
        SPr = Ring(S, es, 'sb', 'at_sp', [128, 512], F32, 4)
        T1r = Ring(S, es, 'sb', 'at_t1', [128, 512], F32, 2)
        Wr = Ring(S, es, 'sb', 'at_w', [128, 512], BF16, 3)
        Rr = Ring(S, es, 'sb', 'at_r', [128, 512], F32, 4)
        PZ = Ring(S, es, 'ps', 'at_pz', [128, 512], F32, 3)
        PL = Ring(S, es, 'ps', 'at_pl', [128, 512], F32, 2)
        PO = Ring(S, es, 'ps', 'at_po', [128, 512], F32, 2)
        tiles = []
        for h in range(SH):
            for g in range(NG):
                blocks = list(range(8 * g + 7, -1, -1)) + [-1]
                for bi, kb in enumerate(blocks):
                    tiles.append(dict(h=h, g=g, kb=kb, first=(bi == 0), last=(kb < 0), hfirst=(g == 0 and bi == 0), hlast=(g == NG - 1 and kb < 0)))
        hd = {}
        gd = {}

        def stA(t):
            h, g, kb = t['h'], t['g'], t['kb']
            if t['hfirst']:
                k, kb_ = kr.next()
                S.dma('sp', lambda e: e.dma_start(out=k[:], in_=KT[h, :, 0:Tk]), writes=[kb_], owner=kb_)
                v, vb = vr.next()
                S.dma('sp', lambda e: e.dma_start(out=v[:], in_=Vtm[NMETA:NMETA + 128 * NXB, h * 128:(h + 1) * 128].rearrange("(kb p) d -> p kb d", p=128)),
                      writes=[vb], owner=vb)
                vm, vmb = vmr.next()
                S.dma('sp', lambda e: e.dma_start(out=vm[:], in_=Vtm[0:NMETA, h * 128:(h + 1) * 128]), writes=[vmb], owner=vmb)
                q, qb_ = qr.next()
                S.dma('sp', lambda e: e.dma_start(out=q[:], in_=QT[h, :, 0:TQ]), writes=[qb_], owner=qb_)
                o, ob = outr.next()
                hd[h] = (k, kb_, v, vb, vm, vmb, q, qb_, o, ob)
            k, kb_, v, vb, vm, vmb, q, qb_, o, ob = hd[h]
            if t['first']:
                gd[(h, g)] = dict(po=PO.next(), R=None)
            meta = t['last']
            nk = NMETA if meta else 128
            kcols = slice(0, NMETA) if meta else slice(NMETA + 128 * kb, NMETA + 128 * (kb + 1))
            qs = slice(g * 512, (g + 1) * 512)
            masked = (not meta) and kb >= 8 * g
            j = kb - 8 * g
            pz, pzb = PZ.next()
            S.op('pe', lambda e: e.matmul(pz[0:nk, :], k[:, kcols], q[:, qs], start=True, stop=True), reads=[kb_, qb_], writes=[pzb])
            E, Eb = Er.next()
            S.op('act', lambda e: e.activation(out=E[0:nk, :], in_=pz[0:nk, :], func=AF.Exp, scale=SB_SCALE), reads=[pzb], writes=[Eb])
            sp, spb = SPr.next()
            S.op('act', lambda e: e.activation(out=sp[0:nk, :], in_=E[0:nk, :], func=AF.Ln, bias=1.0, scale=1.0), reads=[Eb], writes=[spb])
            if masked:
                S.op('pool', lambda e: e.tensor_tensor(out=sp[:, :], in0=sp[:, :], in1=am[:, j, :], op=ALU.mult), reads=[mb, spb], writes=[spb])
            t.update(nk=nk, pz=pz, pzb=pzb, sp=sp, spb=spb, masked=masked, j=j, meta=meta)

        def stB(t):
            h, g = t['h'], t['g']
            G = gd[(h, g)]
            nk, pz, pzb, sp, spb, first = t['nk'], t['pz'], t['pzb'], t['sp'], t['spb'], t['first']
            pl, plb = PL.next()
            S.op('pe', lambda e: e.matmul(pl[0:nk, :], tm[0:nk, 0, 0:nk], sp[0:nk, :], start=True, stop=first), reads=[mb, spb], writes=[plb], signal=first)
            if not first:
                R, Rb = G['R']
                S.op('pe', lambda e: e.matmul(pl[0:nk, :], tm[:, 1, 0:nk], R[:, :], start=False, stop=True), reads=[mb, Rb], writes=[plb])
            if not t['meta']:
                Rn, Rnb = Rr.next()
                if first:
                    S.op('pool', lambda e: e.tensor_copy(out=Rn[:, :], in_=sp[:, :]), reads=[spb], writes=[Rnb])
                else:
                    R, Rb = G['R']
                    S.op('pool', lambda e: e.tensor_tensor(out=Rn[:, :], in0=R[:, :], in1=sp[:, :], op=ALU.add), reads=[spb, Rb], writes=[Rnb])
                G['R'] = (Rn, Rnb)
            t1, t1b = T1r.next()
            S.op('dve', lambda e: e.scalar_tensor_tensor(out=t1[0:nk, :], in0=pz[0:nk, :], scalar=SB_SCALE, in1=sp[0:nk, :],
                                                         op0=ALU.mult, op1=ALU.subtract), reads=[pzb, spb], writes=[t1b])
            S.op('dve', lambda e: e.tensor_tensor(out=t1[0:nk, :], in0=t1[0:nk, :], in1=pl[0:nk, :], op=ALU.add), reads=[plb, t1b], writes=[t1b])
            w, wb = Wr.next()
            S.op('act', lambda e: e.activation(out=w[0:nk, :], in_=t1[0:nk, :], func=AF.Exp), reads=[t1b], writes=[wb])
            if t['masked']:
                j = t['j']
                S.op('pool', lambda e: e.tensor_tensor(out=w[:, :], in0=w[:, :], in1=amb_[:, j, :], op=ALU.mult), reads=[mb, wb], writes=[wb])
            t.update(w=w, wb=wb)

        def stC(t):
            h, g, kb = t['h'], t['g'], t['kb']
            k, kb_, v, vb, vm, vmb, q, qb_, o, ob = hd[h]
            po, pob = gd[(h, g)]['po']
            w, wb, nk, first = t['w'], t['wb'], t['nk'], t['first']
            if t['meta']:
                S.op('pe', lambda e: e.matmul(po[:, :], vm[:, :], w[0:nk, :], start=first, stop=True), reads=[vmb, wb], writes=[pob])
                qs = slice(g * 512, (g + 1) * 512)
                S.op('act', lambda e: e.copy(out=o[:, qs], in_=po[:, :]), reads=[pob], writes=[ob])
                if t['hlast']:
                    S.dma('sp', lambda e: e.dma_start(out=OT[h, :, 0:TQ], in_=o[:]), reads=[ob], owner=ob)
            else:
                S.op('pe', lambda e: e.matmul(po[:, :], v[:, kb, :], w[:, :], start=first, stop=False), reads=[vb, wb], writes=[pob], signal=False)

        nt = len(tiles)
        for step in range(nt + 2):
            if step < nt:
                stA(tiles[step])
            if 0 <= step - 1 < nt:
                stB(tiles[step - 1])
            if 0 <= step - 2 < nt:
                stC(tiles[step - 2])
        S.barrier()


def att_masks(parity):
    am = np.zeros((8, 128, 512), np.float32)
    p = np.arange(128)
    for j in range(8):
        for i in range(4):
            qb = 2 * i + parity
            if j < qb:
                am[j, :, i * 128:(i + 1) * 128] = 1.0
            elif j == qb:
                am[j, :, i * 128:(i + 1) * 128] = (p[:, None] < p[None, :])
    tmk = np.zeros((2, 128, 128), np.float32)
    tmk[0] = -(p[:, None] > p[None, :]).astype(np.float32)
    tmk[1] = -1.0
    return am, tmk


def const_masks():
    cm = np.zeros((4, 128, 128), np.float32)
    i = np.arange(128)
    cm[0] = np.where(i[:, None] <= i[None, :], -DEC_C, 0.0)
    cm[1] = (i[:, None] < i[None, :])
    cm[2] = (i[:, None] <= i[None, :])
    cm[3] = (i[:, None] > i[None, :])
    return cm


IN_SPECS = [
    ('cols', [128, NCOLS, KC]), ('ident', [128, 128]), ('cmask', [4, 128, 128]),
    ('ffn_w_gate', [4, D, FF]), ('ffn_w_up', [4, D, FF]), ('ffn_w_down', [4, FF, D]),
    ('rwkv_w_r', [2, D, D]), ('rwkv_w_k', [2, D, D]), ('rwkv_w_v', [2, D, D]), ('rwkv_w_o', [2, D, D]),
    ('rwkv_dec_w0', [2, D]), ('rwkv_dec_w1', [2, D, 96]), ('rwkv_dec_w2', [2, 96, D]),
    ('rwkv_a_w0', [2, D]), ('rwkv_a_w1', [2, D, 96]), ('rwkv_a_w2', [2, 96, D]),
    ('rwkv_g_w1', [2, D, 256]), ('rwkv_g_w2', [2, 256, D]),
    ('rwkv_k_k', [2, D]), ('rwkv_k_a', [2, D]), ('rwkv_r_k', [2, 32, 64]), ('rwkv_gn_w', [2, D]), ('rwkv_gn_b', [2, D]),
    ('rwkv_v_w0', [1, D]), ('rwkv_v_w1', [1, D, 64]), ('rwkv_v_w2', [1, 64, D]),
    ('amask', [8, 128, 512]), ('tmask', [2, 128, 128]),
    ('sb_w_k', [D, D]), ('sb_w_v', [D, D]), ('sb_k_gain', [128]), ('sb_w_q', [2, D, D]), ('sb_q_gain', [2, 128]), ('sb_w_o', [2, D, D]),
]


def build(NXB, mode='full', dbg=()):
    T = 128 * (NXB + 1)
    nc = bass.Bass("TRN2", target_bir_lowering=False)
    I = {}
    I['xin'] = nc.dram_tensor('xin', [T, D], F32, kind="ExternalInput").ap()
    for name, shape in IN_SPECS:
        I[name] = nc.dram_tensor(name, list(shape), F32, kind="ExternalInput").ap()

    def scratch(name, shape, dt):
        kind = "ExternalOutput" if name in dbg else "Internal"
        return nc.dram_tensor(name, list(shape), dt, kind=kind).ap()

    hT = scratch('hT', [KC, 128, T], F32)
    xn = scratch('xn', [KC, 128, T], BF16)
    actT = scratch('actT', [FC, 128, T], BF16)
    Z = {'xmix': [scratch('xmix%d' % i, [KC, 128, T], BF16) for i in range(6)]}
    for n in ('r', 'k', 'v', 'vfirst', 'a', 'g', 'vg', 'vv', 'bonus', 'kt', 'bn'):
        Z[n] = scratch('z_' + n, [T, D], BF16)
    Z['sigd'] = scratch('z_sigd', [T, D], F32)
    Z['gam'] = scratch('z_gam', [T // 128, 128, KC], F32)
    Z['zT'] = scratch('z_zT', [KC, 128, T], BF16)
    for n in ('rT', 'kpT', 'ktT', 'bnT'):
        Z[n] = scratch('z_' + n, [T // 128, 64, RH, 128], BF16)
    NQB = NXB // 2
    TQ = NQB * 128
    hqT = scratch('hqT', [KC, 128, TQ], F32)
    KT = scratch('KT', [SH, 128, T], BF16)
    Vtm = scratch('Vtm', [T, D], BF16)
    QT = scratch('QT', [SH, 128, TQ], BF16)
    OT = scratch('OT', [SH, 128, TQ], BF16)
    full = mode in ('full', 'att_test')
    out = nc.dram_tensor('out', [TQ if full else T, D], F32, kind="ExternalOutput").ap()

    def ffn(S, CT, layer, hres, Tn):
        phase_norm(S, CT, hres, I['cols'], COLS[('ffn_norm_g', layer)], xn, Tn)
        phase_ffn_gateup(S, CT, xn, I['ffn_w_gate'][layer], I['ffn_w_up'][layer], actT, Tn)
        phase_ffn_down(S, CT, actT, I['ffn_w_down'][layer], hres, Tn)

    with ExitStack() as es:
        S = Sched(nc, es)
        CT = load_consts(S, es, I)
        phase_in_transpose(S, CT, I['xin'], hT, T)
        if mode == 'ffn_test':
            ffn(S, CT, 0, hT, T)
        if mode == 'rwkv_test':
            rwkv_layer(S, CT, I, 0, Z, hT, T)
        if mode == 'full':
            rwkv_layer(S, CT, I, 0, Z, hT, T)
            ffn(S, CT, 0, hT, T)
            rwkv_layer(S, CT, I, 1, Z, hT, T)
            ffn(S, CT, 1, hT, T)
        if mode == 'rwkv2_test':
            rwkv_layer(S, CT, I, 0, Z, hT, T)
            ffn(S, CT, 0, hT, T)
            rwkv_layer(S, CT, I, 1, Z, hT, T)
        if full:
            phase_norm(S, CT, hT, I['cols'], COLS[('kv_norm_g', 0)], xn, T)
            phase_headnorm_fm(S, CT, xn, I['sb_w_k'], I['sb_k_gain'], KT, T)
            phase_proj_tm(S, CT, xn, I['sb_w_v'], Vtm, T, BF16)
            phase_gather_q(S, CT, hT, hqT, NQB)
            for j in range(2):
                phase_norm(S, CT, hqT, I['cols'], COLS[('mix_norm_g', 2 + j)], xn, TQ)
                phase_headnorm_fm(S, CT, xn, I['sb_w_q'][j], I['sb_q_gain'][j], QT, TQ)
                phase_attention(S, CT, I, KT, Vtm, QT, OT, NXB)
                phase_proj_fm_res(S, CT, OT, I['sb_w_o'][j], hqT, TQ)
                ffn(S, CT, 2 + j, hqT, TQ)
            phase_out_transpose(S, CT, hqT, out, TQ, 0)
        else:
            phase_out_transpose(S, CT, hT, out, T, 0)
        print("instructions:", S.ninst)
    return nc


def host_inputs(inputs):
    d = {k: np.ascontiguousarray(np.asarray(v), dtype=np.float32) for k, v in inputs.items()}
    base = {'cols': pack_cols(d), 'ident': np.eye(128, dtype=np.float32), 'cmask': const_masks(), 'tmask': att_masks(0)[1], 'amask': att_masks(0)[0]}
    for name, shape in IN_SPECS:
        if name not in base:
            base[name] = d[name].reshape(shape)
    return base


def kernel(**inputs):
    NXB = 32
    T = 128 * (NXB + 1)
    base = host_inputs(inputs)
    x = np.asarray(inputs['x'], np.float32)
    meta = np.asarray(inputs['meta_tokens'], np.float32)
    in_maps = []
    for c in range(8):
        b = c // 2
        xin = np.zeros((T, D), np.float32)
        xin[:NMETA] = meta
        xin[NMETA:NMETA + 4096] = x[b]
        m = dict(base)
        m['xin'] = xin
        m['amask'] = att_masks(c % 2)[0]
        in_maps.append(m)
    nc = build(NXB, mode='full')
    res = run_bass_kernel_spmd(nc, in_maps, core_ids=list(range(8)))
    out = np.zeros((4, 4096, D), np.float32)
    for c in range(8):
        o = np.asarray(res.results[c]['out']).reshape(NXB // 2, 128, D)
        out[c // 2].reshape(NXB // 2, 2, 128, D)[:, c % 2] = o
    return out
```
